# Optimizing a Trainium2 kernel written in Bass

```python
import jax, jax.numpy as jnp
from jax import lax
import numpy as np

D_MODEL = 1024
BATCH = 8
SEQ = 2048
DEPTH = 1
DEC_BATCH = 128
DEC_SEQ = 4
PAST_LEN = 8192
PAGE_SIZE = 128

M_HEADS = 4
M_DQK = 128
M_DV = 256
M_CHUNK = 64
A_HEADS = 16
A_KV_HEADS = 4
A_DH = 64
WINDOW = 128
ROPE_THETA = 10000.0
ATTN_SCALE = A_DH ** -0.5
D_FF = 2816
LN_EPS = 1e-5
HEAD_NORM_EPS = 1e-6
ADA_CHUNKS = 9
DEEPNORM_ALPHA = (2.0 * DEPTH) ** 0.25
DEEPNORM_BETA = (8.0 * DEPTH) ** -0.25

SPLITS = (M_HEADS * M_DQK, M_HEADS * M_DQK, M_HEADS * M_DV, M_HEADS * M_DV, M_HEADS, M_HEADS,
          A_HEADS * A_DH, A_KV_HEADS * A_DH, A_KV_HEADS * A_DH, D_MODEL, D_MODEL)
D_IN = sum(SPLITS)
SPLIT_IDX = tuple(np.cumsum(SPLITS)[:-1].tolist())

kernel_name = "hybrid_mlstm_swa_macaron_deepnorm_step"


def layer_norm(x, g, b):
    xf = x.astype(jnp.float32)
    mu = xf.mean(-1, keepdims=True)
    var = jnp.mean(jnp.square(xf - mu), -1, keepdims=True)
    return ((xf - mu) * lax.rsqrt(var + LN_EPS)).astype(x.dtype) * g + b


def head_norm(h, g):
    hf = h.astype(jnp.float32)
    mu = hf.mean(-1, keepdims=True)
    var = jnp.mean(jnp.square(hf - mu), -1, keepdims=True)
    y = ((hf - mu) * lax.rsqrt(var + HEAD_NORM_EPS)).astype(h.dtype)
    B, S, H, DV = h.shape
    return y.reshape(B, S, H * DV) * g


def modulate(x, shift, scale):
    return x * (1 + scale) + shift


def swiglu(h, w_up, w_down):
    a, u = jnp.split(h @ w_up, 2, axis=-1)
    return (jax.nn.silu(a) * u) @ w_down


def rope(x, pos):
    half = x.shape[-1] // 2
    inv = ROPE_THETA ** (-jnp.arange(half, dtype=jnp.float32) / half)
    ang = pos.astype(jnp.float32)[:, None] * inv[None, :]
    cos = jnp.cos(ang)[:, None, :].astype(x.dtype)
    sin = jnp.sin(ang)[:, None, :].astype(x.dtype)
    x1, x2 = x[..., :half], x[..., half:]
    return jnp.concatenate([x1 * cos - x2 * sin, x2 * cos + x1 * sin], axis=-1)


def sink_softmax(scores, sinks):
    s = sinks.astype(jnp.float32)[..., None, None]
    mx = jnp.maximum(scores.max(-1, keepdims=True), s)
    p = jnp.exp(scores - mx)
    return p / (p.sum(-1, keepdims=True) + jnp.exp(s - mx))


def mlstm_chunkwise(q, k, v, i_pre, logf, C0, n0, m0, chunk):
    B, S, H, _ = q.shape
    nc = S // chunk

    def to_chunks(t):
        return jnp.moveaxis(t.reshape((B, nc, chunk) + t.shape[2:]), 1, 0)

    xs = tuple(to_chunks(t) for t in (q, k, v, i_pre, logf))
    causal = jnp.tril(jnp.ones((chunk, chunk), dtype=bool))

    def step(carry, inp):
        C, n, m = carry
        qc, kc, vc, ic, fc = inp
        b = jnp.cumsum(fc, axis=1)
        logD = b[:, :, None, :] - b[:, None, :, :] + ic[:, None, :, :]
        logD = jnp.where(causal[None, :, :, None], logD, -jnp.inf)
        inter = b + m[:, None, :]
        m_t = jnp.maximum(inter, logD.max(axis=2))
        dmat = jnp.exp(logD - m_t[:, :, None, :])
        e_inter = jnp.exp(inter - m_t)
        s = jnp.einsum('bthd,bshd->btsh', qc, kc) * dmat
        num = jnp.einsum('btsh,bshv->bthv', s, vc) + e_inter[..., None] * jnp.einsum('bhvd,bthd->bthv', C, qc)
        den = s.sum(axis=2) + e_inter * jnp.einsum('bhd,bthd->bth', n, qc)
        h = num / jnp.maximum(jnp.abs(den), jnp.exp(-m_t))[..., None]
        m_new = m_t[:, -1]
        w_s = jnp.exp(b[:, -1:, :] - b + ic - m_new[:, None, :])
        e_c = jnp.exp(b[:, -1] + m - m_new)
        C_new = e_c[..., None, None] * C + jnp.einsum('bsh,bshv,bshd->bhvd', w_s, vc, kc)
        n_new = e_c[..., None] * n + jnp.einsum('bsh,bshd->bhd', w_s, kc)
        return (C_new.astype(C.dtype), n_new.astype(n.dtype), m_new.astype(m.dtype)), h

    (C, n, m), hs = lax.scan(step, (C0, n0, m0), xs)
    h = jnp.moveaxis(hs, 0, 1).reshape(B, S, H, -1)
    return h, C, n, m


def swa_prompt(q, k, v, sinks):
    B, S, HQ, D = q.shape
    R = HQ // A_KV_HEADS
    nb = S // WINDOW
    qb = q.reshape(B, nb, WINDOW, A_KV_HEADS, R, D)
    kb = k.reshape(B, nb, WINDOW, A_KV_HEADS, D)
    vb = v.reshape(B, nb, WINDOW, A_KV_HEADS, D)
    padw = ((0, 0), (1, 0), (0, 0), (0, 0), (0, 0))
    kk = jnp.concatenate([jnp.pad(kb[:, :-1], padw), kb], axis=2)
    vv = jnp.concatenate([jnp.pad(vb[:, :-1], padw), vb], axis=2)
    scores = jnp.einsum('bnqgrd,bnkgd->bngrqk', qb, kk).astype(jnp.float32) * ATTN_SCALE
    qi = WINDOW + jnp.arange(WINDOW)[:, None]
    ki = jnp.arange(2 * WINDOW)[None, :]
    delta = qi - ki
    band = (delta >= 0) & (delta < WINDOW)
    has_prev = (jnp.arange(nb) > 0)[:, None, None] | (ki >= WINDOW)[None]
    mask = band[None] & has_prev
    scores = jnp.where(mask[None, :, None, None], scores, -jnp.inf)
    pr = sink_softmax(scores, sinks.reshape(A_KV_HEADS, R))
    o = jnp.einsum('bngrqk,bnkgd->bnqgrd', pr.astype(vv.dtype), vv).reshape(B, S, HQ * D)
    nbuf = min(WINDOW, S)
    return o, k[:, S - nbuf:], v[:, S - nbuf:]


def swa_sample(q, k, v, kbuf, vbuf, sinks):
    B, T, HQ, D = q.shape
    R = HQ // A_KV_HEADS
    Wb = kbuf.shape[1]
    kk = jnp.concatenate([kbuf, k.astype(kbuf.dtype)], axis=1)
    vv = jnp.concatenate([vbuf, v.astype(vbuf.dtype)], axis=1)
    qg = q.reshape(B, T, A_KV_HEADS, R, D)
    scores = jnp.einsum('btgrd,bkgd->bgrtk', qg, kk).astype(jnp.float32) * ATTN_SCALE
    delta = (Wb + jnp.arange(T))[:, None] - jnp.arange(Wb + T)[None, :]
    mask = (delta >= 0) & (delta < WINDOW)
    scores = jnp.where(mask, scores, -jnp.inf)
    pr = sink_softmax(scores, sinks.reshape(A_KV_HEADS, R))
    o = jnp.einsum('bgrtk,bkgd->btgrd', pr.astype(vv.dtype), vv).reshape(B, T, HQ * D)
    return o, kk[:, -Wb:], vv[:, -Wb:]


def decoder_layer(x, c, pos, C0, n0, m0, kbuf, vbuf, chunk, p):
    B, S, _ = x.shape
    mod = jax.nn.silu(c) @ p['w_ada'] + p['b_ada']
    sh1, sc1, g1, sh2, sc2, g2, sh3, sc3, g3 = jnp.split(mod[:, None, :], ADA_CHUNKS, axis=-1)
    f = swiglu(modulate(x, sh1, sc1), p['w_ffn1_up'], p['w_ffn1_down'])
    x = layer_norm(DEEPNORM_ALPHA * x + 0.5 * (1 + g1) * f, p['ln1_g'], p['ln1_b'])
    h = modulate(x, sh2, sc2)
    mq, mk, mv, mo, mi, mf, aq, ak, av, gm, ga = jnp.split(h @ p['w_in'], SPLIT_IDX, axis=-1)
    qm = mq.reshape(B, S, M_HEADS, M_DQK)
    km = mk.reshape(B, S, M_HEADS, M_DQK) * (M_DQK ** -0.5)
    vm = mv.reshape(B, S, M_HEADS, M_DV)
    i_pre = (mi + p['b_igate']).astype(jnp.float32)
    logf = jax.nn.log_sigmoid((mf + p['b_fgate']).astype(jnp.float32))
    hm, C, n, m = mlstm_chunkwise(qm, km, vm, i_pre, logf, C0, n0, m0, chunk)
    ym = (head_norm(hm, p['m_norm_g']) * jax.nn.sigmoid(mo)) @ p['w_branch_m']
    qa = rope(aq.reshape(B, S, A_HEADS, A_DH), pos)
    ka = rope(ak.reshape(B, S, A_KV_HEADS, A_DH), pos)
    va = av.reshape(B, S, A_KV_HEADS, A_DH)
    if kbuf is None:
        oa, kb_new, vb_new = swa_prompt(qa, ka, va, p['sinks'])
    else:
        oa, kb_new, vb_new = swa_sample(qa, ka, va, kbuf, vbuf, p['sinks'])
    ya = oa @ p['w_branch_a']
    mix = jax.nn.sigmoid(gm) * ym + jax.nn.sigmoid(ga) * ya
    x = layer_norm(DEEPNORM_ALPHA * x + (1 + g2) * (mix @ p['w_out']), p['ln2_g'], p['ln2_b'])
    f = swiglu(modulate(x, sh3, sc3), p['w_ffn2_up'], p['w_ffn2_down'])
    x = layer_norm(DEEPNORM_ALPHA * x + 0.5 * (1 + g3) * f, p['ln3_g'], p['ln3_b'])
    return x, (C, n, m, kb_new, vb_new)


def setup_inputs(seed: int = 0) -> dict:
    key = jax.random.key(seed)
    keys = jax.random.split(key, 32)

    def nrm(i, shape, scale):
        return jax.random.normal(keys[i], shape, jnp.float32) * scale

    L, D = DEPTH, D_MODEL
    wbuf = min(WINDOW, PAST_LEN)
    fin = D ** -0.5
    return {
        'x_prompt': nrm(0, (BATCH, SEQ, D), 1.0),
        'x_sample': nrm(1, (DEC_BATCH, DEC_SEQ, D), 1.0),
        'state_mlstm_C': nrm(2, (L, DEC_BATCH, M_HEADS, M_DV, M_DQK), 0.5),
        'state_mlstm_n': nrm(3, (L, DEC_BATCH, M_HEADS, M_DQK), 0.5),
        'state_mlstm_m': nrm(4, (L, DEC_BATCH, M_HEADS), 1.0),
        'cache_swa_k': nrm(5, (L, DEC_BATCH, wbuf, A_KV_HEADS, A_DH), 1.0),
        'cache_swa_v': nrm(6, (L, DEC_BATCH, wbuf, A_KV_HEADS, A_DH), 1.0),
        'c_prompt': nrm(7, (BATCH, D), 1.0),
        'c_sample': nrm(8, (DEC_BATCH, D), 1.0),
        'w_ada': nrm(9, (L, D, ADA_CHUNKS * D), 0.1 * fin),
        'b_ada': nrm(10, (L, ADA_CHUNKS * D), 0.01),
        'w_ffn1_up': nrm(11, (L, D, 2 * D_FF), fin),
        'w_ffn1_down': nrm(12, (L, D_FF, D), DEEPNORM_BETA * D_FF ** -0.5),
        'ln1_g': 1.0 + nrm(13, (L, D), 0.02),
        'ln1_b': nrm(14, (L, D), 0.02),
        'w_in': nrm(15, (L, D, D_IN), fin),
        'b_igate': nrm(16, (L, M_HEADS), 0.1),
        'b_fgate': 3.0 + 3.0 * jax.random.uniform(keys[17], (L, M_HEADS), jnp.float32),
        'm_norm_g': 1.0 + nrm(18, (L, M_HEADS * M_DV), 0.02),
        'sinks': nrm(19, (L, A_HEADS), 0.5),
        'w_branch_m': nrm(20, (L, M_HEADS * M_DV, D), DEEPNORM_BETA * (M_HEADS * M_DV) ** -0.5),
        'w_branch_a': nrm(21, (L, A_HEADS * A_DH, D), DEEPNORM_BETA * (A_HEADS * A_DH) ** -0.5),
        'w_out': nrm(22, (L, D, D), DEEPNORM_BETA * fin),
        'ln2_g': 1.0 + nrm(23, (L, D), 0.02),
        'ln2_b': nrm(24, (L, D), 0.02),
        'w_ffn2_up': nrm(25, (L, D, 2 * D_FF), fin),
        'w_ffn2_down': nrm(26, (L, D_FF, D), DEEPNORM_BETA * D_FF ** -0.5),
        'ln3_g': 1.0 + nrm(27, (L, D), 0.02),
        'ln3_b': nrm(28, (L, D), 0.02),
    }


def reference(x_prompt, x_sample, state_mlstm_C, state_mlstm_n, state_mlstm_m, cache_swa_k, cache_swa_v,
              c_prompt, c_sample, w_ada, b_ada, w_ffn1_up, w_ffn1_down, ln1_g, ln1_b, w_in, b_igate,
              b_fgate, m_norm_g, sinks, w_branch_m, w_branch_a, w_out, ln2_g, ln2_b, w_ffn2_up,
              w_ffn2_down, ln3_g, ln3_b):
    Bp, Sp, _ = x_prompt.shape
    Ts = x_sample.shape[1]
    pos_p = jnp.arange(Sp)
    pos_s = PAST_LEN + jnp.arange(Ts)
    yp, ys = x_prompt, x_sample
    outs_p, outs_s = [], []
    for l in range(DEPTH):
        p = dict(w_ada=w_ada[l], b_ada=b_ada[l], w_ffn1_up=w_ffn1_up[l], w_ffn1_down=w_ffn1_down[l],
                 ln1_g=ln1_g[l], ln1_b=ln1_b[l], w_in=w_in[l], b_igate=b_igate[l], b_fgate=b_fgate[l],
                 m_norm_g=m_norm_g[l], sinks=sinks[l], w_branch_m=w_branch_m[l],
                 w_branch_a=w_branch_a[l], w_out=w_out[l], ln2_g=ln2_g[l], ln2_b=ln2_b[l],
                 w_ffn2_up=w_ffn2_up[l], w_ffn2_down=w_ffn2_down[l], ln3_g=ln3_g[l], ln3_b=ln3_b[l])
        C0 = jnp.zeros((Bp, M_HEADS, M_DV, M_DQK), x_prompt.dtype)
        n0 = jnp.zeros((Bp, M_HEADS, M_DQK), x_prompt.dtype)
        m0 = jnp.zeros((Bp, M_HEADS), x_prompt.dtype)
        yp, st_p = decoder_layer(yp, c_prompt, pos_p, C0, n0, m0, None, None, min(M_CHUNK, Sp), p)
        ys, st_s = decoder_layer(ys, c_sample, pos_s, state_mlstm_C[l], state_mlstm_n[l], state_mlstm_m[l],
                                 cache_swa_k[l], cache_swa_v[l], Ts, p)
        outs_p.append(st_p)
        outs_s.append(st_s)
    C_p, n_p, m_p, k_p, v_p = [jnp.stack(t) for t in zip(*outs_p)]
    C_s, n_s, m_s, k_s, v_s = [jnp.stack(t) for t in zip(*outs_s)]
    return (yp, ys, C_p, n_p, m_p, k_p, v_p, C_s, n_s, m_s, k_s, v_s)
```

```python
import os
import numpy as np
from contextlib import ExitStack
import concourse.bass as bass
import concourse.mybir as mybir
from concourse.bass_utils import run_bass_kernel_spmd

F32 = mybir.dt.float32
BF16 = mybir.dt.bfloat16
AF = mybir.ActivationFunctionType
ALU = mybir.AluOpType
AX = mybir.AxisListType

D = 1024
SEQ = 2048
NS = 64
NT = SEQ + NS
NTILE = 17
DFF = 2816
NFC = 22
ALPHA = 2.0 ** 0.25
LN_EPS = 1e-5
HN_EPS = 1e-6
PAST = 8192
SEM_LIMIT = 28000
TBLK = [(0, 512), (512, 512), (1024, 512), (1536, 512), (2048, 64)]


def tile_nt(i):
    return 128 if i < 16 else 64


def blk_tiles(b):
    return [4 * b + j for j in range(4)] if b < 4 else [16]


class Dep:
    __slots__ = ("w", "r")

    def __init__(self):
        self.w = {}
        self.r = {}


class Sched:
    ENGS = ("pe", "act", "dve", "pool", "sp")

    def __init__(self, nc, es):
        self.nc = nc
        self.es = es
        self.ops = {e: [] for e in self.ENGS}
        self.cnt = {e: 0 for e in self.ENGS}
        self.csem = {e: None for e in self.ENGS}
        self.seen = {e: {} for e in self.ENGS}
        self.nsem = 0
        self.dstates = []
        self.misc = {e: [[None, 0] for _ in range(16)] for e in ("sp", "pool", "act")}
        for e in self.misc:
            for m in self.misc[e]:
                self.dstates.append(m)
        self.misc_i = {e: 0 for e in self.misc}
        self.final = []

    def new_sem(self, name):
        self.nsem += 1
        return self.es.enter_context(self.nc.semaphore(f"{name}_{self.nsem}"))

    def dstate(self):
        s = [None, 0]
        self.dstates.append(s)
        return s

    def _counter(self, eng):
        if self.csem[eng] is None or self.cnt[eng] >= SEM_LIMIT:
            self.csem[eng] = self.new_sem("c" + eng)
            self.cnt[eng] = 0
        return self.csem[eng]

    def _collect(self, eng, reads, writes, skip_own=True, extra=()):
        need = {}

        def add(tok):
            sem, v = tok
            k = id(sem)
            if k not in need or need[k][1] < v:
                need[k] = (sem, v)
        for d in reads:
            for tok in d.w.values():
                add(tok)
        for d in writes:
            for tok in d.w.values():
                add(tok)
            for tok in d.r.values():
                add(tok)
        for tok in extra:
            add(tok)
        waits = []
        seen = self.seen[eng]
        own = self.csem[eng]
        for k, (sem, v) in need.items():
            if skip_own and own is not None and sem is own:
                continue
            if seen.get(k, 0) >= v:
                continue
            seen[k] = v
            waits.append((sem, v))
        return waits

    def op(self, eng, fn, reads=(), writes=()):
        waits = self._collect(eng, reads, writes, skip_own=(eng == "pe"))
        sem = self._counter(eng)
        self.cnt[eng] += 1
        tok = (sem, self.cnt[eng])
        k = id(sem)
        for d in reads:
            d.r[k] = tok
        for d in writes:
            d.w[k] = tok
        self.ops[eng].append((waits, fn, (sem, 1)))
        return tok

    def dma(self, eng, fn, reads=(), writes=(), st=None, final=False):
        if st is None:
            st = self.misc[eng][self.misc_i[eng] % len(self.misc[eng])]
            self.misc_i[eng] += 1
        extra = []
        if st[0] is not None and st[1] + 16 > SEM_LIMIT:
            st[0] = None
        if st[0] is None:
            st[0] = self.new_sem("d")
            st[1] = 0
        elif st[1] > 0:
            extra.append((st[0], st[1]))
        waits = self._collect(eng, reads, writes, skip_own=False, extra=extra)
        st[1] += 16
        tok = (st[0], st[1])
        k = id(st[0])
        for d in reads:
            d.r[k] = tok
        for d in writes:
            d.w[k] = tok
        self.ops[eng].append((waits, fn, (st[0], 16)))
        if final:
            self.final.append(tok)
        return tok

    def emit(self, last=False):
        nc = self.nc
        bar = []
        for e in self.ENGS:
            if self.csem[e] is not None and self.cnt[e] > 0:
                bar.append((e, self.csem[e], self.cnt[e]))
        dbar = [(s[0], s[1]) for s in self.dstates if s[0] is not None and s[1] > 0]
        ops = self.ops
        seen = self.seen

        def run(engine, name):
            for waits, fn, (sem, inc) in ops[name]:
                for ws, wv in waits:
                    engine.wait_ge(ws, wv)
                fn(engine).then_inc(sem, inc)
            for e, sem, v in bar:
                if e != name and seen[name].get(id(sem), 0) < v:
                    engine.wait_ge(sem, v)
                    seen[name][id(sem)] = v
            for sem, v in dbar:
                if seen[name].get(id(sem), 0) < v:
                    engine.wait_ge(sem, v)
                    seen[name][id(sem)] = v

        with nc.Block() as block:
            @block.sync
            def _(e):
                run(e, "sp")

            @block.tensor
            def _(e):
                run(e, "pe")

            @block.scalar
            def _(e):
                run(e, "act")

            @block.vector
            def _(e):
                run(e, "dve")

            @block.gpsimd
            def _(e):
                run(e, "pool")
        self.ops = {e: [] for e in self.ENGS}


class KB:
    def __init__(self, debug=()):
        self.debug = set(debug)
        self.nc = bass.Bass("TRN2", target_bir_lowering=False)
        self.dram = {}
        self.dbg_out = {}

    def uname(self, name):
        self.ucnt = getattr(self, "ucnt", 0) + 1
        return f"s{self.ucnt}_{name}"

    def din(self, name, shape):
        self.dram[name] = self.nc.dram_tensor(name, list(shape), F32, kind="ExternalInput").ap()
        return self.dram[name]

    def dout(self, name, shape):
        self.dram[name] = self.nc.dram_tensor(name, list(shape), F32, kind="ExternalOutput").ap()
        return self.dram[name]

    def dscr(self, name, shape):
        self.dram[name] = self.nc.dram_tensor(name, list(shape), F32, kind="Internal").ap()
        return self.dram[name]

    def mm(self, out, lhsT, rhs, start, stop, R, W):
        self.S.op("pe", lambda e: e.matmul(out, lhsT=lhsT, rhs=rhs, start=start, stop=stop), R, W)

    def tr(self, out, in_, R, W):
        n = in_.shape[0]
        ident = self.ident[0:n, 0:n]
        self.S.op("pe", lambda e: e.transpose(out=out, in_=in_, identity=ident), list(R) + [self.dIdent], W)

    def act(self, out, in_, func, R, W, bias=None, scale=None, accum=None):
        kw = {}
        if bias is not None:
            kw["bias"] = bias
        if scale is not None:
            kw["scale"] = scale
        if accum is not None:
            kw["accum_out"] = accum
        self.S.op("act", lambda e: e.activation(out=out, in_=in_, func=func, **kw), R, W)

    def tt(self, eng, out, in0, in1, op, R, W):
        self.S.op(eng, lambda e: e.tensor_tensor(out=out, in0=in0, in1=in1, op=op), R, W)

    def ts(self, eng, out, in0, s1, s2, op0, op1, R, W):
        if s2 is None:
            s2 = 0.0
            op1 = ALU.add
        self.S.op(eng, lambda e: e.tensor_scalar(out=out, in0=in0, scalar1=s1, scalar2=s2, op0=op0, op1=op1), R, W)

    def stt(self, eng, out, in0, scalar, in1, op0, op1, R, W):
        self.S.op(eng, lambda e: e.scalar_tensor_tensor(out=out, in0=in0, scalar=scalar, in1=in1, op0=op0, op1=op1), R, W)

    def cp(self, eng, out, in_, R, W):
        if eng == "act":
            self.S.op("act", lambda e: e.copy(out=out, in_=in_), R, W)
        else:
            self.S.op(eng, lambda e: e.tensor_copy(out=out, in_=in_), R, W)

    def memset(self, eng, ap, val, W):
        self.S.op(eng, lambda e: e.memset(ap, val), (), W)

    def recip(self, out, in_, R, W):
        self.S.op("dve", lambda e: e.reciprocal(out=out, in_=in_), R, W)

    def ld(self, out, in_, W, R=(), eng="sp", st=None):
        return self.S.dma(eng, lambda e: e.dma_start(out=out, in_=in_), R, W, st=st)

    def ldc(self, out, in_, W, R=(), st=None):
        return self.S.dma("pool", lambda e: e.dma_start(out=out, in_=in_), R, W, st=st)

    def store(self, out, in_, R, W=(), final=True, eng="sp"):
        return self.S.dma(eng, lambda e: e.dma_start(out=out, in_=in_), R, W, final=final)

    def ld_nc(self, out, in_, W, R=()):
        return self.S.dma("sp", lambda e: e.dma_start(out=out, in_=in_, allow_slow_non_contiguous=True), R, W)

    def store_nc(self, out, in_, R):
        return self.S.dma("sp", lambda e: e.dma_start(out=out, in_=in_, allow_slow_non_contiguous=True), R, (), final=True)

    def dbg(self, name, ap, dep, shape):
        if name not in self.debug:
            return
        o = self.nc.dram_tensor("dbg_" + name, list(shape), ap.dtype, kind="ExternalOutput").ap()
        self.dbg_out[name] = o
        deps = dep if isinstance(dep, (list, tuple)) else [dep]
        self.store(o, ap, list(deps))

    def bank(self):
        i = self.bank_i % 8
        self.bank_i += 1
        return self.PB[i], self.dPB[i]

    def build(self):
        nc = self.nc
        din, dout = self.din, self.dout
        xp = din("xp", [SEQ, D]); xs = din("xs", [NS, D]); c_all = din("c_all", [17, D])
        stC = din("stC", [16, 4, 256, 128]); stn = din("stn", [16, 512]); stm = din("stm", [16, 4])
        ck = din("ck", [16, 128, 256]); cv = din("cv", [16, 128, 256])
        w_ada = din("w_ada", [D, 9 * D]); b_ada = din("b_ada", [72, 128]); self.b_ada_row = din("b_ada_row", [9, D])
        w_up1 = din("w_up1", [D, 2 * DFF]); w_dn1 = din("w_dn1", [DFF, D])
        w_up2 = din("w_up2", [D, 2 * DFF]); w_dn2 = din("w_dn2", [DFF, D])
        ln = {k: din(k, [1, D]) for k in ("ln1_g", "ln1_b", "ln2_g", "ln2_b", "ln3_g", "ln3_b")}
        w_in = din("w_in", [D, 6664]); b_ig = din("b_ig", [1, 4]); b_fg = din("b_fg", [1, 4])
        mng = din("mng", [8, 128]); sinks = din("sinks", [1, 16])
        w_bm = din("w_bm", [D, D]); w_ba = din("w_ba", [D, D]); w_out = din("w_out", [D, D])
        identd = din("ident", [128, 128]); ropec = din("ropec", [128, NT]); ropes = din("ropes", [128, NT])
        cmask = din("cmask", [128, 128 * 6]); bmask = din("bmask", [128, 1024])
        oa_scr = nc.dram_tensor("oa_scr", [128, 8 * NT], BF16, kind="Internal").ap()
        yp = dout("yp", [SEQ, D]); ys = dout("ys", [NS, D])
        Cp = dout("Cp", [4, 256, 128]); np_ = dout("np", [4, 128]); mp = dout("mp", [4, 1])
        kp = dout("kp", [128, 256]); vp = dout("vp", [128, 256])
        Cs = dout("Cs", [16, 4, 256, 128]); ns_ = dout("ns", [16, 512]); ms = dout("ms", [16, 4])
        ks = dout("ks", [16, 128, 256]); vs = dout("vs", [16, 128, 256])
        x1p = self.dscr("x1p", [SEQ, D]); x1s = self.dscr("x1s", [NS, D])
        x2p = self.dscr("x2p", [SEQ, D]); x2s = self.dscr("x2s", [NS, D])

        with ExitStack() as es:
            self.es = es
            self.S = S = Sched(nc, es)
            sb = lambda name, shape, dt=F32: es.enter_context(nc.sbuf_tensor(self.uname(name), list(shape), dt))
            self.PB = [es.enter_context(nc.psum_tensor(f"pb{i}", [128, 512], F32)) for i in range(8)]
            self.dPB = [Dep() for _ in range(8)]
            self.bank_i = 0
            self.ident = sb("ident", [128, 128]); self.dIdent = Dep()
            self.ld(self.ident[:], identd[:, :], [self.dIdent])
            self.modT = sb("modT", [128, 2, 8, 17]); self.dModT = Dep()
            self.modS = sb("modS", [128, 2, 8, 64]); self.dModS = Dep()
            self.G = sb("G", [128, 2, D]); self.dG = Dep()
            self.scT = sb("scT", [128, 8, 17], BF16); self.dScT = Dep()
            self.lhsP = sb("lhsP", [128, 8, 128], BF16); self.lhsS = sb("lhsS", [128, 8, 64], BF16); self.dLhs = Dep()
            self.badaT = sb("badaT", [128, 72]); self.dBada = Dep()
            self.setup_ada(c_all, b_ada)
            S.emit()
            self.ada_stage(0, w_ada, b_ada)
            self.dbg("modT", self.modT[:], self.dModT, [128, 2, 8, 17])
            self.dbg("modS", self.modS[:], self.dModS, [128, 2, 8, 64])
            self.dbg("G", self.G[:], self.dG, [128, 2, D])
            S.emit()
            self.ffn(xp, xs, x1p, x1s, w_up1, w_dn1, ln["ln1_g"], ln["ln1_b"], "f1")
            if "x1" in self.debug:
                o1 = nc.dram_tensor("dbg_x1p", [SEQ, D], F32, kind="ExternalOutput").ap()
                o2 = nc.dram_tensor("dbg_x1s", [NS, D], F32, kind="ExternalOutput").ap()
                self.store(o1[:, :], x1p[:, :], [])
                self.store(o2[:, :], x1s[:, :], [])
                S.emit()
            if "stop1" in self.debug:
                return nc
            self.ada_stage(1, w_ada, b_ada)
            S.emit()
            self.mixer(dict(locals(), **self.dram))
            self.ada_stage(2, w_ada, b_ada)
            S.emit()
            self.ffn(x2p, x2s, yp, ys, w_up2, w_dn2, ln["ln3_g"], ln["ln3_b"], "f2")
        return nc

    def setup_ada(self, c_all, b_ada):
        with ExitStack() as es:
            nc = self.nc
            sb = lambda name, shape, dt=F32: es.enter_context(nc.sbuf_tensor(self.uname(name), list(shape), dt))
            craw = sb("craw", [17, D]); dC = Dep()
            csil = sb("csil", [17, D]); dCs = Dep()
            braw = sb("braw", [72, 128]); dB = Dep()
            scTf = sb("scTf", [128, 8, 17]); dScTf = Dep()
            self.ld(craw[:], c_all[:, :], [dC])
            self.ld(braw[:], b_ada[:, :], [dB])
            self.act(csil[:], craw[:], AF.Silu, [dC], [dCs])
            pb, dpb = self.bank()
            for k in range(8):
                self.tr(pb[:, k * 17:(k + 1) * 17], csil[0:17, k * 128:(k + 1) * 128], [dCs], [dpb])
            self.cp("dve", scTf[:].rearrange("p a b -> p (a b)"), pb[:, 0:136], [dpb], [dScTf])
            self.cp("dve", self.scT[:], scTf[:], [dScTf], [self.dScT])
            self.cp("dve", self.lhsP[:], scTf[:, :, 0:1].to_broadcast([128, 8, 128]), [dScTf], [self.dLhs])
            for k in range(8):
                self.cp("dve", self.lhsS[:, k, :].rearrange("p (b j) -> p b j", j=4),
                        scTf[:, k, 1:17].unsqueeze(2).to_broadcast([128, 16, 4]), [dScTf], [self.dLhs])
            pb2, dpb2 = self.bank()
            self.tr(pb2[:, 0:72], braw[0:72, :], [dB], [dpb2])
            self.cp("dve", self.badaT[:], pb2[:, 0:72], [dpb2], [self.dBada])
            self.S.emit()

    def ada_stage(self, st, w_ada, b_ada):
        nc = self.nc
        with ExitStack() as es:
            sb = lambda name, shape, dt=F32: es.enter_context(nc.sbuf_tensor(self.uname(name), list(shape), dt))
            wa = [sb(f"wa{i}", [128, 8, D], BF16) for i in range(2)]
            dwa = [Dep(), Dep()]
            bG = sb("bG", [128, D]); dbG = Dep()
            tmpG = sb("tmpG", [128, 512]); dtG = Dep()
            waf = [sb(f"waf{i}", [128, 8, D]) for i in range(2)]
            dwaf = [Dep(), Dep()]
            engs = ("act", "dve", "pool")
            for ci in range(3):
                j = 3 * st + ci
                s = ci % 2
                for kh in range(2):
                    self.ld(waf[s][:, 4 * kh:4 * kh + 4, :],
                            w_ada[512 * kh:512 * (kh + 1), j * D:(j + 1) * D].rearrange("(kc p) n -> p kc n", p=128), [dwaf[s]])
                for k in range(8):
                    self.cp(engs[k % 3], wa[s][:, k, :], waf[s][:, k, :], [dwaf[s]], [dwa[s]])
                if ci < 2:
                    pb, dpb = self.bank()
                    for f in range(8):
                        for k in range(8):
                            self.mm(pb[:, f * 17:(f + 1) * 17], wa[s][:, k, f * 128:(f + 1) * 128], self.scT[:, k, :],
                                    k == 0, k == 7, [dwa[s], self.dScT], [dpb])
                    self.stt("dve", self.modT[:, ci, :, :], pb[:, 0:136].rearrange("p (a b) -> p a b", a=8),
                             1.0 if ci == 1 else 0.0,
                             self.badaT[:, j * 8:(j + 1) * 8].unsqueeze(2).to_broadcast([128, 8, 17]),
                             ALU.add, ALU.add, [dpb, self.dBada], [self.dModT])
                else:
                    a = 1.0 if st == 1 else 0.5
                    self.ld(bG[:], self.b_ada_row[j:j + 1, :].broadcast_to([128, D]), [dbG])
                    for which, lhs, n in ((0, self.lhsP, 128), (1, self.lhsS, 64)):
                        for cb in range(2):
                            pb, dpb = self.bank()
                            for k in range(8):
                                self.mm(pb[0:n, :], lhs[:, k, :], wa[s][:, k, cb * 512:(cb + 1) * 512], k == 0, k == 7,
                                        [dwa[s], self.dLhs], [dpb])
                            self.tt("dve", tmpG[0:n, :], pb[0:n, :], bG[0:n, cb * 512:(cb + 1) * 512], ALU.add, [dpb, dbG], [dtG])
                            self.ts("dve", self.G[0:n, which, cb * 512:(cb + 1) * 512], tmpG[0:n, :], a, a, ALU.mult, ALU.add,
                                    [dtG], [self.dG])
            for w in range(2):
                self.cp("dve", self.modS[:, w, :, :].rearrange("p k (b j) -> p k b j", j=4),
                        self.modT[:, w, :, 1:17].unsqueeze(3).to_broadcast([128, 8, 16, 4]), [self.dModT], [self.dModS])
            self.S.emit()

    def make_xT(self, xT, dXT, src_tile_ap, dsrc, i, tmp, dtmp):
        nt = tile_nt(i)
        t0 = 128 * i
        for half in range(2):
            pb, dpb = self.bank()
            for kk in range(4):
                k = half * 4 + kk
                self.tr(pb[:, kk * nt:(kk + 1) * nt], src_tile_ap[0:nt, k * 128:(k + 1) * 128], [dsrc], [dpb])
            if i < 16:
                for kk in range(4):
                    k = half * 4 + kk
                    self.act(xT[:, k, t0:t0 + nt], pb[:, kk * nt:(kk + 1) * nt], AF.Identity, [dpb, self.dModT], [dXT],
                             bias=self.modT[:, 0, k, 0:1], scale=self.modT[:, 1, k, 0:1])
            else:
                k0 = half * 4
                self.tt("dve", tmp[:, 0:256].rearrange("p (a b) -> p a b", a=4), pb[:, 0:256].rearrange("p (a b) -> p a b", a=4),
                        self.modS[:, 1, k0:k0 + 4, :], ALU.mult, [dpb, self.dModS], [dtmp])
                self.tt("dve", xT[:, k0:k0 + 4, t0:t0 + nt], tmp[:, 0:256].rearrange("p (a b) -> p a b", a=4),
                        self.modS[:, 0, k0:k0 + 4, :], ALU.add, [dtmp, self.dModS], [dXT])

    def layer_norm_tiles(self, xres, dX, lnG, lnB, dLn, st, dSt, junk, dJ, eps):
        for i in range(NTILE):
            nt = tile_nt(i)
            self.act(junk[0:nt, :], xres[0:nt, i, :], AF.Identity, [dX[i]], [dJ, dSt], accum=st[0:nt, 0, i:i + 1])
            self.act(junk[0:nt, :], xres[0:nt, i, :], AF.Square, [dX[i]], [dJ, dSt], accum=st[0:nt, 1, i:i + 1])
        m = st[:, 2, :]; msq = st[:, 3, :]; var = st[:, 4, :]; rstd = st[:, 5, :]; nb = st[:, 6, :]
        self.ts("dve", m, st[:, 0, :], 1.0 / D, None, ALU.mult, None, [dSt], [dSt])
        self.tt("dve", msq, m, m, ALU.mult, [dSt], [dSt])
        self.ts("dve", var, st[:, 1, :], 1.0 / D, eps, ALU.mult, ALU.add, [dSt], [dSt])
        self.tt("dve", var, var, msq, ALU.subtract, [dSt], [dSt])
        self.act(var, var, AF.Sqrt, [dSt], [dSt])
        self.recip(rstd, var, [dSt], [dSt])
        self.tt("dve", nb, m, rstd, ALU.mult, [dSt], [dSt])
        self.ts("dve", nb, nb, -1.0, None, ALU.mult, None, [dSt], [dSt])
        for i in range(NTILE):
            nt = tile_nt(i)
            self.act(xres[0:nt, i, :], xres[0:nt, i, :], AF.Identity, [dSt], [dX[i]], bias=nb[0:nt, i:i + 1], scale=rstd[0:nt, i:i + 1])
            self.tt("dve", xres[0:nt, i, :], xres[0:nt, i, :], lnG[0:nt, :], ALU.mult, [dLn], [dX[i]])
            self.tt("dve", xres[0:nt, i, :], xres[0:nt, i, :], lnB[0:nt, :], ALU.add, [dLn], [dX[i]])

    def ffn(self, src_p, src_s, dst_p, dst_s, w_up, w_dn, ln_g, ln_b, tag):
        nc = self.nc
        S = self.S
        groups = [(0, 4), (4, 4), (8, 4), (12, 4), (16, 3), (19, 3)]
        GC = 4
        with ExitStack() as es:
            sb = lambda name, shape, dt=F32: es.enter_context(nc.sbuf_tensor(self.uname(name), list(shape), dt))
            xres = sb("xres", [128, NTILE, D]); dX = [Dep() for _ in range(NTILE)]
            xT = sb("xT", [128, 8, NT], BF16); dXT = [Dep() for _ in range(5)]
            wu = [sb(f"wu{i}", [128, 8, 2, GC * 128], BF16) for i in range(2)]; dwu = [Dep(), Dep()]
            wd = [sb(f"wd{i}", [128, GC, D], BF16) for i in range(2)]; dwd = [Dep(), Dep()]
            gT = [sb(f"gT{i}", [128, GC, 512], BF16) for i in range(2)]; dgT = [Dep(), Dep()]
            tsil = [sb(f"tsil{i}", [128, 512], BF16) for i in range(2)]; dts = [Dep(), Dep()]
            tG = [sb(f"tG{i}", [128, 512]) for i in range(2)]; dtG = [Dep(), Dep()]
            lnG = sb("lnG", [128, D]); lnB = sb("lnB", [128, D]); dLn = Dep()
            st = sb("lnst", [128, 7, NTILE]); dSt = Dep()
            junk = sb("junk", [128, D], BF16); dJ = Dep()
            tmpm = sb("tmpm", [128, 256]); dtm = Dep()
            self.memset("pool", st[:], 0.0, [dSt])
            self.ld(lnG[:], ln_g[0:1, :].broadcast_to([128, D]), [dLn])
            self.ld(lnB[:], ln_b[0:1, :].broadcast_to([128, D]), [dLn])

            def load_group(q):
                c0, gc = groups[q]
                s = q % 2
                for half in range(2):
                    col0 = half * DFF + c0 * 128
                    self.ldc(wu[s][:, :, half, 0:gc * 128], w_up[:, col0:col0 + gc * 128].rearrange("(kc p) n -> p kc n", p=128), [dwu[s]])
                self.ldc(wd[s][:, 0:gc, :], w_dn[c0 * 128:(c0 + gc) * 128, :].rearrange("(c p) n -> p c n", p=128), [dwd[s]])

            load_group(0)
            for i in range(NTILE):
                nt = tile_nt(i)
                src = src_p[128 * i:128 * i + nt, :] if i < 16 else src_s[:, :]
                self.ld(xres[0:nt, i, :], src, [dX[i]])
                self.make_xT(xT, dXT[min(i // 4, 4)], xres[:, i, :], dX[i], i, tmpm, dtm)
                self.S.op("act", lambda e, i=i, nt=nt: e.mul(out=xres[0:nt, i, :], in_=xres[0:nt, i, :], mul=ALPHA), [dX[i]], [dX[i]])
            items = [(q, b) for q in range(len(groups)) for b in range(5)]

            def up(n):
                q, b = items[n]
                c0, gc = groups[q]
                s = q % 2
                g = n % 2
                t0, tn = TBLK[b]
                for j in range(gc):
                    pa, dpa = self.bank()
                    pu, dpu = self.bank()
                    for half, (pp, dpp) in enumerate(((pa, dpa), (pu, dpu))):
                        for k in range(8):
                            self.mm(pp[:, 0:tn], wu[s][:, k, half, j * 128:(j + 1) * 128], xT[:, k, t0:t0 + tn], k == 0, k == 7,
                                    [dwu[s], dXT[b]], [dpp])
                    sl = (n * GC + j) % 2
                    self.act(tsil[sl][:, 0:tn], pa[:, 0:tn], AF.Silu, [dpa], [dts[sl]])
                    self.tt("dve", gT[g][:, j, 0:tn], tsil[sl][:, 0:tn], pu[:, 0:tn], ALU.mult, [dts[sl], dpu], [dgT[g]])

            def down(n):
                q, b = items[n]
                c0, gc = groups[q]
                s = q % 2
                g = n % 2
                for ti, i in enumerate(blk_tiles(b)):
                    nt = tile_nt(i)
                    which = 0 if i < 16 else 1
                    for cb in range(2):
                        po, dpo = self.bank()
                        for j in range(gc):
                            self.mm(po[0:nt, :], gT[g][:, j, ti * 128:ti * 128 + nt], wd[s][:, j, cb * 512:(cb + 1) * 512],
                                    j == 0, j == gc - 1, [dgT[g], dwd[s]], [dpo])
                        sl = (i * 2 + cb) % 2
                        self.tt("dve", tG[sl][0:nt, :], po[0:nt, :], self.G[0:nt, which, cb * 512:(cb + 1) * 512], ALU.mult,
                                [dpo, self.dG], [dtG[sl]])
                        self.tt("pool", xres[0:nt, i, cb * 512:(cb + 1) * 512], xres[0:nt, i, cb * 512:(cb + 1) * 512], tG[sl][0:nt, :],
                                ALU.add, [dtG[sl]], [dX[i]])

            load_group(1)
            up(0)
            for n in range(len(items)):
                if n + 1 < len(items):
                    up(n + 1)
                down(n)
                q, b = items[n]
                if b == 4 and q + 2 < len(groups):
                    load_group(q + 2)
            if tag == "f1":
                self.dbg("pre", xres[:], dX, [128, NTILE, D])
                self.dbg("xT", xT[:], dXT, [128, 8, NT])
            self.layer_norm_tiles(xres, dX, lnG, lnB, dLn, st, dSt, junk, dJ, LN_EPS)
            if tag == "f1":
                self.dbg("lnst", st[:], dSt, [128, 7, NTILE])
            for i in range(NTILE):
                nt = tile_nt(i)
                dst = dst_p[128 * i:128 * i + nt, :] if i < 16 else dst_s[:, :]
                self.store(dst, xres[0:nt, i, :], [dX[i]])
            S.emit()

    def mixer(self, L):
        nc, S = self.nc, self.S
        x1p, x1s = L["x1p"], L["x1s"]
        with ExitStack() as es0:
            sb0 = lambda name, shape, dt=F32: es0.enter_context(nc.sbuf_tensor(self.uname(name), list(shape), dt))
            hT = sb0("hT", [128, 8, NT], BF16); dHT = [Dep() for _ in range(5)]
            with ExitStack() as es:
                sb = lambda name, shape, dt=F32: es.enter_context(nc.sbuf_tensor(self.uname(name), list(shape), dt))
                xt = sb("xt", [128, 2, D]); dxt = [Dep(), Dep()]
                tmpm = sb("tmpm", [128, 256]); dtm = Dep()
                for i in range(NTILE):
                    nt = tile_nt(i)
                    src = x1p[128 * i:128 * i + nt, :] if i < 16 else x1s[:, :]
                    self.ld(xt[0:nt, i % 2, :], src, [dxt[i % 2]])
                    self.make_xT(hT, dHT[min(i // 4, 4)], xt[:, i % 2, :], dxt[i % 2], i, tmpm, dtm)
                S.emit()
            with ExitStack() as es:
                oaT = es.enter_context(nc.sbuf_tensor(self.uname("oaT"), [128, 8, NT], BF16)); dOA = [Dep() for _ in range(5)]
                self.swa(L, hT, dHT, oaT, dOA)
                self.store(L["oa_scr"][:, :], oaT[:].rearrange("p c t -> p (c t)"), dOA)
                if "oaT" in self.debug:
                    self.dbg("oaT", oaT[:], dOA, [128, 8, NT])
                S.emit()
            if "stop2" in self.debug:
                return
            hgT = sb0("hgT", [128, 8, NT], BF16); dHG = [Dep() for _ in range(5)]
            self.mlstm(L, hT, dHT, hgT, dHG)
            if "hgT" in self.debug:
                self.dbg("hgT", hgT[:], dHG, [128, 8, NT])
                S.emit()
            if "stop3" in self.debug:
                return
            self.merge(L, hT, dHT, hgT, dHG)

    def swa(self, L, hT, dHT, oaT, dOA):
        nc, S = self.nc, self.S
        w_in = L["w_in"]
        AQ0, AK0, AV0 = 3080, 4104, 4360
        with ExitStack() as es:
            sb = lambda name, shape, dt=F32: es.enter_context(nc.sbuf_tensor(self.uname(name), list(shape), dt))
            cosT = sb("cosT", [128, NT]); sinT = sb("sinT", [128, NT]); dRope = Dep()
            self.ld(cosT[:], L["ropec"][:, :], [dRope]); self.ld(sinT[:], L["ropes"][:, :], [dRope])
            mk = sb("mk", [128, 768]); dMk = Dep()
            self.ld(mk[:], L["cmask"][:, :], [dMk])
            mle = sb("mle", [128, 128], BF16); mgt = sb("mgt", [128, 128], BF16); mblkc = sb("mblkc", [128, 128], BF16)
            mcache = sb("mcache", [128, 4], BF16); dMb = Dep()
            self.cp("dve", mle[:], mk[:, 0:128], [dMk], [dMb]); self.cp("dve", mgt[:], mk[:, 128:256], [dMk], [dMb])
            self.cp("dve", mblkc[:], mk[:, 256:384], [dMk], [dMb]); self.cp("dve", mcache[:], mk[:, 512:516], [dMk], [dMb])
            ones = sb("ones", [128, 128], BF16); dOnes = Dep()
            self.memset("pool", ones[:], 1.0, [dOnes])
            esink = sb("esink", [128, 16]); dEs = Dep()
            self.ld(esink[:], L["sinks"][0:1, :].broadcast_to([128, 16]), [dEs])
            self.act(esink[:], esink[:], AF.Exp, [dEs], [dEs])
            ckf = sb("ckf", [128, 16, 256]); dCk = Dep()
            for b0 in range(0, 16, 4):
                self.ld(ckf[:, b0:b0 + 4, :], L["ck"][b0:b0 + 4].rearrange("b p c -> p b c"), [dCk])
            cvv = L["cv"].rearrange("b p c -> p b c")
            KcTn = sb("KcTn", [128, 2, 16, 128], BF16); dKn = Dep()
            for gp in range(2):
                for b0 in range(0, 16, 4):
                    pb, dpb = self.bank()
                    for bb in range(4):
                        self.tr(pb[:, bb * 128:(bb + 1) * 128], ckf[:, b0 + bb, gp * 128:(gp + 1) * 128], [dCk], [dpb])
                    self.cp("act", KcTn[:, gp, b0:b0 + 4, :].rearrange("p b c -> p (b c)"), pb[:, :], [dpb], [dKn])
            kout_p = sb("kout_p", [128, 256]); vfin = sb("vfin", [128, 2, 256]); knew_s = sb("knew_s", [128, 256])
            dKo = Dep(); dVf = Dep(); dKs = Dep()
            WQ = sb("WQ", [128, 8, 256], BF16); WQs = sb("WQs", [128, 8, 256], BF16)
            WK2 = sb("WK2", [128, 8, 128], BF16); WK2s = sb("WK2s", [128, 8, 128], BF16)
            WV = sb("WV", [128, 8, 64], BF16); dW = Dep(); dWs = Dep()
            qT = sb("qT", [128, 2, NT], BF16); dQ = [Dep() for _ in range(5)]
            kT2 = sb("kT2", [128, NT], BF16); dK = [Dep() for _ in range(5)]
            vtm2 = sb("vtm2", [128, NTILE, 128], BF16); dV = [Dep() for _ in range(NTILE)]
            vlh = sb("vlh", [128, 16, 2, 128], BF16)
            ones_lh = sb("ones_lh", [128, 2, 128], BF16); dOlh = Dep()
            esg = sb("esg", [128, 2]); dEsg = Dep()
            self.memset("pool", vlh[:], 0.0, dV[0:16])
            self.memset("pool", ones_lh[:], 0.0, [dOlh])
            self.memset("pool", ones_lh[:, 0, 0:64], 1.0, [dOlh])
            self.memset("pool", ones_lh[:, 1, 64:128], 1.0, [dOlh])
            kfin = sb("kfin", [128, 192]); dKf = Dep()
            t1 = [sb(f"t1_{i}", [128, 512]) for i in range(2)]; dt1 = [Dep(), Dep()]
            t2 = [sb(f"t2_{i}", [128, 512]) for i in range(2)]; dt2 = [Dep(), Dep()]
            pT = [sb(f"pT{i}", [128, 512], BF16) for i in range(4)]; dpT = [Dep() for _ in range(4)]
            pTm = [sb(f"pTm{i}", [128, 512], BF16) for i in range(4)]; dpTm = [Dep() for _ in range(4)]
            recs = [sb(f"rec{i}", [128, 512]) for i in range(2)]; dRecs = [Dep(), Dep()]
            rec = recs[0]; dRec = dRecs[0]
            KcT2 = sb("KcT2", [128, 16, 128], BF16); dKc2 = Dep()
            Vc2 = sb("Vc2", [128, 16, 128], BF16); dVc2 = Dep()
            pTn = sb("pTn", [128, 256], BF16); dpTn = Dep()
            pTnm = sb("pTnm", [128, 256], BF16); dpTnm = Dep()
            cnt = [0]

            def rope_evac(pa, dpa, pb_, dpb_, t0, tn, out_ap, dOut, fin=None):
                i = cnt[0] % 2; cnt[0] += 1
                self.tt("dve", t1[i][:, 0:tn], pa[:, 0:tn], cosT[:, t0:t0 + tn], ALU.mult, [dpa, dRope], [dt1[i]])
                self.tt("dve", t2[i][:, 0:tn], pb_[:, 0:tn], sinT[:, t0:t0 + tn], ALU.mult, [dpb_, dRope], [dt2[i]])
                self.tt("pool", out_ap, t1[i][:, 0:tn], t2[i][:, 0:tn], ALU.add, [dt1[i], dt2[i]], [dOut])
                if fin is not None:
                    fo, a, n = fin
                    self.tt("pool", kfin[:, fo:fo + n], t1[i][:, a:a + n], t2[i][:, a:a + n], ALU.add, [dt1[i], dt2[i]], [dKf])

            lvl = 9
            att = 9
            for f_ in self.debug:
                if f_.startswith("att"):
                    att = int(f_[3:])
            for f_ in self.debug:
                if f_.startswith("swa"):
                    lvl = int(f_[3:])
            for g in range(4 if lvl >= 5 else 1):
                if lvl < 1:
                    break
                wsrc = lambda c0, n: w_in[:, c0:c0 + n].rearrange("(kc p) n -> p kc n", p=128)
                self.ldc(WQ[:], wsrc(AQ0 + g * 256, 256), [dW])
                self.ldc(WK2[:, :, 0:64], wsrc(AK0 + g * 64, 64), [dW])
                self.ldc(WK2[:, :, 64:128], wsrc(AK0 + g * 64, 64), [dW])
                self.ldc(WV[:], wsrc(AV0 + g * 64, 64), [dW])
                for src_t, dst_t, nh in ((WQ, WQs, 4), (WK2, WK2s, 2)):
                    sv = src_t[:].rearrange("p k (h two d) -> p k h two d", two=2, d=32)
                    dv = dst_t[:].rearrange("p k (h two d) -> p k h two d", two=2, d=32)
                    for k in range(8):
                        self.cp("pool", dv[:, k, :, 0, :], sv[:, k, :, 1, :], [dW], [dWs])
                        self.cp("pool", dv[:, k, :, 1, :], sv[:, k, :, 0, :], [dW], [dWs])
                for b, (t0, tn) in enumerate(TBLK):
                    for c in range(2):
                        pa, dpa = self.bank(); pb_, dpb_ = self.bank()
                        for k in range(8):
                            self.mm(pa[:, 0:tn], WQ[:, k, c * 128:(c + 1) * 128], hT[:, k, t0:t0 + tn], k == 0, k == 7, [dW, dHT[b]], [dpa])
                        for k in range(8):
                            self.mm(pb_[:, 0:tn], WQs[:, k, c * 128:(c + 1) * 128], hT[:, k, t0:t0 + tn], k == 0, k == 7, [dWs, dHT[b]], [dpb_])
                        rope_evac(pa, dpa, pb_, dpb_, t0, tn, qT[:, c, t0:t0 + tn], dQ[b])
                    pa, dpa = self.bank(); pb_, dpb_ = self.bank()
                    for k in range(8):
                        self.mm(pa[:, 0:tn], WK2[:, k, :], hT[:, k, t0:t0 + tn], k == 0, k == 7, [dW, dHT[b]], [dpa])
                    for k in range(8):
                        self.mm(pb_[:, 0:tn], WK2s[:, k, :], hT[:, k, t0:t0 + tn], k == 0, k == 7, [dWs, dHT[b]], [dpb_])
                    fin = (0, 384, 128) if b == 3 else ((128, 0, 64) if b == 4 else None)
                    rope_evac(pa, dpa, pb_, dpb_, t0, tn, kT2[:, t0:t0 + tn], dK[b], fin)
                for i in range(NTILE):
                    nt = tile_nt(i)
                    pv, dpv = self.bank()
                    for k in range(8):
                        self.mm(pv[0:nt, 0:64], hT[:, k, 128 * i:128 * i + nt], WV[:, k, :], k == 0, k == 7, [dW, dHT[min(i // 4, 4)]], [dpv])
                    if i < 16:
                        self.cp("act", vlh[0:nt, i, 0, 0:64], pv[0:nt, 0:64], [dpv], [dV[i]])
                        self.cp("dve", vlh[0:nt, i, 1, 64:128], pv[0:nt, 0:64], [dpv], [dV[i]])
                    else:
                        self.cp("act", vtm2[0:nt, i, 0:64], pv[0:nt, 0:64], [dpv], [dV[i]])
                        self.cp("dve", vtm2[0:nt, i, 64:128], pv[0:nt, 0:64], [dpv], [dV[i]])
                    if i >= 15:
                        self.cp("act", vfin[0:nt, i - 15, g * 64:(g + 1) * 64], pv[0:nt, 0:64], [dpv], [dVf])
                if lvl < 2:
                    continue
                pk, dpk = self.bank()
                self.tr(pk[:, 0:64], kfin[0:64, 0:128], [dKf], [dpk])
                self.tr(pk[0:64, 64:128], kfin[0:64, 128:192], [dKf], [dpk])
                self.cp("act", kout_p[:, g * 64:(g + 1) * 64], pk[:, 0:64], [dpk], [dKo])
                self.cp("act", knew_s[0:64, g * 64:(g + 1) * 64], pk[0:64, 64:128], [dpk], [dKs])
                if lvl < 3:
                    continue
                def attA(n):
                    q0 = 128 * n
                    kbs = ([n - 1] if n > 0 else []) + [n]
                    for ki, kb in enumerate(kbs):
                        psA, dpsA = self.bank(); psB, dpsB = self.bank()
                        for c in range(2):
                            self.mm(psA[:, c * 128:(c + 1) * 128], kT2[0:64, 128 * kb:128 * kb + 128], qT[0:64, c, q0:q0 + 128], True, True,
                                    [dK[kb // 4], dQ[n // 4]], [dpsA])
                            self.mm(psB[:, c * 128:(c + 1) * 128], kT2[64:128, 128 * kb:128 * kb + 128], qT[64:128, c, q0:q0 + 128], True, True,
                                    [dK[kb // 4], dQ[n // 4]], [dpsB])
                        sl = (2 * n + ki) % 4
                        self.act(pT[sl][:, 0:256], psA[:, 0:256], AF.Exp, [dpsA], [dpT[sl]], scale=0.125)
                        self.act(pT[sl][:, 256:512], psB[:, 0:256], AF.Exp, [dpsB], [dpT[sl]], scale=0.125)
                        msk = mle if kb == n else mgt
                        self.tt("dve", pTm[sl][:].rearrange("p (r q) -> p r q", r=4), pT[sl][:].rearrange("p (r q) -> p r q", r=4),
                                msk[:, :].unsqueeze(1).to_broadcast([128, 4, 128]), ALU.mult, [dpT[sl], dMb], [dpTm[sl]])

                esv_ = esink[:, 4 * g:4 * g + 4].rearrange("p (c h) -> p h c", h=2)
                self.cp("dve", esg[0:64, :], esv_[0:64, 0, :], [dEs], [dEsg])
                self.cp("dve", esg[64:128, :], esv_[64:128, 1, :], [dEs], [dEsg])

                def attB(n):
                    q0 = 128 * n
                    kbs = ([n - 1] if n > 0 else []) + [n]
                    pnum, dpnum = self.bank(); pden, dpden = self.bank()
                    nmm = 2 * len(kbs)
                    j = 0
                    for ki, kb in enumerate(kbs):
                        sl = (2 * n + ki) % 4
                        for hh in range(2):
                            self.mm(pnum[:, 0:256], vlh[:, kb, hh, :], pTm[sl][:, hh * 256:(hh + 1) * 256], j == 0, j == nmm - 1,
                                    [dV[kb], dpTm[sl]], [dpnum])
                            self.mm(pden[:, 0:256], ones_lh[:, hh, :], pTm[sl][:, hh * 256:(hh + 1) * 256], j == 0, j == nmm - 1,
                                    [dOlh, dpTm[sl]], [dpden])
                            j += 1
                    rc = recs[n % 2]; drc = dRecs[n % 2]
                    self.tt("dve", rc[:, 0:256].rearrange("p (c q) -> p c q", c=2), pden[:, 0:256].rearrange("p (c q) -> p c q", c=2),
                            esg[:, :].unsqueeze(2).to_broadcast([128, 2, 128]), ALU.add, [dpden, dEsg], [drc])
                    self.act(rc[:, 0:256], rc[:, 0:256], AF.Ln, [drc], [drc])
                    self.act(rc[:, 0:256], rc[:, 0:256], AF.Exp, [drc], [drc], scale=-1.0)
                    self.tt("dve", oaT[:, 2 * g:2 * g + 2, q0:q0 + 128], pnum[:, 0:256].rearrange("p (c q) -> p c q", c=2),
                            rc[:, 0:256].rearrange("p (c q) -> p c q", c=2), ALU.mult, [dpnum, drc], [dOA[n // 4]])

                attA(0)
                for n in range(16):
                    if n + 1 < 16:
                        attA(n + 1)
                    attB(n)
                if lvl < 4:
                    continue
                gp, go = g // 2, (g % 2) * 64
                self.cp("act", KcT2[0:64, :, :], KcTn[go:go + 64, gp, :, :], [dKn], [dKc2])
                self.cp("dve", KcT2[64:128, :, :], KcTn[go:go + 64, gp, :, :], [dKn], [dKc2])
                for b0 in range(0, 16, 4):
                    self.ldc(Vc2[:, b0:b0 + 4, 0:64], cvv[:, b0:b0 + 4, g * 64:(g + 1) * 64], [dVc2])
                    self.ldc(Vc2[:, b0:b0 + 4, 64:128], cvv[:, b0:b0 + 4, g * 64:(g + 1) * 64], [dVc2])
                pscA, dpscA = self.bank(); pscB, dpscB = self.bank()
                psnA, dpsnA = self.bank(); psnB, dpsnB = self.bank()
                for b in range(16):
                    for c in range(2):
                        c0 = b * 8 + c * 4
                        self.mm(pscA[:, c0:c0 + 4], KcT2[0:64, b, :], qT[0:64, c, SEQ + 4 * b:SEQ + 4 * b + 4], True, True, [dKc2, dQ[4]], [dpscA])
                        self.mm(pscB[:, c0:c0 + 4], KcT2[64:128, b, :], qT[64:128, c, SEQ + 4 * b:SEQ + 4 * b + 4], True, True, [dKc2, dQ[4]], [dpscB])
                for c in range(2):
                    for hh, (pp, dpp) in enumerate(((psnA, dpsnA), (psnB, dpsnB))):
                        off = hh * 64
                        self.mm(pp[0:64, 0:128].rearrange("p (b c t) -> p b c t", b=16, c=2)[:, :, c, :], kT2[off:off + 64, SEQ:NT],
                                qT[off:off + 64, c, SEQ:NT].rearrange("p (b t) -> p b t", t=4), True, True, [dK[4], dQ[4]], [dpp])
                self.act(pTn[:, 0:128], pscA[:, 0:128], AF.Exp, [dpscA], [dpTn], scale=0.125)
                self.act(pTn[:, 128:256], pscB[:, 0:128], AF.Exp, [dpscB], [dpTn], scale=0.125)
                self.tt("dve", pTnm[:, :].rearrange("p (a t) -> p a t", t=4), pTn[:, :].rearrange("p (a t) -> p a t", t=4),
                        mcache[:, :].unsqueeze(1).to_broadcast([128, 64, 4]), ALU.mult, [dpTn, dMb], [dpTnm])
                sl = cnt[0] % 2; cnt[0] += 1
                self.act(pT[sl][0:64, 0:128], psnA[0:64, 0:128], AF.Exp, [dpsnA], [dpT[sl]], scale=0.125)
                self.act(pT[sl][0:64, 128:256], psnB[0:64, 0:128], AF.Exp, [dpsnB], [dpT[sl]], scale=0.125)
                for hh in range(2):
                    self.tt("dve", pTm[sl][0:64, hh * 128:(hh + 1) * 128].rearrange("p (b c t) -> p b c t", b=16, c=2),
                            pT[sl][0:64, hh * 128:(hh + 1) * 128].rearrange("p (b c t) -> p b c t", b=16, c=2),
                            mblkc[0:64, 0:64].rearrange("p (b t) -> p b t", t=4).unsqueeze(2).to_broadcast([64, 16, 2, 4]), ALU.mult,
                            [dpT[sl], dMb], [dpTm[sl]])
                pnum, dpnum = self.bank(); pden, dpden = self.bank()
                self.mm(pnum[:, 0:256], vtm2[0:64, 16, :], pTm[sl][0:64, 0:256], True, False, [dV[16], dpTm[sl]], [dpnum])
                self.mm(pden[:, 0:256], ones[0:64, :], pTm[sl][0:64, 0:256], True, False, [dOnes, dpTm[sl]], [dpden])
                for b in range(16):
                    pv_ = pTnm[:, :].rearrange("p (h b x) -> p h b x", h=2, b=16)[:, :, b, :]
                    self.mm(pnum[:, 0:256].rearrange("p (h b x) -> p h b x", h=2, b=16)[:, :, b, :], Vc2[:, b, :], pv_, False, True,
                            [dVc2, dpTnm], [dpnum])
                    self.mm(pden[:, 0:256].rearrange("p (h b x) -> p h b x", h=2, b=16)[:, :, b, :], ones[:, :], pv_, False, True,
                            [dOnes, dpTnm], [dpden])
                esv = esink[:, 4 * g:4 * g + 4].rearrange("p (c h) -> p h c", h=2)
                for hh in range(2):
                    self.tt("dve", rec[:, hh * 128:(hh + 1) * 128].rearrange("p (b c t) -> p b c t", b=16, c=2),
                            pden[:, hh * 128:(hh + 1) * 128].rearrange("p (b c t) -> p b c t", b=16, c=2),
                            esv[:, hh, :].unsqueeze(1).unsqueeze(3).to_broadcast([128, 16, 2, 4]), ALU.add, [dpden, dEs], [dRec])
                self.recip(rec[:, 0:256], rec[:, 0:256], [dRec], [dRec])
                for hh in range(2):
                    off = hh * 64
                    for c in range(2):
                        nv = pnum[off:off + 64, hh * 128:(hh + 1) * 128].rearrange("p (b c t) -> p b c t", b=16, c=2)[:, :, c, :]
                        rv = rec[off:off + 64, hh * 128:(hh + 1) * 128].rearrange("p (b c t) -> p b c t", b=16, c=2)[:, :, c, :]
                        self.tt("dve", oaT[off:off + 64, 2 * g + c, SEQ:NT].rearrange("p (b t) -> p b t", t=4), nv, rv, ALU.mult,
                                [dpnum, dRec], [dOA[4]])
            if lvl < 6:
                S.emit()
                return
            self.store(L["kp"][:, :], kout_p[:], [dKo])
            self.store(L["vp"][:, :], vfin[:, 0, :], [dVf])
            for b in range(16):
                self.store(L["ks"][b, 0:124, :], L["ck"][b, 4:128, :], [])
                self.store(L["vs"][b, 0:124, :], L["cv"][b, 4:128, :], [])
                self.store(L["ks"][b, 124:128, :], knew_s[4 * b:4 * b + 4, :], [dKs])
                self.store(L["vs"][b, 124:128, :], vfin[4 * b:4 * b + 4, 1, :], [dVf])
            S.emit()

    def mlstm(self, L, hT, dHT, hgT, dHG):
        nc, S = self.nc, self.S
        w_in = L["w_in"]
        MQ0, MK0, MV0, MO0, MG0 = 0, 512, 1024, 2048, 3072
        KS = 128.0 ** -0.5
        wsrc = lambda c0, n: w_in[:, c0:c0 + n].rearrange("(kc p) n -> p kc n", p=128)
        with ExitStack() as es:
            sb = lambda name, shape, dt=F32: es.enter_context(nc.sbuf_tensor(self.uname(name), list(shape), dt))
            A_tm = sb("A_tm", [128, NTILE, 4]); B_tm = sb("B_tm", [128, NTILE, 4]); MendTok = sb("MendTok", [128, NTILE, 4])
            W_tm = sb("W_tm", [128, NTILE, 4]); THR = sb("THR", [128, NTILE, 4]); dGt = Dep()
            ECrep = sb("ECrep", [128, 4, 32]); MendRep = sb("MendRep", [128, 4, 16]); dRep = Dep()
            EcTok = sb("EcTok", [128, 4]); dEt = Dep()
            mk = sb("mk", [128, 768]); dMk = Dep()
            self.ld(mk[:], L["cmask"][:, :], [dMk])
            mle_f = mk[:, 0:128]; mblkc_f = mk[:, 256:384]; blk_f = mk[:, 384:512]; rmask_f = mk[:, 640:656]
            n0tok = sb("n0tok", [128, 512]); dN0 = Dep()
            nnew = sb("nnew", [128, 512]); dNn = Dep()
            for b in range(16):
                self.ld(n0tok[4 * b:4 * b + 4, :], L["stn"][b:b + 1, :].broadcast_to([4, 512]), [dN0])
            for t_ in (A_tm, B_tm, MendTok):
                self.memset("pool", t_[:], 0.0, [dGt])
            with ExitStack() as es2:
                sb2 = lambda name, shape, dt=F32: es2.enter_context(nc.sbuf_tensor(self.uname(name), list(shape), dt))
                Wg = sb2("Wg", [128, 8, 8], BF16); dWg = Dep()
                self.ldc(Wg[:], wsrc(MG0, 8), [dWg])
                bigf = sb2("bigf", [128, 8]); dBg = Dep()
                self.ld(bigf[:, 0:4], L["b_ig"][0:1, :].broadcast_to([128, 4]), [dBg])
                self.ld(bigf[:, 4:8], L["b_fg"][0:1, :].broadcast_to([128, 4]), [dBg])
                g_tm = sb2("g_tm", [128, NTILE, 8]); dGtm = Dep()
                GI = sb2("GI", [4, NT]); GF = sb2("GF", [4, NT]); Bf = sb2("Bf", [4, NT]); Mf = sb2("Mf", [4, NT])
                Z = sb2("Z", [4, NT]); T1 = sb2("T1", [4, NT]); T2 = sb2("T2", [4, NT]); dF = Dep()
                ones4 = sb2("ones4", [4, 128]); dO4 = Dep()
                m0T = sb2("m0T", [4, 16]); dM0 = Dep()
                V48 = sb2("V48", [4, 48]); X = sb2("X", [4, 4, 48]); dV = Dep()
                Sx = sb2("Sx", [4, 2, 16, 4]); dSx = Dep()
                mout = sb2("mout", [4, 17]); dMo = Dep()
                self.memset("pool", Z[:], 0.0, [dF]); self.memset("pool", ones4[:], 1.0, [dO4])
                self.memset("pool", g_tm[:], 0.0, [dGtm])
                self.ld_nc(m0T[:], L["stm"].rearrange("b h -> h b"), [dM0])
                for i in range(NTILE):
                    nt = tile_nt(i)
                    pg, dpg = self.bank()
                    for k in range(8):
                        self.mm(pg[0:nt, 0:8], hT[:, k, 128 * i:128 * i + nt], Wg[:, k, :], k == 0, k == 7, [dWg, dHT[min(i // 4, 4)]], [dpg])
                    self.tt("dve", g_tm[0:nt, i, :], pg[0:nt, 0:8], bigf[0:nt, :], ALU.add, [dpg, dBg], [dGtm])
                for b, (t0, tn) in enumerate(TBLK):
                    p1, dp1 = self.bank(); p2, dp2 = self.bank()
                    for j, i in enumerate(blk_tiles(b)):
                        nt = tile_nt(i)
                        self.tr(p1[0:4, j * 128:j * 128 + nt], g_tm[0:nt, i, 0:4], [dGtm], [dp1])
                        self.tr(p2[0:4, j * 128:j * 128 + nt], g_tm[0:nt, i, 4:8], [dGtm], [dp2])
                    self.cp("dve", GI[:, t0:t0 + tn], p1[0:4, 0:tn], [dp1], [dF])
                    self.cp("dve", GF[:, t0:t0 + tn], p2[0:4, 0:tn], [dp2], [dF])
                self.act(T1[:], GF[:], AF.Abs, [dF], [dF])
                self.act(T1[:], T1[:], AF.Exp, [dF], [dF], scale=-1.0)
                self.act(T1[:], T1[:], AF.Ln, [dF], [dF], bias=1.0)
                self.ts("dve", T2[:], GF[:], 0.0, None, ALU.min, None, [dF], [dF])
                self.tt("dve", GF[:], T2[:], T1[:], ALU.subtract, [dF], [dF])
                self.S.op("dve", lambda e: e.tensor_tensor_scan(out=Bf[:, 0:SEQ], data0=GF[:, 0:SEQ], data1=Z[:, 0:SEQ], initial=0.0,
                                                                op0=ALU.add, op1=ALU.add), [dF], [dF])
                gfs = GF[:, SEQ:NT].rearrange("h (b j) -> h b j", j=4); bfs = Bf[:, SEQ:NT].rearrange("h (b j) -> h b j", j=4)
                self.cp("dve", bfs[:, :, 0], gfs[:, :, 0], [dF], [dF])
                for j in range(1, 4):
                    self.tt("dve", bfs[:, :, j], bfs[:, :, j - 1], gfs[:, :, j], ALU.add, [dF], [dF])
                self.tt("dve", GI[:], GI[:], Bf[:], ALU.subtract, [dF], [dF])
                self.S.op("dve", lambda e: e.tensor_tensor_scan(out=Mf[:, 0:SEQ], data0=Z[:, 0:SEQ], data1=GI[:, 0:SEQ], initial=0.0,
                                                                op0=ALU.add, op1=ALU.max), [dF], [dF])
                as_ = GI[:, SEQ:NT].rearrange("h (b j) -> h b j", j=4); ms_ = Mf[:, SEQ:NT].rearrange("h (b j) -> h b j", j=4)
                self.tt("dve", ms_[:, :, 0], as_[:, :, 0], m0T[:, :], ALU.max, [dF, dM0], [dF])
                for j in range(1, 4):
                    self.tt("dve", ms_[:, :, j], ms_[:, :, j - 1], as_[:, :, j], ALU.max, [dF], [dF])
                mend_p = Mf[:, 0:SEQ].rearrange("h (c t) -> h c t", t=128)[:, :, 127]
                self.cp("dve", V48[:, 0:16], mend_p, [dF], [dV])
                self.cp("dve", V48[:, 16:17], Z[:, 0:1], [dF], [dV])
                self.cp("dve", V48[:, 17:32], V48[:, 0:15], [dV], [dV])
                self.tt("dve", V48[:, 16:32], V48[:, 16:32], V48[:, 0:16], ALU.subtract, [dV], [dV])
                self.tt("dve", V48[:, 32:48], m0T[:, :], ms_[:, :, 3], ALU.subtract, [dF, dM0], [dV])
                self.act(V48[:, 16:48], V48[:, 16:48], AF.Exp, [dV], [dV])
                self.tt("dve", X[:], V48[:, :].unsqueeze(1).to_broadcast([4, 4, 48]),
                        self.ident[0:4, 0:4].unsqueeze(2).to_broadcast([4, 4, 48]), ALU.mult, [dV, self.dIdent], [dV])
                pr, dpr = self.bank()
                self.mm(pr[:, 0:192], ones4[:, :], X[:].rearrange("h a c -> h (a c)"), True, True, [dO4, dV], [dpr])
                prv = pr[:, 0:192].rearrange("p (a c) -> p a c", a=4)
                self.cp("dve", MendRep[:], prv[:, :, 0:16], [dpr], [dRep])
                self.cp("dve", ECrep[:], prv[:, :, 16:48], [dpr], [dRep])
                self.cp("dve", Sx[:, 0, :, :], ms_[:, :, 3].unsqueeze(2).to_broadcast([4, 16, 4]), [dF], [dSx])
                self.cp("dve", Sx[:, 1, :, :], V48[:, 32:48].unsqueeze(2).to_broadcast([4, 16, 4]), [dV], [dSx])
                pa_, dpa_ = self.bank(); pb_, dpb_ = self.bank()
                for i in range(NTILE):
                    nt = tile_nt(i)
                    self.tr(pa_[0:nt, i * 4:i * 4 + 4], GI[0:4, 128 * i:128 * i + nt], [dF], [dpa_])
                    self.tr(pb_[0:nt, i * 4:i * 4 + 4], Bf[0:4, 128 * i:128 * i + nt], [dF], [dpb_])
                self.tr(pa_[0:64, 68:72], Sx[0:4, 0, :, :].rearrange("h b j -> h (b j)"), [dSx], [dpa_])
                self.tr(pb_[0:64, 68:72], Sx[0:4, 1, :, :].rearrange("h b j -> h (b j)"), [dSx], [dpb_])
                self.cp("dve", A_tm[:, 0:16, :].rearrange("p c h -> p (c h)"), pa_[:, 0:64], [dpa_], [dGt])
                self.cp("dve", A_tm[0:64, 16, :], pa_[0:64, 64:68], [dpa_], [dGt])
                self.cp("dve", B_tm[:, 0:16, :].rearrange("p c h -> p (c h)"), pb_[:, 0:64], [dpb_], [dGt])
                self.cp("dve", B_tm[0:64, 16, :], pb_[0:64, 64:68], [dpb_], [dGt])
                self.cp("dve", MendTok[0:64, 16, :], pa_[0:64, 68:72], [dpa_], [dGt])
                self.cp("dve", EcTok[0:64, :], pb_[0:64, 68:72], [dpb_], [dEt])
                self.cp("dve", MendTok[:, 0:16, :], MendRep[:].rearrange("p h c -> p c h"), [dRep], [dGt])
                self.tt("dve", W_tm[:], A_tm[:], MendTok[:], ALU.subtract, [dGt], [dGt])
                self.act(W_tm[:], W_tm[:], AF.Exp, [dGt], [dGt])
                self.tt("dve", THR[:], B_tm[:], MendTok[:], ALU.add, [dGt], [dGt])
                self.act(THR[:], THR[:], AF.Exp, [dGt], [dGt], scale=-1.0)
                self.tt("dve", mout[:, 0:1], Mf[:, SEQ - 1:SEQ], Bf[:, SEQ - 1:SEQ], ALU.add, [dF], [dMo])
                self.tt("dve", mout[:, 1:17], ms_[:, :, 3], bfs[:, :, 3], ALU.add, [dF], [dMo])
                self.store(L["mp"][:, :], mout[:, 0:1], [dMo])
                self.store_nc(L["ms"].rearrange("b h -> h b"), mout[:, 1:17], [dMo])
                S.emit()
            Wm = sb("Wm", [128, 8, 768], BF16); dWm = Dep()
            qTh = sb("qTh", [128, NT], BF16); kTh = sb("kTh", [128, NT], BF16); dQK = [Dep() for _ in range(5)]
            k_tm = sb("k_tm", [128, NTILE, 128], BF16); v_aug = sb("v_aug", [128, NTILE, 258], BF16); so_tm = sb("so_tm", [128, NTILE, 256], BF16)
            dTM = [Dep() for _ in range(NTILE)]
            ST = sb("ST", [128, 257]); dST = Dep()
            tmpSs = [sb(f"tmpS{i}", [128, 128]) for i in range(2)]; dtSs = [Dep(), Dep()]
            sTms = [sb(f"sTm{i}", [128, 128], BF16) for i in range(3)]; dsTs = [Dep() for _ in range(3)]
            wvs = [sb(f"wv{i}", [128, 258], BF16) for i in range(3)]; dwvs = [Dep() for _ in range(3)]
            tmpS = tmpSs[0]; dtS = dtSs[0]
            sTm = sb("sTm_s", [128, 64], BF16); dsT = Dep()
            Cdec = sb("Cdec", [128, 258], BF16); dCd = Dep()
            junk = sb("junk", [128, 256], BF16); dJ = Dep()
            wv = sb("wv_s", [128, 258], BF16); dwv = Dep()
            bmk = sb("bmk", [128, 16, 64], BF16); dBm = Dep()
            self.ldc(bmk[:].rearrange("p b t -> p (b t)"), L["bmask"][:, :], [dBm])
            C0n = sb("C0n", [128, 8, 2, 128]); dC0 = Dep()
            C0T = sb("C0T", [128, 16, 258], BF16); dC0T = Dep()
            n0T = sb("n0T", [128, 16]); dn0T = Dep()
            maskE = sb("maskE", [128, 16, 64], BF16); dmE = Dep()
            Qblk = sb("Qblk", [128, 16, 64], BF16); dQb = Dep()
            wvblk = sb("wvblk", [128, 8, 256], BF16); dwb = Dep()
            Cout = [sb(f"Cout{i}", [128, 2, 128]) for i in range(2)]; dCo = [Dep(), Dep()]
            BselW = sb("BselW", [128, 64], BF16); dBs = Dep()
            self.memset("pool", v_aug[:], 1.0, dTM)

            Hn = sb("Hn", [128, NTILE, 257]); dHn = [Dep() for _ in range(NTILE)]
            if os.environ.get("MK_PAD"):
                sb("pad", [128, int(os.environ["MK_PAD"]) * 256])
            hst = sb("hst", [128, 8, NTILE]); dhst = Dep()

            def head_norm_all(h):
                dn = hst[:, 0, :]; s1 = hst[:, 1, :]; s2 = hst[:, 2, :]; mean = hst[:, 3, :]; msq = hst[:, 4, :]
                rstd = hst[:, 5, :]; nb = hst[:, 6, :]
                self.memset("pool", hst[:], 0.0, [dhst])
                self.act(dn, Hn[:, :, 256], AF.Abs, dHn, [dhst])
                self.tt("dve", dn, dn, THR[:, :, h], ALU.max, [dGt], [dhst])
                self.recip(dn, dn, [dhst], [dhst])
                for i in range(NTILE):
                    nt = tile_nt(i)
                    self.act(Hn[0:nt, i, 0:256], Hn[0:nt, i, 0:256], AF.Identity, [dhst], [dHn[i], dhst], scale=hst[0:nt, 0, i:i + 1],
                             accum=hst[0:nt, 1, i:i + 1])
                    self.act(junk[0:nt, :], Hn[0:nt, i, 0:256], AF.Square, [dHn[i]], [dJ, dhst], accum=hst[0:nt, 2, i:i + 1])
                self.ts("dve", mean, s1, 1.0 / 256, None, ALU.mult, None, [dhst], [dhst])
                self.tt("dve", msq, mean, mean, ALU.mult, [dhst], [dhst])
                self.ts("dve", rstd, s2, 1.0 / 256, HN_EPS, ALU.mult, ALU.add, [dhst], [dhst])
                self.tt("dve", rstd, rstd, msq, ALU.subtract, [dhst], [dhst])
                self.act(rstd, rstd, AF.Sqrt, [dhst], [dhst])
                self.recip(rstd, rstd, [dhst], [dhst])
                self.tt("dve", nb, mean, rstd, ALU.mult, [dhst], [dhst])
                self.ts("dve", nb, nb, -1.0, None, ALU.mult, None, [dhst], [dhst])
                pts = {}

                def n1(i):
                    nt = tile_nt(i)
                    self.act(Hn[0:nt, i, 0:256], Hn[0:nt, i, 0:256], AF.Identity, [dhst], [dHn[i]], bias=hst[0:nt, 6, i:i + 1],
                             scale=hst[0:nt, 5, i:i + 1])
                    self.tt("dve", Hn[0:nt, i, 0:256], Hn[0:nt, i, 0:256], so_tms[h % 2][0:nt, i, :], ALU.mult, [dSO[h % 2][i]], [dHn[i]])

                def n2(i):
                    nt = tile_nt(i)
                    pt, dpt = self.bank()
                    for vb in range(2):
                        self.tr(pt[:, vb * 128:vb * 128 + nt], Hn[0:nt, i, vb * 128:(vb + 1) * 128], [dHn[i]], [dpt])
                    pts[i] = (pt, dpt)

                def n3(i):
                    nt = tile_nt(i)
                    t0 = 128 * i
                    pt, dpt = pts.pop(i)
                    self.cp("act" if i % 2 == 0 else "dve", hgT[:, 2 * h:2 * h + 2, t0:t0 + nt],
                            pt[:, 0:256].rearrange("p (a b) -> p a b", a=2)[:, :, 0:nt], [dpt], [dHG[min(i // 4, 4)]])

                for step in range(NTILE + 4):
                    if step < NTILE:
                        n1(step)
                    if 0 <= step - 2 < NTILE:
                        n2(step - 2)
                    if 0 <= step - 4 < NTILE:
                        n3(step - 4)

            def head_tail(po, dpo, i, h, nt):
                self.cp("act", Hn[0:nt, i, :], po[0:nt, 0:257], [dpo], [dHn[i]])

            Wm2 = sb("Wm2", [128, 8, 768], BF16); dWm2 = Dep()
            so_tm2 = sb("so_tm2", [128, NTILE, 256], BF16)
            Wms = [Wm, Wm2]; dWms = [dWm, dWm2]; so_tms = [so_tm, so_tm2]
            dSO = [[Dep() for _ in range(NTILE)] for _ in range(2)]

            def load_w(h):
                W_ = Wms[h % 2]; d_ = dWms[h % 2]
                self.ldc(W_[:, :, 0:128], wsrc(MQ0 + h * 128, 128), [d_])
                self.ldc(W_[:, :, 128:256], wsrc(MK0 + h * 128, 128), [d_])
                self.ldc(W_[:, :, 256:512], wsrc(MV0 + h * 256, 256), [d_])
                self.ldc(W_[:, :, 512:768], wsrc(MO0 + h * 256, 256), [d_])

            def c0n_load(h, half):
                for bq in range(8):
                    self.ld(C0n[:, bq, :, :], L["stC"][8 * half + bq, h].rearrange("(vb p) d -> p vb d", p=128), [dC0])

            def proj(h):
                    for b, (t0, tn) in enumerate(TBLK):
                        pq, dpq = self.bank(); pk, dpk = self.bank()
                        for k in range(8):
                            self.mm(pq[:, 0:tn], Wms[h % 2][:, k, 0:128], hT[:, k, t0:t0 + tn], k == 0, k == 7, [dWms[h % 2], dHT[b]], [dpq])
                        for k in range(8):
                            self.mm(pk[:, 0:tn], Wms[h % 2][:, k, 128:256], hT[:, k, t0:t0 + tn], k == 0, k == 7, [dWms[h % 2], dHT[b]], [dpk])
                        self.cp("act", qTh[:, t0:t0 + tn], pq[:, 0:tn], [dpq], [dQK[b]])
                        self.S.op("act", lambda e, t0=t0, tn=tn, pk=pk: e.mul(out=kTh[:, t0:t0 + tn], in_=pk[:, 0:tn], mul=KS), [dpk], [dQK[b]])
                    for i in range(NTILE):
                        nt = tile_nt(i)
                        pa, dpa = self.bank(); pb_, dpb_ = self.bank()
                        for k in range(8):
                            self.mm(pa[0:nt, 0:384], hT[:, k, 128 * i:128 * i + nt], Wms[h % 2][:, k, 128:512], k == 0, k == 7, [dWms[h % 2], dHT[min(i // 4, 4)]], [dpa])
                        for k in range(8):
                            self.mm(pb_[0:nt, 0:256], hT[:, k, 128 * i:128 * i + nt], Wms[h % 2][:, k, 512:768], k == 0, k == 7, [dWms[h % 2], dHT[min(i // 4, 4)]], [dpb_])
                        self.S.op("act", lambda e, i=i, nt=nt, pa=pa: e.mul(out=k_tm[0:nt, i, :], in_=pa[0:nt, 0:128], mul=KS), [dpa], [dTM[i]])
                        self.cp("dve", v_aug[0:nt, i, 0:256], pa[0:nt, 128:384], [dpa], [dTM[i]])
                        self.act(so_tms[h % 2][0:nt, i, :], pb_[0:nt, 0:256], AF.Sigmoid, [dpb_], [dSO[h % 2][i]])

            def body_(h):
                    i = 16
                    ps_, dps = self.bank()
                    self.mm(ps_[0:64, 0:64], kTh[:, SEQ:NT], qTh[:, SEQ:NT], True, True, [dQK[4]], [dps])
                    self.act(tmpS[0:64, 0:64], ps_[0:64, 0:64], AF.Identity, [dps, dGt], [dtS], scale=W_tm[0:64, 16, h:h + 1])
                    self.tt("dve", sTm[0:64, 0:64], tmpS[0:64, 0:64], mblkc_f[0:64, 0:64], ALU.mult, [dtS, dMk], [dsT])
                    self.tt("dve", maskE[:], bmk[:], ECrep[:, h, 16:32].unsqueeze(2).to_broadcast([128, 16, 64]), ALU.mult, [dBm, dRep], [dmE])
                    self.tt("dve", Qblk[:], maskE[:], qTh[:, SEQ:NT].unsqueeze(1).to_broadcast([128, 16, 64]), ALU.mult, [dmE, dQK[4]], [dQb])
                    pn0, dpn0 = self.bank()
                    self.tr(pn0[:, 0:64], n0tok[0:64, h * 128:(h + 1) * 128], [dN0], [dpn0])
                    self.cp("dve", n0T[:], pn0[:, 0:64].rearrange("p (b j) -> p b j", j=4)[:, :, 0], [dpn0], [dn0T])
                    self.act(wv[0:64, 0:257], v_aug[0:64, 16, 0:257], AF.Identity, [dTM[16], dGt], [dwv], scale=W_tm[0:64, 16, h:h + 1])
                    def halfproc(half):
                            b0 = 8 * half
                            for bb in range(8):
                                if bb % 2 == 0:
                                    pt, dpt = self.bank()
                                for vb in range(2):
                                    c0 = (bb % 2) * 256 + vb * 128
                                    self.tr(pt[:, c0:c0 + 128], C0n[:, bb, vb, :], [dC0], [dpt])
                                if bb % 2 == 1:
                                    self.cp("act", C0T[:, b0 + bb - 1:b0 + bb + 1, 0:256], pt[:, :].rearrange("p (a b) -> p a b", a=2), [dpt], [dC0T])
                            self.tt("dve", wvblk[0:64, :, :], wv[0:64, 0:256].unsqueeze(1).to_broadcast([64, 8, 256]),
                                    rmask_f[0:64, b0:b0 + 8].unsqueeze(2).to_broadcast([64, 8, 256]), ALU.mult, [dwv, dMk], [dwb])
                            for bb in range(8):
                                b = b0 + bb
                                pC, dpC = self.bank()
                                for vb in range(2):
                                    self.mm(pC[:, vb * 128:(vb + 1) * 128], wvblk[0:64, bb, vb * 128:(vb + 1) * 128], k_tm[0:64, 16, :], True, True,
                                            [dwb, dTM[16]], [dpC])
                                co = b % 2
                                self.stt("dve", Cout[co][:].rearrange("p a b -> p (a b)"), C0n[:, bb, :, :].rearrange("p a b -> p (a b)"),
                                         ECrep[:, h, 16 + b:17 + b], pC[:, 0:256], ALU.mult, ALU.add, [dC0, dRep, dpC], [dCo[co]])
                                self.store(L["Cs"][b, h].rearrange("(vb p) d -> p vb d", p=128), Cout[co][:], [dCo[co]])
                    halfproc(0)
                    c0n_load(h, 1)
                    pcs = {}

                    def pre(c):
                        t0 = 128 * c
                        ps_, dps = self.bank()
                        self.mm(ps_[:, 0:128], kTh[:, t0:t0 + 128], qTh[:, t0:t0 + 128], True, True, [dQK[c // 4]], [dps])
                        a = c % 2; b3 = c % 3
                        self.act(tmpSs[a][:], ps_[:, 0:128], AF.Identity, [dps, dGt], [dtSs[a]], scale=W_tm[:, c, h:h + 1])
                        self.tt("dve", sTms[b3][:], tmpSs[a][:], mle_f, ALU.mult, [dtSs[a], dMk], [dsTs[b3]])
                        self.act(wvs[b3][:, 0:257], v_aug[:, c, 0:257], AF.Identity, [dTM[c], dGt], [dwvs[b3]], scale=W_tm[:, c, h:h + 1])
                        pc, dpc = self.bank()
                        self.mm(pc[:, 0:257], k_tm[:, c, :], wvs[b3][:, 0:257], True, True, [dTM[c], dwvs[b3]], [dpc])
                        pcs[c] = (pc, dpc)

                    def main(c):
                        t0 = 128 * c
                        b3 = c % 3
                        po, dpo = self.bank()
                        self.mm(po[:, 0:257], sTms[b3][:], v_aug[:, c, 0:257], True, c == 0, [dsTs[b3], dTM[c]], [dpo])
                        if c > 0:
                            self.act(Cdec[:, 0:257], ST[:], AF.Identity, [dST, dRep], [dCd], scale=ECrep[:, h, c:c + 1])
                            self.mm(po[:, 0:257], qTh[:, t0:t0 + 128], Cdec[:, 0:257], False, True, [dQK[c // 4], dCd], [dpo])
                        head_tail(po, dpo, c, h, 128)
                        pc, dpc = pcs.pop(c)
                        if c == 0:
                            self.cp("dve", ST[:], pc[:, 0:257], [dpc], [dST])
                        else:
                            self.stt("dve", ST[:], ST[:], ECrep[:, h, c:c + 1], pc[:, 0:257], ALU.mult, ALU.add, [dpc, dRep], [dST])

                    pre(0)
                    for c in range(16):
                        if c + 1 < 16:
                            pre(c + 1)
                        main(c)
                    pt, dpt = self.bank()
                    for vb in range(2):
                        self.tr(pt[:, vb * 128:(vb + 1) * 128], ST[:, vb * 128:(vb + 1) * 128], [dST], [dpt])
                    self.cp("dve", Cout[0][:].rearrange("p a b -> p (a b)"), pt[:, 0:256], [dpt], [dCo[0]])
                    self.store(L["Cp"][h].rearrange("(vb p) d -> p vb d", p=128), Cout[0][:], [dCo[0]])
                    self.store_nc(L["np"][h:h + 1, :].rearrange("o d -> d o"), ST[:, 256:257], [dST])
                    halfproc(1)
                    self.cp("dve", C0T[:, :, 256], n0T[:, :], [dn0T], [dC0T])
                    po, dpo = self.bank()
                    self.mm(po[0:64, 0:257], sTm[0:64, 0:64], v_aug[0:64, 16, 0:257], True, False, [dsT, dTM[16]], [dpo])
                    for b in range(16):
                        self.mm(po[0:64, 0:257], Qblk[:, b, :], C0T[:, b, 0:257], False, b == 15, [dQb, dC0T], [dpo])
                    head_tail(po, dpo, 16, h, 64)
                    self.act(BselW[0:64, :], blk_f[0:64, 0:64], AF.Identity, [dMk, dGt], [dBs], scale=W_tm[0:64, 16, h:h + 1])
                    pN, dpN = self.bank()
                    self.mm(pN[0:64, 0:128], BselW[0:64, :], k_tm[0:64, 16, :], True, True, [dBs, dTM[16]], [dpN])
                    self.stt("dve", nnew[0:64, h * 128:(h + 1) * 128], n0tok[0:64, h * 128:(h + 1) * 128], EcTok[0:64, h:h + 1], pN[0:64, 0:128],
                             ALU.mult, ALU.add, [dN0, dEt, dpN], [dNn])

            load_w(0); c0n_load(0, 0); proj(0); load_w(1)
            for h in range(4):
                body_(h)
                if h + 1 < 4:
                    c0n_load(h + 1, 0)
                    proj(h + 1)
                    if h + 2 < 4:
                        load_w(h + 2)
                head_norm_all(h)
            for b in range(16):
                self.store(L["ns"][b:b + 1, :], nnew[4 * b:4 * b + 1, :], [dNn])
            S.emit()

    def ln_tile(self, x, dX, nt, lnG, lnB, dLn, st, dSt, junk, dJ, eps):
        self.memset("pool", st[0:nt, 0:2], 0.0, [dSt])
        self.act(junk[0:nt, :], x, AF.Identity, [dX], [dJ, dSt], accum=st[0:nt, 0:1])
        self.act(junk[0:nt, :], x, AF.Square, [dX], [dJ, dSt], accum=st[0:nt, 1:2])
        self.ts("dve", st[0:nt, 2:3], st[0:nt, 0:1], 1.0 / D, None, ALU.mult, None, [dSt], [dSt])
        self.tt("dve", st[0:nt, 3:4], st[0:nt, 2:3], st[0:nt, 2:3], ALU.mult, [dSt], [dSt])
        self.ts("dve", st[0:nt, 4:5], st[0:nt, 1:2], 1.0 / D, eps, ALU.mult, ALU.add, [dSt], [dSt])
        self.tt("dve", st[0:nt, 4:5], st[0:nt, 4:5], st[0:nt, 3:4], ALU.subtract, [dSt], [dSt])
        self.act(st[0:nt, 4:5], st[0:nt, 4:5], AF.Sqrt, [dSt], [dSt])
        self.recip(st[0:nt, 5:6], st[0:nt, 4:5], [dSt], [dSt])
        self.tt("dve", st[0:nt, 6:7], st[0:nt, 2:3], st[0:nt, 5:6], ALU.mult, [dSt], [dSt])
        self.ts("dve", st[0:nt, 6:7], st[0:nt, 6:7], -1.0, None, ALU.mult, None, [dSt], [dSt])
        self.act(x, x, AF.Identity, [dSt], [dX], bias=st[0:nt, 6:7], scale=st[0:nt, 5:6])
        self.tt("dve", x, x, lnG[0:nt, :], ALU.mult, [dLn], [dX])
        self.tt("dve", x, x, lnB[0:nt, :], ALU.add, [dLn], [dX])

    def merge(self, L, hT, dHT, hgT, dHG):
        nc, S = self.nc, self.S
        w_in = L["w_in"]
        GM0, GA0 = 4616, 5640
        wsrc = lambda w, c0, n: w[:, c0:c0 + n].rearrange("(kc p) n -> p kc n", p=128)
        with ExitStack() as es:
            sb = lambda name, shape, dt=F32: es.enter_context(nc.sbuf_tensor(self.uname(name), list(shape), dt))
            oaT = sb("oaT2", [128, 8, NT], BF16); dOA = Dep()
            self.ld(oaT[:].rearrange("p c t -> p (c t)"), L["oa_scr"][:, :], [dOA])
            Wa = sb("Wa", [128, 8, D], BF16); dWa = Dep()
            Wg = sb("Wg", [128, 8, D], BF16); dWg = Dep()
            mraw = sb("mraw", [8, 128]); dMr = Dep(); mngT = sb("mngT", [128, 8]); dMn = Dep()
            zt = sb("zt", [128, 8, 512], BF16); dzt = Dep()
            sg = [sb(f"sg{i}", [128, 512]) for i in range(2)]; dsg = [Dep(), Dep()]
            za = [sb(f"za{i}", [128, 512]) for i in range(2)]; dza = [Dep(), Dep()]
            xt = sb("xt", [128, 3, D]); dxt = [Dep(), Dep(), Dep()]
            tG = [sb(f"tG{i}", [128, 512]) for i in range(2)]; dtG = [Dep(), Dep()]
            lnG = sb("lnG", [128, D]); lnB = sb("lnB", [128, D]); dLn = Dep()
            sts = [sb(f"st{i}", [128, 8]) for i in range(2)]; dSts = [Dep(), Dep()]
            junk = sb("junk", [128, D], BF16); dJ = Dep()
            self.ld(lnG[:], L["ln"]["ln2_g"][0:1, :].broadcast_to([128, D]), [dLn])
            self.ld(lnB[:], L["ln"]["ln2_b"][0:1, :].broadcast_to([128, D]), [dLn])
            self.ld(mraw[:], L["mng"][:, :], [dMr])
            pm, dpm = self.bank()
            self.tr(pm[:, 0:8], mraw[0:8, :], [dMr], [dpm])
            self.cp("dve", mngT[:], pm[:, 0:8], [dpm], [dMn])
            self.ldc(Wa[:], wsrc(L["w_bm"], 0, D), [dWa])
            for k in range(8):
                self.act(Wa[:, k, :], Wa[:, k, :], AF.Identity, [dMn], [dWa], scale=mngT[:, k:k + 1])
            self.ldc(Wg[:], wsrc(w_in, GM0, D), [dWg])
            cnt = [0]

            def branch(src, dsrc, stage):
                for b, (t0, tn) in enumerate(TBLK):
                    for oc in range(8):
                        py, dpy = self.bank(); pg, dpg = self.bank()
                        for k in range(8):
                            self.mm(py[:, 0:tn], Wa[:, k, oc * 128:(oc + 1) * 128], src[:, k, t0:t0 + tn], k == 0, k == 7, [dWa, dsrc(b)], [dpy])
                        for k in range(8):
                            self.mm(pg[:, 0:tn], Wg[:, k, oc * 128:(oc + 1) * 128], hT[:, k, t0:t0 + tn], k == 0, k == 7, [dWg, dHT[b]], [dpg])
                        i = cnt[0] % 2; cnt[0] += 1
                        self.act(sg[i][:, 0:tn], pg[:, 0:tn], AF.Sigmoid, [dpg], [dsg[i]])
                        if stage == 1:
                            self.tt("dve", zt[:, oc, 0:tn], sg[i][:, 0:tn], py[:, 0:tn], ALU.mult, [dsg[i], dpy], [dzt])
                        else:
                            self.tt("dve", za[i][:, 0:tn], sg[i][:, 0:tn], py[:, 0:tn], ALU.mult, [dsg[i], dpy], [dza[i]])
                            self.tt("pool", hgT[:, oc, t0:t0 + tn], hgT[:, oc, t0:t0 + tn], za[i][:, 0:tn], ALU.add, [dza[i]], [dHG[b]])
                    if stage == 1:
                        self.cp("pool", hgT[:, :, t0:t0 + tn], zt[:, :, 0:tn], [dzt], [dHG[b]])

            branch(hgT, lambda b: dHG[b], 1)
            self.ldc(Wa[:], wsrc(L["w_ba"], 0, D), [dWa])
            self.ldc(Wg[:], wsrc(w_in, GA0, D), [dWg])
            branch(oaT, lambda b: dOA, 2)
            if "mixT" in self.debug:
                self.dbg("mixT", hgT[:], dHG, [128, 8, NT])
            S.emit()
        with ExitStack() as es:
            sb = lambda name, shape, dt=F32: es.enter_context(nc.sbuf_tensor(self.uname(name), list(shape), dt))
            Wo = sb("Wo", [128, 8, D], BF16); dWo = Dep()
            xres = sb("xres", [128, NTILE, D]); dX = [Dep() for _ in range(NTILE)]
            tG = [sb(f"tG{i}", [128, 512]) for i in range(2)]; dtG = [Dep(), Dep()]
            lnG = sb("lnG", [128, D]); lnB = sb("lnB", [128, D]); dLn = Dep()
            st = sb("lnst", [128, 7, NTILE]); dSt = Dep()
            junk = sb("junk", [128, D], BF16); dJ = Dep()
            self.memset("pool", st[:], 0.0, [dSt])
            self.ldc(Wo[:], wsrc(L["w_out"], 0, D), [dWo])
            self.ld(lnG[:], L["ln"]["ln2_g"][0:1, :].broadcast_to([128, D]), [dLn])
            self.ld(lnB[:], L["ln"]["ln2_b"][0:1, :].broadcast_to([128, D]), [dLn])
            for i in range(NTILE):
                nt = tile_nt(i)
                src = L["x1p"][128 * i:128 * i + nt, :] if i < 16 else L["x1s"][:, :]
                self.ld(xres[0:nt, i, :], src, [dX[i]])
                self.S.op("act", lambda e, i=i, nt=nt: e.mul(out=xres[0:nt, i, :], in_=xres[0:nt, i, :], mul=ALPHA), [dX[i]], [dX[i]])
            for i in range(NTILE):
                nt = tile_nt(i)
                which = 0 if i < 16 else 1
                for cb in range(2):
                    po, dpo = self.bank()
                    for k in range(8):
                        self.mm(po[0:nt, :], hgT[:, k, 128 * i:128 * i + nt], Wo[:, k, cb * 512:(cb + 1) * 512], k == 0, k == 7,
                                [dHG[min(i // 4, 4)], dWo], [dpo])
                    self.tt("dve", tG[cb][0:nt, :], po[0:nt, :], self.G[0:nt, which, cb * 512:(cb + 1) * 512], ALU.mult, [dpo, self.dG], [dtG[cb]])
                    self.tt("pool", xres[0:nt, i, cb * 512:(cb + 1) * 512], xres[0:nt, i, cb * 512:(cb + 1) * 512], tG[cb][0:nt, :], ALU.add,
                            [dtG[cb]], [dX[i]])
            self.layer_norm_tiles(xres, dX, lnG, lnB, dLn, st, dSt, junk, dJ, LN_EPS)
            for i in range(NTILE):
                nt = tile_nt(i)
                dst = L["x2p"][128 * i:128 * i + nt, :] if i < 16 else L["x2s"][:, :]
                self.store(dst, xres[0:nt, i, :], [dX[i]])
            S.emit()
            if "x2" in self.debug:
                o1 = nc.dram_tensor("dbg_x2p", [SEQ, D], F32, kind="ExternalOutput").ap()
                o2 = nc.dram_tensor("dbg_x2s", [NS, D], F32, kind="ExternalOutput").ap()
                self.store(o1[:, :], L["x2p"][:, :], [])
                self.store(o2[:, :], L["x2s"][:, :], [])
                S.emit()


_CACHE = {}


def _consts():
    half = 32
    inv = (10000.0 ** (-np.arange(half, dtype=np.float32) / half)).astype(np.float32)
    pos = np.concatenate([np.arange(SEQ), PAST + (np.arange(NS) % 4)]).astype(np.float32)
    ang = pos[None, :] * inv[:, None]
    cos = np.cos(ang).astype(np.float32); sin = np.sin(ang).astype(np.float32)
    ropec = np.concatenate([cos, cos, cos, cos], 0)
    ropes = np.concatenate([-sin, sin, -sin, sin], 0)
    ident = np.eye(128, dtype=np.float32)
    s = np.arange(128)[:, None]; t = np.arange(128)[None, :]
    m_le = (s <= t).astype(np.float32)
    m_gt = (s > t).astype(np.float32)
    blk = ((s // 4) == (t // 4)).astype(np.float32)
    m_blkc = blk * m_le
    m_cache = np.zeros((128, 128), np.float32)
    for col in range(128):
        tt = col % 4
        m_cache[:, col] = (np.arange(128) > tt)
    last = np.zeros((128, 128), np.float32)
    last[:, 0:16] = ((np.arange(128)[:, None] // 4) == np.arange(16)[None, :])
    cmask = np.concatenate([m_le, m_gt, m_blkc, blk, m_cache, last], 1)
    bm = ((np.arange(64)[None, :] // 4) == np.arange(16)[:, None]).astype(np.float32)
    bmask = np.broadcast_to(bm.reshape(1, 1024), (128, 1024)).copy()
    return dict(ident=ident, ropec=ropec.astype(np.float32), ropes=ropes.astype(np.float32), cmask=cmask.astype(np.float32),
                bmask=bmask)


def kernel(**inputs):
    debug = tuple(os.environ.get("MK_DEBUG", "").split(",")) if os.environ.get("MK_DEBUG") else ()
    key = debug
    if key not in _CACHE:
        kb = KB(debug)
        kb.build()
        _CACHE[key] = kb
    kb = _CACHE[key]
    f = lambda a: np.ascontiguousarray(np.asarray(a, dtype=np.float32))
    I = {k: f(v) for k, v in inputs.items()}
    cst = _consts()
    shared = dict(
        w_ada=I["w_ada"][0], b_ada=I["b_ada"][0].reshape(72, 128), b_ada_row=I["b_ada"][0].reshape(9, D),
        w_up1=I["w_ffn1_up"][0], w_dn1=I["w_ffn1_down"][0], w_up2=I["w_ffn2_up"][0], w_dn2=I["w_ffn2_down"][0],
        ln1_g=I["ln1_g"], ln1_b=I["ln1_b"], ln2_g=I["ln2_g"], ln2_b=I["ln2_b"], ln3_g=I["ln3_g"], ln3_b=I["ln3_b"],
        w_in=I["w_in"][0], b_ig=I["b_igate"], b_fg=I["b_fgate"], mng=I["m_norm_g"][0].reshape(8, 128), sinks=I["sinks"],
        w_bm=I["w_branch_m"][0], w_ba=I["w_branch_a"][0], w_out=I["w_out"][0], **cst)
    in_maps = []
    for c in range(8):
        sl = slice(16 * c, 16 * c + 16)
        m = dict(shared)
        m["xp"] = I["x_prompt"][c]
        m["xs"] = I["x_sample"][sl].reshape(NS, D)
        m["c_all"] = np.concatenate([I["c_prompt"][c:c + 1], I["c_sample"][sl]], 0)
        m["stC"] = I["state_mlstm_C"][0][sl]
        m["stn"] = I["state_mlstm_n"][0][sl].reshape(16, 512)
        m["stm"] = I["state_mlstm_m"][0][sl]
        m["ck"] = I["cache_swa_k"][0][sl].reshape(16, 128, 256)
        m["cv"] = I["cache_swa_v"][0][sl].reshape(16, 128, 256)
        in_maps.append(m)
    res = run_bass_kernel_spmd(kb.nc, in_maps, core_ids=list(range(8)))
    R = res.results
    kernel.last_results = R
    cat = lambda k: np.stack([np.asarray(r[k]) for r in R], 0)
    y_p = cat("yp").reshape(8, SEQ, D)
    y_s = cat("ys").reshape(128, 4, D)
    C_p = cat("Cp").reshape(1, 8, 4, 256, 128)
    n_p = cat("np").reshape(1, 8, 4, 128)
    m_p = cat("mp").reshape(1, 8, 4)
    k_p = cat("kp").reshape(1, 8, 128, 4, 64)
    v_p = cat("vp").reshape(1, 8, 128, 4, 64)
    C_s = cat("Cs").reshape(1, 128, 4, 256, 128)
    n_s = cat("ns").reshape(1, 128, 4, 128)
    m_s = cat("ms").reshape(1, 128, 4)
    k_s = cat("ks").reshape(1, 128, 128, 4, 64)
    v_s = cat("vs").reshape(1, 128, 128, 4, 64)
    return tuple(np.ascontiguousarray(a, dtype=np.float32) for a in (y_p, y_s, C_p, n_p, m_p, k_p, v_p, C_s, n_s, m_s, k_s, v_s))
```

```python
import os
import numpy as np
from contextlib import ExitStack
import concourse.bass as bass
import concourse.mybir as mybir
from concourse.bass_utils import run_bass_kernel_spmd

F32 = mybir.dt.float32
BF16 = mybir.dt.bfloat16
AF = mybir.ActivationFunctionType
ALU = mybir.AluOpType
AX = mybir.AxisListType

D = 1024
SEQ = 2048
NS = 64
NT = SEQ + NS
NTILE = 17
DFF = 2816
NFC = 22
ALPHA = 2.0 ** 0.25
LN_EPS = 1e-5
HN_EPS = 1e-6
PAST = 8192
SEM_LIMIT = 28000
TBLK = [(0, 512), (512, 512), (1024, 512), (1536, 512), (2048, 64)]


def tile_nt(i):
    return 128 if i < 16 else 64


def blk_tiles(b):
    return [4 * b + j for j in range(4)] if b < 4 else [16]


class Dep:
    __slots__ = ("w", "r")

    def __init__(self):
        self.w = {}
        self.r = {}


class Sched:
    ENGS = ("pe", "act", "dve", "pool", "sp")

    def __init__(self, nc, es):
        self.nc = nc
        self.es = es
        self.ops = {e: [] for e in self.ENGS}
        self.cnt = {e: 0 for e in self.ENGS}
        self.csem = {e: None for e in self.ENGS}
        self.seen = {e: {} for e in self.ENGS}
        self.nsem = 0
        self.dstates = []
        self.misc = {e: [[None, 0] for _ in range(16)] for e in ("sp", "pool", "act")}
        for e in self.misc:
            for m in self.misc[e]:
                self.dstates.append(m)
        self.misc_i = {e: 0 for e in self.misc}
        self.final = []

    def new_sem(self, name):
        self.nsem += 1
        return self.es.enter_context(self.nc.semaphore(f"{name}_{self.nsem}"))

    def dstate(self):
        s = [None, 0]
        self.dstates.append(s)
        return s

    def _counter(self, eng):
        if self.csem[eng] is None or self.cnt[eng] >= SEM_LIMIT:
            self.csem[eng] = self.new_sem("c" + eng)
            self.cnt[eng] = 0
        return self.csem[eng]

    def _collect(self, eng, reads, writes, skip_own=True, extra=()):
        need = {}

        def add(tok):
            sem, v = tok
            k = id(sem)
            if k not in need or need[k][1] < v:
                need[k] = (sem, v)
        for d in reads:
            for tok in d.w.values():
                add(tok)
        for d in writes:
            for tok in d.w.values():
                add(tok)
            for tok in d.r.values():
                add(tok)
        for tok in extra:
            add(tok)
        waits = []
        seen = self.seen[eng]
        own = self.csem[eng]
        for k, (sem, v) in need.items():
            if skip_own and own is not None and sem is own:
                continue
            if seen.get(k, 0) >= v:
                continue
            seen[k] = v
            waits.append((sem, v))
        return waits

    def op(self, eng, fn, reads=(), writes=()):
        waits = self._collect(eng, reads, writes, skip_own=(eng == "pe"))
        sem = self._counter(eng)
        self.cnt[eng] += 1
        tok = (sem, self.cnt[eng])
        k = id(sem)
        for d in reads:
            d.r[k] = tok
        for d in writes:
            d.w[k] = tok
        self.ops[eng].append((waits, fn, (sem, 1)))
        return tok

    def dma(self, eng, fn, reads=(), writes=(), st=None, final=False):
        if st is None:
            st = self.misc[eng][self.misc_i[eng] % len(self.misc[eng])]
            self.misc_i[eng] += 1
        extra = []
        if st[0] is not None and st[1] + 16 > SEM_LIMIT:
            st[0] = None
        if st[0] is None:
            st[0] = self.new_sem("d")
            st[1] = 0
        elif st[1] > 0:
            extra.append((st[0], st[1]))
        waits = self._collect(eng, reads, writes, skip_own=False, extra=extra)
        st[1] += 16
        tok = (st[0], st[1])
        k = id(st[0])
        for d in reads:
            d.r[k] = tok
        for d in writes:
            d.w[k] = tok
        self.ops[eng].append((waits, fn, (st[0], 16)))
        if final:
            self.final.append(tok)
        return tok

    def emit(self, last=False):
        nc = self.nc
        bar = []
        for e in self.ENGS:
            if self.csem[e] is not None and self.cnt[e] > 0:
                bar.append((e, self.csem[e], self.cnt[e]))
        dbar = [(s[0], s[1]) for s in self.dstates if s[0] is not None and s[1] > 0]
        ops = self.ops
        seen = self.seen

        def run(engine, name):
            for waits, fn, (sem, inc) in ops[name]:
                for ws, wv in waits:
                    engine.wait_ge(ws, wv)
                fn(engine).then_inc(sem, inc)
            for e, sem, v in bar:
                if e != name and seen[name].get(id(sem), 0) < v:
                    engine.wait_ge(sem, v)
                    seen[name][id(sem)] = v
            for sem, v in dbar:
                if seen[name].get(id(sem), 0) < v:
                    engine.wait_ge(sem, v)
                    seen[name][id(sem)] = v

        with nc.Block() as block:
            @block.sync
            def _(e):
                run(e, "sp")

            @block.tensor
            def _(e):
                run(e, "pe")

            @block.scalar
            def _(e):
                run(e, "act")

            @block.vector
            def _(e):
                run(e, "dve")

            @block.gpsimd
            def _(e):
                run(e, "pool")
        self.ops = {e: [] for e in self.ENGS}


class KB:
    def __init__(self, debug=()):
        self.debug = set(debug)
        self.nc = bass.Bass("TRN2", target_bir_lowering=False)
        self.dram = {}
        self.dbg_out = {}

    def uname(self, name):
        self.ucnt = getattr(self, "ucnt", 0) + 1
        return f"s{self.ucnt}_{name}"

    def din(self, name, shape):
        self.dram[name] = self.nc.dram_tensor(name, list(shape), F32, kind="ExternalInput").ap()
        return self.dram[name]

    def dout(self, name, shape):
        self.dram[name] = self.nc.dram_tensor(name, list(shape), F32, kind="ExternalOutput").ap()
        return self.dram[name]

    def dscr(self, name, shape):
        self.dram[name] = self.nc.dram_tensor(name, list(shape), F32, kind="Internal").ap()
        return self.dram[name]

    def mm(self, out, lhsT, rhs, start, stop, R, W):
        self.S.op("pe", lambda e: e.matmul(out, lhsT=lhsT, rhs=rhs, start=start, stop=stop), R, W)

    def tr(self, out, in_, R, W):
        n = in_.shape[0]
        ident = self.ident[0:n, 0:n]
        self.S.op("pe", lambda e: e.transpose(out=out, in_=in_, identity=ident), list(R) + [self.dIdent], W)

    def act(self, out, in_, func, R, W, bias=None, scale=None, accum=None):
        kw = {}
        if bias is not None:
            kw["bias"] = bias
        if scale is not None:
            kw["scale"] = scale
        if accum is not None:
            kw["accum_out"] = accum
        self.S.op("act", lambda e: e.activation(out=out, in_=in_, func=func, **kw), R, W)

    def tt(self, eng, out, in0, in1, op, R, W):
        self.S.op(eng, lambda e: e.tensor_tensor(out=out, in0=in0, in1=in1, op=op), R, W)

    def ts(self, eng, out, in0, s1, s2, op0, op1, R, W):
        if s2 is None:
            s2 = 0.0
            op1 = ALU.add
        self.S.op(eng, lambda e: e.tensor_scalar(out=out, in0=in0, scalar1=s1, scalar2=s2, op0=op0, op1=op1), R, W)

    def stt(self, eng, out, in0, scalar, in1, op0, op1, R, W):
        self.S.op(eng, lambda e: e.scalar_tensor_tensor(out=out, in0=in0, scalar=scalar, in1=in1, op0=op0, op1=op1), R, W)

    def cp(self, eng, out, in_, R, W):
        if eng == "act":
            self.S.op("act", lambda e: e.copy(out=out, in_=in_), R, W)
        else:
            self.S.op(eng, lambda e: e.tensor_copy(out=out, in_=in_), R, W)

    def memset(self, eng, ap, val, W):
        self.S.op(eng, lambda e: e.memset(ap, val), (), W)

    def recip(self, out, in_, R, W):
        self.S.op("dve", lambda e: e.reciprocal(out=out, in_=in_), R, W)

    def ld(self, out, in_, W, R=(), eng="sp", st=None):
        return self.S.dma(eng, lambda e: e.dma_start(out=out, in_=in_), R, W, st=st)

    def ldc(self, out, in_, W, R=(), st=None):
        return self.S.dma("pool", lambda e: e.dma_start(out=out, in_=in_), R, W, st=st)

    def store(self, out, in_, R, W=(), final=True, eng="sp"):
        return self.S.dma(eng, lambda e: e.dma_start(out=out, in_=in_), R, W, final=final)

    def ld_nc(self, out, in_, W, R=()):
        return self.S.dma("sp", lambda e: e.dma_start(out=out, in_=in_, allow_slow_non_contiguous=True), R, W)

    def store_nc(self, out, in_, R):
        return self.S.dma("sp", lambda e: e.dma_start(out=out, in_=in_, allow_slow_non_contiguous=True), R, (), final=True)

    def dbg(self, name, ap, dep, shape):
        if name not in self.debug:
            return
        o = self.nc.dram_tensor("dbg_" + name, list(shape), ap.dtype, kind="ExternalOutput").ap()
        self.dbg_out[name] = o
        deps = dep if isinstance(dep, (list, tuple)) else [dep]
        self.store(o, ap, list(deps))

    def bank(self):
        i = self.bank_i % 8
        self.bank_i += 1
        return self.PB[i], self.dPB[i]

    def build(self):
        nc = self.nc
        din, dout = self.din, self.dout
        xp = din("xp", [SEQ, D]); xs = din("xs", [NS, D]); c_all = din("c_all", [17, D])
        stC = din("stC", [16, 4, 256, 128]); stn = din("stn", [16, 512]); stm = din("stm", [16, 4])
        ck = din("ck", [16, 128, 256]); cv = din("cv", [16, 128, 256])
        w_ada = din("w_ada", [D, 9 * D]); b_ada = din("b_ada", [72, 128]); self.b_ada_row = din("b_ada_row", [9, D])
        w_up1 = din("w_up1", [D, 2 * DFF]); w_dn1 = din("w_dn1", [DFF, D])
        w_up2 = din("w_up2", [D, 2 * DFF]); w_dn2 = din("w_dn2", [DFF, D])
        ln = {k: din(k, [1, D]) for k in ("ln1_g", "ln1_b", "ln2_g", "ln2_b", "ln3_g", "ln3_b")}
        w_in = din("w_in", [D, 6664]); b_ig = din("b_ig", [1, 4]); b_fg = din("b_fg", [1, 4])
        mng = din("mng", [8, 128]); sinks = din("sinks", [1, 16])
        w_bm = din("w_bm", [D, D]); w_ba = din("w_ba", [D, D]); w_out = din("w_out", [D, D])
        identd = din("ident", [128, 128]); ropec = din("ropec", [128, NT]); ropes = din("ropes", [128, NT])
        cmask = din("cmask", [128, 128 * 6]); bmask = din("bmask", [128, 1024])
        oa_scr = nc.dram_tensor("oa_scr", [128, 8 * NT], BF16, kind="Internal").ap()
        yp = dout("yp", [SEQ, D]); ys = dout("ys", [NS, D])
        Cp = dout("Cp", [4, 256, 128]); np_ = dout("np", [4, 128]); mp = dout("mp", [4, 1])
        kp = dout("kp", [128, 256]); vp = dout("vp", [128, 256])
        Cs = dout("Cs", [16, 4, 256, 128]); ns_ = dout("ns", [16, 512]); ms = dout("ms", [16, 4])
        ks = dout("ks", [16, 128, 256]); vs = dout("vs", [16, 128, 256])
        x1p = self.dscr("x1p", [SEQ, D]); x1s = self.dscr("x1s", [NS, D])
        x2p = self.dscr("x2p", [SEQ, D]); x2s = self.dscr("x2s", [NS, D])

        with ExitStack() as es:
            self.es = es
            self.S = S = Sched(nc, es)
            sb = lambda name, shape, dt=F32: es.enter_context(nc.sbuf_tensor(self.uname(name), list(shape), dt))
            self.PB = [es.enter_context(nc.psum_tensor(f"pb{i}", [128, 512], F32)) for i in range(8)]
            self.dPB = [Dep() for _ in range(8)]
            self.bank_i = 0
            self.ident = sb("ident", [128, 128]); self.dIdent = Dep()
            self.ld(self.ident[:], identd[:, :], [self.dIdent])
            self.modT = sb("modT", [128, 2, 8, 17]); self.dModT = Dep()
            self.modS = sb("modS", [128, 2, 8, 64]); self.dModS = Dep()
            self.G = sb("G", [128, 2, D]); self.dG = Dep()
            self.scT = sb("scT", [128, 8, 17], BF16); self.dScT = Dep()
            self.lhsP = sb("lhsP", [128, 8, 128], BF16); self.lhsS = sb("lhsS", [128, 8, 64], BF16); self.dLhs = Dep()
            self.badaT = sb("badaT", [128, 72]); self.dBada = Dep()
            self.setup_ada(c_all, b_ada)
            S.emit()
            self.ada_stage(0, w_ada, b_ada)
            self.dbg("modT", self.modT[:], self.dModT, [128, 2, 8, 17])
            self.dbg("modS", self.modS[:], self.dModS, [128, 2, 8, 64])
            self.dbg("G", self.G[:], self.dG, [128, 2, D])
            S.emit()
            self.ffn(xp, xs, x1p, x1s, w_up1, w_dn1, ln["ln1_g"], ln["ln1_b"], "f1")
            if "x1" in self.debug:
                o1 = nc.dram_tensor("dbg_x1p", [SEQ, D], F32, kind="ExternalOutput").ap()
                o2 = nc.dram_tensor("dbg_x1s", [NS, D], F32, kind="ExternalOutput").ap()
                self.store(o1[:, :], x1p[:, :], [])
                self.store(o2[:, :], x1s[:, :], [])
                S.emit()
            if "stop1" in self.debug:
                return nc
            self.ada_stage(1, w_ada, b_ada)
            S.emit()
            self.mixer(dict(locals(), **self.dram))
            self.ada_stage(2, w_ada, b_ada)
            S.emit()
            self.ffn(x2p, x2s, yp, ys, w_up2, w_dn2, ln["ln3_g"], ln["ln3_b"], "f2")
        return nc

    def setup_ada(self, c_all, b_ada):
        with ExitStack() as es:
            nc = self.nc
            sb = lambda name, shape, dt=F32: es.enter_context(nc.sbuf_tensor(self.uname(name), list(shape), dt))
            craw = sb("craw", [17, D]); dC = Dep()
            csil = sb("csil", [17, D]); dCs = Dep()
            braw = sb("braw", [72, 128]); dB = Dep()
            scTf = sb("scTf", [128, 8, 17]); dScTf = Dep()
            self.ld(craw[:], c_all[:, :], [dC])
            self.ld(braw[:], b_ada[:, :], [dB])
            self.act(csil[:], craw[:], AF.Silu, [dC], [dCs])
            pb, dpb = self.bank()
            for k in range(8):
                self.tr(pb[:, k * 17:(k + 1) * 17], csil[0:17, k * 128:(k + 1) * 128], [dCs], [dpb])
            self.cp("dve", scTf[:].rearrange("p a b -> p (a b)"), pb[:, 0:136], [dpb], [dScTf])
            self.cp("dve", self.scT[:], scTf[:], [dScTf], [self.dScT])
            self.cp("dve", self.lhsP[:], scTf[:, :, 0:1].to_broadcast([128, 8, 128]), [dScTf], [self.dLhs])
            for k in range(8):
                self.cp("dve", self.lhsS[:, k, :].rearrange("p (b j) -> p b j", j=4),
                        scTf[:, k, 1:17].unsqueeze(2).to_broadcast([128, 16, 4]), [dScTf], [self.dLhs])
            pb2, dpb2 = self.bank()
            self.tr(pb2[:, 0:72], braw[0:72, :], [dB], [dpb2])
            self.cp("dve", self.badaT[:], pb2[:, 0:72], [dpb2], [self.dBada])
            self.S.emit()

    def ada_stage(self, st, w_ada, b_ada):
        nc = self.nc
        with ExitStack() as es:
            sb = lambda name, shape, dt=F32: es.enter_context(nc.sbuf_tensor(self.uname(name), list(shape), dt))
            wa = [sb(f"wa{i}", [128, 8, D], BF16) for i in range(2)]
            dwa = [Dep(), Dep()]
            bG = sb("bG", [128, D]); dbG = Dep()
            tmpG = sb("tmpG", [128, 512]); dtG = Dep()
            for ci in range(3):
                j = 3 * st + ci
                s = ci % 2
                self.ldc(wa[s][:], w_ada[:, j * D:(j + 1) * D].rearrange("(kc p) n -> p kc n", p=128), [dwa[s]])
                if ci < 2:
                    pb, dpb = self.bank()
                    for f in range(8):
                        for k in range(8):
                            self.mm(pb[:, f * 17:(f + 1) * 17], wa[s][:, k, f * 128:(f + 1) * 128], self.scT[:, k, :],
                                    k == 0, k == 7, [dwa[s], self.dScT], [dpb])
                    self.stt("dve", self.modT[:, ci, :, :], pb[:, 0:136].rearrange("p (a b) -> p a b", a=8),
                             1.0 if ci == 1 else 0.0,
                             self.badaT[:, j * 8:(j + 1) * 8].unsqueeze(2).to_broadcast([128, 8, 17]),
                             ALU.add, ALU.add, [dpb, self.dBada], [self.dModT])
                else:
                    a = 1.0 if st == 1 else 0.5
                    self.ld(bG[:], self.b_ada_row[j:j + 1, :].broadcast_to([128, D]), [dbG])
                    for which, lhs, n in ((0, self.lhsP, 128), (1, self.lhsS, 64)):
                        for cb in range(2):
                            pb, dpb = self.bank()
                            for k in range(8):
                                self.mm(pb[0:n, :], lhs[:, k, :], wa[s][:, k, cb * 512:(cb + 1) * 512], k == 0, k == 7,
                                        [dwa[s], self.dLhs], [dpb])
                            self.tt("dve", tmpG[0:n, :], pb[0:n, :], bG[0:n, cb * 512:(cb + 1) * 512], ALU.add, [dpb, dbG], [dtG])
                            self.ts("dve", self.G[0:n, which, cb * 512:(cb + 1) * 512], tmpG[0:n, :], a, a, ALU.mult, ALU.add,
                                    [dtG], [self.dG])
            for w in range(2):
                self.cp("dve", self.modS[:, w, :, :].rearrange("p k (b j) -> p k b j", j=4),
                        self.modT[:, w, :, 1:17].unsqueeze(3).to_broadcast([128, 8, 16, 4]), [self.dModT], [self.dModS])
            self.S.emit()

    def make_xT(self, xT, dXT, src_tile_ap, dsrc, i, tmp, dtmp):
        nt = tile_nt(i)
        t0 = 128 * i
        for half in range(2):
            pb, dpb = self.bank()
            for kk in range(4):
                k = half * 4 + kk
                self.tr(pb[:, kk * nt:(kk + 1) * nt], src_tile_ap[0:nt, k * 128:(k + 1) * 128], [dsrc], [dpb])
            if i < 16:
                for kk in range(4):
                    k = half * 4 + kk
                    self.act(xT[:, k, t0:t0 + nt], pb[:, kk * nt:(kk + 1) * nt], AF.Identity, [dpb, self.dModT], [dXT],
                             bias=self.modT[:, 0, k, 0:1], scale=self.modT[:, 1, k, 0:1])
            else:
                k0 = half * 4
                self.tt("dve", tmp[:, 0:256].rearrange("p (a b) -> p a b", a=4), pb[:, 0:256].rearrange("p (a b) -> p a b", a=4),
                        self.modS[:, 1, k0:k0 + 4, :], ALU.mult, [dpb, self.dModS], [dtmp])
                self.tt("dve", xT[:, k0:k0 + 4, t0:t0 + nt], tmp[:, 0:256].rearrange("p (a b) -> p a b", a=4),
                        self.modS[:, 0, k0:k0 + 4, :], ALU.add, [dtmp, self.dModS], [dXT])

    def layer_norm_tiles(self, xres, dX, lnG, lnB, dLn, st, dSt, junk, dJ, eps):
        for i in range(NTILE):
            nt = tile_nt(i)
            self.act(junk[0:nt, :], xres[0:nt, i, :], AF.Identity, [dX[i]], [dJ, dSt], accum=st[0:nt, 0, i:i + 1])
            self.act(junk[0:nt, :], xres[0:nt, i, :], AF.Square, [dX[i]], [dJ, dSt], accum=st[0:nt, 1, i:i + 1])
        m = st[:, 2, :]; msq = st[:, 3, :]; var = st[:, 4, :]; rstd = st[:, 5, :]; nb = st[:, 6, :]
        self.ts("dve", m, st[:, 0, :], 1.0 / D, None, ALU.mult, None, [dSt], [dSt])
        self.tt("dve", msq, m, m, ALU.mult, [dSt], [dSt])
        self.ts("dve", var, st[:, 1, :], 1.0 / D, eps, ALU.mult, ALU.add, [dSt], [dSt])
        self.tt("dve", var, var, msq, ALU.subtract, [dSt], [dSt])
        self.act(var, var, AF.Sqrt, [dSt], [dSt])
        self.recip(rstd, var, [dSt], [dSt])
        self.tt("dve", nb, m, rstd, ALU.mult, [dSt], [dSt])
        self.ts("dve", nb, nb, -1.0, None, ALU.mult, None, [dSt], [dSt])
        for i in range(NTILE):
            nt = tile_nt(i)
            self.act(xres[0:nt, i, :], xres[0:nt, i, :], AF.Identity, [dSt], [dX[i]], bias=nb[0:nt, i:i + 1], scale=rstd[0:nt, i:i + 1])
            self.tt("dve", xres[0:nt, i, :], xres[0:nt, i, :], lnG[0:nt, :], ALU.mult, [dLn], [dX[i]])
            self.tt("dve", xres[0:nt, i, :], xres[0:nt, i, :], lnB[0:nt, :], ALU.add, [dLn], [dX[i]])

    def ffn(self, src_p, src_s, dst_p, dst_s, w_up, w_dn, ln_g, ln_b, tag):
        nc = self.nc
        S = self.S
        groups = [(0, 4), (4, 4), (8, 4), (12, 4), (16, 3), (19, 3)]
        GC = 4
        with ExitStack() as es:
            sb = lambda name, shape, dt=F32: es.enter_context(nc.sbuf_tensor(self.uname(name), list(shape), dt))
            xres = sb("xres", [128, NTILE, D]); dX = [Dep() for _ in range(NTILE)]
            xT = sb("xT", [128, 8, NT], BF16); dXT = [Dep() for _ in range(5)]
            wu = [sb(f"wu{i}", [128, 8, 2, GC * 128], BF16) for i in range(2)]; dwu = [Dep(), Dep()]
            wd = [sb(f"wd{i}", [128, GC, D], BF16) for i in range(2)]; dwd = [Dep(), Dep()]
            gT = [sb(f"gT{i}", [128, GC, 512], BF16) for i in range(2)]; dgT = [Dep(), Dep()]
            tsil = [sb(f"tsil{i}", [128, 512], BF16) for i in range(2)]; dts = [Dep(), Dep()]
            tG = [sb(f"tG{i}", [128, 512]) for i in range(2)]; dtG = [Dep(), Dep()]
            lnG = sb("lnG", [128, D]); lnB = sb("lnB", [128, D]); dLn = Dep()
            st = sb("lnst", [128, 7, NTILE]); dSt = Dep()
            junk = sb("junk", [128, D], BF16); dJ = Dep()
            tmpm = sb("tmpm", [128, 256]); dtm = Dep()
            if os.environ.get("MK_PADF"):
                sb("padf", [128, int(os.environ["MK_PADF"]) * 256])
            self.memset("pool", st[:], 0.0, [dSt])
            self.ld(lnG[:], ln_g[0:1, :].broadcast_to([128, D]), [dLn])
            self.ld(lnB[:], ln_b[0:1, :].broadcast_to([128, D]), [dLn])

            def load_group(q):
                c0, gc = groups[q]
                s = q % 2
                for half in range(2):
                    col0 = half * DFF + c0 * 128
                    self.ldc(wu[s][:, :, half, 0:gc * 128], w_up[:, col0:col0 + gc * 128].rearrange("(kc p) n -> p kc n", p=128), [dwu[s]])
                self.ldc(wd[s][:, 0:gc, :], w_dn[c0 * 128:(c0 + gc) * 128, :].rearrange("(c p) n -> p c n", p=128), [dwd[s]])

            load_group(0)
            for i in range(NTILE):
                nt = tile_nt(i)
                src = src_p[128 * i:128 * i + nt, :] if i < 16 else src_s[:, :]
                self.ld(xres[0:nt, i, :], src, [dX[i]])
                self.make_xT(xT, dXT[min(i // 4, 4)], xres[:, i, :], dX[i], i, tmpm, dtm)
                self.S.op("act", lambda e, i=i, nt=nt: e.mul(out=xres[0:nt, i, :], in_=xres[0:nt, i, :], mul=ALPHA), [dX[i]], [dX[i]])
            items = [(q, b) for q in range(len(groups)) for b in range(5)]

            def up(n):
                q, b = items[n]
                c0, gc = groups[q]
                s = q % 2
                g = n % 2
                t0, tn = TBLK[b]
                for j in range(gc):
                    pa, dpa = self.bank()
                    pu, dpu = self.bank()
                    for half, (pp, dpp) in enumerate(((pa, dpa), (pu, dpu))):
                        for k in range(8):
                            self.mm(pp[:, 0:tn], wu[s][:, k, half, j * 128:(j + 1) * 128], xT[:, k, t0:t0 + tn], k == 0, k == 7,
                                    [dwu[s], dXT[b]], [dpp])
                    sl = (n * GC + j) % 2
                    self.act(tsil[sl][:, 0:tn], pa[:, 0:tn], AF.Silu, [dpa], [dts[sl]])
                    self.tt("dve", gT[g][:, j, 0:tn], tsil[sl][:, 0:tn], pu[:, 0:tn], ALU.mult, [dts[sl], dpu], [dgT[g]])

            def down(n):
                q, b = items[n]
                c0, gc = groups[q]
                s = q % 2
                g = n % 2
                for ti, i in enumerate(blk_tiles(b)):
                    nt = tile_nt(i)
                    which = 0 if i < 16 else 1
                    for cb in range(2):
                        po, dpo = self.bank()
                        for j in range(gc):
                            self.mm(po[0:nt, :], gT[g][:, j, ti * 128:ti * 128 + nt], wd[s][:, j, cb * 512:(cb + 1) * 512],
                                    j == 0, j == gc - 1, [dgT[g], dwd[s]], [dpo])
                        sl = (i * 2 + cb) % 2
                        self.tt("dve", tG[sl][0:nt, :], po[0:nt, :], self.G[0:nt, which, cb * 512:(cb + 1) * 512], ALU.mult,
                                [dpo, self.dG], [dtG[sl]])
                        self.tt("pool", xres[0:nt, i, cb * 512:(cb + 1) * 512], xres[0:nt, i, cb * 512:(cb + 1) * 512], tG[sl][0:nt, :],
                                ALU.add, [dtG[sl]], [dX[i]])

            load_group(1)
            up(0)
            for n in range(len(items)):
                if n + 1 < len(items):
                    up(n + 1)
                down(n)
                q, b = items[n]
                if b == 4 and q + 2 < len(groups):
                    load_group(q + 2)
            if tag == "f1":
                self.dbg("pre", xres[:], dX, [128, NTILE, D])
                self.dbg("xT", xT[:], dXT, [128, 8, NT])
            self.layer_norm_tiles(xres, dX, lnG, lnB, dLn, st, dSt, junk, dJ, LN_EPS)
            if tag == "f1":
                self.dbg("lnst", st[:], dSt, [128, 7, NTILE])
            for i in range(NTILE):
                nt = tile_nt(i)
                dst = dst_p[128 * i:128 * i + nt, :] if i < 16 else dst_s[:, :]
                self.store(dst, xres[0:nt, i, :], [dX[i]])
            S.emit()

    def mixer(self, L):
        nc, S = self.nc, self.S
        x1p, x1s = L["x1p"], L["x1s"]
        with ExitStack() as es0:
            sb0 = lambda name, shape, dt=F32: es0.enter_context(nc.sbuf_tensor(self.uname(name), list(shape), dt))
            hT = sb0("hT", [128, 8, NT], BF16); dHT = [Dep() for _ in range(5)]
            with ExitStack() as es:
                sb = lambda name, shape, dt=F32: es.enter_context(nc.sbuf_tensor(self.uname(name), list(shape), dt))
                xt = sb("xt", [128, 2, D]); dxt = [Dep(), Dep()]
                tmpm = sb("tmpm", [128, 256]); dtm = Dep()
                for i in range(NTILE):
                    nt = tile_nt(i)
                    src = x1p[128 * i:128 * i + nt, :] if i < 16 else x1s[:, :]
                    self.ld(xt[0:nt, i % 2, :], src, [dxt[i % 2]])
                    self.make_xT(hT, dHT[min(i // 4, 4)], xt[:, i % 2, :], dxt[i % 2], i, tmpm, dtm)
                S.emit()
            with ExitStack() as es:
                oaT = es.enter_context(nc.sbuf_tensor(self.uname("oaT"), [128, 8, NT], BF16)); dOA = [Dep() for _ in range(5)]
                self.swa(L, hT, dHT, oaT, dOA)
                self.store(L["oa_scr"][:, :], oaT[:].rearrange("p c t -> p (c t)"), dOA)
                if "oaT" in self.debug:
                    self.dbg("oaT", oaT[:], dOA, [128, 8, NT])
                S.emit()
            if "stop2" in self.debug:
                return
            hgT = sb0("hgT", [128, 8, NT], BF16); dHG = [Dep() for _ in range(5)]
            self.mlstm(L, hT, dHT, hgT, dHG)
            if "hgT" in self.debug:
                self.dbg("hgT", hgT[:], dHG, [128, 8, NT])
                S.emit()
            if "stop3" in self.debug:
                return
            self.merge(L, hT, dHT, hgT, dHG)

    def swa(self, L, hT, dHT, oaT, dOA):
        nc, S = self.nc, self.S
        w_in = L["w_in"]
        AQ0, AK0, AV0 = 3080, 4104, 4360
        with ExitStack() as es:
            sb = lambda name, shape, dt=F32: es.enter_context(nc.sbuf_tensor(self.uname(name), list(shape), dt))
            if os.environ.get("MK_PADS"):
                sb("pads", [128, int(os.environ["MK_PADS"]) * 256])
            cosT = sb("cosT", [128, NT]); sinT = sb("sinT", [128, NT]); dRope = Dep()
            self.ld(cosT[:], L["ropec"][:, :], [dRope]); self.ld(sinT[:], L["ropes"][:, :], [dRope])
            mk = sb("mk", [128, 768]); dMk = Dep()
            self.ld(mk[:], L["cmask"][:, :], [dMk])
            mle = sb("mle", [128, 128], BF16); mgt = sb("mgt", [128, 128], BF16); mblkc = sb("mblkc", [128, 128], BF16)
            mcache = sb("mcache", [128, 4], BF16); dMb = Dep()
            self.cp("dve", mle[:], mk[:, 0:128], [dMk], [dMb]); self.cp("dve", mgt[:], mk[:, 128:256], [dMk], [dMb])
            self.cp("dve", mblkc[:], mk[:, 256:384], [dMk], [dMb]); self.cp("dve", mcache[:], mk[:, 512:516], [dMk], [dMb])
            ones = sb("ones", [128, 128], BF16); dOnes = Dep()
            self.memset("pool", ones[:], 1.0, [dOnes])
            esink = sb("esink", [128, 16]); dEs = Dep()
            self.ld(esink[:], L["sinks"][0:1, :].broadcast_to([128, 16]), [dEs])
            self.act(esink[:], esink[:], AF.Exp, [dEs], [dEs])
            ckf = sb("ckf", [128, 16, 256]); dCk = Dep()
            for b0 in range(0, 16, 4):
                self.ld(ckf[:, b0:b0 + 4, :], L["ck"][b0:b0 + 4].rearrange("b p c -> p b c"), [dCk])
            cvv = L["cv"].rearrange("b p c -> p b c")
            KcTn = sb("KcTn", [128, 2, 16, 128], BF16); dKn = Dep()
            for gp in range(2):
                for b0 in range(0, 16, 4):
                    pb, dpb = self.bank()
                    for bb in range(4):
                        self.tr(pb[:, bb * 128:(bb + 1) * 128], ckf[:, b0 + bb, gp * 128:(gp + 1) * 128], [dCk], [dpb])
                    self.cp("act", KcTn[:, gp, b0:b0 + 4, :].rearrange("p b c -> p (b c)"), pb[:, :], [dpb], [dKn])
            kout_p = sb("kout_p", [128, 256]); vfin = sb("vfin", [128, 2, 256]); knew_s = sb("knew_s", [128, 256])
            dKo = Dep(); dVf = Dep(); dKs = Dep()
            WQ = sb("WQ", [128, 8, 256], BF16); WQs = sb("WQs", [128, 8, 256], BF16)
            WK2 = sb("WK2", [128, 8, 128], BF16); WK2s = sb("WK2s", [128, 8, 128], BF16)
            WV = sb("WV", [128, 8, 64], BF16); dW = Dep(); dWs = Dep()
            qT = sb("qT", [128, 2, NT], BF16); dQ = [Dep() for _ in range(5)]
            kT2 = sb("kT2", [128, NT], BF16); dK = [Dep() for _ in range(5)]
            vtm2 = sb("vtm2", [128, NTILE, 128], BF16); dV = [Dep() for _ in range(NTILE)]
            vlh = sb("vlh", [128, 16, 2, 128], BF16)
            ones_lh = sb("ones_lh", [128, 2, 128], BF16); dOlh = Dep()
            esg = sb("esg", [128, 2]); dEsg = Dep()
            self.memset("pool", vlh[:], 0.0, dV[0:16])
            self.memset("pool", ones_lh[:], 0.0, [dOlh])
            self.memset("pool", ones_lh[:, 0, 0:64], 1.0, [dOlh])
            self.memset("pool", ones_lh[:, 1, 64:128], 1.0, [dOlh])
            kfin = sb("kfin", [128, 192]); dKf = Dep()
            t1 = [sb(f"t1_{i}", [128, 512]) for i in range(2)]; dt1 = [Dep(), Dep()]
            t2 = [sb(f"t2_{i}", [128, 512]) for i in range(2)]; dt2 = [Dep(), Dep()]
            pT = [sb(f"pT{i}", [128, 512], BF16) for i in range(4)]; dpT = [Dep() for _ in range(4)]
            pTm = [sb(f"pTm{i}", [128, 512], BF16) for i in range(4)]; dpTm = [Dep() for _ in range(4)]
            recs = [sb(f"rec{i}", [128, 512]) for i in range(2)]; dRecs = [Dep(), Dep()]
            rec = recs[0]; dRec = dRecs[0]
            KcT2 = sb("KcT2", [128, 16, 128], BF16); dKc2 = Dep()
            Vc2 = sb("Vc2", [128, 16, 128], BF16); dVc2 = Dep()
            pTn = sb("pTn", [128, 256], BF16); dpTn = Dep()
            pTnm = sb("pTnm", [128, 256], BF16); dpTnm = Dep()
            cnt = [0]

            def rope_evac(pa, dpa, pb_, dpb_, t0, tn, out_ap, dOut, fin=None):
                i = cnt[0] % 2; cnt[0] += 1
                self.tt("dve", t1[i][:, 0:tn], pa[:, 0:tn], cosT[:, t0:t0 + tn], ALU.mult, [dpa, dRope], [dt1[i]])
                self.tt("dve", t2[i][:, 0:tn], pb_[:, 0:tn], sinT[:, t0:t0 + tn], ALU.mult, [dpb_, dRope], [dt2[i]])
                self.tt("pool", out_ap, t1[i][:, 0:tn], t2[i][:, 0:tn], ALU.add, [dt1[i], dt2[i]], [dOut])
                if fin is not None:
                    fo, a, n = fin
                    self.tt("pool", kfin[:, fo:fo + n], t1[i][:, a:a + n], t2[i][:, a:a + n], ALU.add, [dt1[i], dt2[i]], [dKf])

            lvl = 9
            att = 9
            for f_ in self.debug:
                if f_.startswith("att"):
                    att = int(f_[3:])
            for f_ in self.debug:
                if f_.startswith("swa"):
                    lvl = int(f_[3:])
            for g in range(4 if lvl >= 5 else 1):
                if lvl < 1:
                    break
                wsrc = lambda c0, n: w_in[:, c0:c0 + n].rearrange("(kc p) n -> p kc n", p=128)
                self.ldc(WQ[:], wsrc(AQ0 + g * 256, 256), [dW])
                self.ldc(WK2[:, :, 0:64], wsrc(AK0 + g * 64, 64), [dW])
                self.ldc(WK2[:, :, 64:128], wsrc(AK0 + g * 64, 64), [dW])
                self.ldc(WV[:], wsrc(AV0 + g * 64, 64), [dW])
                for src_t, dst_t, nh in ((WQ, WQs, 4), (WK2, WK2s, 2)):
                    sv = src_t[:].rearrange("p k (h two d) -> p k h two d", two=2, d=32)
                    dv = dst_t[:].rearrange("p k (h two d) -> p k h two d", two=2, d=32)
                    for k in range(8):
                        self.cp("pool", dv[:, k, :, 0, :], sv[:, k, :, 1, :], [dW], [dWs])
                        self.cp("pool", dv[:, k, :, 1, :], sv[:, k, :, 0, :], [dW], [dWs])
                for b, (t0, tn) in enumerate(TBLK):
                    for c in range(2):
                        pa, dpa = self.bank(); pb_, dpb_ = self.bank()
                        for k in range(8):
                            self.mm(pa[:, 0:tn], WQ[:, k, c * 128:(c + 1) * 128], hT[:, k, t0:t0 + tn], k == 0, k == 7, [dW, dHT[b]], [dpa])
                        for k in range(8):
                            self.mm(pb_[:, 0:tn], WQs[:, k, c * 128:(c + 1) * 128], hT[:, k, t0:t0 + tn], k == 0, k == 7, [dWs, dHT[b]], [dpb_])
                        rope_evac(pa, dpa, pb_, dpb_, t0, tn, qT[:, c, t0:t0 + tn], dQ[b])
                    pa, dpa = self.bank(); pb_, dpb_ = self.bank()
                    for k in range(8):
                        self.mm(pa[:, 0:tn], WK2[:, k, :], hT[:, k, t0:t0 + tn], k == 0, k == 7, [dW, dHT[b]], [dpa])
                    for k in range(8):
                        self.mm(pb_[:, 0:tn], WK2s[:, k, :], hT[:, k, t0:t0 + tn], k == 0, k == 7, [dWs, dHT[b]], [dpb_])
                    fin = (0, 384, 128) if b == 3 else ((128, 0, 64) if b == 4 else None)
                    rope_evac(pa, dpa, pb_, dpb_, t0, tn, kT2[:, t0:t0 + tn], dK[b], fin)
                for i in range(NTILE):
                    nt = tile_nt(i)
                    pv, dpv = self.bank()
                    for k in range(8):
                        self.mm(pv[0:nt, 0:64], hT[:, k, 128 * i:128 * i + nt], WV[:, k, :], k == 0, k == 7, [dW, dHT[min(i // 4, 4)]], [dpv])
                    if i < 16:
                        self.cp("act", vlh[0:nt, i, 0, 0:64], pv[0:nt, 0:64], [dpv], [dV[i]])
                        self.cp("dve", vlh[0:nt, i, 1, 64:128], pv[0:nt, 0:64], [dpv], [dV[i]])
                    else:
                        self.cp("act", vtm2[0:nt, i, 0:64], pv[0:nt, 0:64], [dpv], [dV[i]])
                        self.cp("dve", vtm2[0:nt, i, 64:128], pv[0:nt, 0:64], [dpv], [dV[i]])
                    if i >= 15:
                        self.cp("act", vfin[0:nt, i - 15, g * 64:(g + 1) * 64], pv[0:nt, 0:64], [dpv], [dVf])
                if lvl < 2:
                    continue
                pk, dpk = self.bank()
                self.tr(pk[:, 0:64], kfin[0:64, 0:128], [dKf], [dpk])
                self.tr(pk[0:64, 64:128], kfin[0:64, 128:192], [dKf], [dpk])
                self.cp("act", kout_p[:, g * 64:(g + 1) * 64], pk[:, 0:64], [dpk], [dKo])
                self.cp("act", knew_s[0:64, g * 64:(g + 1) * 64], pk[0:64, 64:128], [dpk], [dKs])
                if lvl < 3:
                    continue
                def attA(n):
                    q0 = 128 * n
                    kbs = ([n - 1] if n > 0 else []) + [n]
                    for ki, kb in enumerate(kbs):
                        psA, dpsA = self.bank(); psB, dpsB = self.bank()
                        for c in range(2):
                            self.mm(psA[:, c * 128:(c + 1) * 128], kT2[0:64, 128 * kb:128 * kb + 128], qT[0:64, c, q0:q0 + 128], True, True,
                                    [dK[kb // 4], dQ[n // 4]], [dpsA])
                            self.mm(psB[:, c * 128:(c + 1) * 128], kT2[64:128, 128 * kb:128 * kb + 128], qT[64:128, c, q0:q0 + 128], True, True,
                                    [dK[kb // 4], dQ[n // 4]], [dpsB])
                        sl = (2 * n + ki) % 4
                        self.act(pT[sl][:, 0:256], psA[:, 0:256], AF.Exp, [dpsA], [dpT[sl]], scale=0.125)
                        self.act(pT[sl][:, 256:512], psB[:, 0:256], AF.Exp, [dpsB], [dpT[sl]], scale=0.125)
                        msk = mle if kb == n else mgt
                        self.tt("dve", pTm[sl][:].rearrange("p (r q) -> p r q", r=4), pT[sl][:].rearrange("p (r q) -> p r q", r=4),
                                msk[:, :].unsqueeze(1).to_broadcast([128, 4, 128]), ALU.mult, [dpT[sl], dMb], [dpTm[sl]])

                esv_ = esink[:, 4 * g:4 * g + 4].rearrange("p (c h) -> p h c", h=2)
                self.cp("dve", esg[0:64, :], esv_[0:64, 0, :], [dEs], [dEsg])
                self.cp("dve", esg[64:128, :], esv_[64:128, 1, :], [dEs], [dEsg])

                def attB(n):
                    q0 = 128 * n
                    kbs = ([n - 1] if n > 0 else []) + [n]
                    pnum, dpnum = self.bank(); pden, dpden = self.bank()
                    nmm = 2 * len(kbs)
                    j = 0
                    for ki, kb in enumerate(kbs):
                        sl = (2 * n + ki) % 4
                        for hh in range(2):
                            self.mm(pnum[:, 0:256], vlh[:, kb, hh, :], pTm[sl][:, hh * 256:(hh + 1) * 256], j == 0, j == nmm - 1,
                                    [dV[kb], dpTm[sl]], [dpnum])
                            self.mm(pden[:, 0:256], ones_lh[:, hh, :], pTm[sl][:, hh * 256:(hh + 1) * 256], j == 0, j == nmm - 1,
                                    [dOlh, dpTm[sl]], [dpden])
                            j += 1
                    rc = recs[n % 2]; drc = dRecs[n % 2]
                    self.tt("dve", rc[:, 0:256].rearrange("p (c q) -> p c q", c=2), pden[:, 0:256].rearrange("p (c q) -> p c q", c=2),
                            esg[:, :].unsqueeze(2).to_broadcast([128, 2, 128]), ALU.add, [dpden, dEsg], [drc])
                    self.act(rc[:, 0:256], rc[:, 0:256], AF.Ln, [drc], [drc])
                    self.act(rc[:, 0:256], rc[:, 0:256], AF.Exp, [drc], [drc], scale=-1.0)
                    self.tt("dve", oaT[:, 2 * g:2 * g + 2, q0:q0 + 128], pnum[:, 0:256].rearrange("p (c q) -> p c q", c=2),
                            rc[:, 0:256].rearrange("p (c q) -> p c q", c=2), ALU.mult, [dpnum, drc], [dOA[n // 4]])

                attA(0)
                for n in range(16):
                    if n + 1 < 16:
                        attA(n + 1)
                    attB(n)
                if lvl < 4:
                    continue
                gp, go = g // 2, (g % 2) * 64
                self.cp("act", KcT2[0:64, :, :], KcTn[go:go + 64, gp, :, :], [dKn], [dKc2])
                self.cp("dve", KcT2[64:128, :, :], KcTn[go:go + 64, gp, :, :], [dKn], [dKc2])
                for b0 in range(0, 16, 4):
                    self.ldc(Vc2[:, b0:b0 + 4, 0:64], cvv[:, b0:b0 + 4, g * 64:(g + 1) * 64], [dVc2])
                    self.ldc(Vc2[:, b0:b0 + 4, 64:128], cvv[:, b0:b0 + 4, g * 64:(g + 1) * 64], [dVc2])
                pscA, dpscA = self.bank(); pscB, dpscB = self.bank()
                psnA, dpsnA = self.bank(); psnB, dpsnB = self.bank()
                for b in range(16):
                    for c in range(2):
                        c0 = b * 8 + c * 4
                        self.mm(pscA[:, c0:c0 + 4], KcT2[0:64, b, :], qT[0:64, c, SEQ + 4 * b:SEQ + 4 * b + 4], True, True, [dKc2, dQ[4]], [dpscA])
                        self.mm(pscB[:, c0:c0 + 4], KcT2[64:128, b, :], qT[64:128, c, SEQ + 4 * b:SEQ + 4 * b + 4], True, True, [dKc2, dQ[4]], [dpscB])
                for c in range(2):
                    for hh, (pp, dpp) in enumerate(((psnA, dpsnA), (psnB, dpsnB))):
                        off = hh * 64
                        self.mm(pp[0:64, 0:128].rearrange("p (b c t) -> p b c t", b=16, c=2)[:, :, c, :], kT2[off:off + 64, SEQ:NT],
                                qT[off:off + 64, c, SEQ:NT].rearrange("p (b t) -> p b t", t=4), True, True, [dK[4], dQ[4]], [dpp])
                self.act(pTn[:, 0:128], pscA[:, 0:128], AF.Exp, [dpscA], [dpTn], scale=0.125)
                self.act(pTn[:, 128:256], pscB[:, 0:128], AF.Exp, [dpscB], [dpTn], scale=0.125)
                self.tt("dve", pTnm[:, :].rearrange("p (a t) -> p a t", t=4), pTn[:, :].rearrange("p (a t) -> p a t", t=4),
                        mcache[:, :].unsqueeze(1).to_broadcast([128, 64, 4]), ALU.mult, [dpTn, dMb], [dpTnm])
                sl = cnt[0] % 2; cnt[0] += 1
                self.act(pT[sl][0:64, 0:128], psnA[0:64, 0:128], AF.Exp, [dpsnA], [dpT[sl]], scale=0.125)
                self.act(pT[sl][0:64, 128:256], psnB[0:64, 0:128], AF.Exp, [dpsnB], [dpT[sl]], scale=0.125)
                for hh in range(2):
                    self.tt("dve", pTm[sl][0:64, hh * 128:(hh + 1) * 128].rearrange("p (b c t) -> p b c t", b=16, c=2),
                            pT[sl][0:64, hh * 128:(hh + 1) * 128].rearrange("p (b c t) -> p b c t", b=16, c=2),
                            mblkc[0:64, 0:64].rearrange("p (b t) -> p b t", t=4).unsqueeze(2).to_broadcast([64, 16, 2, 4]), ALU.mult,
                            [dpT[sl], dMb], [dpTm[sl]])
                pnum, dpnum = self.bank(); pden, dpden = self.bank()
                self.mm(pnum[:, 0:256], vtm2[0:64, 16, :], pTm[sl][0:64, 0:256], True, False, [dV[16], dpTm[sl]], [dpnum])
                self.mm(pden[:, 0:256], ones[0:64, :], pTm[sl][0:64, 0:256], True, False, [dOnes, dpTm[sl]], [dpden])
                for b in range(16):
                    pv_ = pTnm[:, :].rearrange("p (h b x) -> p h b x", h=2, b=16)[:, :, b, :]
                    self.mm(pnum[:, 0:256].rearrange("p (h b x) -> p h b x", h=2, b=16)[:, :, b, :], Vc2[:, b, :], pv_, False, True,
                            [dVc2, dpTnm], [dpnum])
                    self.mm(pden[:, 0:256].rearrange("p (h b x) -> p h b x", h=2, b=16)[:, :, b, :], ones[:, :], pv_, False, True,
                            [dOnes, dpTnm], [dpden])
                esv = esink[:, 4 * g:4 * g + 4].rearrange("p (c h) -> p h c", h=2)
                for hh in range(2):
                    self.tt("dve", rec[:, hh * 128:(hh + 1) * 128].rearrange("p (b c t) -> p b c t", b=16, c=2),
                            pden[:, hh * 128:(hh + 1) * 128].rearrange("p (b c t) -> p b c t", b=16, c=2),
                            esv[:, hh, :].unsqueeze(1).unsqueeze(3).to_broadcast([128, 16, 2, 4]), ALU.add, [dpden, dEs], [dRec])
                self.recip(rec[:, 0:256], rec[:, 0:256], [dRec], [dRec])
                for hh in range(2):
                    off = hh * 64
                    for c in range(2):
                        nv = pnum[off:off + 64, hh * 128:(hh + 1) * 128].rearrange("p (b c t) -> p b c t", b=16, c=2)[:, :, c, :]
                        rv = rec[off:off + 64, hh * 128:(hh + 1) * 128].rearrange("p (b c t) -> p b c t", b=16, c=2)[:, :, c, :]
                        self.tt("dve", oaT[off:off + 64, 2 * g + c, SEQ:NT].rearrange("p (b t) -> p b t", t=4), nv, rv, ALU.mult,
                                [dpnum, dRec], [dOA[4]])
            if lvl < 6:
                S.emit()
                return
            self.store(L["kp"][:, :], kout_p[:], [dKo])
            self.store(L["vp"][:, :], vfin[:, 0, :], [dVf])
            for b in range(16):
                self.store(L["ks"][b, 0:124, :], L["ck"][b, 4:128, :], [])
                self.store(L["vs"][b, 0:124, :], L["cv"][b, 4:128, :], [])
                self.store(L["ks"][b, 124:128, :], knew_s[4 * b:4 * b + 4, :], [dKs])
                self.store(L["vs"][b, 124:128, :], vfin[4 * b:4 * b + 4, 1, :], [dVf])
            S.emit()

    def mlstm(self, L, hT, dHT, hgT, dHG):
        nc, S = self.nc, self.S
        w_in = L["w_in"]
        MQ0, MK0, MV0, MO0, MG0 = 0, 512, 1024, 2048, 3072
        KS = 128.0 ** -0.5
        wsrc = lambda c0, n: w_in[:, c0:c0 + n].rearrange("(kc p) n -> p kc n", p=128)
        with ExitStack() as es:
            sb = lambda name, shape, dt=F32: es.enter_context(nc.sbuf_tensor(self.uname(name), list(shape), dt))
            A_tm = sb("A_tm", [128, NTILE, 4]); B_tm = sb("B_tm", [128, NTILE, 4]); MendTok = sb("MendTok", [128, NTILE, 4])
            W_tm = sb("W_tm", [128, NTILE, 4]); THR = sb("THR", [128, NTILE, 4]); dGt = Dep()
            ECrep = sb("ECrep", [128, 4, 32]); MendRep = sb("MendRep", [128, 4, 16]); dRep = Dep()
            EcTok = sb("EcTok", [128, 4]); dEt = Dep()
            mk = sb("mk", [128, 768]); dMk = Dep()
            self.ld(mk[:], L["cmask"][:, :], [dMk])
            mle_f = mk[:, 0:128]; mblkc_f = mk[:, 256:384]; blk_f = mk[:, 384:512]; rmask_f = mk[:, 640:656]
            n0tok = sb("n0tok", [128, 512]); dN0 = Dep()
            nnew = sb("nnew", [128, 512]); dNn = Dep()
            for b in range(16):
                self.ld(n0tok[4 * b:4 * b + 4, :], L["stn"][b:b + 1, :].broadcast_to([4, 512]), [dN0])
            for t_ in (A_tm, B_tm, MendTok):
                self.memset("pool", t_[:], 0.0, [dGt])
            with ExitStack() as es2:
                sb2 = lambda name, shape, dt=F32: es2.enter_context(nc.sbuf_tensor(self.uname(name), list(shape), dt))
                Wg = sb2("Wg", [128, 8, 8], BF16); dWg = Dep()
                self.ldc(Wg[:], wsrc(MG0, 8), [dWg])
                bigf = sb2("bigf", [128, 8]); dBg = Dep()
                self.ld(bigf[:, 0:4], L["b_ig"][0:1, :].broadcast_to([128, 4]), [dBg])
                self.ld(bigf[:, 4:8], L["b_fg"][0:1, :].broadcast_to([128, 4]), [dBg])
                g_tm = sb2("g_tm", [128, NTILE, 8]); dGtm = Dep()
                GI = sb2("GI", [4, NT]); GF = sb2("GF", [4, NT]); Bf = sb2("Bf", [4, NT]); Mf = sb2("Mf", [4, NT])
                Z = sb2("Z", [4, NT]); T1 = sb2("T1", [4, NT]); T2 = sb2("T2", [4, NT]); dF = Dep()
                ones4 = sb2("ones4", [4, 128]); dO4 = Dep()
                m0T = sb2("m0T", [4, 16]); dM0 = Dep()
                V48 = sb2("V48", [4, 48]); X = sb2("X", [4, 4, 48]); dV = Dep()
                Sx = sb2("Sx", [4, 2, 16, 4]); dSx = Dep()
                mout = sb2("mout", [4, 17]); dMo = Dep()
                self.memset("pool", Z[:], 0.0, [dF]); self.memset("pool", ones4[:], 1.0, [dO4])
                self.memset("pool", g_tm[:], 0.0, [dGtm])
                self.ld_nc(m0T[:], L["stm"].rearrange("b h -> h b"), [dM0])
                for i in range(NTILE):
                    nt = tile_nt(i)
                    pg, dpg = self.bank()
                    for k in range(8):
                        self.mm(pg[0:nt, 0:8], hT[:, k, 128 * i:128 * i + nt], Wg[:, k, :], k == 0, k == 7, [dWg, dHT[min(i // 4, 4)]], [dpg])
                    self.tt("dve", g_tm[0:nt, i, :], pg[0:nt, 0:8], bigf[0:nt, :], ALU.add, [dpg, dBg], [dGtm])
                for b, (t0, tn) in enumerate(TBLK):
                    p1, dp1 = self.bank(); p2, dp2 = self.bank()
                    for j, i in enumerate(blk_tiles(b)):
                        nt = tile_nt(i)
                        self.tr(p1[0:4, j * 128:j * 128 + nt], g_tm[0:nt, i, 0:4], [dGtm], [dp1])
                        self.tr(p2[0:4, j * 128:j * 128 + nt], g_tm[0:nt, i, 4:8], [dGtm], [dp2])
                    self.cp("dve", GI[:, t0:t0 + tn], p1[0:4, 0:tn], [dp1], [dF])
                    self.cp("dve", GF[:, t0:t0 + tn], p2[0:4, 0:tn], [dp2], [dF])
                self.act(T1[:], GF[:], AF.Abs, [dF], [dF])
                self.act(T1[:], T1[:], AF.Exp, [dF], [dF], scale=-1.0)
                self.act(T1[:], T1[:], AF.Ln, [dF], [dF], bias=1.0)
                self.ts("dve", T2[:], GF[:], 0.0, None, ALU.min, None, [dF], [dF])
                self.tt("dve", GF[:], T2[:], T1[:], ALU.subtract, [dF], [dF])
                self.S.op("dve", lambda e: e.tensor_tensor_scan(out=Bf[:, 0:SEQ], data0=GF[:, 0:SEQ], data1=Z[:, 0:SEQ], initial=0.0,
                                                                op0=ALU.add, op1=ALU.add), [dF], [dF])
                gfs = GF[:, SEQ:NT].rearrange("h (b j) -> h b j", j=4); bfs = Bf[:, SEQ:NT].rearrange("h (b j) -> h b j", j=4)
                self.cp("dve", bfs[:, :, 0], gfs[:, :, 0], [dF], [dF])
                for j in range(1, 4):
                    self.tt("dve", bfs[:, :, j], bfs[:, :, j - 1], gfs[:, :, j], ALU.add, [dF], [dF])
                self.tt("dve", GI[:], GI[:], Bf[:], ALU.subtract, [dF], [dF])
                self.S.op("dve", lambda e: e.tensor_tensor_scan(out=Mf[:, 0:SEQ], data0=Z[:, 0:SEQ], data1=GI[:, 0:SEQ], initial=0.0,
                                                                op0=ALU.add, op1=ALU.max), [dF], [dF])
                as_ = GI[:, SEQ:NT].rearrange("h (b j) -> h b j", j=4); ms_ = Mf[:, SEQ:NT].rearrange("h (b j) -> h b j", j=4)
                self.tt("dve", ms_[:, :, 0], as_[:, :, 0], m0T[:, :], ALU.max, [dF, dM0], [dF])
                for j in range(1, 4):
                    self.tt("dve", ms_[:, :, j], ms_[:, :, j - 1], as_[:, :, j], ALU.max, [dF], [dF])
                mend_p = Mf[:, 0:SEQ].rearrange("h (c t) -> h c t", t=128)[:, :, 127]
                self.cp("dve", V48[:, 0:16], mend_p, [dF], [dV])
                self.cp("dve", V48[:, 16:17], Z[:, 0:1], [dF], [dV])
                self.cp("dve", V48[:, 17:32], V48[:, 0:15], [dV], [dV])
                self.tt("dve", V48[:, 16:32], V48[:, 16:32], V48[:, 0:16], ALU.subtract, [dV], [dV])
                self.tt("dve", V48[:, 32:48], m0T[:, :], ms_[:, :, 3], ALU.subtract, [dF, dM0], [dV])
                self.act(V48[:, 16:48], V48[:, 16:48], AF.Exp, [dV], [dV])
                self.tt("dve", X[:], V48[:, :].unsqueeze(1).to_broadcast([4, 4, 48]),
                        self.ident[0:4, 0:4].unsqueeze(2).to_broadcast([4, 4, 48]), ALU.mult, [dV, self.dIdent], [dV])
                pr, dpr = self.bank()
                self.mm(pr[:, 0:192], ones4[:, :], X[:].rearrange("h a c -> h (a c)"), True, True, [dO4, dV], [dpr])
                prv = pr[:, 0:192].rearrange("p (a c) -> p a c", a=4)
                self.cp("dve", MendRep[:], prv[:, :, 0:16], [dpr], [dRep])
                self.cp("dve", ECrep[:], prv[:, :, 16:48], [dpr], [dRep])
                self.cp("dve", Sx[:, 0, :, :], ms_[:, :, 3].unsqueeze(2).to_broadcast([4, 16, 4]), [dF], [dSx])
                self.cp("dve", Sx[:, 1, :, :], V48[:, 32:48].unsqueeze(2).to_broadcast([4, 16, 4]), [dV], [dSx])
                pa_, dpa_ = self.bank(); pb_, dpb_ = self.bank()
                for i in range(NTILE):
                    nt = tile_nt(i)
                    self.tr(pa_[0:nt, i * 4:i * 4 + 4], GI[0:4, 128 * i:128 * i + nt], [dF], [dpa_])
                    self.tr(pb_[0:nt, i * 4:i * 4 + 4], Bf[0:4, 128 * i:128 * i + nt], [dF], [dpb_])
                self.tr(pa_[0:64, 68:72], Sx[0:4, 0, :, :].rearrange("h b j -> h (b j)"), [dSx], [dpa_])
                self.tr(pb_[0:64, 68:72], Sx[0:4, 1, :, :].rearrange("h b j -> h (b j)"), [dSx], [dpb_])
                self.cp("dve", A_tm[:, 0:16, :].rearrange("p c h -> p (c h)"), pa_[:, 0:64], [dpa_], [dGt])
                self.cp("dve", A_tm[0:64, 16, :], pa_[0:64, 64:68], [dpa_], [dGt])
                self.cp("dve", B_tm[:, 0:16, :].rearrange("p c h -> p (c h)"), pb_[:, 0:64], [dpb_], [dGt])
                self.cp("dve", B_tm[0:64, 16, :], pb_[0:64, 64:68], [dpb_], [dGt])
                self.cp("dve", MendTok[0:64, 16, :], pa_[0:64, 68:72], [dpa_], [dGt])
                self.cp("dve", EcTok[0:64, :], pb_[0:64, 68:72], [dpb_], [dEt])
                self.cp("dve", MendTok[:, 0:16, :], MendRep[:].rearrange("p h c -> p c h"), [dRep], [dGt])
                self.tt("dve", W_tm[:], A_tm[:], MendTok[:], ALU.subtract, [dGt], [dGt])
                self.act(W_tm[:], W_tm[:], AF.Exp, [dGt], [dGt])
                self.tt("dve", THR[:], B_tm[:], MendTok[:], ALU.add, [dGt], [dGt])
                self.act(THR[:], THR[:], AF.Exp, [dGt], [dGt], scale=-1.0)
                self.tt("dve", mout[:, 0:1], Mf[:, SEQ - 1:SEQ], Bf[:, SEQ - 1:SEQ], ALU.add, [dF], [dMo])
                self.tt("dve", mout[:, 1:17], ms_[:, :, 3], bfs[:, :, 3], ALU.add, [dF], [dMo])
                self.store(L["mp"][:, :], mout[:, 0:1], [dMo])
                self.store_nc(L["ms"].rearrange("b h -> h b"), mout[:, 1:17], [dMo])
                S.emit()
            Wm = sb("Wm", [128, 8, 768], BF16); dWm = Dep()
            qTh = sb("qTh", [128, NT], BF16); kTh = sb("kTh", [128, NT], BF16); dQK = [Dep() for _ in range(5)]
            k_tm = sb("k_tm", [128, NTILE, 128], BF16); v_aug = sb("v_aug", [128, NTILE, 258], BF16); so_tm = sb("so_tm", [128, NTILE, 256], BF16)
            dTM = [Dep() for _ in range(NTILE)]
            ST = sb("ST", [128, 257]); dST = Dep()
            tmpSs = [sb(f"tmpS{i}", [128, 128]) for i in range(2)]; dtSs = [Dep(), Dep()]
            sTms = [sb(f"sTm{i}", [128, 128], BF16) for i in range(3)]; dsTs = [Dep() for _ in range(3)]
            wvs = [sb(f"wv{i}", [128, 258], BF16) for i in range(3)]; dwvs = [Dep() for _ in range(3)]
            tmpS = tmpSs[0]; dtS = dtSs[0]
            sTm = sb("sTm_s", [128, 64], BF16); dsT = Dep()
            Cdec = sb("Cdec", [128, 258], BF16); dCd = Dep()
            junk = sb("junk", [128, 256], BF16); dJ = Dep()
            wv = sb("wv_s", [128, 258], BF16); dwv = Dep()
            bmk = sb("bmk", [128, 16, 64], BF16); dBm = Dep()
            self.ldc(bmk[:].rearrange("p b t -> p (b t)"), L["bmask"][:, :], [dBm])
            C0n = sb("C0n", [128, 8, 2, 128]); dC0 = Dep()
            C0T = sb("C0T", [128, 16, 258], BF16); dC0T = Dep()
            n0T = sb("n0T", [128, 16]); dn0T = Dep()
            maskE = sb("maskE", [128, 16, 64], BF16); dmE = Dep()
            Qblk = sb("Qblk", [128, 16, 64], BF16); dQb = Dep()
            wvblk = sb("wvblk", [128, 8, 256], BF16); dwb = Dep()
            Cout = [sb(f"Cout{i}", [128, 2, 128]) for i in range(2)]; dCo = [Dep(), Dep()]
            BselW = sb("BselW", [128, 64], BF16); dBs = Dep()
            self.memset("pool", v_aug[:], 1.0, dTM)

            Hn = sb("Hn", [128, NTILE, 257]); dHn = [Dep() for _ in range(NTILE)]
            if os.environ.get("MK_PAD"):
                sb("pad", [128, int(os.environ["MK_PAD"]) * 256])
            hst = sb("hst", [128, 8, NTILE]); dhst = Dep(); dhst2 = Dep()

            def hn_pre(h):
                dn = hst[:, 0, :]
                self.memset("pool", hst[:], 0.0, [dhst])
                self.act(dn, Hn[:, :, 256], AF.Abs, dHn, [dhst])
                self.tt("dve", dn, dn, THR[:, :, h], ALU.max, [dGt], [dhst])
                self.recip(dn, dn, [dhst], [dhst2])

            def hn_stats(h, i):
                nt = tile_nt(i)
                self.act(Hn[0:nt, i, 0:256], Hn[0:nt, i, 0:256], AF.Identity, [dhst2], [dHn[i], dhst], scale=hst[0:nt, 0, i:i + 1],
                         accum=hst[0:nt, 1, i:i + 1])
                self.act(junk[0:nt, :], Hn[0:nt, i, 0:256], AF.Square, [dHn[i]], [dJ, dhst], accum=hst[0:nt, 2, i:i + 1])

            def hn_post(h):
                dn = hst[:, 0, :]; s1 = hst[:, 1, :]; s2 = hst[:, 2, :]; mean = hst[:, 3, :]; msq = hst[:, 4, :]
                rstd = hst[:, 5, :]; nb = hst[:, 6, :]
                self.ts("dve", mean, s1, 1.0 / 256, None, ALU.mult, None, [dhst], [dhst])
                self.tt("dve", msq, mean, mean, ALU.mult, [dhst], [dhst])
                self.ts("dve", rstd, s2, 1.0 / 256, HN_EPS, ALU.mult, ALU.add, [dhst], [dhst])
                self.tt("dve", rstd, rstd, msq, ALU.subtract, [dhst], [dhst])
                self.act(rstd, rstd, AF.Sqrt, [dhst], [dhst])
                self.recip(rstd, rstd, [dhst], [dhst])
                self.tt("dve", nb, mean, rstd, ALU.mult, [dhst], [dhst])
                self.ts("dve", nb, nb, -1.0, None, ALU.mult, None, [dhst], [dhst])
                pts = {}

                def n1(i):
                    nt = tile_nt(i)
                    self.act(Hn[0:nt, i, 0:256], Hn[0:nt, i, 0:256], AF.Identity, [dhst], [dHn[i]], bias=hst[0:nt, 6, i:i + 1],
                             scale=hst[0:nt, 5, i:i + 1])
                    self.tt("dve", Hn[0:nt, i, 0:256], Hn[0:nt, i, 0:256], so_tms[h % 2][0:nt, i, :], ALU.mult, [dSO[h % 2][i]], [dHn[i]])

                def n2(i):
                    nt = tile_nt(i)
                    pt, dpt = self.bank()
                    for vb in range(2):
                        self.tr(pt[:, vb * 128:vb * 128 + nt], Hn[0:nt, i, vb * 128:(vb + 1) * 128], [dHn[i]], [dpt])
                    pts[i] = (pt, dpt)

                def n3(i):
                    nt = tile_nt(i)
                    t0 = 128 * i
                    pt, dpt = pts.pop(i)
                    self.cp("act" if i % 2 == 0 else "dve", hgT[:, 2 * h:2 * h + 2, t0:t0 + nt],
                            pt[:, 0:256].rearrange("p (a b) -> p a b", a=2)[:, :, 0:nt], [dpt], [dHG[min(i // 4, 4)]])

                for step in range(NTILE + 4):
                    if step < NTILE:
                        n1(step)
                    if 0 <= step - 2 < NTILE:
                        n2(step - 2)
                    if 0 <= step - 4 < NTILE:
                        n3(step - 4)

            def head_tail(po, dpo, i, h, nt):
                self.cp("act", Hn[0:nt, i, :], po[0:nt, 0:257], [dpo], [dHn[i]])

            Wm2 = sb("Wm2", [128, 8, 768], BF16); dWm2 = Dep()
            so_tm2 = sb("so_tm2", [128, NTILE, 256], BF16)
            Wms = [Wm, Wm2]; dWms = [dWm, dWm2]; so_tms = [so_tm, so_tm2]
            dSO = [[Dep() for _ in range(NTILE)] for _ in range(2)]

            def load_w(h):
                W_ = Wms[h % 2]; d_ = dWms[h % 2]
                self.ldc(W_[:, :, 0:128], wsrc(MQ0 + h * 128, 128), [d_])
                self.ldc(W_[:, :, 128:256], wsrc(MK0 + h * 128, 128), [d_])
                self.ldc(W_[:, :, 256:512], wsrc(MV0 + h * 256, 256), [d_])
                self.ldc(W_[:, :, 512:768], wsrc(MO0 + h * 256, 256), [d_])

            def c0n_load(h, half):
                for bq in range(8):
                    self.ld(C0n[:, bq, :, :], L["stC"][8 * half + bq, h].rearrange("(vb p) d -> p vb d", p=128), [dC0])

            def proj(h, per_tile=None):
                    for b, (t0, tn) in enumerate(TBLK):
                        pq, dpq = self.bank(); pk, dpk = self.bank()
                        for k in range(8):
                            self.mm(pq[:, 0:tn], Wms[h % 2][:, k, 0:128], hT[:, k, t0:t0 + tn], k == 0, k == 7, [dWms[h % 2], dHT[b]], [dpq])
                        for k in range(8):
                            self.mm(pk[:, 0:tn], Wms[h % 2][:, k, 128:256], hT[:, k, t0:t0 + tn], k == 0, k == 7, [dWms[h % 2], dHT[b]], [dpk])
                        self.cp("act", qTh[:, t0:t0 + tn], pq[:, 0:tn], [dpq], [dQK[b]])
                        self.S.op("act", lambda e, t0=t0, tn=tn, pk=pk: e.mul(out=kTh[:, t0:t0 + tn], in_=pk[:, 0:tn], mul=KS), [dpk], [dQK[b]])
                    for i in range(NTILE):
                        nt = tile_nt(i)
                        pa, dpa = self.bank(); pb_, dpb_ = self.bank()
                        for k in range(8):
                            self.mm(pa[0:nt, 0:384], hT[:, k, 128 * i:128 * i + nt], Wms[h % 2][:, k, 128:512], k == 0, k == 7, [dWms[h % 2], dHT[min(i // 4, 4)]], [dpa])
                        for k in range(8):
                            self.mm(pb_[0:nt, 0:256], hT[:, k, 128 * i:128 * i + nt], Wms[h % 2][:, k, 512:768], k == 0, k == 7, [dWms[h % 2], dHT[min(i // 4, 4)]], [dpb_])
                        self.S.op("act", lambda e, i=i, nt=nt, pa=pa: e.mul(out=k_tm[0:nt, i, :], in_=pa[0:nt, 0:128], mul=KS), [dpa], [dTM[i]])
                        self.cp("dve", v_aug[0:nt, i, 0:256], pa[0:nt, 128:384], [dpa], [dTM[i]])
                        self.act(so_tms[h % 2][0:nt, i, :], pb_[0:nt, 0:256], AF.Sigmoid, [dpb_], [dSO[h % 2][i]])
                        if per_tile is not None:
                            per_tile(i)

            def body_(h):
                    i = 16
                    ps_, dps = self.bank()
                    self.mm(ps_[0:64, 0:64], kTh[:, SEQ:NT], qTh[:, SEQ:NT], True, True, [dQK[4]], [dps])
                    self.act(tmpS[0:64, 0:64], ps_[0:64, 0:64], AF.Identity, [dps, dGt], [dtS], scale=W_tm[0:64, 16, h:h + 1])
                    self.tt("dve", sTm[0:64, 0:64], tmpS[0:64, 0:64], mblkc_f[0:64, 0:64], ALU.mult, [dtS, dMk], [dsT])
                    self.tt("dve", maskE[:], bmk[:], ECrep[:, h, 16:32].unsqueeze(2).to_broadcast([128, 16, 64]), ALU.mult, [dBm, dRep], [dmE])
                    self.tt("dve", Qblk[:], maskE[:], qTh[:, SEQ:NT].unsqueeze(1).to_broadcast([128, 16, 64]), ALU.mult, [dmE, dQK[4]], [dQb])
                    pn0, dpn0 = self.bank()
                    self.tr(pn0[:, 0:64], n0tok[0:64, h * 128:(h + 1) * 128], [dN0], [dpn0])
                    self.cp("dve", n0T[:], pn0[:, 0:64].rearrange("p (b j) -> p b j", j=4)[:, :, 0], [dpn0], [dn0T])
                    self.act(wv[0:64, 0:257], v_aug[0:64, 16, 0:257], AF.Identity, [dTM[16], dGt], [dwv], scale=W_tm[0:64, 16, h:h + 1])
                    def halfproc(half):
                            b0 = 8 * half
                            for bb in range(8):
                                if bb % 2 == 0:
                                    pt, dpt = self.bank()
                                for vb in range(2):
                                    c0 = (bb % 2) * 256 + vb * 128
                                    self.tr(pt[:, c0:c0 + 128], C0n[:, bb, vb, :], [dC0], [dpt])
                                if bb % 2 == 1:
                                    self.cp("act", C0T[:, b0 + bb - 1:b0 + bb + 1, 0:256], pt[:, :].rearrange("p (a b) -> p a b", a=2), [dpt], [dC0T])
                            self.tt("dve", wvblk[0:64, :, :], wv[0:64, 0:256].unsqueeze(1).to_broadcast([64, 8, 256]),
                                    rmask_f[0:64, b0:b0 + 8].unsqueeze(2).to_broadcast([64, 8, 256]), ALU.mult, [dwv, dMk], [dwb])
                            for bb in range(8):
                                b = b0 + bb
                                pC, dpC = self.bank()
                                for vb in range(2):
                                    self.mm(pC[:, vb * 128:(vb + 1) * 128], wvblk[0:64, bb, vb * 128:(vb + 1) * 128], k_tm[0:64, 16, :], True, True,
                                            [dwb, dTM[16]], [dpC])
                                co = b % 2
                                self.stt("dve", Cout[co][:].rearrange("p a b -> p (a b)"), C0n[:, bb, :, :].rearrange("p a b -> p (a b)"),
                                         ECrep[:, h, 16 + b:17 + b], pC[:, 0:256], ALU.mult, ALU.add, [dC0, dRep, dpC], [dCo[co]])
                                self.store(L["Cs"][b, h].rearrange("(vb p) d -> p vb d", p=128), Cout[co][:], [dCo[co]])
                    halfproc(0)
                    c0n_load(h, 1)
                    pcs = {}

                    def pre(c):
                        t0 = 128 * c
                        ps_, dps = self.bank()
                        self.mm(ps_[:, 0:128], kTh[:, t0:t0 + 128], qTh[:, t0:t0 + 128], True, True, [dQK[c // 4]], [dps])
                        a = c % 2; b3 = c % 3
                        self.act(tmpSs[a][:], ps_[:, 0:128], AF.Identity, [dps, dGt], [dtSs[a]], scale=W_tm[:, c, h:h + 1])
                        self.tt("dve", sTms[b3][:], tmpSs[a][:], mle_f, ALU.mult, [dtSs[a], dMk], [dsTs[b3]])
                        self.act(wvs[b3][:, 0:257], v_aug[:, c, 0:257], AF.Identity, [dTM[c], dGt], [dwvs[b3]], scale=W_tm[:, c, h:h + 1])
                        pc, dpc = self.bank()
                        self.mm(pc[:, 0:257], k_tm[:, c, :], wvs[b3][:, 0:257], True, True, [dTM[c], dwvs[b3]], [dpc])
                        pcs[c] = (pc, dpc)

                    def main(c):
                        t0 = 128 * c
                        b3 = c % 3
                        po, dpo = self.bank()
                        self.mm(po[:, 0:257], sTms[b3][:], v_aug[:, c, 0:257], True, c == 0, [dsTs[b3], dTM[c]], [dpo])
                        if c > 0:
                            self.act(Cdec[:, 0:257], ST[:], AF.Identity, [dST, dRep], [dCd], scale=ECrep[:, h, c:c + 1])
                            self.mm(po[:, 0:257], qTh[:, t0:t0 + 128], Cdec[:, 0:257], False, True, [dQK[c // 4], dCd], [dpo])
                        head_tail(po, dpo, c, h, 128)
                        pc, dpc = pcs.pop(c)
                        if c == 0:
                            self.cp("dve", ST[:], pc[:, 0:257], [dpc], [dST])
                        else:
                            self.stt("dve", ST[:], ST[:], ECrep[:, h, c:c + 1], pc[:, 0:257], ALU.mult, ALU.add, [dpc, dRep], [dST])

                    pre(0)
                    for c in range(16):
                        if c + 1 < 16:
                            pre(c + 1)
                        main(c)
                    pt, dpt = self.bank()
                    for vb in range(2):
                        self.tr(pt[:, vb * 128:(vb + 1) * 128], ST[:, vb * 128:(vb + 1) * 128], [dST], [dpt])
                    self.cp("dve", Cout[0][:].rearrange("p a b -> p (a b)"), pt[:, 0:256], [dpt], [dCo[0]])
                    self.store(L["Cp"][h].rearrange("(vb p) d -> p vb d", p=128), Cout[0][:], [dCo[0]])
                    self.store_nc(L["np"][h:h + 1, :].rearrange("o d -> d o"), ST[:, 256:257], [dST])
                    halfproc(1)
                    self.cp("dve", C0T[:, :, 256], n0T[:, :], [dn0T], [dC0T])
                    po, dpo = self.bank()
                    self.mm(po[0:64, 0:257], sTm[0:64, 0:64], v_aug[0:64, 16, 0:257], True, False, [dsT, dTM[16]], [dpo])
                    for b in range(16):
                        self.mm(po[0:64, 0:257], Qblk[:, b, :], C0T[:, b, 0:257], False, b == 15, [dQb, dC0T], [dpo])
                    head_tail(po, dpo, 16, h, 64)
                    self.act(BselW[0:64, :], blk_f[0:64, 0:64], AF.Identity, [dMk, dGt], [dBs], scale=W_tm[0:64, 16, h:h + 1])
                    pN, dpN = self.bank()
                    self.mm(pN[0:64, 0:128], BselW[0:64, :], k_tm[0:64, 16, :], True, True, [dBs, dTM[16]], [dpN])
                    self.stt("dve", nnew[0:64, h * 128:(h + 1) * 128], n0tok[0:64, h * 128:(h + 1) * 128], EcTok[0:64, h:h + 1], pN[0:64, 0:128],
                             ALU.mult, ALU.add, [dN0, dEt, dpN], [dNn])

            load_w(0); c0n_load(0, 0); proj(0); load_w(1)
            for h in range(4):
                body_(h)
                hn_pre(h)
                if h + 1 < 4:
                    c0n_load(h + 1, 0)
                    proj(h + 1, per_tile=lambda i, h=h: hn_stats(h, i))
                    if h + 2 < 4:
                        load_w(h + 2)
                else:
                    for i in range(NTILE):
                        hn_stats(h, i)
                hn_post(h)
            for b in range(16):
                self.store(L["ns"][b:b + 1, :], nnew[4 * b:4 * b + 1, :], [dNn])
            S.emit()

    def ln_tile(self, x, dX, nt, lnG, lnB, dLn, st, dSt, junk, dJ, eps):
        self.memset("pool", st[0:nt, 0:2], 0.0, [dSt])
        self.act(junk[0:nt, :], x, AF.Identity, [dX], [dJ, dSt], accum=st[0:nt, 0:1])
        self.act(junk[0:nt, :], x, AF.Square, [dX], [dJ, dSt], accum=st[0:nt, 1:2])
        self.ts("dve", st[0:nt, 2:3], st[0:nt, 0:1], 1.0 / D, None, ALU.mult, None, [dSt], [dSt])
        self.tt("dve", st[0:nt, 3:4], st[0:nt, 2:3], st[0:nt, 2:3], ALU.mult, [dSt], [dSt])
        self.ts("dve", st[0:nt, 4:5], st[0:nt, 1:2], 1.0 / D, eps, ALU.mult, ALU.add, [dSt], [dSt])
        self.tt("dve", st[0:nt, 4:5], st[0:nt, 4:5], st[0:nt, 3:4], ALU.subtract, [dSt], [dSt])
        self.act(st[0:nt, 4:5], st[0:nt, 4:5], AF.Sqrt, [dSt], [dSt])
        self.recip(st[0:nt, 5:6], st[0:nt, 4:5], [dSt], [dSt])
        self.tt("dve", st[0:nt, 6:7], st[0:nt, 2:3], st[0:nt, 5:6], ALU.mult, [dSt], [dSt])
        self.ts("dve", st[0:nt, 6:7], st[0:nt, 6:7], -1.0, None, ALU.mult, None, [dSt], [dSt])
        self.act(x, x, AF.Identity, [dSt], [dX], bias=st[0:nt, 6:7], scale=st[0:nt, 5:6])
        self.tt("dve", x, x, lnG[0:nt, :], ALU.mult, [dLn], [dX])
        self.tt("dve", x, x, lnB[0:nt, :], ALU.add, [dLn], [dX])

    def merge(self, L, hT, dHT, hgT, dHG):
        nc, S = self.nc, self.S
        w_in = L["w_in"]
        GM0, GA0 = 4616, 5640
        wsrc = lambda w, c0, n: w[:, c0:c0 + n].rearrange("(kc p) n -> p kc n", p=128)
        with ExitStack() as es:
            sb = lambda name, shape, dt=F32: es.enter_context(nc.sbuf_tensor(self.uname(name), list(shape), dt))
            oaT = sb("oaT2", [128, 8, NT], BF16); dOA = Dep()
            self.ld(oaT[:].rearrange("p c t -> p (c t)"), L["oa_scr"][:, :], [dOA])
            Wa = sb("Wa", [128, 8, D], BF16); dWa = Dep()
            Wg = sb("Wg", [128, 8, D], BF16); dWg = Dep()
            mraw = sb("mraw", [8, 128]); dMr = Dep(); mngT = sb("mngT", [128, 8]); dMn = Dep()
            zt = sb("zt", [128, 8, 512], BF16); dzt = Dep()
            sg = [sb(f"sg{i}", [128, 512]) for i in range(2)]; dsg = [Dep(), Dep()]
            za = [sb(f"za{i}", [128, 512]) for i in range(2)]; dza = [Dep(), Dep()]
            xt = sb("xt", [128, 3, D]); dxt = [Dep(), Dep(), Dep()]
            tG = [sb(f"tG{i}", [128, 512]) for i in range(2)]; dtG = [Dep(), Dep()]
            lnG = sb("lnG", [128, D]); lnB = sb("lnB", [128, D]); dLn = Dep()
            sts = [sb(f"st{i}", [128, 8]) for i in range(2)]; dSts = [Dep(), Dep()]
            junk = sb("junk", [128, D], BF16); dJ = Dep()
            self.ld(lnG[:], L["ln"]["ln2_g"][0:1, :].broadcast_to([128, D]), [dLn])
            self.ld(lnB[:], L["ln"]["ln2_b"][0:1, :].broadcast_to([128, D]), [dLn])
            self.ld(mraw[:], L["mng"][:, :], [dMr])
            pm, dpm = self.bank()
            self.tr(pm[:, 0:8], mraw[0:8, :], [dMr], [dpm])
            self.cp("dve", mngT[:], pm[:, 0:8], [dpm], [dMn])
            self.ldc(Wa[:], wsrc(L["w_bm"], 0, D), [dWa])
            for k in range(8):
                self.act(Wa[:, k, :], Wa[:, k, :], AF.Identity, [dMn], [dWa], scale=mngT[:, k:k + 1])
            self.ldc(Wg[:], wsrc(w_in, GM0, D), [dWg])
            cnt = [0]

            def branch(src, dsrc, stage):
                for b, (t0, tn) in enumerate(TBLK):
                    for oc in range(8):
                        py, dpy = self.bank(); pg, dpg = self.bank()
                        for k in range(8):
                            self.mm(py[:, 0:tn], Wa[:, k, oc * 128:(oc + 1) * 128], src[:, k, t0:t0 + tn], k == 0, k == 7, [dWa, dsrc(b)], [dpy])
                        for k in range(8):
                            self.mm(pg[:, 0:tn], Wg[:, k, oc * 128:(oc + 1) * 128], hT[:, k, t0:t0 + tn], k == 0, k == 7, [dWg, dHT[b]], [dpg])
                        i = cnt[0] % 2; cnt[0] += 1
                        self.act(sg[i][:, 0:tn], pg[:, 0:tn], AF.Sigmoid, [dpg], [dsg[i]])
                        if stage == 1:
                            self.tt("dve", zt[:, oc, 0:tn], sg[i][:, 0:tn], py[:, 0:tn], ALU.mult, [dsg[i], dpy], [dzt])
                        else:
                            self.tt("dve", za[i][:, 0:tn], sg[i][:, 0:tn], py[:, 0:tn], ALU.mult, [dsg[i], dpy], [dza[i]])
                            self.tt("pool", hgT[:, oc, t0:t0 + tn], hgT[:, oc, t0:t0 + tn], za[i][:, 0:tn], ALU.add, [dza[i]], [dHG[b]])
                    if stage == 1:
                        self.cp("pool", hgT[:, :, t0:t0 + tn], zt[:, :, 0:tn], [dzt], [dHG[b]])

            branch(hgT, lambda b: dHG[b], 1)
            self.ldc(Wa[:], wsrc(L["w_ba"], 0, D), [dWa])
            self.ldc(Wg[:], wsrc(w_in, GA0, D), [dWg])
            branch(oaT, lambda b: dOA, 2)
            if "mixT" in self.debug:
                self.dbg("mixT", hgT[:], dHG, [128, 8, NT])
            S.emit()
        with ExitStack() as es:
            sb = lambda name, shape, dt=F32: es.enter_context(nc.sbuf_tensor(self.uname(name), list(shape), dt))
            Wo = sb("Wo", [128, 8, D], BF16); dWo = Dep()
            xres = sb("xres", [128, NTILE, D]); dX = [Dep() for _ in range(NTILE)]
            tG = [sb(f"tG{i}", [128, 512]) for i in range(2)]; dtG = [Dep(), Dep()]
            lnG = sb("lnG", [128, D]); lnB = sb("lnB", [128, D]); dLn = Dep()
            st = sb("lnst", [128, 7, NTILE]); dSt = Dep()
            junk = sb("junk", [128, D], BF16); dJ = Dep()
            self.memset("pool", st[:], 0.0, [dSt])
            self.ldc(Wo[:], wsrc(L["w_out"], 0, D), [dWo])
            self.ld(lnG[:], L["ln"]["ln2_g"][0:1, :].broadcast_to([128, D]), [dLn])
            self.ld(lnB[:], L["ln"]["ln2_b"][0:1, :].broadcast_to([128, D]), [dLn])
            for i in range(NTILE):
                nt = tile_nt(i)
                src = L["x1p"][128 * i:128 * i + nt, :] if i < 16 else L["x1s"][:, :]
                self.ld(xres[0:nt, i, :], src, [dX[i]])
                self.S.op("act", lambda e, i=i, nt=nt: e.mul(out=xres[0:nt, i, :], in_=xres[0:nt, i, :], mul=ALPHA), [dX[i]], [dX[i]])
            for i in range(NTILE):
                nt = tile_nt(i)
                which = 0 if i < 16 else 1
                for cb in range(2):
                    po, dpo = self.bank()
                    for k in range(8):
                        self.mm(po[0:nt, :], hgT[:, k, 128 * i:128 * i + nt], Wo[:, k, cb * 512:(cb + 1) * 512], k == 0, k == 7,
                                [dHG[min(i // 4, 4)], dWo], [dpo])
                    self.tt("dve", tG[cb][0:nt, :], po[0:nt, :], self.G[0:nt, which, cb * 512:(cb + 1) * 512], ALU.mult, [dpo, self.dG], [dtG[cb]])
                    self.tt("pool", xres[0:nt, i, cb * 512:(cb + 1) * 512], xres[0:nt, i, cb * 512:(cb + 1) * 512], tG[cb][0:nt, :], ALU.add,
                            [dtG[cb]], [dX[i]])
            self.layer_norm_tiles(xres, dX, lnG, lnB, dLn, st, dSt, junk, dJ, LN_EPS)
            for i in range(NTILE):
                nt = tile_nt(i)
                dst = L["x2p"][128 * i:128 * i + nt, :] if i < 16 else L["x2s"][:, :]
                self.store(dst, xres[0:nt, i, :], [dX[i]])
            S.emit()
            if "x2" in self.debug:
                o1 = nc.dram_tensor("dbg_x2p", [SEQ, D], F32, kind="ExternalOutput").ap()
                o2 = nc.dram_tensor("dbg_x2s", [NS, D], F32, kind="ExternalOutput").ap()
                self.store(o1[:, :], L["x2p"][:, :], [])
                self.store(o2[:, :], L["x2s"][:, :], [])
                S.emit()


_CACHE = {}


def _consts():
    half = 32
    inv = (10000.0 ** (-np.arange(half, dtype=np.float32) / half)).astype(np.float32)
    pos = np.concatenate([np.arange(SEQ), PAST + (np.arange(NS) % 4)]).astype(np.float32)
    ang = pos[None, :] * inv[:, None]
    cos = np.cos(ang).astype(np.float32); sin = np.sin(ang).astype(np.float32)
    ropec = np.concatenate([cos, cos, cos, cos], 0)
    ropes = np.concatenate([-sin, sin, -sin, sin], 0)
    ident = np.eye(128, dtype=np.float32)
    s = np.arange(128)[:, None]; t = np.arange(128)[None, :]
    m_le = (s <= t).astype(np.float32)
    m_gt = (s > t).astype(np.float32)
    blk = ((s // 4) == (t // 4)).astype(np.float32)
    m_blkc = blk * m_le
    m_cache = np.zeros((128, 128), np.float32)
    for col in range(128):
        tt = col % 4
        m_cache[:, col] = (np.arange(128) > tt)
    last = np.zeros((128, 128), np.float32)
    last[:, 0:16] = ((np.arange(128)[:, None] // 4) == np.arange(16)[None, :])
    cmask = np.concatenate([m_le, m_gt, m_blkc, blk, m_cache, last], 1)
    bm = ((np.arange(64)[None, :] // 4) == np.arange(16)[:, None]).astype(np.float32)
    bmask = np.broadcast_to(bm.reshape(1, 1024), (128, 1024)).copy()
    return dict(ident=ident, ropec=ropec.astype(np.float32), ropes=ropes.astype(np.float32), cmask=cmask.astype(np.float32),
                bmask=bmask)


def kernel(**inputs):
    debug = tuple(os.environ.get("MK_DEBUG", "").split(",")) if os.environ.get("MK_DEBUG") else ()
    key = debug
    if key not in _CACHE:
        kb = KB(debug)
        kb.build()
        _CACHE[key] = kb
    kb = _CACHE[key]
    f = lambda a: np.ascontiguousarray(np.asarray(a, dtype=np.float32))
    I = {k: f(v) for k, v in inputs.items()}
    cst = _consts()
    shared = dict(
        w_ada=I["w_ada"][0], b_ada=I["b_ada"][0].reshape(72, 128), b_ada_row=I["b_ada"][0].reshape(9, D),
        w_up1=I["w_ffn1_up"][0], w_dn1=I["w_ffn1_down"][0], w_up2=I["w_ffn2_up"][0], w_dn2=I["w_ffn2_down"][0],
        ln1_g=I["ln1_g"], ln1_b=I["ln1_b"], ln2_g=I["ln2_g"], ln2_b=I["ln2_b"], ln3_g=I["ln3_g"], ln3_b=I["ln3_b"],
        w_in=I["w_in"][0], b_ig=I["b_igate"], b_fg=I["b_fgate"], mng=I["m_norm_g"][0].reshape(8, 128), sinks=I["sinks"],
        w_bm=I["w_branch_m"][0], w_ba=I["w_branch_a"][0], w_out=I["w_out"][0], **cst)
    in_maps = []
    for c in range(8):
        sl = slice(16 * c, 16 * c + 16)
        m = dict(shared)
        m["xp"] = I["x_prompt"][c]
        m["xs"] = I["x_sample"][sl].reshape(NS, D)
        m["c_all"] = np.concatenate([I["c_prompt"][c:c + 1], I["c_sample"][sl]], 0)
        m["stC"] = I["state_mlstm_C"][0][sl]
        m["stn"] = I["state_mlstm_n"][0][sl].reshape(16, 512)
        m["stm"] = I["state_mlstm_m"][0][sl]
        m["ck"] = I["cache_swa_k"][0][sl].reshape(16, 128, 256)
        m["cv"] = I["cache_swa_v"][0][sl].reshape(16, 128, 256)
        in_maps.append(m)
    res = run_bass_kernel_spmd(kb.nc, in_maps, core_ids=list(range(8)))
    R = res.results
    kernel.last_results = R
    cat = lambda k: np.stack([np.asarray(r[k]) for r in R], 0)
    y_p = cat("yp").reshape(8, SEQ, D)
    y_s = cat("ys").reshape(128, 4, D)
    C_p = cat("Cp").reshape(1, 8, 4, 256, 128)
    n_p = cat("np").reshape(1, 8, 4, 128)
    m_p = cat("mp").reshape(1, 8, 4)
    k_p = cat("kp").reshape(1, 8, 128, 4, 64)
    v_p = cat("vp").reshape(1, 8, 128, 4, 64)
    C_s = cat("Cs").reshape(1, 128, 4, 256, 128)
    n_s = cat("ns").reshape(1, 128, 4, 128)
    m_s = cat("ms").reshape(1, 128, 4)
    k_s = cat("ks").reshape(1, 128, 128, 4, 64)
    v_s = cat("vs").reshape(1, 128, 128, 4, 64)
    return tuple(np.ascontiguousarray(a, dtype=np.float32) for a in (y_p, y_s, C_p, n_p, m_p, k_p, v_p, C_s, n_s, m_s, k_s, v_s))
```

```python
import os
import numpy as np
from contextlib import ExitStack
import concourse.bass as bass
import concourse.mybir as mybir
from concourse.bass_utils import run_bass_kernel_spmd

F32 = mybir.dt.float32
BF16 = mybir.dt.bfloat16
AF = mybir.ActivationFunctionType
ALU = mybir.AluOpType
AX = mybir.AxisListType

D = 1024
SEQ = 2048
NS = 64
NT = SEQ + NS
NTILE = 17
DFF = 2816
NFC = 22
ALPHA = 2.0 ** 0.25
LN_EPS = 1e-5
HN_EPS = 1e-6
PAST = 8192
SEM_LIMIT = 28000
TBLK = [(0, 512), (512, 512), (1024, 512), (1536, 512), (2048, 64)]


def tile_nt(i):
    return 128 if i < 16 else 64


def blk_tiles(b):
    return [4 * b + j for j in range(4)] if b < 4 else [16]


class Dep:
    __slots__ = ("w", "r")

    def __init__(self):
        self.w = {}
        self.r = {}


class Sched:
    ENGS = ("pe", "act", "dve", "pool", "sp")

    def __init__(self, nc, es):
        self.nc = nc
        self.es = es
        self.ops = {e: [] for e in self.ENGS}
        self.cnt = {e: 0 for e in self.ENGS}
        self.csem = {e: None for e in self.ENGS}
        self.seen = {e: {} for e in self.ENGS}
        self.nsem = 0
        self.dstates = []
        self.misc = {e: [[None, 0] for _ in range(16)] for e in ("sp", "pool", "act")}
        for e in self.misc:
            for m in self.misc[e]:
                self.dstates.append(m)
        self.misc_i = {e: 0 for e in self.misc}
        self.final = []

    def new_sem(self, name):
        self.nsem += 1
        return self.es.enter_context(self.nc.semaphore(f"{name}_{self.nsem}"))

    def dstate(self):
        s = [None, 0]
        self.dstates.append(s)
        return s

    def _counter(self, eng):
        if self.csem[eng] is None or self.cnt[eng] >= SEM_LIMIT:
            self.csem[eng] = self.new_sem("c" + eng)
            self.cnt[eng] = 0
        return self.csem[eng]

    def _collect(self, eng, reads, writes, skip_own=True, extra=()):
        need = {}

        def add(tok):
            sem, v = tok
            k = id(sem)
            if k not in need or need[k][1] < v:
                need[k] = (sem, v)
        for d in reads:
            for tok in d.w.values():
                add(tok)
        for d in writes:
            for tok in d.w.values():
                add(tok)
            for tok in d.r.values():
                add(tok)
        for tok in extra:
            add(tok)
        waits = []
        seen = self.seen[eng]
        own = self.csem[eng]
        for k, (sem, v) in need.items():
            if skip_own and own is not None and sem is own:
                continue
            if seen.get(k, 0) >= v:
                continue
            seen[k] = v
            waits.append((sem, v))
        return waits

    def op(self, eng, fn, reads=(), writes=()):
        waits = self._collect(eng, reads, writes, skip_own=(eng == "pe"))
        sem = self._counter(eng)
        self.cnt[eng] += 1
        tok = (sem, self.cnt[eng])
        k = id(sem)
        for d in reads:
            d.r[k] = tok
        for d in writes:
            d.w[k] = tok
        self.ops[eng].append((waits, fn, (sem, 1)))
        return tok

    def dma(self, eng, fn, reads=(), writes=(), st=None, final=False):
        if st is None:
            st = self.misc[eng][self.misc_i[eng] % len(self.misc[eng])]
            self.misc_i[eng] += 1
        extra = []
        if st[0] is not None and st[1] + 16 > SEM_LIMIT:
            st[0] = None
        if st[0] is None:
            st[0] = self.new_sem("d")
            st[1] = 0
        elif st[1] > 0:
            extra.append((st[0], st[1]))
        waits = self._collect(eng, reads, writes, skip_own=False, extra=extra)
        st[1] += 16
        tok = (st[0], st[1])
        k = id(st[0])
        for d in reads:
            d.r[k] = tok
        for d in writes:
            d.w[k] = tok
        self.ops[eng].append((waits, fn, (st[0], 16)))
        if final:
            self.final.append(tok)
        return tok

    def emit(self, last=False):
        nc = self.nc
        bar = []
        for e in self.ENGS:
            if self.csem[e] is not None and self.cnt[e] > 0:
                bar.append((e, self.csem[e], self.cnt[e]))
        dbar = [(s[0], s[1]) for s in self.dstates if s[0] is not None and s[1] > 0]
        ops = self.ops
        seen = self.seen

        def run(engine, name):
            for waits, fn, (sem, inc) in ops[name]:
                for ws, wv in waits:
                    engine.wait_ge(ws, wv)
                fn(engine).then_inc(sem, inc)
            for e, sem, v in bar:
                if e != name and seen[name].get(id(sem), 0) < v:
                    engine.wait_ge(sem, v)
                    seen[name][id(sem)] = v
            for sem, v in dbar:
                if seen[name].get(id(sem), 0) < v:
                    engine.wait_ge(sem, v)
                    seen[name][id(sem)] = v

        with nc.Block() as block:
            @block.sync
            def _(e):
                run(e, "sp")

            @block.tensor
            def _(e):
                run(e, "pe")

            @block.scalar
            def _(e):
                run(e, "act")

            @block.vector
            def _(e):
                run(e, "dve")

            @block.gpsimd
            def _(e):
                run(e, "pool")
        self.ops = {e: [] for e in self.ENGS}


class KB:
    def __init__(self, debug=()):
        self.debug = set(debug)
        self.nc = bass.Bass("TRN2", target_bir_lowering=False)
        self.dram = {}
        self.dbg_out = {}

    def uname(self, name):
        self.ucnt = getattr(self, "ucnt", 0) + 1
        return f"s{self.ucnt}_{name}"

    def din(self, name, shape):
        self.dram[name] = self.nc.dram_tensor(name, list(shape), F32, kind="ExternalInput").ap()
        return self.dram[name]

    def dout(self, name, shape):
        self.dram[name] = self.nc.dram_tensor(name, list(shape), F32, kind="ExternalOutput").ap()
        return self.dram[name]

    def dscr(self, name, shape):
        self.dram[name] = self.nc.dram_tensor(name, list(shape), F32, kind="Internal").ap()
        return self.dram[name]

    def mm(self, out, lhsT, rhs, start, stop, R, W):
        self.S.op("pe", lambda e: e.matmul(out, lhsT=lhsT, rhs=rhs, start=start, stop=stop), R, W)

    def tr(self, out, in_, R, W):
        n = in_.shape[0]
        ident = self.ident[0:n, 0:n]
        self.S.op("pe", lambda e: e.transpose(out=out, in_=in_, identity=ident), list(R) + [self.dIdent], W)

    def act(self, out, in_, func, R, W, bias=None, scale=None, accum=None):
        kw = {}
        if bias is not None:
            kw["bias"] = bias
        if scale is not None:
            kw["scale"] = scale
        if accum is not None:
            kw["accum_out"] = accum
        self.S.op("act", lambda e: e.activation(out=out, in_=in_, func=func, **kw), R, W)

    def tt(self, eng, out, in0, in1, op, R, W):
        self.S.op(eng, lambda e: e.tensor_tensor(out=out, in0=in0, in1=in1, op=op), R, W)

    def ts(self, eng, out, in0, s1, s2, op0, op1, R, W):
        if s2 is None:
            s2 = 0.0
            op1 = ALU.add
        self.S.op(eng, lambda e: e.tensor_scalar(out=out, in0=in0, scalar1=s1, scalar2=s2, op0=op0, op1=op1), R, W)

    def stt(self, eng, out, in0, scalar, in1, op0, op1, R, W):
        self.S.op(eng, lambda e: e.scalar_tensor_tensor(out=out, in0=in0, scalar=scalar, in1=in1, op0=op0, op1=op1), R, W)

    def cp(self, eng, out, in_, R, W):
        if eng == "act":
            self.S.op("act", lambda e: e.copy(out=out, in_=in_), R, W)
        else:
            self.S.op(eng, lambda e: e.tensor_copy(out=out, in_=in_), R, W)

    def memset(self, eng, ap, val, W):
        self.S.op(eng, lambda e: e.memset(ap, val), (), W)

    def recip(self, out, in_, R, W):
        self.S.op("dve", lambda e: e.reciprocal(out=out, in_=in_), R, W)

    def ld(self, out, in_, W, R=(), eng="sp", st=None):
        return self.S.dma(eng, lambda e: e.dma_start(out=out, in_=in_), R, W, st=st)

    def ldc(self, out, in_, W, R=(), st=None):
        return self.S.dma("pool", lambda e: e.dma_start(out=out, in_=in_), R, W, st=st)

    def store(self, out, in_, R, W=(), final=True, eng="sp"):
        return self.S.dma(eng, lambda e: e.dma_start(out=out, in_=in_), R, W, final=final)

    def ld_nc(self, out, in_, W, R=()):
        return self.S.dma("sp", lambda e: e.dma_start(out=out, in_=in_, allow_slow_non_contiguous=True), R, W)

    def store_nc(self, out, in_, R):
        return self.S.dma("sp", lambda e: e.dma_start(out=out, in_=in_, allow_slow_non_contiguous=True), R, (), final=True)

    def dbg(self, name, ap, dep, shape):
        if name not in self.debug:
            return
        o = self.nc.dram_tensor("dbg_" + name, list(shape), ap.dtype, kind="ExternalOutput").ap()
        self.dbg_out[name] = o
        deps = dep if isinstance(dep, (list, tuple)) else [dep]
        self.store(o, ap, list(deps))

    def bank(self):
        i = self.bank_i % 8
        self.bank_i += 1
        return self.PB[i], self.dPB[i]

    def build(self):
        nc = self.nc
        din, dout = self.din, self.dout
        xp = din("xp", [SEQ, D]); xs = din("xs", [NS, D]); c_all = din("c_all", [17, D])
        stC = din("stC", [16, 4, 256, 128]); stn = din("stn", [16, 512]); stm = din("stm", [16, 4])
        ck = din("ck", [16, 128, 256]); cv = din("cv", [16, 128, 256])
        w_ada = din("w_ada", [D, 9 * D]); b_ada = din("b_ada", [72, 128]); self.b_ada_row = din("b_ada_row", [9, D])
        w_up1 = din("w_up1", [D, 2 * DFF]); w_dn1 = din("w_dn1", [DFF, D])
        w_up2 = din("w_up2", [D, 2 * DFF]); w_dn2 = din("w_dn2", [DFF, D])
        ln = {k: din(k, [1, D]) for k in ("ln1_g", "ln1_b", "ln2_g", "ln2_b", "ln3_g", "ln3_b")}
        w_in = din("w_in", [D, 6664]); b_ig = din("b_ig", [1, 4]); b_fg = din("b_fg", [1, 4])
        mng = din("mng", [8, 128]); sinks = din("sinks", [1, 16])
        w_bm = din("w_bm", [D, D]); w_ba = din("w_ba", [D, D]); w_out = din("w_out", [D, D])
        identd = din("ident", [128, 128]); ropec = din("ropec", [128, NT]); ropes = din("ropes", [128, NT])
        cmask = din("cmask", [128, 128 * 6]); bmask = din("bmask", [128, 1024])
        oa_scr = nc.dram_tensor("oa_scr", [128, 8 * NT], BF16, kind="Internal").ap()
        yp = dout("yp", [SEQ, D]); ys = dout("ys", [NS, D])
        Cp = dout("Cp", [4, 256, 128]); np_ = dout("np", [4, 128]); mp = dout("mp", [4, 1])
        kp = dout("kp", [128, 256]); vp = dout("vp", [128, 256])
        Cs = dout("Cs", [16, 4, 256, 128]); ns_ = dout("ns", [16, 512]); ms = dout("ms", [16, 4])
        ks = dout("ks", [16, 128, 256]); vs = dout("vs", [16, 128, 256])
        x1p = self.dscr("x1p", [SEQ, D]); x1s = self.dscr("x1s", [NS, D])
        x2p = self.dscr("x2p", [SEQ, D]); x2s = self.dscr("x2s", [NS, D])

        with ExitStack() as es:
            self.es = es
            self.S = S = Sched(nc, es)
            sb = lambda name, shape, dt=F32: es.enter_context(nc.sbuf_tensor(self.uname(name), list(shape), dt))
            self.PB = [es.enter_context(nc.psum_tensor(f"pb{i}", [128, 512], F32)) for i in range(8)]
            self.dPB = [Dep() for _ in range(8)]
            self.bank_i = 0
            self.ident = sb("ident", [128, 128]); self.dIdent = Dep()
            self.ld(self.ident[:], identd[:, :], [self.dIdent])
            self.modT = sb("modT", [128, 2, 8, 17]); self.dModT = Dep()
            self.modS = sb("modS", [128, 2, 8, 64]); self.dModS = Dep()
            self.G = sb("G", [128, 2, D]); self.dG = Dep()
            self.scT = sb("scT", [128, 8, 17], BF16); self.dScT = Dep()
            self.lhsP = sb("lhsP", [128, 8, 128], BF16); self.lhsS = sb("lhsS", [128, 8, 64], BF16); self.dLhs = Dep()
            self.badaT = sb("badaT", [128, 72]); self.dBada = Dep()
            self.setup_ada(c_all, b_ada)
            S.emit()
            self.ada_stage(0, w_ada, b_ada)
            self.dbg("modT", self.modT[:], self.dModT, [128, 2, 8, 17])
            self.dbg("modS", self.modS[:], self.dModS, [128, 2, 8, 64])
            self.dbg("G", self.G[:], self.dG, [128, 2, D])
            S.emit()
            self.ffn(xp, xs, x1p, x1s, w_up1, w_dn1, ln["ln1_g"], ln["ln1_b"], "f1", next_ada=(1, w_ada, b_ada))
            if "x1" in self.debug:
                o1 = nc.dram_tensor("dbg_x1p", [SEQ, D], F32, kind="ExternalOutput").ap()
                o2 = nc.dram_tensor("dbg_x1s", [NS, D], F32, kind="ExternalOutput").ap()
                self.store(o1[:, :], x1p[:, :], [])
                self.store(o2[:, :], x1s[:, :], [])
                S.emit()
            if "stop1" in self.debug:
                return nc
            self.mixer(dict(locals(), **self.dram))
            self.ada_stage(2, w_ada, b_ada)
            S.emit()
            self.ffn(x2p, x2s, yp, ys, w_up2, w_dn2, ln["ln3_g"], ln["ln3_b"], "f2")
        return nc

    def setup_ada(self, c_all, b_ada):
        with ExitStack() as es:
            nc = self.nc
            sb = lambda name, shape, dt=F32: es.enter_context(nc.sbuf_tensor(self.uname(name), list(shape), dt))
            craw = sb("craw", [17, D]); dC = Dep()
            csil = sb("csil", [17, D]); dCs = Dep()
            braw = sb("braw", [72, 128]); dB = Dep()
            scTf = sb("scTf", [128, 8, 17]); dScTf = Dep()
            self.ld(craw[:], c_all[:, :], [dC])
            self.ld(braw[:], b_ada[:, :], [dB])
            self.act(csil[:], craw[:], AF.Silu, [dC], [dCs])
            pb, dpb = self.bank()
            for k in range(8):
                self.tr(pb[:, k * 17:(k + 1) * 17], csil[0:17, k * 128:(k + 1) * 128], [dCs], [dpb])
            self.cp("dve", scTf[:].rearrange("p a b -> p (a b)"), pb[:, 0:136], [dpb], [dScTf])
            self.cp("dve", self.scT[:], scTf[:], [dScTf], [self.dScT])
            self.cp("dve", self.lhsP[:], scTf[:, :, 0:1].to_broadcast([128, 8, 128]), [dScTf], [self.dLhs])
            for k in range(8):
                self.cp("dve", self.lhsS[:, k, :].rearrange("p (b j) -> p b j", j=4),
                        scTf[:, k, 1:17].unsqueeze(2).to_broadcast([128, 16, 4]), [dScTf], [self.dLhs])
            pb2, dpb2 = self.bank()
            self.tr(pb2[:, 0:72], braw[0:72, :], [dB], [dpb2])
            self.cp("dve", self.badaT[:], pb2[:, 0:72], [dpb2], [self.dBada])
            self.S.emit()

    def ada_load(self, st, ci, w_ada, wa, dwa):
        j = 3 * st + ci
        self.ldc(wa, w_ada[:, j * D:(j + 1) * D].rearrange("(kc p) n -> p kc n", p=128), [dwa])

    def ada_compute(self, st, ci, wa, dwa, bG, dbG, tmpG, dtG):
        j = 3 * st + ci
        if ci < 2:
            pb, dpb = self.bank()
            for f in range(8):
                for k in range(8):
                    self.mm(pb[:, f * 17:(f + 1) * 17], wa[:, k, f * 128:(f + 1) * 128], self.scT[:, k, :],
                            k == 0, k == 7, [dwa, self.dScT], [dpb])
            self.stt("dve", self.modT[:, ci, :, :], pb[:, 0:136].rearrange("p (a b) -> p a b", a=8),
                     1.0 if ci == 1 else 0.0,
                     self.badaT[:, j * 8:(j + 1) * 8].unsqueeze(2).to_broadcast([128, 8, 17]),
                     ALU.add, ALU.add, [dpb, self.dBada], [self.dModT])
        else:
            a = 1.0 if st == 1 else 0.5
            self.ld(bG[:], self.b_ada_row[j:j + 1, :].broadcast_to([128, D]), [dbG])
            for which, lhs, n in ((0, self.lhsP, 128), (1, self.lhsS, 64)):
                for cb in range(2):
                    pb, dpb = self.bank()
                    for k in range(8):
                        self.mm(pb[0:n, :], lhs[:, k, :], wa[:, k, cb * 512:(cb + 1) * 512], k == 0, k == 7,
                                [dwa, self.dLhs], [dpb])
                    self.tt("dve", tmpG[0:n, :], pb[0:n, :], bG[0:n, cb * 512:(cb + 1) * 512], ALU.add, [dpb, dbG], [dtG])
                    self.ts("dve", self.G[0:n, which, cb * 512:(cb + 1) * 512], tmpG[0:n, :], a, a, ALU.mult, ALU.add,
                            [dtG], [self.dG])

    def ada_finish(self):
        for w in range(2):
            self.cp("dve", self.modS[:, w, :, :].rearrange("p k (b j) -> p k b j", j=4),
                    self.modT[:, w, :, 1:17].unsqueeze(3).to_broadcast([128, 8, 16, 4]), [self.dModT], [self.dModS])

    def ada_stage(self, st, w_ada, b_ada):
        nc = self.nc
        with ExitStack() as es:
            sb = lambda name, shape, dt=F32: es.enter_context(nc.sbuf_tensor(self.uname(name), list(shape), dt))
            wa = [sb(f"wa{i}", [128, 8, D], BF16)[:] for i in range(2)]
            dwa = [Dep(), Dep()]
            bG = sb("bG", [128, D]); dbG = Dep()
            tmpG = sb("tmpG", [128, 512]); dtG = Dep()
            for ci in range(3):
                self.ada_load(st, ci, w_ada, wa[ci % 2], dwa[ci % 2])
                self.ada_compute(st, ci, wa[ci % 2], dwa[ci % 2], bG, dbG, tmpG, dtG)
            self.ada_finish()
            self.S.emit()

    def make_xT(self, xT, dXT, src_tile_ap, dsrc, i, tmp, dtmp):
        nt = tile_nt(i)
        t0 = 128 * i
        for half in range(2):
            pb, dpb = self.bank()
            for kk in range(4):
                k = half * 4 + kk
                self.tr(pb[:, kk * nt:(kk + 1) * nt], src_tile_ap[0:nt, k * 128:(k + 1) * 128], [dsrc], [dpb])
            if i < 16:
                for kk in range(4):
                    k = half * 4 + kk
                    self.act(xT[:, k, t0:t0 + nt], pb[:, kk * nt:(kk + 1) * nt], AF.Identity, [dpb, self.dModT], [dXT],
                             bias=self.modT[:, 0, k, 0:1], scale=self.modT[:, 1, k, 0:1])
            else:
                k0 = half * 4
                self.tt("dve", tmp[:, 0:256].rearrange("p (a b) -> p a b", a=4), pb[:, 0:256].rearrange("p (a b) -> p a b", a=4),
                        self.modS[:, 1, k0:k0 + 4, :], ALU.mult, [dpb, self.dModS], [dtmp])
                self.tt("dve", xT[:, k0:k0 + 4, t0:t0 + nt], tmp[:, 0:256].rearrange("p (a b) -> p a b", a=4),
                        self.modS[:, 0, k0:k0 + 4, :], ALU.add, [dtmp, self.dModS], [dXT])

    def layer_norm_tiles(self, xres, dX, lnG, lnB, dLn, st, dSt, junk, dJ, eps):
        for i in range(NTILE):
            nt = tile_nt(i)
            self.act(junk[0:nt, :], xres[0:nt, i, :], AF.Identity, [dX[i]], [dJ, dSt], accum=st[0:nt, 0, i:i + 1])
            self.act(junk[0:nt, :], xres[0:nt, i, :], AF.Square, [dX[i]], [dJ, dSt], accum=st[0:nt, 1, i:i + 1])
        m = st[:, 2, :]; msq = st[:, 3, :]; var = st[:, 4, :]; rstd = st[:, 5, :]; nb = st[:, 6, :]
        self.ts("dve", m, st[:, 0, :], 1.0 / D, None, ALU.mult, None, [dSt], [dSt])
        self.tt("dve", msq, m, m, ALU.mult, [dSt], [dSt])
        self.ts("dve", var, st[:, 1, :], 1.0 / D, eps, ALU.mult, ALU.add, [dSt], [dSt])
        self.tt("dve", var, var, msq, ALU.subtract, [dSt], [dSt])
        self.act(var, var, AF.Sqrt, [dSt], [dSt])
        self.recip(rstd, var, [dSt], [dSt])
        self.tt("dve", nb, m, rstd, ALU.mult, [dSt], [dSt])
        self.ts("dve", nb, nb, -1.0, None, ALU.mult, None, [dSt], [dSt])
        for i in range(NTILE):
            nt = tile_nt(i)
            self.act(xres[0:nt, i, :], xres[0:nt, i, :], AF.Identity, [dSt], [dX[i]], bias=nb[0:nt, i:i + 1], scale=rstd[0:nt, i:i + 1])
            self.tt("dve", xres[0:nt, i, :], xres[0:nt, i, :], lnG[0:nt, :], ALU.mult, [dLn], [dX[i]])
            self.tt("dve", xres[0:nt, i, :], xres[0:nt, i, :], lnB[0:nt, :], ALU.add, [dLn], [dX[i]])

    def ffn(self, src_p, src_s, dst_p, dst_s, w_up, w_dn, ln_g, ln_b, tag, next_ada=None):
        nc = self.nc
        S = self.S
        groups = [(0, 4), (4, 4), (8, 4), (12, 4), (16, 3), (19, 3)]
        GC = 4
        with ExitStack() as es:
            sb = lambda name, shape, dt=F32: es.enter_context(nc.sbuf_tensor(self.uname(name), list(shape), dt))
            xres = sb("xres", [128, NTILE, D]); dX = [Dep() for _ in range(NTILE)]
            xT = sb("xT", [128, 8, NT], BF16); dXT = [Dep() for _ in range(5)]
            wu = [sb(f"wu{i}", [128, 8, 2, GC * 128], BF16) for i in range(2)]; dwu = [Dep(), Dep()]
            wd = [sb(f"wd{i}", [128, GC, D], BF16) for i in range(2)]; dwd = [Dep(), Dep()]
            gT = [sb(f"gT{i}", [128, GC, 512], BF16) for i in range(2)]; dgT = [Dep(), Dep()]
            tsil = [sb(f"tsil{i}", [128, 512], BF16) for i in range(2)]; dts = [Dep(), Dep()]
            tG = [sb(f"tG{i}", [128, 512]) for i in range(2)]; dtG = [Dep(), Dep()]
            lnG = sb("lnG", [128, D]); lnB = sb("lnB", [128, D]); dLn = Dep()
            st = sb("lnst", [128, 7, NTILE]); dSt = Dep()
            junk = sb("junk", [128, D], BF16); dJ = Dep()
            tmpm = sb("tmpm", [128, 256]); dtm = Dep()
            if next_ada is not None:
                a_bG = sb("a_bG", [128, D]); a_tmpG = sb("a_tmpG", [128, 512])
                wa_views = [wu[i][:].rearrange("p k h c -> p k (h c)") for i in range(2)]
            if os.environ.get("MK_PADF"):
                sb("padf", [128, int(os.environ["MK_PADF"]) * 256])
            self.memset("pool", st[:], 0.0, [dSt])
            self.ld(lnG[:], ln_g[0:1, :].broadcast_to([128, D]), [dLn])
            self.ld(lnB[:], ln_b[0:1, :].broadcast_to([128, D]), [dLn])

            def load_group(q):
                c0, gc = groups[q]
                s = q % 2
                for half in range(2):
                    col0 = half * DFF + c0 * 128
                    self.ldc(wu[s][:, :, half, 0:gc * 128], w_up[:, col0:col0 + gc * 128].rearrange("(kc p) n -> p kc n", p=128), [dwu[s]])
                self.ldc(wd[s][:, 0:gc, :], w_dn[c0 * 128:(c0 + gc) * 128, :].rearrange("(c p) n -> p c n", p=128), [dwd[s]])

            load_group(0)
            for i in range(NTILE):
                nt = tile_nt(i)
                src = src_p[128 * i:128 * i + nt, :] if i < 16 else src_s[:, :]
                self.ld(xres[0:nt, i, :], src, [dX[i]])
                self.make_xT(xT, dXT[min(i // 4, 4)], xres[:, i, :], dX[i], i, tmpm, dtm)
                self.S.op("act", lambda e, i=i, nt=nt: e.mul(out=xres[0:nt, i, :], in_=xres[0:nt, i, :], mul=ALPHA), [dX[i]], [dX[i]])
            items = [(q, b) for q in range(len(groups)) for b in range(5)]

            def up(n):
                q, b = items[n]
                c0, gc = groups[q]
                s = q % 2
                g = n % 2
                t0, tn = TBLK[b]
                for j in range(gc):
                    pa, dpa = self.bank()
                    pu, dpu = self.bank()
                    for half, (pp, dpp) in enumerate(((pa, dpa), (pu, dpu))):
                        for k in range(8):
                            self.mm(pp[:, 0:tn], wu[s][:, k, half, j * 128:(j + 1) * 128], xT[:, k, t0:t0 + tn], k == 0, k == 7,
                                    [dwu[s], dXT[b]], [dpp])
                    sl = (n * GC + j) % 2
                    self.act(tsil[sl][:, 0:tn], pa[:, 0:tn], AF.Silu, [dpa], [dts[sl]])
                    self.tt("dve", gT[g][:, j, 0:tn], tsil[sl][:, 0:tn], pu[:, 0:tn], ALU.mult, [dts[sl], dpu], [dgT[g]])

            def down(n):
                q, b = items[n]
                c0, gc = groups[q]
                s = q % 2
                g = n % 2
                for ti, i in enumerate(blk_tiles(b)):
                    nt = tile_nt(i)
                    which = 0 if i < 16 else 1
                    for cb in range(2):
                        po, dpo = self.bank()
                        for j in range(gc):
                            self.mm(po[0:nt, :], gT[g][:, j, ti * 128:ti * 128 + nt], wd[s][:, j, cb * 512:(cb + 1) * 512],
                                    j == 0, j == gc - 1, [dgT[g], dwd[s]], [dpo])
                        sl = (i * 2 + cb) % 2
                        self.tt("dve", tG[sl][0:nt, :], po[0:nt, :], self.G[0:nt, which, cb * 512:(cb + 1) * 512], ALU.mult,
                                [dpo, self.dG], [dtG[sl]])
                        self.tt("pool", xres[0:nt, i, cb * 512:(cb + 1) * 512], xres[0:nt, i, cb * 512:(cb + 1) * 512], tG[sl][0:nt, :],
                                ALU.add, [dtG[sl]], [dX[i]])

            load_group(1)
            up(0)
            for n in range(len(items)):
                if n + 1 < len(items):
                    up(n + 1)
                down(n)
                q, b = items[n]
                if b == 4 and q + 2 < len(groups):
                    load_group(q + 2)
                if next_ada is not None and b == 4 and q == len(groups) - 2:
                    sA = q % 2
                    self.ada_load(next_ada[0], 0, next_ada[1], wa_views[sA], dwu[sA])
            if next_ada is not None:
                st_, wad = next_ada[0], next_ada[1]
                sB = 1 - sA
                a_dbG = Dep(); a_dtG = Dep()
                self.ada_load(st_, 1, wad, wa_views[sB], dwu[sB])
                self.ada_compute(st_, 0, wa_views[sA], dwu[sA], a_bG, a_dbG, a_tmpG, a_dtG)
                self.ada_load(st_, 2, wad, wa_views[sA], dwu[sA])
            if tag == "f1":
                self.dbg("pre", xres[:], dX, [128, NTILE, D])
                self.dbg("xT", xT[:], dXT, [128, 8, NT])
            self.layer_norm_tiles(xres, dX, lnG, lnB, dLn, st, dSt, junk, dJ, LN_EPS)
            if tag == "f1":
                self.dbg("lnst", st[:], dSt, [128, 7, NTILE])
            for i in range(NTILE):
                nt = tile_nt(i)
                dst = dst_p[128 * i:128 * i + nt, :] if i < 16 else dst_s[:, :]
                self.store(dst, xres[0:nt, i, :], [dX[i]])
            if next_ada is not None:
                self.ada_compute(st_, 1, wa_views[sB], dwu[sB], a_bG, a_dbG, a_tmpG, a_dtG)
                self.ada_compute(st_, 2, wa_views[sA], dwu[sA], a_bG, a_dbG, a_tmpG, a_dtG)
                self.ada_finish()
            S.emit()

    def mixer(self, L):
        nc, S = self.nc, self.S
        x1p, x1s = L["x1p"], L["x1s"]
        with ExitStack() as es0:
            sb0 = lambda name, shape, dt=F32: es0.enter_context(nc.sbuf_tensor(self.uname(name), list(shape), dt))
            hT = sb0("hT", [128, 8, NT], BF16); dHT = [Dep() for _ in range(5)]
            with ExitStack() as es:
                sb = lambda name, shape, dt=F32: es.enter_context(nc.sbuf_tensor(self.uname(name), list(shape), dt))
                xt = sb("xt", [128, 2, D]); dxt = [Dep(), Dep()]
                tmpm = sb("tmpm", [128, 256]); dtm = Dep()
                for i in range(NTILE):
                    nt = tile_nt(i)
                    src = x1p[128 * i:128 * i + nt, :] if i < 16 else x1s[:, :]
                    self.ld(xt[0:nt, i % 2, :], src, [dxt[i % 2]])
                    self.make_xT(hT, dHT[min(i // 4, 4)], xt[:, i % 2, :], dxt[i % 2], i, tmpm, dtm)
                S.emit()
            with ExitStack() as es:
                oaT = es.enter_context(nc.sbuf_tensor(self.uname("oaT"), [128, 8, NT], BF16)); dOA = [Dep() for _ in range(5)]
                self.swa(L, hT, dHT, oaT, dOA)
                self.store(L["oa_scr"][:, :], oaT[:].rearrange("p c t -> p (c t)"), dOA)
                if "oaT" in self.debug:
                    self.dbg("oaT", oaT[:], dOA, [128, 8, NT])
                S.emit()
            if "stop2" in self.debug:
                return
            hgT = sb0("hgT", [128, 8, NT], BF16); dHG = [Dep() for _ in range(5)]
            self.mlstm(L, hT, dHT, hgT, dHG)
            if "hgT" in self.debug:
                self.dbg("hgT", hgT[:], dHG, [128, 8, NT])
                S.emit()
            if "stop3" in self.debug:
                return
            self.merge(L, hT, dHT, hgT, dHG)

    def swa(self, L, hT, dHT, oaT, dOA):
        nc, S = self.nc, self.S
        w_in = L["w_in"]
        AQ0, AK0, AV0 = 3080, 4104, 4360
        with ExitStack() as es:
            sb = lambda name, shape, dt=F32: es.enter_context(nc.sbuf_tensor(self.uname(name), list(shape), dt))
            if os.environ.get("MK_PADS"):
                sb("pads", [128, int(os.environ["MK_PADS"]) * 256])
            cosT = sb("cosT", [128, NT]); sinT = sb("sinT", [128, NT]); dRope = Dep()
            self.ld(cosT[:], L["ropec"][:, :], [dRope]); self.ld(sinT[:], L["ropes"][:, :], [dRope])
            mk = sb("mk", [128, 768]); dMk = Dep()
            self.ld(mk[:], L["cmask"][:, :], [dMk])
            mle = sb("mle", [128, 128], BF16); mgt = sb("mgt", [128, 128], BF16); mblkc = sb("mblkc", [128, 128], BF16)
            mcache = sb("mcache", [128, 4], BF16); dMb = Dep()
            self.cp("dve", mle[:], mk[:, 0:128], [dMk], [dMb]); self.cp("dve", mgt[:], mk[:, 128:256], [dMk], [dMb])
            self.cp("dve", mblkc[:], mk[:, 256:384], [dMk], [dMb]); self.cp("dve", mcache[:], mk[:, 512:516], [dMk], [dMb])
            ones = sb("ones", [128, 128], BF16); dOnes = Dep()
            self.memset("pool", ones[:], 1.0, [dOnes])
            esink = sb("esink", [128, 16]); dEs = Dep()
            self.ld(esink[:], L["sinks"][0:1, :].broadcast_to([128, 16]), [dEs])
            self.act(esink[:], esink[:], AF.Exp, [dEs], [dEs])
            ckf = sb("ckf", [128, 16, 256]); dCk = Dep()
            for b0 in range(0, 16, 4):
                self.ld(ckf[:, b0:b0 + 4, :], L["ck"][b0:b0 + 4].rearrange("b p c -> p b c"), [dCk])
            cvv = L["cv"].rearrange("b p c -> p b c")
            KcTn = sb("KcTn", [128, 2, 16, 128], BF16); dKn = Dep()
            for gp in range(2):
                for b0 in range(0, 16, 4):
                    pb, dpb = self.bank()
                    for bb in range(4):
                        self.tr(pb[:, bb * 128:(bb + 1) * 128], ckf[:, b0 + bb, gp * 128:(gp + 1) * 128], [dCk], [dpb])
                    self.cp("act", KcTn[:, gp, b0:b0 + 4, :].rearrange("p b c -> p (b c)"), pb[:, :], [dpb], [dKn])
            kout_p = sb("kout_p", [128, 256]); vfin = sb("vfin", [128, 2, 256]); knew_s = sb("knew_s", [128, 256])
            dKo = Dep(); dVf = Dep(); dKs = Dep()
            WQ = sb("WQ", [128, 8, 256], BF16); WQs = sb("WQs", [128, 8, 256], BF16)
            WK2 = sb("WK2", [128, 8, 128], BF16); WK2s = sb("WK2s", [128, 8, 128], BF16)
            WV = sb("WV", [128, 8, 64], BF16); dW = Dep(); dWs = Dep()
            qT = sb("qT", [128, 2, NT], BF16); dQ = [Dep() for _ in range(5)]
            kT2 = sb("kT2", [128, NT], BF16); dK = [Dep() for _ in range(5)]
            vtm2 = sb("vtm2", [128, NTILE, 128], BF16); dV = [Dep() for _ in range(NTILE)]
            vlh = sb("vlh", [128, 16, 2, 128], BF16)
            ones_lh = sb("ones_lh", [128, 2, 128], BF16); dOlh = Dep()
            esg = sb("esg", [128, 2]); dEsg = Dep()
            self.memset("pool", vlh[:], 0.0, dV[0:16])
            self.memset("pool", ones_lh[:], 0.0, [dOlh])
            self.memset("pool", ones_lh[:, 0, 0:64], 1.0, [dOlh])
            self.memset("pool", ones_lh[:, 1, 64:128], 1.0, [dOlh])
            kfin = sb("kfin", [128, 192]); dKf = Dep()
            t1 = [sb(f"t1_{i}", [128, 512]) for i in range(2)]; dt1 = [Dep(), Dep()]
            t2 = [sb(f"t2_{i}", [128, 512]) for i in range(2)]; dt2 = [Dep(), Dep()]
            pT = [sb(f"pT{i}", [128, 512], BF16) for i in range(4)]; dpT = [Dep() for _ in range(4)]
            pTm = [sb(f"pTm{i}", [128, 512], BF16) for i in range(4)]; dpTm = [Dep() for _ in range(4)]
            recs = [sb(f"rec{i}", [128, 512]) for i in range(2)]; dRecs = [Dep(), Dep()]
            rec = recs[0]; dRec = dRecs[0]
            KcT2 = sb("KcT2", [128, 16, 128], BF16); dKc2 = Dep()
            Vc2 = sb("Vc2", [128, 16, 128], BF16); dVc2 = Dep()
            pTn = sb("pTn", [128, 256], BF16); dpTn = Dep()
            pTnm = sb("pTnm", [128, 256], BF16); dpTnm = Dep()
            cnt = [0]

            def rope_evac(pa, dpa, pb_, dpb_, t0, tn, out_ap, dOut, fin=None):
                i = cnt[0] % 2; cnt[0] += 1
                self.tt("dve", t1[i][:, 0:tn], pa[:, 0:tn], cosT[:, t0:t0 + tn], ALU.mult, [dpa, dRope], [dt1[i]])
                self.tt("dve", t2[i][:, 0:tn], pb_[:, 0:tn], sinT[:, t0:t0 + tn], ALU.mult, [dpb_, dRope], [dt2[i]])
                self.tt("pool", out_ap, t1[i][:, 0:tn], t2[i][:, 0:tn], ALU.add, [dt1[i], dt2[i]], [dOut])
                if fin is not None:
                    fo, a, n = fin
                    self.tt("pool", kfin[:, fo:fo + n], t1[i][:, a:a + n], t2[i][:, a:a + n], ALU.add, [dt1[i], dt2[i]], [dKf])

            lvl = 9
            att = 9
            for f_ in self.debug:
                if f_.startswith("att"):
                    att = int(f_[3:])
            for f_ in self.debug:
                if f_.startswith("swa"):
                    lvl = int(f_[3:])
            for g in range(4 if lvl >= 5 else 1):
                if lvl < 1:
                    break
                wsrc = lambda c0, n: w_in[:, c0:c0 + n].rearrange("(kc p) n -> p kc n", p=128)
                self.ldc(WQ[:], wsrc(AQ0 + g * 256, 256), [dW])
                self.ldc(WK2[:, :, 0:64], wsrc(AK0 + g * 64, 64), [dW])
                self.ldc(WK2[:, :, 64:128], wsrc(AK0 + g * 64, 64), [dW])
                self.ldc(WV[:], wsrc(AV0 + g * 64, 64), [dW])
                for src_t, dst_t, nh in ((WQ, WQs, 4), (WK2, WK2s, 2)):
                    sv = src_t[:].rearrange("p k (h two d) -> p k h two d", two=2, d=32)
                    dv = dst_t[:].rearrange("p k (h two d) -> p k h two d", two=2, d=32)
                    for k in range(8):
                        self.cp("pool", dv[:, k, :, 0, :], sv[:, k, :, 1, :], [dW], [dWs])
                        self.cp("pool", dv[:, k, :, 1, :], sv[:, k, :, 0, :], [dW], [dWs])
                for b, (t0, tn) in enumerate(TBLK):
                    for c in range(2):
                        pa, dpa = self.bank(); pb_, dpb_ = self.bank()
                        for k in range(8):
                            self.mm(pa[:, 0:tn], WQ[:, k, c * 128:(c + 1) * 128], hT[:, k, t0:t0 + tn], k == 0, k == 7, [dW, dHT[b]], [dpa])
                        for k in range(8):
                            self.mm(pb_[:, 0:tn], WQs[:, k, c * 128:(c + 1) * 128], hT[:, k, t0:t0 + tn], k == 0, k == 7, [dWs, dHT[b]], [dpb_])
                        rope_evac(pa, dpa, pb_, dpb_, t0, tn, qT[:, c, t0:t0 + tn], dQ[b])
                    pa, dpa = self.bank(); pb_, dpb_ = self.bank()
                    for k in range(8):
                        self.mm(pa[:, 0:tn], WK2[:, k, :], hT[:, k, t0:t0 + tn], k == 0, k == 7, [dW, dHT[b]], [dpa])
                    for k in range(8):
                        self.mm(pb_[:, 0:tn], WK2s[:, k, :], hT[:, k, t0:t0 + tn], k == 0, k == 7, [dWs, dHT[b]], [dpb_])
                    fin = (0, 384, 128) if b == 3 else ((128, 0, 64) if b == 4 else None)
                    rope_evac(pa, dpa, pb_, dpb_, t0, tn, kT2[:, t0:t0 + tn], dK[b], fin)
                for i in range(NTILE):
                    nt = tile_nt(i)
                    pv, dpv = self.bank()
                    for k in range(8):
                        self.mm(pv[0:nt, 0:64], hT[:, k, 128 * i:128 * i + nt], WV[:, k, :], k == 0, k == 7, [dW, dHT[min(i // 4, 4)]], [dpv])
                    if i < 16:
                        self.cp("act", vlh[0:nt, i, 0, 0:64], pv[0:nt, 0:64], [dpv], [dV[i]])
                        self.cp("dve", vlh[0:nt, i, 1, 64:128], pv[0:nt, 0:64], [dpv], [dV[i]])
                    else:
                        self.cp("act", vtm2[0:nt, i, 0:64], pv[0:nt, 0:64], [dpv], [dV[i]])
                        self.cp("dve", vtm2[0:nt, i, 64:128], pv[0:nt, 0:64], [dpv], [dV[i]])
                    if i >= 15:
                        self.cp("act", vfin[0:nt, i - 15, g * 64:(g + 1) * 64], pv[0:nt, 0:64], [dpv], [dVf])
                if lvl < 2:
                    continue
                pk, dpk = self.bank()
                self.tr(pk[:, 0:64], kfin[0:64, 0:128], [dKf], [dpk])
                self.tr(pk[0:64, 64:128], kfin[0:64, 128:192], [dKf], [dpk])
                self.cp("act", kout_p[:, g * 64:(g + 1) * 64], pk[:, 0:64], [dpk], [dKo])
                self.cp("act", knew_s[0:64, g * 64:(g + 1) * 64], pk[0:64, 64:128], [dpk], [dKs])
                if lvl < 3:
                    continue
                def attA(n):
                    q0 = 128 * n
                    kbs = ([n - 1] if n > 0 else []) + [n]
                    for ki, kb in enumerate(kbs):
                        psA, dpsA = self.bank(); psB, dpsB = self.bank()
                        for c in range(2):
                            self.mm(psA[:, c * 128:(c + 1) * 128], kT2[0:64, 128 * kb:128 * kb + 128], qT[0:64, c, q0:q0 + 128], True, True,
                                    [dK[kb // 4], dQ[n // 4]], [dpsA])
                            self.mm(psB[:, c * 128:(c + 1) * 128], kT2[64:128, 128 * kb:128 * kb + 128], qT[64:128, c, q0:q0 + 128], True, True,
                                    [dK[kb // 4], dQ[n // 4]], [dpsB])
                        sl = (2 * n + ki) % 4
                        self.act(pT[sl][:, 0:256], psA[:, 0:256], AF.Exp, [dpsA], [dpT[sl]], scale=0.125)
                        self.act(pT[sl][:, 256:512], psB[:, 0:256], AF.Exp, [dpsB], [dpT[sl]], scale=0.125)
                        msk = mle if kb == n else mgt
                        self.tt("dve", pTm[sl][:].rearrange("p (r q) -> p r q", r=4), pT[sl][:].rearrange("p (r q) -> p r q", r=4),
                                msk[:, :].unsqueeze(1).to_broadcast([128, 4, 128]), ALU.mult, [dpT[sl], dMb], [dpTm[sl]])

                esv_ = esink[:, 4 * g:4 * g + 4].rearrange("p (c h) -> p h c", h=2)
                self.cp("dve", esg[0:64, :], esv_[0:64, 0, :], [dEs], [dEsg])
                self.cp("dve", esg[64:128, :], esv_[64:128, 1, :], [dEs], [dEsg])

                def attB(n):
                    q0 = 128 * n
                    kbs = ([n - 1] if n > 0 else []) + [n]
                    pnum, dpnum = self.bank(); pden, dpden = self.bank()
                    nmm = 2 * len(kbs)
                    j = 0
                    for ki, kb in enumerate(kbs):
                        sl = (2 * n + ki) % 4
                        for hh in range(2):
                            self.mm(pnum[:, 0:256], vlh[:, kb, hh, :], pTm[sl][:, hh * 256:(hh + 1) * 256], j == 0, j == nmm - 1,
                                    [dV[kb], dpTm[sl]], [dpnum])
                            self.mm(pden[:, 0:256], ones_lh[:, hh, :], pTm[sl][:, hh * 256:(hh + 1) * 256], j == 0, j == nmm - 1,
                                    [dOlh, dpTm[sl]], [dpden])
                            j += 1
                    rc = recs[n % 2]; drc = dRecs[n % 2]
                    self.tt("dve", rc[:, 0:256].rearrange("p (c q) -> p c q", c=2), pden[:, 0:256].rearrange("p (c q) -> p c q", c=2),
                            esg[:, :].unsqueeze(2).to_broadcast([128, 2, 128]), ALU.add, [dpden, dEsg], [drc])
                    self.act(rc[:, 0:256], rc[:, 0:256], AF.Ln, [drc], [drc])
                    self.act(rc[:, 0:256], rc[:, 0:256], AF.Exp, [drc], [drc], scale=-1.0)
                    self.tt("dve", oaT[:, 2 * g:2 * g + 2, q0:q0 + 128], pnum[:, 0:256].rearrange("p (c q) -> p c q", c=2),
                            rc[:, 0:256].rearrange("p (c q) -> p c q", c=2), ALU.mult, [dpnum, drc], [dOA[n // 4]])

                attA(0)
                for n in range(16):
                    if n + 1 < 16:
                        attA(n + 1)
                    attB(n)
                if lvl < 4:
                    continue
                gp, go = g // 2, (g % 2) * 64
                self.cp("act", KcT2[0:64, :, :], KcTn[go:go + 64, gp, :, :], [dKn], [dKc2])
                self.cp("dve", KcT2[64:128, :, :], KcTn[go:go + 64, gp, :, :], [dKn], [dKc2])
                for b0 in range(0, 16, 4):
                    self.ldc(Vc2[:, b0:b0 + 4, 0:64], cvv[:, b0:b0 + 4, g * 64:(g + 1) * 64], [dVc2])
                    self.ldc(Vc2[:, b0:b0 + 4, 64:128], cvv[:, b0:b0 + 4, g * 64:(g + 1) * 64], [dVc2])
                pscA, dpscA = self.bank(); pscB, dpscB = self.bank()
                psnA, dpsnA = self.bank(); psnB, dpsnB = self.bank()
                for b in range(16):
                    for c in range(2):
                        c0 = b * 8 + c * 4
                        self.mm(pscA[:, c0:c0 + 4], KcT2[0:64, b, :], qT[0:64, c, SEQ + 4 * b:SEQ + 4 * b + 4], True, True, [dKc2, dQ[4]], [dpscA])
                        self.mm(pscB[:, c0:c0 + 4], KcT2[64:128, b, :], qT[64:128, c, SEQ + 4 * b:SEQ + 4 * b + 4], True, True, [dKc2, dQ[4]], [dpscB])
                for c in range(2):
                    for hh, (pp, dpp) in enumerate(((psnA, dpsnA), (psnB, dpsnB))):
                        off = hh * 64
                        self.mm(pp[0:64, 0:128].rearrange("p (b c t) -> p b c t", b=16, c=2)[:, :, c, :], kT2[off:off + 64, SEQ:NT],
                                qT[off:off + 64, c, SEQ:NT].rearrange("p (b t) -> p b t", t=4), True, True, [dK[4], dQ[4]], [dpp])
                self.act(pTn[:, 0:128], pscA[:, 0:128], AF.Exp, [dpscA], [dpTn], scale=0.125)
                self.act(pTn[:, 128:256], pscB[:, 0:128], AF.Exp, [dpscB], [dpTn], scale=0.125)
                self.tt("dve", pTnm[:, :].rearrange("p (a t) -> p a t", t=4), pTn[:, :].rearrange("p (a t) -> p a t", t=4),
                        mcache[:, :].unsqueeze(1).to_broadcast([128, 64, 4]), ALU.mult, [dpTn, dMb], [dpTnm])
                sl = cnt[0] % 2; cnt[0] += 1
                self.act(pT[sl][0:64, 0:128], psnA[0:64, 0:128], AF.Exp, [dpsnA], [dpT[sl]], scale=0.125)
                self.act(pT[sl][0:64, 128:256], psnB[0:64, 0:128], AF.Exp, [dpsnB], [dpT[sl]], scale=0.125)
                for hh in range(2):
                    self.tt("dve", pTm[sl][0:64, hh * 128:(hh + 1) * 128].rearrange("p (b c t) -> p b c t", b=16, c=2),
                            pT[sl][0:64, hh * 128:(hh + 1) * 128].rearrange("p (b c t) -> p b c t", b=16, c=2),
                            mblkc[0:64, 0:64].rearrange("p (b t) -> p b t", t=4).unsqueeze(2).to_broadcast([64, 16, 2, 4]), ALU.mult,
                            [dpT[sl], dMb], [dpTm[sl]])
                pnum, dpnum = self.bank(); pden, dpden = self.bank()
                self.mm(pnum[:, 0:256], vtm2[0:64, 16, :], pTm[sl][0:64, 0:256], True, False, [dV[16], dpTm[sl]], [dpnum])
                self.mm(pden[:, 0:256], ones[0:64, :], pTm[sl][0:64, 0:256], True, False, [dOnes, dpTm[sl]], [dpden])
                for b in range(16):
                    pv_ = pTnm[:, :].rearrange("p (h b x) -> p h b x", h=2, b=16)[:, :, b, :]
                    self.mm(pnum[:, 0:256].rearrange("p (h b x) -> p h b x", h=2, b=16)[:, :, b, :], Vc2[:, b, :], pv_, False, True,
                            [dVc2, dpTnm], [dpnum])
                    self.mm(pden[:, 0:256].rearrange("p (h b x) -> p h b x", h=2, b=16)[:, :, b, :], ones[:, :], pv_, False, True,
                            [dOnes, dpTnm], [dpden])
                esv = esink[:, 4 * g:4 * g + 4].rearrange("p (c h) -> p h c", h=2)
                for hh in range(2):
                    self.tt("dve", rec[:, hh * 128:(hh + 1) * 128].rearrange("p (b c t) -> p b c t", b=16, c=2),
                            pden[:, hh * 128:(hh + 1) * 128].rearrange("p (b c t) -> p b c t", b=16, c=2),
                            esv[:, hh, :].unsqueeze(1).unsqueeze(3).to_broadcast([128, 16, 2, 4]), ALU.add, [dpden, dEs], [dRec])
                self.recip(rec[:, 0:256], rec[:, 0:256], [dRec], [dRec])
                for hh in range(2):
                    off = hh * 64
                    for c in range(2):
                        nv = pnum[off:off + 64, hh * 128:(hh + 1) * 128].rearrange("p (b c t) -> p b c t", b=16, c=2)[:, :, c, :]
                        rv = rec[off:off + 64, hh * 128:(hh + 1) * 128].rearrange("p (b c t) -> p b c t", b=16, c=2)[:, :, c, :]
                        self.tt("dve", oaT[off:off + 64, 2 * g + c, SEQ:NT].rearrange("p (b t) -> p b t", t=4), nv, rv, ALU.mult,
                                [dpnum, dRec], [dOA[4]])
            if lvl < 6:
                S.emit()
                return
            self.store(L["kp"][:, :], kout_p[:], [dKo])
            self.store(L["vp"][:, :], vfin[:, 0, :], [dVf])
            for b in range(16):
                self.store(L["ks"][b, 0:124, :], L["ck"][b, 4:128, :], [])
                self.store(L["vs"][b, 0:124, :], L["cv"][b, 4:128, :], [])
                self.store(L["ks"][b, 124:128, :], knew_s[4 * b:4 * b + 4, :], [dKs])
                self.store(L["vs"][b, 124:128, :], vfin[4 * b:4 * b + 4, 1, :], [dVf])
            S.emit()

    def mlstm(self, L, hT, dHT, hgT, dHG):
        nc, S = self.nc, self.S
        w_in = L["w_in"]
        MQ0, MK0, MV0, MO0, MG0 = 0, 512, 1024, 2048, 3072
        KS = 128.0 ** -0.5
        wsrc = lambda c0, n: w_in[:, c0:c0 + n].rearrange("(kc p) n -> p kc n", p=128)
        with ExitStack() as es:
            sb = lambda name, shape, dt=F32: es.enter_context(nc.sbuf_tensor(self.uname(name), list(shape), dt))
            A_tm = sb("A_tm", [128, NTILE, 4]); B_tm = sb("B_tm", [128, NTILE, 4]); MendTok = sb("MendTok", [128, NTILE, 4])
            W_tm = sb("W_tm", [128, NTILE, 4]); THR = sb("THR", [128, NTILE, 4]); dGt = Dep()
            ECrep = sb("ECrep", [128, 4, 32]); MendRep = sb("MendRep", [128, 4, 16]); dRep = Dep()
            EcTok = sb("EcTok", [128, 4]); dEt = Dep()
            mk = sb("mk", [128, 768]); dMk = Dep()
            self.ld(mk[:], L["cmask"][:, :], [dMk])
            mle_f = mk[:, 0:128]; mblkc_f = mk[:, 256:384]; blk_f = mk[:, 384:512]; rmask_f = mk[:, 640:656]
            n0tok = sb("n0tok", [128, 512]); dN0 = Dep()
            nnew = sb("nnew", [128, 512]); dNn = Dep()
            for b in range(16):
                self.ld(n0tok[4 * b:4 * b + 4, :], L["stn"][b:b + 1, :].broadcast_to([4, 512]), [dN0])
            for t_ in (A_tm, B_tm, MendTok):
                self.memset("pool", t_[:], 0.0, [dGt])
            with ExitStack() as es2:
                sb2 = lambda name, shape, dt=F32: es2.enter_context(nc.sbuf_tensor(self.uname(name), list(shape), dt))
                Wg = sb2("Wg", [128, 8, 8], BF16); dWg = Dep()
                self.ldc(Wg[:], wsrc(MG0, 8), [dWg])
                bigf = sb2("bigf", [128, 8]); dBg = Dep()
                self.ld(bigf[:, 0:4], L["b_ig"][0:1, :].broadcast_to([128, 4]), [dBg])
                self.ld(bigf[:, 4:8], L["b_fg"][0:1, :].broadcast_to([128, 4]), [dBg])
                g_tm = sb2("g_tm", [128, NTILE, 8]); dGtm = Dep()
                GI = sb2("GI", [4, NT]); GF = sb2("GF", [4, NT]); Bf = sb2("Bf", [4, NT]); Mf = sb2("Mf", [4, NT])
                Z = sb2("Z", [4, NT]); T1 = sb2("T1", [4, NT]); T2 = sb2("T2", [4, NT]); dF = Dep()
                ones4 = sb2("ones4", [4, 128]); dO4 = Dep()
                m0T = sb2("m0T", [4, 16]); dM0 = Dep()
                V48 = sb2("V48", [4, 48]); X = sb2("X", [4, 4, 48]); dV = Dep()
                Sx = sb2("Sx", [4, 2, 16, 4]); dSx = Dep()
                mout = sb2("mout", [4, 17]); dMo = Dep()
                self.memset("pool", Z[:], 0.0, [dF]); self.memset("pool", ones4[:], 1.0, [dO4])
                self.memset("pool", g_tm[:], 0.0, [dGtm])
                self.ld_nc(m0T[:], L["stm"].rearrange("b h -> h b"), [dM0])
                for i in range(NTILE):
                    nt = tile_nt(i)
                    pg, dpg = self.bank()
                    for k in range(8):
                        self.mm(pg[0:nt, 0:8], hT[:, k, 128 * i:128 * i + nt], Wg[:, k, :], k == 0, k == 7, [dWg, dHT[min(i // 4, 4)]], [dpg])
                    self.tt("dve", g_tm[0:nt, i, :], pg[0:nt, 0:8], bigf[0:nt, :], ALU.add, [dpg, dBg], [dGtm])
                for b, (t0, tn) in enumerate(TBLK):
                    p1, dp1 = self.bank(); p2, dp2 = self.bank()
                    for j, i in enumerate(blk_tiles(b)):
                        nt = tile_nt(i)
                        self.tr(p1[0:4, j * 128:j * 128 + nt], g_tm[0:nt, i, 0:4], [dGtm], [dp1])
                        self.tr(p2[0:4, j * 128:j * 128 + nt], g_tm[0:nt, i, 4:8], [dGtm], [dp2])
                    self.cp("dve", GI[:, t0:t0 + tn], p1[0:4, 0:tn], [dp1], [dF])
                    self.cp("dve", GF[:, t0:t0 + tn], p2[0:4, 0:tn], [dp2], [dF])
                self.act(T1[:], GF[:], AF.Abs, [dF], [dF])
                self.act(T1[:], T1[:], AF.Exp, [dF], [dF], scale=-1.0)
                self.act(T1[:], T1[:], AF.Ln, [dF], [dF], bias=1.0)
                self.ts("dve", T2[:], GF[:], 0.0, None, ALU.min, None, [dF], [dF])
                self.tt("dve", GF[:], T2[:], T1[:], ALU.subtract, [dF], [dF])
                self.S.op("dve", lambda e: e.tensor_tensor_scan(out=Bf[:, 0:SEQ], data0=GF[:, 0:SEQ], data1=Z[:, 0:SEQ], initial=0.0,
                                                                op0=ALU.add, op1=ALU.add), [dF], [dF])
                gfs = GF[:, SEQ:NT].rearrange("h (b j) -> h b j", j=4); bfs = Bf[:, SEQ:NT].rearrange("h (b j) -> h b j", j=4)
                self.cp("dve", bfs[:, :, 0], gfs[:, :, 0], [dF], [dF])
                for j in range(1, 4):
                    self.tt("dve", bfs[:, :, j], bfs[:, :, j - 1], gfs[:, :, j], ALU.add, [dF], [dF])
                self.tt("dve", GI[:], GI[:], Bf[:], ALU.subtract, [dF], [dF])
                self.S.op("dve", lambda e: e.tensor_tensor_scan(out=Mf[:, 0:SEQ], data0=Z[:, 0:SEQ], data1=GI[:, 0:SEQ], initial=0.0,
                                                                op0=ALU.add, op1=ALU.max), [dF], [dF])
                as_ = GI[:, SEQ:NT].rearrange("h (b j) -> h b j", j=4); ms_ = Mf[:, SEQ:NT].rearrange("h (b j) -> h b j", j=4)
                self.tt("dve", ms_[:, :, 0], as_[:, :, 0], m0T[:, :], ALU.max, [dF, dM0], [dF])
                for j in range(1, 4):
                    self.tt("dve", ms_[:, :, j], ms_[:, :, j - 1], as_[:, :, j], ALU.max, [dF], [dF])
                mend_p = Mf[:, 0:SEQ].rearrange("h (c t) -> h c t", t=128)[:, :, 127]
                self.cp("dve", V48[:, 0:16], mend_p, [dF], [dV])
                self.cp("dve", V48[:, 16:17], Z[:, 0:1], [dF], [dV])
                self.cp("dve", V48[:, 17:32], V48[:, 0:15], [dV], [dV])
                self.tt("dve", V48[:, 16:32], V48[:, 16:32], V48[:, 0:16], ALU.subtract, [dV], [dV])
                self.tt("dve", V48[:, 32:48], m0T[:, :], ms_[:, :, 3], ALU.subtract, [dF, dM0], [dV])
                self.act(V48[:, 16:48], V48[:, 16:48], AF.Exp, [dV], [dV])
                self.tt("dve", X[:], V48[:, :].unsqueeze(1).to_broadcast([4, 4, 48]),
                        self.ident[0:4, 0:4].unsqueeze(2).to_broadcast([4, 4, 48]), ALU.mult, [dV, self.dIdent], [dV])
                pr, dpr = self.bank()
                self.mm(pr[:, 0:192], ones4[:, :], X[:].rearrange("h a c -> h (a c)"), True, True, [dO4, dV], [dpr])
                prv = pr[:, 0:192].rearrange("p (a c) -> p a c", a=4)
                self.cp("dve", MendRep[:], prv[:, :, 0:16], [dpr], [dRep])
                self.cp("dve", ECrep[:], prv[:, :, 16:48], [dpr], [dRep])
                self.cp("dve", Sx[:, 0, :, :], ms_[:, :, 3].unsqueeze(2).to_broadcast([4, 16, 4]), [dF], [dSx])
                self.cp("dve", Sx[:, 1, :, :], V48[:, 32:48].unsqueeze(2).to_broadcast([4, 16, 4]), [dV], [dSx])
                pa_, dpa_ = self.bank(); pb_, dpb_ = self.bank()
                for i in range(NTILE):
                    nt = tile_nt(i)
                    self.tr(pa_[0:nt, i * 4:i * 4 + 4], GI[0:4, 128 * i:128 * i + nt], [dF], [dpa_])
                    self.tr(pb_[0:nt, i * 4:i * 4 + 4], Bf[0:4, 128 * i:128 * i + nt], [dF], [dpb_])
                self.tr(pa_[0:64, 68:72], Sx[0:4, 0, :, :].rearrange("h b j -> h (b j)"), [dSx], [dpa_])
                self.tr(pb_[0:64, 68:72], Sx[0:4, 1, :, :].rearrange("h b j -> h (b j)"), [dSx], [dpb_])
                self.cp("dve", A_tm[:, 0:16, :].rearrange("p c h -> p (c h)"), pa_[:, 0:64], [dpa_], [dGt])
                self.cp("dve", A_tm[0:64, 16, :], pa_[0:64, 64:68], [dpa_], [dGt])
                self.cp("dve", B_tm[:, 0:16, :].rearrange("p c h -> p (c h)"), pb_[:, 0:64], [dpb_], [dGt])
                self.cp("dve", B_tm[0:64, 16, :], pb_[0:64, 64:68], [dpb_], [dGt])
                self.cp("dve", MendTok[0:64, 16, :], pa_[0:64, 68:72], [dpa_], [dGt])
                self.cp("dve", EcTok[0:64, :], pb_[0:64, 68:72], [dpb_], [dEt])
                self.cp("dve", MendTok[:, 0:16, :], MendRep[:].rearrange("p h c -> p c h"), [dRep], [dGt])
                self.tt("dve", W_tm[:], A_tm[:], MendTok[:], ALU.subtract, [dGt], [dGt])
                self.act(W_tm[:], W_tm[:], AF.Exp, [dGt], [dGt])
                self.tt("dve", THR[:], B_tm[:], MendTok[:], ALU.add, [dGt], [dGt])
                self.act(THR[:], THR[:], AF.Exp, [dGt], [dGt], scale=-1.0)
                self.tt("dve", mout[:, 0:1], Mf[:, SEQ - 1:SEQ], Bf[:, SEQ - 1:SEQ], ALU.add, [dF], [dMo])
                self.tt("dve", mout[:, 1:17], ms_[:, :, 3], bfs[:, :, 3], ALU.add, [dF], [dMo])
                self.store(L["mp"][:, :], mout[:, 0:1], [dMo])
                self.store_nc(L["ms"].rearrange("b h -> h b"), mout[:, 1:17], [dMo])
                S.emit()
            Wm = sb("Wm", [128, 8, 768], BF16); dWm = Dep()
            qTh = sb("qTh", [128, NT], BF16); kTh = sb("kTh", [128, NT], BF16); dQK = [Dep() for _ in range(5)]
            k_tm = sb("k_tm", [128, NTILE, 128], BF16); v_aug = sb("v_aug", [128, NTILE, 258], BF16); so_tm = sb("so_tm", [128, NTILE, 256], BF16)
            dTM = [Dep() for _ in range(NTILE)]
            ST = sb("ST", [128, 257]); dST = Dep()
            tmpSs = [sb(f"tmpS{i}", [128, 128]) for i in range(2)]; dtSs = [Dep(), Dep()]
            sTms = [sb(f"sTm{i}", [128, 128], BF16) for i in range(3)]; dsTs = [Dep() for _ in range(3)]
            wvs = [sb(f"wv{i}", [128, 258], BF16) for i in range(3)]; dwvs = [Dep() for _ in range(3)]
            tmpS = tmpSs[0]; dtS = dtSs[0]
            sTm = sb("sTm_s", [128, 64], BF16); dsT = Dep()
            Cdec = sb("Cdec", [128, 258], BF16); dCd = Dep()
            junk = sb("junk", [128, 256], BF16); dJ = Dep()
            wv = sb("wv_s", [128, 258], BF16); dwv = Dep()
            bmk = sb("bmk", [128, 16, 64], BF16); dBm = Dep()
            self.ldc(bmk[:].rearrange("p b t -> p (b t)"), L["bmask"][:, :], [dBm])
            C0n = sb("C0n", [128, 8, 2, 128]); dC0 = Dep()
            C0T = sb("C0T", [128, 16, 258], BF16); dC0T = Dep()
            n0T = sb("n0T", [128, 16]); dn0T = Dep()
            maskE = sb("maskE", [128, 16, 64], BF16); dmE = Dep()
            Qblk = sb("Qblk", [128, 16, 64], BF16); dQb = Dep()
            wvblk = sb("wvblk", [128, 8, 256], BF16); dwb = Dep()
            Cout = [sb(f"Cout{i}", [128, 2, 128]) for i in range(2)]; dCo = [Dep(), Dep()]
            BselW = sb("BselW", [128, 64], BF16); dBs = Dep()
            self.memset("pool", v_aug[:], 1.0, dTM)

            Hn = sb("Hn", [128, NTILE, 257]); dHn = [Dep() for _ in range(NTILE)]
            if os.environ.get("MK_PAD"):
                sb("pad", [128, int(os.environ["MK_PAD"]) * 256])
            hst = sb("hst", [128, 8, NTILE]); dhst = Dep(); dhst2 = Dep()

            def hn_pre(h):
                dn = hst[:, 0, :]
                self.memset("pool", hst[:], 0.0, [dhst])
                self.act(dn, Hn[:, :, 256], AF.Abs, dHn, [dhst])
                self.tt("dve", dn, dn, THR[:, :, h], ALU.max, [dGt], [dhst])
                self.recip(dn, dn, [dhst], [dhst2])

            def hn_stats(h, i):
                nt = tile_nt(i)
                self.act(Hn[0:nt, i, 0:256], Hn[0:nt, i, 0:256], AF.Identity, [dhst2], [dHn[i], dhst], scale=hst[0:nt, 0, i:i + 1],
                         accum=hst[0:nt, 1, i:i + 1])
                self.act(junk[0:nt, :], Hn[0:nt, i, 0:256], AF.Square, [dHn[i]], [dJ, dhst], accum=hst[0:nt, 2, i:i + 1])

            def hn_post(h):
                dn = hst[:, 0, :]; s1 = hst[:, 1, :]; s2 = hst[:, 2, :]; mean = hst[:, 3, :]; msq = hst[:, 4, :]
                rstd = hst[:, 5, :]; nb = hst[:, 6, :]
                self.ts("dve", mean, s1, 1.0 / 256, None, ALU.mult, None, [dhst], [dhst])
                self.tt("dve", msq, mean, mean, ALU.mult, [dhst], [dhst])
                self.ts("dve", rstd, s2, 1.0 / 256, HN_EPS, ALU.mult, ALU.add, [dhst], [dhst])
                self.tt("dve", rstd, rstd, msq, ALU.subtract, [dhst], [dhst])
                self.act(rstd, rstd, AF.Sqrt, [dhst], [dhst])
                self.recip(rstd, rstd, [dhst], [dhst])
                self.tt("dve", nb, mean, rstd, ALU.mult, [dhst], [dhst])
                self.ts("dve", nb, nb, -1.0, None, ALU.mult, None, [dhst], [dhst])
                pts = {}

                def n1(i):
                    nt = tile_nt(i)
                    self.act(Hn[0:nt, i, 0:256], Hn[0:nt, i, 0:256], AF.Identity, [dhst], [dHn[i]], bias=hst[0:nt, 6, i:i + 1],
                             scale=hst[0:nt, 5, i:i + 1])
                    self.tt("dve", Hn[0:nt, i, 0:256], Hn[0:nt, i, 0:256], so_tms[h % 2][0:nt, i, :], ALU.mult, [dSO[h % 2][i]], [dHn[i]])

                def n2(i):
                    nt = tile_nt(i)
                    pt, dpt = self.bank()
                    for vb in range(2):
                        self.tr(pt[:, vb * 128:vb * 128 + nt], Hn[0:nt, i, vb * 128:(vb + 1) * 128], [dHn[i]], [dpt])
                    pts[i] = (pt, dpt)

                def n3(i):
                    nt = tile_nt(i)
                    t0 = 128 * i
                    pt, dpt = pts.pop(i)
                    self.cp("act" if i % 2 == 0 else "dve", hgT[:, 2 * h:2 * h + 2, t0:t0 + nt],
                            pt[:, 0:256].rearrange("p (a b) -> p a b", a=2)[:, :, 0:nt], [dpt], [dHG[min(i // 4, 4)]])

                for step in range(NTILE + 4):
                    if step < NTILE:
                        n1(step)
                    if 0 <= step - 2 < NTILE:
                        n2(step - 2)
                    if 0 <= step - 4 < NTILE:
                        n3(step - 4)

            def head_tail(po, dpo, i, h, nt):
                self.cp("act", Hn[0:nt, i, :], po[0:nt, 0:257], [dpo], [dHn[i]])

            Wm2 = sb("Wm2", [128, 8, 768], BF16); dWm2 = Dep()
            so_tm2 = sb("so_tm2", [128, NTILE, 256], BF16)
            Wms = [Wm, Wm2]; dWms = [dWm, dWm2]; so_tms = [so_tm, so_tm2]
            dSO = [[Dep() for _ in range(NTILE)] for _ in range(2)]

            def load_w(h):
                W_ = Wms[h % 2]; d_ = dWms[h % 2]
                self.ldc(W_[:, :, 0:128], wsrc(MQ0 + h * 128, 128), [d_])
                self.ldc(W_[:, :, 128:256], wsrc(MK0 + h * 128, 128), [d_])
                self.ldc(W_[:, :, 256:512], wsrc(MV0 + h * 256, 256), [d_])
                self.ldc(W_[:, :, 512:768], wsrc(MO0 + h * 256, 256), [d_])

            def c0n_load(h, half):
                for bq in range(8):
                    self.ld(C0n[:, bq, :, :], L["stC"][8 * half + bq, h].rearrange("(vb p) d -> p vb d", p=128), [dC0])

            def proj(h, per_tile=None):
                    for b, (t0, tn) in enumerate(TBLK):
                        pq, dpq = self.bank(); pk, dpk = self.bank()
                        for k in range(8):
                            self.mm(pq[:, 0:tn], Wms[h % 2][:, k, 0:128], hT[:, k, t0:t0 + tn], k == 0, k == 7, [dWms[h % 2], dHT[b]], [dpq])
                        for k in range(8):
                            self.mm(pk[:, 0:tn], Wms[h % 2][:, k, 128:256], hT[:, k, t0:t0 + tn], k == 0, k == 7, [dWms[h % 2], dHT[b]], [dpk])
                        self.cp("act", qTh[:, t0:t0 + tn], pq[:, 0:tn], [dpq], [dQK[b]])
                        self.S.op("act", lambda e, t0=t0, tn=tn, pk=pk: e.mul(out=kTh[:, t0:t0 + tn], in_=pk[:, 0:tn], mul=KS), [dpk], [dQK[b]])
                    for i in range(NTILE):
                        nt = tile_nt(i)
                        pa, dpa = self.bank(); pb_, dpb_ = self.bank()
                        for k in range(8):
                            self.mm(pa[0:nt, 0:384], hT[:, k, 128 * i:128 * i + nt], Wms[h % 2][:, k, 128:512], k == 0, k == 7, [dWms[h % 2], dHT[min(i // 4, 4)]], [dpa])
                        for k in range(8):
                            self.mm(pb_[0:nt, 0:256], hT[:, k, 128 * i:128 * i + nt], Wms[h % 2][:, k, 512:768], k == 0, k == 7, [dWms[h % 2], dHT[min(i // 4, 4)]], [dpb_])
                        self.S.op("act", lambda e, i=i, nt=nt, pa=pa: e.mul(out=k_tm[0:nt, i, :], in_=pa[0:nt, 0:128], mul=KS), [dpa], [dTM[i]])
                        self.cp("dve", v_aug[0:nt, i, 0:256], pa[0:nt, 128:384], [dpa], [dTM[i]])
                        self.act(so_tms[h % 2][0:nt, i, :], pb_[0:nt, 0:256], AF.Sigmoid, [dpb_], [dSO[h % 2][i]])
                        if per_tile is not None:
                            per_tile(i)

            def body_(h):
                    i = 16
                    ps_, dps = self.bank()
                    self.mm(ps_[0:64, 0:64], kTh[:, SEQ:NT], qTh[:, SEQ:NT], True, True, [dQK[4]], [dps])
                    self.act(tmpS[0:64, 0:64], ps_[0:64, 0:64], AF.Identity, [dps, dGt], [dtS], scale=W_tm[0:64, 16, h:h + 1])
                    self.tt("dve", sTm[0:64, 0:64], tmpS[0:64, 0:64], mblkc_f[0:64, 0:64], ALU.mult, [dtS, dMk], [dsT])
                    self.tt("dve", maskE[:], bmk[:], ECrep[:, h, 16:32].unsqueeze(2).to_broadcast([128, 16, 64]), ALU.mult, [dBm, dRep], [dmE])
                    self.tt("dve", Qblk[:], maskE[:], qTh[:, SEQ:NT].unsqueeze(1).to_broadcast([128, 16, 64]), ALU.mult, [dmE, dQK[4]], [dQb])
                    pn0, dpn0 = self.bank()
                    self.tr(pn0[:, 0:64], n0tok[0:64, h * 128:(h + 1) * 128], [dN0], [dpn0])
                    self.cp("dve", n0T[:], pn0[:, 0:64].rearrange("p (b j) -> p b j", j=4)[:, :, 0], [dpn0], [dn0T])
                    self.act(wv[0:64, 0:257], v_aug[0:64, 16, 0:257], AF.Identity, [dTM[16], dGt], [dwv], scale=W_tm[0:64, 16, h:h + 1])
                    def halfproc(half):
                            b0 = 8 * half
                            for bb in range(8):
                                if bb % 2 == 0:
                                    pt, dpt = self.bank()
                                for vb in range(2):
                                    c0 = (bb % 2) * 256 + vb * 128
                                    self.tr(pt[:, c0:c0 + 128], C0n[:, bb, vb, :], [dC0], [dpt])
                                if bb % 2 == 1:
                                    self.cp("act", C0T[:, b0 + bb - 1:b0 + bb + 1, 0:256], pt[:, :].rearrange("p (a b) -> p a b", a=2), [dpt], [dC0T])
                            self.tt("dve", wvblk[0:64, :, :], wv[0:64, 0:256].unsqueeze(1).to_broadcast([64, 8, 256]),
                                    rmask_f[0:64, b0:b0 + 8].unsqueeze(2).to_broadcast([64, 8, 256]), ALU.mult, [dwv, dMk], [dwb])
                            for bb in range(8):
                                b = b0 + bb
                                pC, dpC = self.bank()
                                for vb in range(2):
                                    self.mm(pC[:, vb * 128:(vb + 1) * 128], wvblk[0:64, bb, vb * 128:(vb + 1) * 128], k_tm[0:64, 16, :], True, True,
                                            [dwb, dTM[16]], [dpC])
                                co = b % 2
                                self.stt("dve", Cout[co][:].rearrange("p a b -> p (a b)"), C0n[:, bb, :, :].rearrange("p a b -> p (a b)"),
                                         ECrep[:, h, 16 + b:17 + b], pC[:, 0:256], ALU.mult, ALU.add, [dC0, dRep, dpC], [dCo[co]])
                                self.store(L["Cs"][b, h].rearrange("(vb p) d -> p vb d", p=128), Cout[co][:], [dCo[co]])
                    halfproc(0)
                    c0n_load(h, 1)
                    pcs = {}

                    def pre(c):
                        t0 = 128 * c
                        ps_, dps = self.bank()
                        self.mm(ps_[:, 0:128], kTh[:, t0:t0 + 128], qTh[:, t0:t0 + 128], True, True, [dQK[c // 4]], [dps])
                        a = c % 2; b3 = c % 3
                        self.act(tmpSs[a][:], ps_[:, 0:128], AF.Identity, [dps, dGt], [dtSs[a]], scale=W_tm[:, c, h:h + 1])
                        self.tt("dve", sTms[b3][:], tmpSs[a][:], mle_f, ALU.mult, [dtSs[a], dMk], [dsTs[b3]])
                        self.act(wvs[b3][:, 0:257], v_aug[:, c, 0:257], AF.Identity, [dTM[c], dGt], [dwvs[b3]], scale=W_tm[:, c, h:h + 1])
                        pc, dpc = self.bank()
                        self.mm(pc[:, 0:257], k_tm[:, c, :], wvs[b3][:, 0:257], True, True, [dTM[c], dwvs[b3]], [dpc])
                        pcs[c] = (pc, dpc)

                    def main(c):
                        t0 = 128 * c
                        b3 = c % 3
                        po, dpo = self.bank()
                        self.mm(po[:, 0:257], sTms[b3][:], v_aug[:, c, 0:257], True, c == 0, [dsTs[b3], dTM[c]], [dpo])
                        if c > 0:
                            self.act(Cdec[:, 0:257], ST[:], AF.Identity, [dST, dRep], [dCd], scale=ECrep[:, h, c:c + 1])
                            self.mm(po[:, 0:257], qTh[:, t0:t0 + 128], Cdec[:, 0:257], False, True, [dQK[c // 4], dCd], [dpo])
                        head_tail(po, dpo, c, h, 128)
                        pc, dpc = pcs.pop(c)
                        if c == 0:
                            self.cp("dve", ST[:], pc[:, 0:257], [dpc], [dST])
                        else:
                            self.stt("dve", ST[:], ST[:], ECrep[:, h, c:c + 1], pc[:, 0:257], ALU.mult, ALU.add, [dpc, dRep], [dST])

                    pre(0)
                    for c in range(16):
                        if c + 1 < 16:
                            pre(c + 1)
                        main(c)
                    pt, dpt = self.bank()
                    for vb in range(2):
                        self.tr(pt[:, vb * 128:(vb + 1) * 128], ST[:, vb * 128:(vb + 1) * 128], [dST], [dpt])
                    self.cp("dve", Cout[0][:].rearrange("p a b -> p (a b)"), pt[:, 0:256], [dpt], [dCo[0]])
                    self.store(L["Cp"][h].rearrange("(vb p) d -> p vb d", p=128), Cout[0][:], [dCo[0]])
                    self.store_nc(L["np"][h:h + 1, :].rearrange("o d -> d o"), ST[:, 256:257], [dST])
                    halfproc(1)
                    self.cp("dve", C0T[:, :, 256], n0T[:, :], [dn0T], [dC0T])
                    po, dpo = self.bank()
                    self.mm(po[0:64, 0:257], sTm[0:64, 0:64], v_aug[0:64, 16, 0:257], True, False, [dsT, dTM[16]], [dpo])
                    for b in range(16):
                        self.mm(po[0:64, 0:257], Qblk[:, b, :], C0T[:, b, 0:257], False, b == 15, [dQb, dC0T], [dpo])
                    head_tail(po, dpo, 16, h, 64)
                    self.act(BselW[0:64, :], blk_f[0:64, 0:64], AF.Identity, [dMk, dGt], [dBs], scale=W_tm[0:64, 16, h:h + 1])
                    pN, dpN = self.bank()
                    self.mm(pN[0:64, 0:128], BselW[0:64, :], k_tm[0:64, 16, :], True, True, [dBs, dTM[16]], [dpN])
                    self.stt("dve", nnew[0:64, h * 128:(h + 1) * 128], n0tok[0:64, h * 128:(h + 1) * 128], EcTok[0:64, h:h + 1], pN[0:64, 0:128],
                             ALU.mult, ALU.add, [dN0, dEt, dpN], [dNn])

            load_w(0); c0n_load(0, 0); proj(0); load_w(1)
            for h in range(4):
                body_(h)
                hn_pre(h)
                if h + 1 < 4:
                    c0n_load(h + 1, 0)
                    proj(h + 1, per_tile=lambda i, h=h: hn_stats(h, i))
                    if h + 2 < 4:
                        load_w(h + 2)
                else:
                    for i in range(NTILE):
                        hn_stats(h, i)
                hn_post(h)
            for b in range(16):
                self.store(L["ns"][b:b + 1, :], nnew[4 * b:4 * b + 1, :], [dNn])
            S.emit()

    def ln_tile(self, x, dX, nt, lnG, lnB, dLn, st, dSt, junk, dJ, eps):
        self.memset("pool", st[0:nt, 0:2], 0.0, [dSt])
        self.act(junk[0:nt, :], x, AF.Identity, [dX], [dJ, dSt], accum=st[0:nt, 0:1])
        self.act(junk[0:nt, :], x, AF.Square, [dX], [dJ, dSt], accum=st[0:nt, 1:2])
        self.ts("dve", st[0:nt, 2:3], st[0:nt, 0:1], 1.0 / D, None, ALU.mult, None, [dSt], [dSt])
        self.tt("dve", st[0:nt, 3:4], st[0:nt, 2:3], st[0:nt, 2:3], ALU.mult, [dSt], [dSt])
        self.ts("dve", st[0:nt, 4:5], st[0:nt, 1:2], 1.0 / D, eps, ALU.mult, ALU.add, [dSt], [dSt])
        self.tt("dve", st[0:nt, 4:5], st[0:nt, 4:5], st[0:nt, 3:4], ALU.subtract, [dSt], [dSt])
        self.act(st[0:nt, 4:5], st[0:nt, 4:5], AF.Sqrt, [dSt], [dSt])
        self.recip(st[0:nt, 5:6], st[0:nt, 4:5], [dSt], [dSt])
        self.tt("dve", st[0:nt, 6:7], st[0:nt, 2:3], st[0:nt, 5:6], ALU.mult, [dSt], [dSt])
        self.ts("dve", st[0:nt, 6:7], st[0:nt, 6:7], -1.0, None, ALU.mult, None, [dSt], [dSt])
        self.act(x, x, AF.Identity, [dSt], [dX], bias=st[0:nt, 6:7], scale=st[0:nt, 5:6])
        self.tt("dve", x, x, lnG[0:nt, :], ALU.mult, [dLn], [dX])
        self.tt("dve", x, x, lnB[0:nt, :], ALU.add, [dLn], [dX])

    def merge(self, L, hT, dHT, hgT, dHG):
        nc, S = self.nc, self.S
        w_in = L["w_in"]
        GM0, GA0 = 4616, 5640
        wsrc = lambda w, c0, n: w[:, c0:c0 + n].rearrange("(kc p) n -> p kc n", p=128)
        with ExitStack() as es:
            sb = lambda name, shape, dt=F32: es.enter_context(nc.sbuf_tensor(self.uname(name), list(shape), dt))
            oaT = sb("oaT2", [128, 8, NT], BF16); dOA = Dep()
            self.ld(oaT[:].rearrange("p c t -> p (c t)"), L["oa_scr"][:, :], [dOA])
            Wa = sb("Wa", [128, 8, D], BF16); dWa = Dep()
            Wg = sb("Wg", [128, 8, D], BF16); dWg = Dep()
            mraw = sb("mraw", [8, 128]); dMr = Dep(); mngT = sb("mngT", [128, 8]); dMn = Dep()
            zt = sb("zt", [128, 8, 512], BF16); dzt = Dep()
            sg = [sb(f"sg{i}", [128, 512]) for i in range(2)]; dsg = [Dep(), Dep()]
            za = [sb(f"za{i}", [128, 512]) for i in range(2)]; dza = [Dep(), Dep()]
            xt = sb("xt", [128, 3, D]); dxt = [Dep(), Dep(), Dep()]
            tG = [sb(f"tG{i}", [128, 512]) for i in range(2)]; dtG = [Dep(), Dep()]
            lnG = sb("lnG", [128, D]); lnB = sb("lnB", [128, D]); dLn = Dep()
            sts = [sb(f"st{i}", [128, 8]) for i in range(2)]; dSts = [Dep(), Dep()]
            junk = sb("junk", [128, D], BF16); dJ = Dep()
            self.ld(lnG[:], L["ln"]["ln2_g"][0:1, :].broadcast_to([128, D]), [dLn])
            self.ld(lnB[:], L["ln"]["ln2_b"][0:1, :].broadcast_to([128, D]), [dLn])
            self.ld(mraw[:], L["mng"][:, :], [dMr])
            pm, dpm = self.bank()
            self.tr(pm[:, 0:8], mraw[0:8, :], [dMr], [dpm])
            self.cp("dve", mngT[:], pm[:, 0:8], [dpm], [dMn])
            self.ldc(Wa[:], wsrc(L["w_bm"], 0, D), [dWa])
            for k in range(8):
                self.act(Wa[:, k, :], Wa[:, k, :], AF.Identity, [dMn], [dWa], scale=mngT[:, k:k + 1])
            self.ldc(Wg[:], wsrc(w_in, GM0, D), [dWg])
            cnt = [0]

            def branch(src, dsrc, stage):
                for b, (t0, tn) in enumerate(TBLK):
                    for oc in range(8):
                        py, dpy = self.bank(); pg, dpg = self.bank()
                        for k in range(8):
                            self.mm(py[:, 0:tn], Wa[:, k, oc * 128:(oc + 1) * 128], src[:, k, t0:t0 + tn], k == 0, k == 7, [dWa, dsrc(b)], [dpy])
                        for k in range(8):
                            self.mm(pg[:, 0:tn], Wg[:, k, oc * 128:(oc + 1) * 128], hT[:, k, t0:t0 + tn], k == 0, k == 7, [dWg, dHT[b]], [dpg])
                        i = cnt[0] % 2; cnt[0] += 1
                        self.act(sg[i][:, 0:tn], pg[:, 0:tn], AF.Sigmoid, [dpg], [dsg[i]])
                        if stage == 1:
                            self.tt("dve", zt[:, oc, 0:tn], sg[i][:, 0:tn], py[:, 0:tn], ALU.mult, [dsg[i], dpy], [dzt])
                        else:
                            self.tt("dve", za[i][:, 0:tn], sg[i][:, 0:tn], py[:, 0:tn], ALU.mult, [dsg[i], dpy], [dza[i]])
                            self.tt("pool", hgT[:, oc, t0:t0 + tn], hgT[:, oc, t0:t0 + tn], za[i][:, 0:tn], ALU.add, [dza[i]], [dHG[b]])
                    if stage == 1:
                        self.cp("pool", hgT[:, :, t0:t0 + tn], zt[:, :, 0:tn], [dzt], [dHG[b]])

            branch(hgT, lambda b: dHG[b], 1)
            self.ldc(Wa[:], wsrc(L["w_ba"], 0, D), [dWa])
            self.ldc(Wg[:], wsrc(w_in, GA0, D), [dWg])
            branch(oaT, lambda b: dOA, 2)
            if "mixT" in self.debug:
                self.dbg("mixT", hgT[:], dHG, [128, 8, NT])
            S.emit()
        with ExitStack() as es:
            sb = lambda name, shape, dt=F32: es.enter_context(nc.sbuf_tensor(self.uname(name), list(shape), dt))
            Wo = sb("Wo", [128, 8, D], BF16); dWo = Dep()
            xres = sb("xres", [128, NTILE, D]); dX = [Dep() for _ in range(NTILE)]
            tG = [sb(f"tG{i}", [128, 512]) for i in range(2)]; dtG = [Dep(), Dep()]
            lnG = sb("lnG", [128, D]); lnB = sb("lnB", [128, D]); dLn = Dep()
            st = sb("lnst", [128, 7, NTILE]); dSt = Dep()
            junk = sb("junk", [128, D], BF16); dJ = Dep()
            self.memset("pool", st[:], 0.0, [dSt])
            self.ldc(Wo[:], wsrc(L["w_out"], 0, D), [dWo])
            self.ld(lnG[:], L["ln"]["ln2_g"][0:1, :].broadcast_to([128, D]), [dLn])
            self.ld(lnB[:], L["ln"]["ln2_b"][0:1, :].broadcast_to([128, D]), [dLn])
            for i in range(NTILE):
                nt = tile_nt(i)
                src = L["x1p"][128 * i:128 * i + nt, :] if i < 16 else L["x1s"][:, :]
                self.ld(xres[0:nt, i, :], src, [dX[i]])
                self.S.op("act", lambda e, i=i, nt=nt: e.mul(out=xres[0:nt, i, :], in_=xres[0:nt, i, :], mul=ALPHA), [dX[i]], [dX[i]])
            for i in range(NTILE):
                nt = tile_nt(i)
                which = 0 if i < 16 else 1
                for cb in range(2):
                    po, dpo = self.bank()
                    for k in range(8):
                        self.mm(po[0:nt, :], hgT[:, k, 128 * i:128 * i + nt], Wo[:, k, cb * 512:(cb + 1) * 512], k == 0, k == 7,
                                [dHG[min(i // 4, 4)], dWo], [dpo])
                    self.tt("dve", tG[cb][0:nt, :], po[0:nt, :], self.G[0:nt, which, cb * 512:(cb + 1) * 512], ALU.mult, [dpo, self.dG], [dtG[cb]])
                    self.tt("pool", xres[0:nt, i, cb * 512:(cb + 1) * 512], xres[0:nt, i, cb * 512:(cb + 1) * 512], tG[cb][0:nt, :], ALU.add,
                            [dtG[cb]], [dX[i]])
            self.layer_norm_tiles(xres, dX, lnG, lnB, dLn, st, dSt, junk, dJ, LN_EPS)
            for i in range(NTILE):
                nt = tile_nt(i)
                dst = L["x2p"][128 * i:128 * i + nt, :] if i < 16 else L["x2s"][:, :]
                self.store(dst, xres[0:nt, i, :], [dX[i]])
            S.emit()
            if "x2" in self.debug:
                o1 = nc.dram_tensor("dbg_x2p", [SEQ, D], F32, kind="ExternalOutput").ap()
                o2 = nc.dram_tensor("dbg_x2s", [NS, D], F32, kind="ExternalOutput").ap()
                self.store(o1[:, :], L["x2p"][:, :], [])
                self.store(o2[:, :], L["x2s"][:, :], [])
                S.emit()


_CACHE = {}


def _consts():
    half = 32
    inv = (10000.0 ** (-np.arange(half, dtype=np.float32) / half)).astype(np.float32)
    pos = np.concatenate([np.arange(SEQ), PAST + (np.arange(NS) % 4)]).astype(np.float32)
    ang = pos[None, :] * inv[:, None]
    cos = np.cos(ang).astype(np.float32); sin = np.sin(ang).astype(np.float32)
    ropec = np.concatenate([cos, cos, cos, cos], 0)
    ropes = np.concatenate([-sin, sin, -sin, sin], 0)
    ident = np.eye(128, dtype=np.float32)
    s = np.arange(128)[:, None]; t = np.arange(128)[None, :]
    m_le = (s <= t).astype(np.float32)
    m_gt = (s > t).astype(np.float32)
    blk = ((s // 4) == (t // 4)).astype(np.float32)
    m_blkc = blk * m_le
    m_cache = np.zeros((128, 128), np.float32)
    for col in range(128):
        tt = col % 4
        m_cache[:, col] = (np.arange(128) > tt)
    last = np.zeros((128, 128), np.float32)
    last[:, 0:16] = ((np.arange(128)[:, None] // 4) == np.arange(16)[None, :])
    cmask = np.concatenate([m_le, m_gt, m_blkc, blk, m_cache, last], 1)
    bm = ((np.arange(64)[None, :] // 4) == np.arange(16)[:, None]).astype(np.float32)
    bmask = np.broadcast_to(bm.reshape(1, 1024), (128, 1024)).copy()
    return dict(ident=ident, ropec=ropec.astype(np.float32), ropes=ropes.astype(np.float32), cmask=cmask.astype(np.float32),
                bmask=bmask)


def kernel(**inputs):
    debug = tuple(os.environ.get("MK_DEBUG", "").split(",")) if os.environ.get("MK_DEBUG") else ()
    key = debug
    if key not in _CACHE:
        kb = KB(debug)
        kb.build()
        _CACHE[key] = kb
    kb = _CACHE[key]
    f = lambda a: np.ascontiguousarray(np.asarray(a, dtype=np.float32))
    I = {k: f(v) for k, v in inputs.items()}
    cst = _consts()
    shared = dict(
        w_ada=I["w_ada"][0], b_ada=I["b_ada"][0].reshape(72, 128), b_ada_row=I["b_ada"][0].reshape(9, D),
        w_up1=I["w_ffn1_up"][0], w_dn1=I["w_ffn1_down"][0], w_up2=I["w_ffn2_up"][0], w_dn2=I["w_ffn2_down"][0],
        ln1_g=I["ln1_g"], ln1_b=I["ln1_b"], ln2_g=I["ln2_g"], ln2_b=I["ln2_b"], ln3_g=I["ln3_g"], ln3_b=I["ln3_b"],
        w_in=I["w_in"][0], b_ig=I["b_igate"], b_fg=I["b_fgate"], mng=I["m_norm_g"][0].reshape(8, 128), sinks=I["sinks"],
        w_bm=I["w_branch_m"][0], w_ba=I["w_branch_a"][0], w_out=I["w_out"][0], **cst)
    in_maps = []
    for c in range(8):
        sl = slice(16 * c, 16 * c + 16)
        m = dict(shared)
        m["xp"] = I["x_prompt"][c]
        m["xs"] = I["x_sample"][sl].reshape(NS, D)
        m["c_all"] = np.concatenate([I["c_prompt"][c:c + 1], I["c_sample"][sl]], 0)
        m["stC"] = I["state_mlstm_C"][0][sl]
        m["stn"] = I["state_mlstm_n"][0][sl].reshape(16, 512)
        m["stm"] = I["state_mlstm_m"][0][sl]
        m["ck"] = I["cache_swa_k"][0][sl].reshape(16, 128, 256)
        m["cv"] = I["cache_swa_v"][0][sl].reshape(16, 128, 256)
        in_maps.append(m)
    res = run_bass_kernel_spmd(kb.nc, in_maps, core_ids=list(range(8)))
    R = res.results
    kernel.last_results = R
    cat = lambda k: np.stack([np.asarray(r[k]) for r in R], 0)
    y_p = cat("yp").reshape(8, SEQ, D)
    y_s = cat("ys").reshape(128, 4, D)
    C_p = cat("Cp").reshape(1, 8, 4, 256, 128)
    n_p = cat("np").reshape(1, 8, 4, 128)
    m_p = cat("mp").reshape(1, 8, 4)
    k_p = cat("kp").reshape(1, 8, 128, 4, 64)
    v_p = cat("vp").reshape(1, 8, 128, 4, 64)
    C_s = cat("Cs").reshape(1, 128, 4, 256, 128)
    n_s = cat("ns").reshape(1, 128, 4, 128)
    m_s = cat("ms").reshape(1, 128, 4)
    k_s = cat("ks").reshape(1, 128, 128, 4, 64)
    v_s = cat("vs").reshape(1, 128, 128, 4, 64)
    return tuple(np.ascontiguousarray(a, dtype=np.float32) for a in (y_p, y_s, C_p, n_p, m_p, k_p, v_p, C_s, n_s, m_s, k_s, v_s))
```

```python
import os
import numpy as np
from contextlib import ExitStack
import concourse.bass as bass
import concourse.mybir as mybir
from concourse.bass_utils import run_bass_kernel_spmd

F32 = mybir.dt.float32
BF16 = mybir.dt.bfloat16
AF = mybir.ActivationFunctionType
ALU = mybir.AluOpType
AX = mybir.AxisListType

D = 1024
SEQ = 2048
NS = 64
NT = SEQ + NS
NTILE = 17
DFF = 2816
NFC = 22
ALPHA = 2.0 ** 0.25
LN_EPS = 1e-5
HN_EPS = 1e-6
PAST = 8192
SEM_LIMIT = 28000
TBLK = [(0, 512), (512, 512), (1024, 512), (1536, 512), (2048, 64)]


def tile_nt(i):
    return 128 if i < 16 else 64


def blk_tiles(b):
    return [4 * b + j for j in range(4)] if b < 4 else [16]


class Dep:
    __slots__ = ("w", "r")

    def __init__(self):
        self.w = {}
        self.r = {}


class Sched:
    ENGS = ("pe", "act", "dve", "pool", "sp")

    def __init__(self, nc, es):
        self.nc = nc
        self.es = es
        self.ops = {e: [] for e in self.ENGS}
        self.cnt = {e: 0 for e in self.ENGS}
        self.csem = {e: None for e in self.ENGS}
        self.seen = {e: {} for e in self.ENGS}
        self.nsem = 0
        self.dstates = []
        self.misc = {e: [[None, 0] for _ in range(16)] for e in ("sp", "pool", "act")}
        for e in self.misc:
            for m in self.misc[e]:
                self.dstates.append(m)
        self.misc_i = {e: 0 for e in self.misc}
        self.final = []

    def new_sem(self, name):
        self.nsem += 1
        return self.es.enter_context(self.nc.semaphore(f"{name}_{self.nsem}"))

    def dstate(self):
        s = [None, 0]
        self.dstates.append(s)
        return s

    def _counter(self, eng):
        if self.csem[eng] is None or self.cnt[eng] >= SEM_LIMIT:
            self.csem[eng] = self.new_sem("c" + eng)
            self.cnt[eng] = 0
        return self.csem[eng]

    def _collect(self, eng, reads, writes, skip_own=True, extra=()):
        need = {}

        def add(tok):
            sem, v = tok
            k = id(sem)
            if k not in need or need[k][1] < v:
                need[k] = (sem, v)
        for d in reads:
            for tok in d.w.values():
                add(tok)
        for d in writes:
            for tok in d.w.values():
                add(tok)
            for tok in d.r.values():
                add(tok)
        for tok in extra:
            add(tok)
        waits = []
        seen = self.seen[eng]
        own = self.csem[eng]
        for k, (sem, v) in need.items():
            if skip_own and own is not None and sem is own:
                continue
            if seen.get(k, 0) >= v:
                continue
            seen[k] = v
            waits.append((sem, v))
        return waits

    def op(self, eng, fn, reads=(), writes=()):
        waits = self._collect(eng, reads, writes, skip_own=(eng == "pe"))
        sem = self._counter(eng)
        self.cnt[eng] += 1
        tok = (sem, self.cnt[eng])
        k = id(sem)
        for d in reads:
            d.r[k] = tok
        for d in writes:
            d.w[k] = tok
        self.ops[eng].append((waits, fn, (sem, 1)))
        return tok

    def dma(self, eng, fn, reads=(), writes=(), st=None, final=False):
        if st is None:
            st = self.misc[eng][self.misc_i[eng] % len(self.misc[eng])]
            self.misc_i[eng] += 1
        extra = []
        if st[0] is not None and st[1] + 16 > SEM_LIMIT:
            st[0] = None
        if st[0] is None:
            st[0] = self.new_sem("d")
            st[1] = 0
        elif st[1] > 0:
            extra.append((st[0], st[1]))
        waits = self._collect(eng, reads, writes, skip_own=False, extra=extra)
        st[1] += 16
        tok = (st[0], st[1])
        k = id(st[0])
        for d in reads:
            d.r[k] = tok
        for d in writes:
            d.w[k] = tok
        self.ops[eng].append((waits, fn, (st[0], 16)))
        if final:
            self.final.append(tok)
        return tok

    def emit(self, last=False):
        nc = self.nc
        bar = []
        for e in self.ENGS:
            if self.csem[e] is not None and self.cnt[e] > 0:
                bar.append((e, self.csem[e], self.cnt[e]))
        dbar = [(s[0], s[1]) for s in self.dstates if s[0] is not None and s[1] > 0]
        ops = self.ops
        seen = self.seen

        def run(engine, name):
            for waits, fn, (sem, inc) in ops[name]:
                for ws, wv in waits:
                    engine.wait_ge(ws, wv)
                fn(engine).then_inc(sem, inc)
            for e, sem, v in bar:
                if e != name and seen[name].get(id(sem), 0) < v:
                    engine.wait_ge(sem, v)
                    seen[name][id(sem)] = v
            for sem, v in dbar:
                if seen[name].get(id(sem), 0) < v:
                    engine.wait_ge(sem, v)
                    seen[name][id(sem)] = v

        with nc.Block() as block:
            @block.sync
            def _(e):
                run(e, "sp")

            @block.tensor
            def _(e):
                run(e, "pe")

            @block.scalar
            def _(e):
                run(e, "act")

            @block.vector
            def _(e):
                run(e, "dve")

            @block.gpsimd
            def _(e):
                run(e, "pool")
        self.ops = {e: [] for e in self.ENGS}


class KB:
    def __init__(self, debug=()):
        self.debug = set(debug)
        self.nc = bass.Bass("TRN2", target_bir_lowering=False)
        self.dram = {}
        self.dbg_out = {}

    def uname(self, name):
        self.ucnt = getattr(self, "ucnt", 0) + 1
        return f"s{self.ucnt}_{name}"

    def din(self, name, shape):
        self.dram[name] = self.nc.dram_tensor(name, list(shape), F32, kind="ExternalInput").ap()
        return self.dram[name]

    def dout(self, name, shape):
        self.dram[name] = self.nc.dram_tensor(name, list(shape), F32, kind="ExternalOutput").ap()
        return self.dram[name]

    def dscr(self, name, shape):
        self.dram[name] = self.nc.dram_tensor(name, list(shape), F32, kind="Internal").ap()
        return self.dram[name]

    def mm(self, out, lhsT, rhs, start, stop, R, W):
        self.S.op("pe", lambda e: e.matmul(out, lhsT=lhsT, rhs=rhs, start=start, stop=stop), R, W)

    def tr(self, out, in_, R, W):
        n = in_.shape[0]
        ident = self.ident[0:n, 0:n]
        self.S.op("pe", lambda e: e.transpose(out=out, in_=in_, identity=ident), list(R) + [self.dIdent], W)

    def act(self, out, in_, func, R, W, bias=None, scale=None, accum=None):
        kw = {}
        if bias is not None:
            kw["bias"] = bias
        if scale is not None:
            kw["scale"] = scale
        if accum is not None:
            kw["accum_out"] = accum
        self.S.op("act", lambda e: e.activation(out=out, in_=in_, func=func, **kw), R, W)

    def tt(self, eng, out, in0, in1, op, R, W):
        self.S.op(eng, lambda e: e.tensor_tensor(out=out, in0=in0, in1=in1, op=op), R, W)

    def ts(self, eng, out, in0, s1, s2, op0, op1, R, W):
        if s2 is None:
            s2 = 0.0
            op1 = ALU.add
        self.S.op(eng, lambda e: e.tensor_scalar(out=out, in0=in0, scalar1=s1, scalar2=s2, op0=op0, op1=op1), R, W)

    def stt(self, eng, out, in0, scalar, in1, op0, op1, R, W):
        self.S.op(eng, lambda e: e.scalar_tensor_tensor(out=out, in0=in0, scalar=scalar, in1=in1, op0=op0, op1=op1), R, W)

    def cp(self, eng, out, in_, R, W):
        if eng == "act":
            self.S.op("act", lambda e: e.copy(out=out, in_=in_), R, W)
        else:
            self.S.op(eng, lambda e: e.tensor_copy(out=out, in_=in_), R, W)

    def memset(self, eng, ap, val, W):
        self.S.op(eng, lambda e: e.memset(ap, val), (), W)

    def recip(self, out, in_, R, W):
        self.S.op("dve", lambda e: e.reciprocal(out=out, in_=in_), R, W)

    def ld(self, out, in_, W, R=(), eng="sp", st=None):
        return self.S.dma(eng, lambda e: e.dma_start(out=out, in_=in_), R, W, st=st)

    def ldc(self, out, in_, W, R=(), st=None):
        return self.S.dma("pool", lambda e: e.dma_start(out=out, in_=in_), R, W, st=st)

    def store(self, out, in_, R, W=(), final=True, eng="sp"):
        return self.S.dma(eng, lambda e: e.dma_start(out=out, in_=in_), R, W, final=final)

    def ld_nc(self, out, in_, W, R=()):
        return self.S.dma("sp", lambda e: e.dma_start(out=out, in_=in_, allow_slow_non_contiguous=True), R, W)

    def store_nc(self, out, in_, R):
        return self.S.dma("sp", lambda e: e.dma_start(out=out, in_=in_, allow_slow_non_contiguous=True), R, (), final=True)

    def dbg(self, name, ap, dep, shape):
        if name not in self.debug:
            return
        o = self.nc.dram_tensor("dbg_" + name, list(shape), ap.dtype, kind="ExternalOutput").ap()
        self.dbg_out[name] = o
        deps = dep if isinstance(dep, (list, tuple)) else [dep]
        self.store(o, ap, list(deps))

    def bank(self):
        i = self.bank_i % 8
        self.bank_i += 1
        return self.PB[i], self.dPB[i]

    def build(self):
        nc = self.nc
        din, dout = self.din, self.dout
        xp = din("xp", [SEQ, D]); xs = din("xs", [NS, D]); c_all = din("c_all", [17, D])
        stC = din("stC", [16, 4, 256, 128]); stn = din("stn", [16, 512]); stm = din("stm", [16, 4])
        ck = din("ck", [16, 128, 256]); cv = din("cv", [16, 128, 256])
        w_ada = din("w_ada", [D, 9 * D]); b_ada = din("b_ada", [72, 128]); self.b_ada_row = din("b_ada_row", [9, D])
        w_up1 = din("w_up1", [D, 2 * DFF]); w_dn1 = din("w_dn1", [DFF, D])
        w_up2 = din("w_up2", [D, 2 * DFF]); w_dn2 = din("w_dn2", [DFF, D])
        ln = {k: din(k, [1, D]) for k in ("ln1_g", "ln1_b", "ln2_g", "ln2_b", "ln3_g", "ln3_b")}
        w_in = din("w_in", [D, 6664]); b_ig = din("b_ig", [1, 4]); b_fg = din("b_fg", [1, 4])
        mng = din("mng", [8, 128]); sinks = din("sinks", [1, 16])
        w_bm = din("w_bm", [D, D]); w_ba = din("w_ba", [D, D]); w_out = din("w_out", [D, D])
        identd = din("ident", [128, 128]); ropec = din("ropec", [128, NT]); ropes = din("ropes", [128, NT])
        cmask = din("cmask", [128, 128 * 6]); bmask = din("bmask", [128, 1024])
        oa_scr = nc.dram_tensor("oa_scr", [128, 8 * NT], BF16, kind="Internal").ap()
        yp = dout("yp", [SEQ, D]); ys = dout("ys", [NS, D])
        Cp = dout("Cp", [4, 256, 128]); np_ = dout("np", [4, 128]); mp = dout("mp", [4, 1])
        kp = dout("kp", [128, 256]); vp = dout("vp", [128, 256])
        Cs = dout("Cs", [16, 4, 256, 128]); ns_ = dout("ns", [16, 512]); ms = dout("ms", [16, 4])
        ks = dout("ks", [16, 128, 256]); vs = dout("vs", [16, 128, 256])
        x1p = self.dscr("x1p", [SEQ, D]); x1s = self.dscr("x1s", [NS, D])
        x2p = self.dscr("x2p", [SEQ, D]); x2s = self.dscr("x2s", [NS, D])

        with ExitStack() as es:
            self.es = es
            self.S = S = Sched(nc, es)
            sb = lambda name, shape, dt=F32: es.enter_context(nc.sbuf_tensor(self.uname(name), list(shape), dt))
            self.PB = [es.enter_context(nc.psum_tensor(f"pb{i}", [128, 512], F32)) for i in range(8)]
            self.dPB = [Dep() for _ in range(8)]
            self.bank_i = 0
            self.ident = sb("ident", [128, 128]); self.dIdent = Dep()
            self.ld(self.ident[:], identd[:, :], [self.dIdent])
            self.modT = sb("modT", [128, 2, 8, 17]); self.dModT = Dep()
            self.modS = sb("modS", [128, 2, 8, 64]); self.dModS = Dep()
            self.G = sb("G", [128, 2, D]); self.dG = Dep()
            self.scT = sb("scT", [128, 8, 17], BF16); self.dScT = Dep()
            self.lhsP = sb("lhsP", [128, 8, 128], BF16); self.lhsS = sb("lhsS", [128, 8, 64], BF16); self.dLhs = Dep()
            self.badaT = sb("badaT", [128, 72]); self.dBada = Dep()
            self.setup_ada(c_all, b_ada)
            S.emit()
            self.ada_stage(0, w_ada, b_ada)
            self.dbg("modT", self.modT[:], self.dModT, [128, 2, 8, 17])
            self.dbg("modS", self.modS[:], self.dModS, [128, 2, 8, 64])
            self.dbg("G", self.G[:], self.dG, [128, 2, D])
            S.emit()
            self.ffn(xp, xs, x1p, x1s, w_up1, w_dn1, ln["ln1_g"], ln["ln1_b"], "f1", next_ada=(1, w_ada, b_ada))
            if "x1" in self.debug:
                o1 = nc.dram_tensor("dbg_x1p", [SEQ, D], F32, kind="ExternalOutput").ap()
                o2 = nc.dram_tensor("dbg_x1s", [NS, D], F32, kind="ExternalOutput").ap()
                self.store(o1[:, :], x1p[:, :], [])
                self.store(o2[:, :], x1s[:, :], [])
                S.emit()
            if "stop1" in self.debug:
                return nc
            self.mixer(dict(locals(), **self.dram))
            self.ffn(x2p, x2s, yp, ys, w_up2, w_dn2, ln["ln3_g"], ln["ln3_b"], "f2")
        return nc

    def setup_ada(self, c_all, b_ada):
        with ExitStack() as es:
            nc = self.nc
            sb = lambda name, shape, dt=F32: es.enter_context(nc.sbuf_tensor(self.uname(name), list(shape), dt))
            craw = sb("craw", [17, D]); dC = Dep()
            csil = sb("csil", [17, D]); dCs = Dep()
            braw = sb("braw", [72, 128]); dB = Dep()
            scTf = sb("scTf", [128, 8, 17]); dScTf = Dep()
            self.ld(craw[:], c_all[:, :], [dC])
            self.ld(braw[:], b_ada[:, :], [dB])
            self.act(csil[:], craw[:], AF.Silu, [dC], [dCs])
            pb, dpb = self.bank()
            for k in range(8):
                self.tr(pb[:, k * 17:(k + 1) * 17], csil[0:17, k * 128:(k + 1) * 128], [dCs], [dpb])
            self.cp("dve", scTf[:].rearrange("p a b -> p (a b)"), pb[:, 0:136], [dpb], [dScTf])
            self.cp("dve", self.scT[:], scTf[:], [dScTf], [self.dScT])
            self.cp("dve", self.lhsP[:], scTf[:, :, 0:1].to_broadcast([128, 8, 128]), [dScTf], [self.dLhs])
            for k in range(8):
                self.cp("dve", self.lhsS[:, k, :].rearrange("p (b j) -> p b j", j=4),
                        scTf[:, k, 1:17].unsqueeze(2).to_broadcast([128, 16, 4]), [dScTf], [self.dLhs])
            pb2, dpb2 = self.bank()
            self.tr(pb2[:, 0:72], braw[0:72, :], [dB], [dpb2])
            self.cp("dve", self.badaT[:], pb2[:, 0:72], [dpb2], [self.dBada])
            self.S.emit()

    def ada_load(self, st, ci, w_ada, wa, dwa):
        j = 3 * st + ci
        self.ldc(wa, w_ada[:, j * D:(j + 1) * D].rearrange("(kc p) n -> p kc n", p=128), [dwa])

    def ada_compute(self, st, ci, wa, dwa, bG, dbG, tmpG, dtG):
        j = 3 * st + ci
        if ci < 2:
            pb, dpb = self.bank()
            for f in range(8):
                for k in range(8):
                    self.mm(pb[:, f * 17:(f + 1) * 17], wa[:, k, f * 128:(f + 1) * 128], self.scT[:, k, :],
                            k == 0, k == 7, [dwa, self.dScT], [dpb])
            self.stt("dve", self.modT[:, ci, :, :], pb[:, 0:136].rearrange("p (a b) -> p a b", a=8),
                     1.0 if ci == 1 else 0.0,
                     self.badaT[:, j * 8:(j + 1) * 8].unsqueeze(2).to_broadcast([128, 8, 17]),
                     ALU.add, ALU.add, [dpb, self.dBada], [self.dModT])
        else:
            a = 1.0 if st == 1 else 0.5
            self.ld(bG[:], self.b_ada_row[j:j + 1, :].broadcast_to([128, D]), [dbG])
            for which, lhs, n in ((0, self.lhsP, 128), (1, self.lhsS, 64)):
                for cb in range(2):
                    pb, dpb = self.bank()
                    for k in range(8):
                        self.mm(pb[0:n, :], lhs[:, k, :], wa[:, k, cb * 512:(cb + 1) * 512], k == 0, k == 7,
                                [dwa, self.dLhs], [dpb])
                    self.tt("dve", tmpG[0:n, :], pb[0:n, :], bG[0:n, cb * 512:(cb + 1) * 512], ALU.add, [dpb, dbG], [dtG])
                    self.ts("dve", self.G[0:n, which, cb * 512:(cb + 1) * 512], tmpG[0:n, :], a, a, ALU.mult, ALU.add,
                            [dtG], [self.dG])

    def ada_finish(self):
        for w in range(2):
            self.cp("dve", self.modS[:, w, :, :].rearrange("p k (b j) -> p k b j", j=4),
                    self.modT[:, w, :, 1:17].unsqueeze(3).to_broadcast([128, 8, 16, 4]), [self.dModT], [self.dModS])

    def ada_stage(self, st, w_ada, b_ada):
        nc = self.nc
        with ExitStack() as es:
            sb = lambda name, shape, dt=F32: es.enter_context(nc.sbuf_tensor(self.uname(name), list(shape), dt))
            wa = [sb(f"wa{i}", [128, 8, D], BF16)[:] for i in range(2)]
            dwa = [Dep(), Dep()]
            bG = sb("bG", [128, D]); dbG = Dep()
            tmpG = sb("tmpG", [128, 512]); dtG = Dep()
            for ci in range(3):
                self.ada_load(st, ci, w_ada, wa[ci % 2], dwa[ci % 2])
                self.ada_compute(st, ci, wa[ci % 2], dwa[ci % 2], bG, dbG, tmpG, dtG)
            self.ada_finish()
            self.S.emit()

    def make_xT(self, xT, dXT, src_tile_ap, dsrc, i, tmp, dtmp):
        nt = tile_nt(i)
        t0 = 128 * i
        for half in range(2):
            pb, dpb = self.bank()
            for kk in range(4):
                k = half * 4 + kk
                self.tr(pb[:, kk * nt:(kk + 1) * nt], src_tile_ap[0:nt, k * 128:(k + 1) * 128], [dsrc], [dpb])
            if i < 16:
                for kk in range(4):
                    k = half * 4 + kk
                    self.act(xT[:, k, t0:t0 + nt], pb[:, kk * nt:(kk + 1) * nt], AF.Identity, [dpb, self.dModT], [dXT],
                             bias=self.modT[:, 0, k, 0:1], scale=self.modT[:, 1, k, 0:1])
            else:
                k0 = half * 4
                self.tt("dve", tmp[:, 0:256].rearrange("p (a b) -> p a b", a=4), pb[:, 0:256].rearrange("p (a b) -> p a b", a=4),
                        self.modS[:, 1, k0:k0 + 4, :], ALU.mult, [dpb, self.dModS], [dtmp])
                self.tt("dve", xT[:, k0:k0 + 4, t0:t0 + nt], tmp[:, 0:256].rearrange("p (a b) -> p a b", a=4),
                        self.modS[:, 0, k0:k0 + 4, :], ALU.add, [dtmp, self.dModS], [dXT])

    def layer_norm_tiles(self, xres, dX, lnG, lnB, dLn, st, dSt, junk, dJ, eps):
        for i in range(NTILE):
            nt = tile_nt(i)
            self.act(junk[0:nt, :], xres[0:nt, i, :], AF.Identity, [dX[i]], [dJ, dSt], accum=st[0:nt, 0, i:i + 1])
            self.act(junk[0:nt, :], xres[0:nt, i, :], AF.Square, [dX[i]], [dJ, dSt], accum=st[0:nt, 1, i:i + 1])
        m = st[:, 2, :]; msq = st[:, 3, :]; var = st[:, 4, :]; rstd = st[:, 5, :]; nb = st[:, 6, :]
        self.ts("dve", m, st[:, 0, :], 1.0 / D, None, ALU.mult, None, [dSt], [dSt])
        self.tt("dve", msq, m, m, ALU.mult, [dSt], [dSt])
        self.ts("dve", var, st[:, 1, :], 1.0 / D, eps, ALU.mult, ALU.add, [dSt], [dSt])
        self.tt("dve", var, var, msq, ALU.subtract, [dSt], [dSt])
        self.act(var, var, AF.Sqrt, [dSt], [dSt])
        self.recip(rstd, var, [dSt], [dSt])
        self.tt("dve", nb, m, rstd, ALU.mult, [dSt], [dSt])
        self.ts("dve", nb, nb, -1.0, None, ALU.mult, None, [dSt], [dSt])
        for i in range(NTILE):
            nt = tile_nt(i)
            self.act(xres[0:nt, i, :], xres[0:nt, i, :], AF.Identity, [dSt], [dX[i]], bias=nb[0:nt, i:i + 1], scale=rstd[0:nt, i:i + 1])
            self.tt("dve", xres[0:nt, i, :], xres[0:nt, i, :], lnG[0:nt, :], ALU.mult, [dLn], [dX[i]])
            self.tt("dve", xres[0:nt, i, :], xres[0:nt, i, :], lnB[0:nt, :], ALU.add, [dLn], [dX[i]])

    def ffn(self, src_p, src_s, dst_p, dst_s, w_up, w_dn, ln_g, ln_b, tag, next_ada=None):
        nc = self.nc
        S = self.S
        groups = [(0, 4), (4, 4), (8, 4), (12, 4), (16, 3), (19, 3)]
        GC = 4
        with ExitStack() as es:
            sb = lambda name, shape, dt=F32: es.enter_context(nc.sbuf_tensor(self.uname(name), list(shape), dt))
            xres = sb("xres", [128, NTILE, D]); dX = [Dep() for _ in range(NTILE)]
            xT = sb("xT", [128, 8, NT], BF16); dXT = [Dep() for _ in range(5)]
            wu = [sb(f"wu{i}", [128, 8, 2, GC * 128], BF16) for i in range(2)]; dwu = [Dep(), Dep()]
            wd = [sb(f"wd{i}", [128, GC, D], BF16) for i in range(2)]; dwd = [Dep(), Dep()]
            gT = [sb(f"gT{i}", [128, GC, 512], BF16) for i in range(2)]; dgT = [Dep(), Dep()]
            tsil = [sb(f"tsil{i}", [128, 512], BF16) for i in range(2)]; dts = [Dep(), Dep()]
            tG = [sb(f"tG{i}", [128, 512]) for i in range(2)]; dtG = [Dep(), Dep()]
            lnG = sb("lnG", [128, D]); lnB = sb("lnB", [128, D]); dLn = Dep()
            st = sb("lnst", [128, 7, NTILE]); dSt = Dep()
            junk = sb("junk", [128, D], BF16); dJ = Dep()
            tmpm = sb("tmpm", [128, 256]); dtm = Dep()
            if next_ada is not None:
                a_bG = sb("a_bG", [128, D]); a_tmpG = sb("a_tmpG", [128, 512])
                wa_views = [wu[i][:].rearrange("p k h c -> p k (h c)") for i in range(2)]
            if os.environ.get("MK_PADF"):
                sb("padf", [128, int(os.environ["MK_PADF"]) * 256])
            self.memset("pool", st[:], 0.0, [dSt])
            self.ld(lnG[:], ln_g[0:1, :].broadcast_to([128, D]), [dLn])
            self.ld(lnB[:], ln_b[0:1, :].broadcast_to([128, D]), [dLn])

            def load_group(q):
                c0, gc = groups[q]
                s = q % 2
                for half in range(2):
                    col0 = half * DFF + c0 * 128
                    self.ldc(wu[s][:, :, half, 0:gc * 128], w_up[:, col0:col0 + gc * 128].rearrange("(kc p) n -> p kc n", p=128), [dwu[s]])
                self.ldc(wd[s][:, 0:gc, :], w_dn[c0 * 128:(c0 + gc) * 128, :].rearrange("(c p) n -> p c n", p=128), [dwd[s]])

            load_group(0)
            for i in range(NTILE):
                nt = tile_nt(i)
                src = src_p[128 * i:128 * i + nt, :] if i < 16 else src_s[:, :]
                self.ld(xres[0:nt, i, :], src, [dX[i]])
                self.make_xT(xT, dXT[min(i // 4, 4)], xres[:, i, :], dX[i], i, tmpm, dtm)
                self.S.op("act", lambda e, i=i, nt=nt: e.mul(out=xres[0:nt, i, :], in_=xres[0:nt, i, :], mul=ALPHA), [dX[i]], [dX[i]])
            items = [(q, b) for q in range(len(groups)) for b in range(5)]

            def up(n):
                q, b = items[n]
                c0, gc = groups[q]
                s = q % 2
                g = n % 2
                t0, tn = TBLK[b]
                for j in range(gc):
                    pa, dpa = self.bank()
                    pu, dpu = self.bank()
                    for half, (pp, dpp) in enumerate(((pa, dpa), (pu, dpu))):
                        for k in range(8):
                            self.mm(pp[:, 0:tn], wu[s][:, k, half, j * 128:(j + 1) * 128], xT[:, k, t0:t0 + tn], k == 0, k == 7,
                                    [dwu[s], dXT[b]], [dpp])
                    sl = (n * GC + j) % 2
                    self.act(tsil[sl][:, 0:tn], pa[:, 0:tn], AF.Silu, [dpa], [dts[sl]])
                    self.tt("dve", gT[g][:, j, 0:tn], tsil[sl][:, 0:tn], pu[:, 0:tn], ALU.mult, [dts[sl], dpu], [dgT[g]])

            def down(n):
                q, b = items[n]
                c0, gc = groups[q]
                s = q % 2
                g = n % 2
                for ti, i in enumerate(blk_tiles(b)):
                    nt = tile_nt(i)
                    which = 0 if i < 16 else 1
                    for cb in range(2):
                        po, dpo = self.bank()
                        for j in range(gc):
                            self.mm(po[0:nt, :], gT[g][:, j, ti * 128:ti * 128 + nt], wd[s][:, j, cb * 512:(cb + 1) * 512],
                                    j == 0, j == gc - 1, [dgT[g], dwd[s]], [dpo])
                        sl = (i * 2 + cb) % 2
                        self.tt("dve", tG[sl][0:nt, :], po[0:nt, :], self.G[0:nt, which, cb * 512:(cb + 1) * 512], ALU.mult,
                                [dpo, self.dG], [dtG[sl]])
                        self.tt("pool", xres[0:nt, i, cb * 512:(cb + 1) * 512], xres[0:nt, i, cb * 512:(cb + 1) * 512], tG[sl][0:nt, :],
                                ALU.add, [dtG[sl]], [dX[i]])

            load_group(1)
            up(0)
            for n in range(len(items)):
                if n + 1 < len(items):
                    up(n + 1)
                down(n)
                q, b = items[n]
                if b == 4 and q + 2 < len(groups):
                    load_group(q + 2)
                if next_ada is not None and b == 4 and q == len(groups) - 2:
                    sA = q % 2
                    self.ada_load(next_ada[0], 0, next_ada[1], wa_views[sA], dwu[sA])
            if next_ada is not None:
                st_, wad = next_ada[0], next_ada[1]
                sB = 1 - sA
                a_dbG = Dep(); a_dtG = Dep()
                self.ada_load(st_, 1, wad, wa_views[sB], dwu[sB])
                self.ada_compute(st_, 0, wa_views[sA], dwu[sA], a_bG, a_dbG, a_tmpG, a_dtG)
                self.ada_load(st_, 2, wad, wa_views[sA], dwu[sA])
            if tag == "f1":
                self.dbg("pre", xres[:], dX, [128, NTILE, D])
                self.dbg("xT", xT[:], dXT, [128, 8, NT])
            self.layer_norm_tiles(xres, dX, lnG, lnB, dLn, st, dSt, junk, dJ, LN_EPS)
            if tag == "f1":
                self.dbg("lnst", st[:], dSt, [128, 7, NTILE])
            for i in range(NTILE):
                nt = tile_nt(i)
                dst = dst_p[128 * i:128 * i + nt, :] if i < 16 else dst_s[:, :]
                self.store(dst, xres[0:nt, i, :], [dX[i]])
            if next_ada is not None:
                self.ada_compute(st_, 1, wa_views[sB], dwu[sB], a_bG, a_dbG, a_tmpG, a_dtG)
                self.ada_compute(st_, 2, wa_views[sA], dwu[sA], a_bG, a_dbG, a_tmpG, a_dtG)
                self.ada_finish()
            S.emit()

    def mixer(self, L):
        nc, S = self.nc, self.S
        x1p, x1s = L["x1p"], L["x1s"]
        with ExitStack() as es0:
            sb0 = lambda name, shape, dt=F32: es0.enter_context(nc.sbuf_tensor(self.uname(name), list(shape), dt))
            hT = sb0("hT", [128, 8, NT], BF16); dHT = [Dep() for _ in range(5)]
            with ExitStack() as es:
                sb = lambda name, shape, dt=F32: es.enter_context(nc.sbuf_tensor(self.uname(name), list(shape), dt))
                xt = sb("xt", [128, 2, D]); dxt = [Dep(), Dep()]
                tmpm = sb("tmpm", [128, 256]); dtm = Dep()
                for i in range(NTILE):
                    nt = tile_nt(i)
                    src = x1p[128 * i:128 * i + nt, :] if i < 16 else x1s[:, :]
                    self.ld(xt[0:nt, i % 2, :], src, [dxt[i % 2]])
                    self.make_xT(hT, dHT[min(i // 4, 4)], xt[:, i % 2, :], dxt[i % 2], i, tmpm, dtm)
                S.emit()
            with ExitStack() as es:
                oaT = es.enter_context(nc.sbuf_tensor(self.uname("oaT"), [128, 8, NT], BF16)); dOA = [Dep() for _ in range(5)]
                self.swa(L, hT, dHT, oaT, dOA)
                self.store(L["oa_scr"][:, :], oaT[:].rearrange("p c t -> p (c t)"), dOA)
                if "oaT" in self.debug:
                    self.dbg("oaT", oaT[:], dOA, [128, 8, NT])
                S.emit()
            if "stop2" in self.debug:
                return
            hgT = sb0("hgT", [128, 8, NT], BF16); dHG = [Dep() for _ in range(5)]
            self.mlstm(L, hT, dHT, hgT, dHG)
            if "hgT" in self.debug:
                self.dbg("hgT", hgT[:], dHG, [128, 8, NT])
                S.emit()
            if "stop3" in self.debug:
                return
            self.merge(L, hT, dHT, hgT, dHG, next_ada=(2, L["w_ada"]))

    def swa(self, L, hT, dHT, oaT, dOA):
        nc, S = self.nc, self.S
        w_in = L["w_in"]
        AQ0, AK0, AV0 = 3080, 4104, 4360
        with ExitStack() as es:
            sb = lambda name, shape, dt=F32: es.enter_context(nc.sbuf_tensor(self.uname(name), list(shape), dt))
            if os.environ.get("MK_PADS"):
                sb("pads", [128, int(os.environ["MK_PADS"]) * 256])
            cosT = sb("cosT", [128, NT]); sinT = sb("sinT", [128, NT]); dRope = Dep()
            self.ld(cosT[:], L["ropec"][:, :], [dRope]); self.ld(sinT[:], L["ropes"][:, :], [dRope])
            mk = sb("mk", [128, 768]); dMk = Dep()
            self.ld(mk[:], L["cmask"][:, :], [dMk])
            mle = sb("mle", [128, 128], BF16); mgt = sb("mgt", [128, 128], BF16); mblkc = sb("mblkc", [128, 128], BF16)
            mcache = sb("mcache", [128, 4], BF16); dMb = Dep()
            self.cp("dve", mle[:], mk[:, 0:128], [dMk], [dMb]); self.cp("dve", mgt[:], mk[:, 128:256], [dMk], [dMb])
            self.cp("dve", mblkc[:], mk[:, 256:384], [dMk], [dMb]); self.cp("dve", mcache[:], mk[:, 512:516], [dMk], [dMb])
            ones = sb("ones", [128, 128], BF16); dOnes = Dep()
            self.memset("pool", ones[:], 1.0, [dOnes])
            esink = sb("esink", [128, 16]); dEs = Dep()
            self.ld(esink[:], L["sinks"][0:1, :].broadcast_to([128, 16]), [dEs])
            self.act(esink[:], esink[:], AF.Exp, [dEs], [dEs])
            ckf = sb("ckf", [128, 16, 256]); dCk = Dep()
            for b0 in range(0, 16, 4):
                self.ld(ckf[:, b0:b0 + 4, :], L["ck"][b0:b0 + 4].rearrange("b p c -> p b c"), [dCk])
            cvv = L["cv"].rearrange("b p c -> p b c")
            KcTn = sb("KcTn", [128, 2, 16, 128], BF16); dKn = Dep()
            for gp in range(2):
                for b0 in range(0, 16, 4):
                    pb, dpb = self.bank()
                    for bb in range(4):
                        self.tr(pb[:, bb * 128:(bb + 1) * 128], ckf[:, b0 + bb, gp * 128:(gp + 1) * 128], [dCk], [dpb])
                    self.cp("act", KcTn[:, gp, b0:b0 + 4, :].rearrange("p b c -> p (b c)"), pb[:, :], [dpb], [dKn])
            kout_p = sb("kout_p", [128, 256]); vfin = sb("vfin", [128, 2, 256]); knew_s = sb("knew_s", [128, 256])
            dKo = Dep(); dVf = Dep(); dKs = Dep()
            WQ = sb("WQ", [128, 8, 256], BF16); WQs = sb("WQs", [128, 8, 256], BF16)
            WK2 = sb("WK2", [128, 8, 128], BF16); WK2s = sb("WK2s", [128, 8, 128], BF16)
            WV = sb("WV", [128, 8, 64], BF16); dW = Dep(); dWs = Dep()
            qT = sb("qT", [128, 2, NT], BF16); dQ = [Dep() for _ in range(5)]
            kT2 = sb("kT2", [128, NT], BF16); dK = [Dep() for _ in range(5)]
            vtm2 = sb("vtm2", [128, NTILE, 128], BF16); dV = [Dep() for _ in range(NTILE)]
            vlh = sb("vlh", [128, 16, 2, 128], BF16)
            ones_lh = sb("ones_lh", [128, 2, 128], BF16); dOlh = Dep()
            esg = sb("esg", [128, 2]); dEsg = Dep()
            self.memset("pool", vlh[:], 0.0, dV[0:16])
            self.memset("pool", ones_lh[:], 0.0, [dOlh])
            self.memset("pool", ones_lh[:, 0, 0:64], 1.0, [dOlh])
            self.memset("pool", ones_lh[:, 1, 64:128], 1.0, [dOlh])
            kfin = sb("kfin", [128, 192]); dKf = Dep()
            t1 = [sb(f"t1_{i}", [128, 512]) for i in range(2)]; dt1 = [Dep(), Dep()]
            t2 = [sb(f"t2_{i}", [128, 512]) for i in range(2)]; dt2 = [Dep(), Dep()]
            pT = [sb(f"pT{i}", [128, 512], BF16) for i in range(4)]; dpT = [Dep() for _ in range(4)]
            pTm = [sb(f"pTm{i}", [128, 512], BF16) for i in range(4)]; dpTm = [Dep() for _ in range(4)]
            recs = [sb(f"rec{i}", [128, 512]) for i in range(2)]; dRecs = [Dep(), Dep()]
            rec = recs[0]; dRec = dRecs[0]
            KcT2 = sb("KcT2", [128, 16, 128], BF16); dKc2 = Dep()
            Vc2 = sb("Vc2", [128, 16, 128], BF16); dVc2 = Dep()
            pTn = sb("pTn", [128, 256], BF16); dpTn = Dep()
            pTnm = sb("pTnm", [128, 256], BF16); dpTnm = Dep()
            cnt = [0]

            def rope_evac(pa, dpa, pb_, dpb_, t0, tn, out_ap, dOut, fin=None):
                i = cnt[0] % 2; cnt[0] += 1
                self.tt("dve", t1[i][:, 0:tn], pa[:, 0:tn], cosT[:, t0:t0 + tn], ALU.mult, [dpa, dRope], [dt1[i]])
                self.tt("dve", t2[i][:, 0:tn], pb_[:, 0:tn], sinT[:, t0:t0 + tn], ALU.mult, [dpb_, dRope], [dt2[i]])
                self.tt("pool", out_ap, t1[i][:, 0:tn], t2[i][:, 0:tn], ALU.add, [dt1[i], dt2[i]], [dOut])
                if fin is not None:
                    fo, a, n = fin
                    self.tt("pool", kfin[:, fo:fo + n], t1[i][:, a:a + n], t2[i][:, a:a + n], ALU.add, [dt1[i], dt2[i]], [dKf])

            lvl = 9
            att = 9
            for f_ in self.debug:
                if f_.startswith("att"):
                    att = int(f_[3:])
            for f_ in self.debug:
                if f_.startswith("swa"):
                    lvl = int(f_[3:])
            for g in range(4 if lvl >= 5 else 1):
                if lvl < 1:
                    break
                wsrc = lambda c0, n: w_in[:, c0:c0 + n].rearrange("(kc p) n -> p kc n", p=128)
                self.ldc(WQ[:], wsrc(AQ0 + g * 256, 256), [dW])
                self.ldc(WK2[:, :, 0:64], wsrc(AK0 + g * 64, 64), [dW])
                self.ldc(WK2[:, :, 64:128], wsrc(AK0 + g * 64, 64), [dW])
                self.ldc(WV[:], wsrc(AV0 + g * 64, 64), [dW])
                for src_t, dst_t, nh in ((WQ, WQs, 4), (WK2, WK2s, 2)):
                    sv = src_t[:].rearrange("p k (h two d) -> p k h two d", two=2, d=32)
                    dv = dst_t[:].rearrange("p k (h two d) -> p k h two d", two=2, d=32)
                    for k in range(8):
                        self.cp("pool", dv[:, k, :, 0, :], sv[:, k, :, 1, :], [dW], [dWs])
                        self.cp("pool", dv[:, k, :, 1, :], sv[:, k, :, 0, :], [dW], [dWs])
                for b, (t0, tn) in enumerate(TBLK):
                    for c in range(2):
                        pa, dpa = self.bank(); pb_, dpb_ = self.bank()
                        for k in range(8):
                            self.mm(pa[:, 0:tn], WQ[:, k, c * 128:(c + 1) * 128], hT[:, k, t0:t0 + tn], k == 0, k == 7, [dW, dHT[b]], [dpa])
                        for k in range(8):
                            self.mm(pb_[:, 0:tn], WQs[:, k, c * 128:(c + 1) * 128], hT[:, k, t0:t0 + tn], k == 0, k == 7, [dWs, dHT[b]], [dpb_])
                        rope_evac(pa, dpa, pb_, dpb_, t0, tn, qT[:, c, t0:t0 + tn], dQ[b])
                    pa, dpa = self.bank(); pb_, dpb_ = self.bank()
                    for k in range(8):
                        self.mm(pa[:, 0:tn], WK2[:, k, :], hT[:, k, t0:t0 + tn], k == 0, k == 7, [dW, dHT[b]], [dpa])
                    for k in range(8):
                        self.mm(pb_[:, 0:tn], WK2s[:, k, :], hT[:, k, t0:t0 + tn], k == 0, k == 7, [dWs, dHT[b]], [dpb_])
                    fin = (0, 384, 128) if b == 3 else ((128, 0, 64) if b == 4 else None)
                    rope_evac(pa, dpa, pb_, dpb_, t0, tn, kT2[:, t0:t0 + tn], dK[b], fin)
                for i in range(NTILE):
                    nt = tile_nt(i)
                    pv, dpv = self.bank()
                    for k in range(8):
                        self.mm(pv[0:nt, 0:64], hT[:, k, 128 * i:128 * i + nt], WV[:, k, :], k == 0, k == 7, [dW, dHT[min(i // 4, 4)]], [dpv])
                    if i < 16:
                        self.cp("act", vlh[0:nt, i, 0, 0:64], pv[0:nt, 0:64], [dpv], [dV[i]])
                        self.cp("dve", vlh[0:nt, i, 1, 64:128], pv[0:nt, 0:64], [dpv], [dV[i]])
                    else:
                        self.cp("act", vtm2[0:nt, i, 0:64], pv[0:nt, 0:64], [dpv], [dV[i]])
                        self.cp("dve", vtm2[0:nt, i, 64:128], pv[0:nt, 0:64], [dpv], [dV[i]])
                    if i >= 15:
                        self.cp("act", vfin[0:nt, i - 15, g * 64:(g + 1) * 64], pv[0:nt, 0:64], [dpv], [dVf])
                if lvl < 2:
                    continue
                pk, dpk = self.bank()
                self.tr(pk[:, 0:64], kfin[0:64, 0:128], [dKf], [dpk])
                self.tr(pk[0:64, 64:128], kfin[0:64, 128:192], [dKf], [dpk])
                self.cp("act", kout_p[:, g * 64:(g + 1) * 64], pk[:, 0:64], [dpk], [dKo])
                self.cp("act", knew_s[0:64, g * 64:(g + 1) * 64], pk[0:64, 64:128], [dpk], [dKs])
                if lvl < 3:
                    continue
                def attA(n):
                    q0 = 128 * n
                    kbs = ([n - 1] if n > 0 else []) + [n]
                    for ki, kb in enumerate(kbs):
                        psA, dpsA = self.bank(); psB, dpsB = self.bank()
                        for c in range(2):
                            self.mm(psA[:, c * 128:(c + 1) * 128], kT2[0:64, 128 * kb:128 * kb + 128], qT[0:64, c, q0:q0 + 128], True, True,
                                    [dK[kb // 4], dQ[n // 4]], [dpsA])
                            self.mm(psB[:, c * 128:(c + 1) * 128], kT2[64:128, 128 * kb:128 * kb + 128], qT[64:128, c, q0:q0 + 128], True, True,
                                    [dK[kb // 4], dQ[n // 4]], [dpsB])
                        sl = (2 * n + ki) % 4
                        self.act(pT[sl][:, 0:256], psA[:, 0:256], AF.Exp, [dpsA], [dpT[sl]], scale=0.125)
                        self.act(pT[sl][:, 256:512], psB[:, 0:256], AF.Exp, [dpsB], [dpT[sl]], scale=0.125)
                        msk = mle if kb == n else mgt
                        self.tt("dve", pTm[sl][:].rearrange("p (r q) -> p r q", r=4), pT[sl][:].rearrange("p (r q) -> p r q", r=4),
                                msk[:, :].unsqueeze(1).to_broadcast([128, 4, 128]), ALU.mult, [dpT[sl], dMb], [dpTm[sl]])

                esv_ = esink[:, 4 * g:4 * g + 4].rearrange("p (c h) -> p h c", h=2)
                self.cp("dve", esg[0:64, :], esv_[0:64, 0, :], [dEs], [dEsg])
                self.cp("dve", esg[64:128, :], esv_[64:128, 1, :], [dEs], [dEsg])

                def attB(n):
                    q0 = 128 * n
                    kbs = ([n - 1] if n > 0 else []) + [n]
                    pnum, dpnum = self.bank(); pden, dpden = self.bank()
                    nmm = 2 * len(kbs)
                    j = 0
                    for ki, kb in enumerate(kbs):
                        sl = (2 * n + ki) % 4
                        for hh in range(2):
                            self.mm(pnum[:, 0:256], vlh[:, kb, hh, :], pTm[sl][:, hh * 256:(hh + 1) * 256], j == 0, j == nmm - 1,
                                    [dV[kb], dpTm[sl]], [dpnum])
                            self.mm(pden[:, 0:256], ones_lh[:, hh, :], pTm[sl][:, hh * 256:(hh + 1) * 256], j == 0, j == nmm - 1,
                                    [dOlh, dpTm[sl]], [dpden])
                            j += 1
                    rc = recs[n % 2]; drc = dRecs[n % 2]
                    self.tt("dve", rc[:, 0:256].rearrange("p (c q) -> p c q", c=2), pden[:, 0:256].rearrange("p (c q) -> p c q", c=2),
                            esg[:, :].unsqueeze(2).to_broadcast([128, 2, 128]), ALU.add, [dpden, dEsg], [drc])
                    self.act(rc[:, 0:256], rc[:, 0:256], AF.Ln, [drc], [drc])
                    self.act(rc[:, 0:256], rc[:, 0:256], AF.Exp, [drc], [drc], scale=-1.0)
                    self.tt("dve", oaT[:, 2 * g:2 * g + 2, q0:q0 + 128], pnum[:, 0:256].rearrange("p (c q) -> p c q", c=2),
                            rc[:, 0:256].rearrange("p (c q) -> p c q", c=2), ALU.mult, [dpnum, drc], [dOA[n // 4]])

                attA(0)
                for n in range(16):
                    if n + 1 < 16:
                        attA(n + 1)
                    attB(n)
                if lvl < 4:
                    continue
                gp, go = g // 2, (g % 2) * 64
                self.cp("act", KcT2[0:64, :, :], KcTn[go:go + 64, gp, :, :], [dKn], [dKc2])
                self.cp("dve", KcT2[64:128, :, :], KcTn[go:go + 64, gp, :, :], [dKn], [dKc2])
                for b0 in range(0, 16, 4):
                    self.ldc(Vc2[:, b0:b0 + 4, 0:64], cvv[:, b0:b0 + 4, g * 64:(g + 1) * 64], [dVc2])
                    self.ldc(Vc2[:, b0:b0 + 4, 64:128], cvv[:, b0:b0 + 4, g * 64:(g + 1) * 64], [dVc2])
                pscA, dpscA = self.bank(); pscB, dpscB = self.bank()
                psnA, dpsnA = self.bank(); psnB, dpsnB = self.bank()
                for b in range(16):
                    for c in range(2):
                        c0 = b * 8 + c * 4
                        self.mm(pscA[:, c0:c0 + 4], KcT2[0:64, b, :], qT[0:64, c, SEQ + 4 * b:SEQ + 4 * b + 4], True, True, [dKc2, dQ[4]], [dpscA])
                        self.mm(pscB[:, c0:c0 + 4], KcT2[64:128, b, :], qT[64:128, c, SEQ + 4 * b:SEQ + 4 * b + 4], True, True, [dKc2, dQ[4]], [dpscB])
                for c in range(2):
                    for hh, (pp, dpp) in enumerate(((psnA, dpsnA), (psnB, dpsnB))):
                        off = hh * 64
                        self.mm(pp[0:64, 0:128].rearrange("p (b c t) -> p b c t", b=16, c=2)[:, :, c, :], kT2[off:off + 64, SEQ:NT],
                                qT[off:off + 64, c, SEQ:NT].rearrange("p (b t) -> p b t", t=4), True, True, [dK[4], dQ[4]], [dpp])
                self.act(pTn[:, 0:128], pscA[:, 0:128], AF.Exp, [dpscA], [dpTn], scale=0.125)
                self.act(pTn[:, 128:256], pscB[:, 0:128], AF.Exp, [dpscB], [dpTn], scale=0.125)
                self.tt("dve", pTnm[:, :].rearrange("p (a t) -> p a t", t=4), pTn[:, :].rearrange("p (a t) -> p a t", t=4),
                        mcache[:, :].unsqueeze(1).to_broadcast([128, 64, 4]), ALU.mult, [dpTn, dMb], [dpTnm])
                sl = cnt[0] % 2; cnt[0] += 1
                self.act(pT[sl][0:64, 0:128], psnA[0:64, 0:128], AF.Exp, [dpsnA], [dpT[sl]], scale=0.125)
                self.act(pT[sl][0:64, 128:256], psnB[0:64, 0:128], AF.Exp, [dpsnB], [dpT[sl]], scale=0.125)
                for hh in range(2):
                    self.tt("dve", pTm[sl][0:64, hh * 128:(hh + 1) * 128].rearrange("p (b c t) -> p b c t", b=16, c=2),
                            pT[sl][0:64, hh * 128:(hh + 1) * 128].rearrange("p (b c t) -> p b c t", b=16, c=2),
                            mblkc[0:64, 0:64].rearrange("p (b t) -> p b t", t=4).unsqueeze(2).to_broadcast([64, 16, 2, 4]), ALU.mult,
                            [dpT[sl], dMb], [dpTm[sl]])
                pnum, dpnum = self.bank(); pden, dpden = self.bank()
                self.mm(pnum[:, 0:256], vtm2[0:64, 16, :], pTm[sl][0:64, 0:256], True, False, [dV[16], dpTm[sl]], [dpnum])
                self.mm(pden[:, 0:256], ones[0:64, :], pTm[sl][0:64, 0:256], True, False, [dOnes, dpTm[sl]], [dpden])
                for b in range(16):
                    pv_ = pTnm[:, :].rearrange("p (h b x) -> p h b x", h=2, b=16)[:, :, b, :]
                    self.mm(pnum[:, 0:256].rearrange("p (h b x) -> p h b x", h=2, b=16)[:, :, b, :], Vc2[:, b, :], pv_, False, True,
                            [dVc2, dpTnm], [dpnum])
                    self.mm(pden[:, 0:256].rearrange("p (h b x) -> p h b x", h=2, b=16)[:, :, b, :], ones[:, :], pv_, False, True,
                            [dOnes, dpTnm], [dpden])
                esv = esink[:, 4 * g:4 * g + 4].rearrange("p (c h) -> p h c", h=2)
                for hh in range(2):
                    self.tt("dve", rec[:, hh * 128:(hh + 1) * 128].rearrange("p (b c t) -> p b c t", b=16, c=2),
                            pden[:, hh * 128:(hh + 1) * 128].rearrange("p (b c t) -> p b c t", b=16, c=2),
                            esv[:, hh, :].unsqueeze(1).unsqueeze(3).to_broadcast([128, 16, 2, 4]), ALU.add, [dpden, dEs], [dRec])
                self.recip(rec[:, 0:256], rec[:, 0:256], [dRec], [dRec])
                for hh in range(2):
                    off = hh * 64
                    for c in range(2):
                        nv = pnum[off:off + 64, hh * 128:(hh + 1) * 128].rearrange("p (b c t) -> p b c t", b=16, c=2)[:, :, c, :]
                        rv = rec[off:off + 64, hh * 128:(hh + 1) * 128].rearrange("p (b c t) -> p b c t", b=16, c=2)[:, :, c, :]
                        self.tt("dve", oaT[off:off + 64, 2 * g + c, SEQ:NT].rearrange("p (b t) -> p b t", t=4), nv, rv, ALU.mult,
                                [dpnum, dRec], [dOA[4]])
            if lvl < 6:
                S.emit()
                return
            self.store(L["kp"][:, :], kout_p[:], [dKo])
            self.store(L["vp"][:, :], vfin[:, 0, :], [dVf])
            for b in range(16):
                self.store(L["ks"][b, 0:124, :], L["ck"][b, 4:128, :], [])
                self.store(L["vs"][b, 0:124, :], L["cv"][b, 4:128, :], [])
                self.store(L["ks"][b, 124:128, :], knew_s[4 * b:4 * b + 4, :], [dKs])
                self.store(L["vs"][b, 124:128, :], vfin[4 * b:4 * b + 4, 1, :], [dVf])
            S.emit()

    def mlstm(self, L, hT, dHT, hgT, dHG):
        nc, S = self.nc, self.S
        w_in = L["w_in"]
        MQ0, MK0, MV0, MO0, MG0 = 0, 512, 1024, 2048, 3072
        KS = 128.0 ** -0.5
        wsrc = lambda c0, n: w_in[:, c0:c0 + n].rearrange("(kc p) n -> p kc n", p=128)
        with ExitStack() as es:
            sb = lambda name, shape, dt=F32: es.enter_context(nc.sbuf_tensor(self.uname(name), list(shape), dt))
            A_tm = sb("A_tm", [128, NTILE, 4]); B_tm = sb("B_tm", [128, NTILE, 4]); MendTok = sb("MendTok", [128, NTILE, 4])
            W_tm = sb("W_tm", [128, NTILE, 4]); THR = sb("THR", [128, NTILE, 4]); dGt = Dep()
            ECrep = sb("ECrep", [128, 4, 32]); MendRep = sb("MendRep", [128, 4, 16]); dRep = Dep()
            EcTok = sb("EcTok", [128, 4]); dEt = Dep()
            mk = sb("mk", [128, 768]); dMk = Dep()
            self.ld(mk[:], L["cmask"][:, :], [dMk])
            mle_f = mk[:, 0:128]; mblkc_f = mk[:, 256:384]; blk_f = mk[:, 384:512]; rmask_f = mk[:, 640:656]
            n0tok = sb("n0tok", [128, 512]); dN0 = Dep()
            nnew = sb("nnew", [128, 512]); dNn = Dep()
            for b in range(16):
                self.ld(n0tok[4 * b:4 * b + 4, :], L["stn"][b:b + 1, :].broadcast_to([4, 512]), [dN0])
            for t_ in (A_tm, B_tm, MendTok):
                self.memset("pool", t_[:], 0.0, [dGt])
            with ExitStack() as es2:
                sb2 = lambda name, shape, dt=F32: es2.enter_context(nc.sbuf_tensor(self.uname(name), list(shape), dt))
                Wg = sb2("Wg", [128, 8, 8], BF16); dWg = Dep()
                self.ldc(Wg[:], wsrc(MG0, 8), [dWg])
                bigf = sb2("bigf", [128, 8]); dBg = Dep()
                self.ld(bigf[:, 0:4], L["b_ig"][0:1, :].broadcast_to([128, 4]), [dBg])
                self.ld(bigf[:, 4:8], L["b_fg"][0:1, :].broadcast_to([128, 4]), [dBg])
                g_tm = sb2("g_tm", [128, NTILE, 8]); dGtm = Dep()
                GI = sb2("GI", [4, NT]); GF = sb2("GF", [4, NT]); Bf = sb2("Bf", [4, NT]); Mf = sb2("Mf", [4, NT])
                Z = sb2("Z", [4, NT]); T1 = sb2("T1", [4, NT]); T2 = sb2("T2", [4, NT]); dF = Dep()
                ones4 = sb2("ones4", [4, 128]); dO4 = Dep()
                m0T = sb2("m0T", [4, 16]); dM0 = Dep()
                V48 = sb2("V48", [4, 48]); X = sb2("X", [4, 4, 48]); dV = Dep()
                Sx = sb2("Sx", [4, 2, 16, 4]); dSx = Dep()
                mout = sb2("mout", [4, 17]); dMo = Dep()
                self.memset("pool", Z[:], 0.0, [dF]); self.memset("pool", ones4[:], 1.0, [dO4])
                self.memset("pool", g_tm[:], 0.0, [dGtm])
                self.ld_nc(m0T[:], L["stm"].rearrange("b h -> h b"), [dM0])
                for i in range(NTILE):
                    nt = tile_nt(i)
                    pg, dpg = self.bank()
                    for k in range(8):
                        self.mm(pg[0:nt, 0:8], hT[:, k, 128 * i:128 * i + nt], Wg[:, k, :], k == 0, k == 7, [dWg, dHT[min(i // 4, 4)]], [dpg])
                    self.tt("dve", g_tm[0:nt, i, :], pg[0:nt, 0:8], bigf[0:nt, :], ALU.add, [dpg, dBg], [dGtm])
                for b, (t0, tn) in enumerate(TBLK):
                    p1, dp1 = self.bank(); p2, dp2 = self.bank()
                    for j, i in enumerate(blk_tiles(b)):
                        nt = tile_nt(i)
                        self.tr(p1[0:4, j * 128:j * 128 + nt], g_tm[0:nt, i, 0:4], [dGtm], [dp1])
                        self.tr(p2[0:4, j * 128:j * 128 + nt], g_tm[0:nt, i, 4:8], [dGtm], [dp2])
                    self.cp("dve", GI[:, t0:t0 + tn], p1[0:4, 0:tn], [dp1], [dF])
                    self.cp("dve", GF[:, t0:t0 + tn], p2[0:4, 0:tn], [dp2], [dF])
                self.act(T1[:], GF[:], AF.Abs, [dF], [dF])
                self.act(T1[:], T1[:], AF.Exp, [dF], [dF], scale=-1.0)
                self.act(T1[:], T1[:], AF.Ln, [dF], [dF], bias=1.0)
                self.ts("dve", T2[:], GF[:], 0.0, None, ALU.min, None, [dF], [dF])
                self.tt("dve", GF[:], T2[:], T1[:], ALU.subtract, [dF], [dF])
                self.S.op("dve", lambda e: e.tensor_tensor_scan(out=Bf[:, 0:SEQ], data0=GF[:, 0:SEQ], data1=Z[:, 0:SEQ], initial=0.0,
                                                                op0=ALU.add, op1=ALU.add), [dF], [dF])
                gfs = GF[:, SEQ:NT].rearrange("h (b j) -> h b j", j=4); bfs = Bf[:, SEQ:NT].rearrange("h (b j) -> h b j", j=4)
                self.cp("dve", bfs[:, :, 0], gfs[:, :, 0], [dF], [dF])
                for j in range(1, 4):
                    self.tt("dve", bfs[:, :, j], bfs[:, :, j - 1], gfs[:, :, j], ALU.add, [dF], [dF])
                self.tt("dve", GI[:], GI[:], Bf[:], ALU.subtract, [dF], [dF])
                self.S.op("dve", lambda e: e.tensor_tensor_scan(out=Mf[:, 0:SEQ], data0=Z[:, 0:SEQ], data1=GI[:, 0:SEQ], initial=0.0,
                                                                op0=ALU.add, op1=ALU.max), [dF], [dF])
                as_ = GI[:, SEQ:NT].rearrange("h (b j) -> h b j", j=4); ms_ = Mf[:, SEQ:NT].rearrange("h (b j) -> h b j", j=4)
                self.tt("dve", ms_[:, :, 0], as_[:, :, 0], m0T[:, :], ALU.max, [dF, dM0], [dF])
                for j in range(1, 4):
                    self.tt("dve", ms_[:, :, j], ms_[:, :, j - 1], as_[:, :, j], ALU.max, [dF], [dF])
                mend_p = Mf[:, 0:SEQ].rearrange("h (c t) -> h c t", t=128)[:, :, 127]
                self.cp("dve", V48[:, 0:16], mend_p, [dF], [dV])
                self.cp("dve", V48[:, 16:17], Z[:, 0:1], [dF], [dV])
                self.cp("dve", V48[:, 17:32], V48[:, 0:15], [dV], [dV])
                self.tt("dve", V48[:, 16:32], V48[:, 16:32], V48[:, 0:16], ALU.subtract, [dV], [dV])
                self.tt("dve", V48[:, 32:48], m0T[:, :], ms_[:, :, 3], ALU.subtract, [dF, dM0], [dV])
                self.act(V48[:, 16:48], V48[:, 16:48], AF.Exp, [dV], [dV])
                self.tt("dve", X[:], V48[:, :].unsqueeze(1).to_broadcast([4, 4, 48]),
                        self.ident[0:4, 0:4].unsqueeze(2).to_broadcast([4, 4, 48]), ALU.mult, [dV, self.dIdent], [dV])
                pr, dpr = self.bank()
                self.mm(pr[:, 0:192], ones4[:, :], X[:].rearrange("h a c -> h (a c)"), True, True, [dO4, dV], [dpr])
                prv = pr[:, 0:192].rearrange("p (a c) -> p a c", a=4)
                self.cp("dve", MendRep[:], prv[:, :, 0:16], [dpr], [dRep])
                self.cp("dve", ECrep[:], prv[:, :, 16:48], [dpr], [dRep])
                self.cp("dve", Sx[:, 0, :, :], ms_[:, :, 3].unsqueeze(2).to_broadcast([4, 16, 4]), [dF], [dSx])
                self.cp("dve", Sx[:, 1, :, :], V48[:, 32:48].unsqueeze(2).to_broadcast([4, 16, 4]), [dV], [dSx])
                pa_, dpa_ = self.bank(); pb_, dpb_ = self.bank()
                for i in range(NTILE):
                    nt = tile_nt(i)
                    self.tr(pa_[0:nt, i * 4:i * 4 + 4], GI[0:4, 128 * i:128 * i + nt], [dF], [dpa_])
                    self.tr(pb_[0:nt, i * 4:i * 4 + 4], Bf[0:4, 128 * i:128 * i + nt], [dF], [dpb_])
                self.tr(pa_[0:64, 68:72], Sx[0:4, 0, :, :].rearrange("h b j -> h (b j)"), [dSx], [dpa_])
                self.tr(pb_[0:64, 68:72], Sx[0:4, 1, :, :].rearrange("h b j -> h (b j)"), [dSx], [dpb_])
                self.cp("dve", A_tm[:, 0:16, :].rearrange("p c h -> p (c h)"), pa_[:, 0:64], [dpa_], [dGt])
                self.cp("dve", A_tm[0:64, 16, :], pa_[0:64, 64:68], [dpa_], [dGt])
                self.cp("dve", B_tm[:, 0:16, :].rearrange("p c h -> p (c h)"), pb_[:, 0:64], [dpb_], [dGt])
                self.cp("dve", B_tm[0:64, 16, :], pb_[0:64, 64:68], [dpb_], [dGt])
                self.cp("dve", MendTok[0:64, 16, :], pa_[0:64, 68:72], [dpa_], [dGt])
                self.cp("dve", EcTok[0:64, :], pb_[0:64, 68:72], [dpb_], [dEt])
                self.cp("dve", MendTok[:, 0:16, :], MendRep[:].rearrange("p h c -> p c h"), [dRep], [dGt])
                self.tt("dve", W_tm[:], A_tm[:], MendTok[:], ALU.subtract, [dGt], [dGt])
                self.act(W_tm[:], W_tm[:], AF.Exp, [dGt], [dGt])
                self.tt("dve", THR[:], B_tm[:], MendTok[:], ALU.add, [dGt], [dGt])
                self.act(THR[:], THR[:], AF.Exp, [dGt], [dGt], scale=-1.0)
                self.tt("dve", mout[:, 0:1], Mf[:, SEQ - 1:SEQ], Bf[:, SEQ - 1:SEQ], ALU.add, [dF], [dMo])
                self.tt("dve", mout[:, 1:17], ms_[:, :, 3], bfs[:, :, 3], ALU.add, [dF], [dMo])
                self.store(L["mp"][:, :], mout[:, 0:1], [dMo])
                self.store_nc(L["ms"].rearrange("b h -> h b"), mout[:, 1:17], [dMo])
                S.emit()
            Wm = sb("Wm", [128, 8, 768], BF16); dWm = Dep()
            qTh = sb("qTh", [128, NT], BF16); kTh = sb("kTh", [128, NT], BF16); dQK = [Dep() for _ in range(5)]
            k_tm = sb("k_tm", [128, NTILE, 128], BF16); v_aug = sb("v_aug", [128, NTILE, 258], BF16); so_tm = sb("so_tm", [128, NTILE, 256], BF16)
            dTM = [Dep() for _ in range(NTILE)]
            ST = sb("ST", [128, 257]); dST = Dep()
            tmpSs = [sb(f"tmpS{i}", [128, 128]) for i in range(2)]; dtSs = [Dep(), Dep()]
            sTms = [sb(f"sTm{i}", [128, 128], BF16) for i in range(3)]; dsTs = [Dep() for _ in range(3)]
            wvs = [sb(f"wv{i}", [128, 258], BF16) for i in range(3)]; dwvs = [Dep() for _ in range(3)]
            tmpS = tmpSs[0]; dtS = dtSs[0]
            sTm = sb("sTm_s", [128, 64], BF16); dsT = Dep()
            Cdec = sb("Cdec", [128, 258], BF16); dCd = Dep()
            junk = sb("junk", [128, 256], BF16); dJ = Dep()
            wv = sb("wv_s", [128, 258], BF16); dwv = Dep()
            bmk = sb("bmk", [128, 16, 64], BF16); dBm = Dep()
            self.ldc(bmk[:].rearrange("p b t -> p (b t)"), L["bmask"][:, :], [dBm])
            C0n = sb("C0n", [128, 8, 2, 128]); dC0 = Dep()
            C0T = sb("C0T", [128, 16, 258], BF16); dC0T = Dep()
            n0T = sb("n0T", [128, 16]); dn0T = Dep()
            maskE = sb("maskE", [128, 16, 64], BF16); dmE = Dep()
            Qblk = sb("Qblk", [128, 16, 64], BF16); dQb = Dep()
            wvblk = sb("wvblk", [128, 8, 256], BF16); dwb = Dep()
            Cout = [sb(f"Cout{i}", [128, 2, 128]) for i in range(2)]; dCo = [Dep(), Dep()]
            BselW = sb("BselW", [128, 64], BF16); dBs = Dep()
            self.memset("pool", v_aug[:], 1.0, dTM)

            Hn = sb("Hn", [128, NTILE, 257]); dHn = [Dep() for _ in range(NTILE)]
            if os.environ.get("MK_PAD"):
                sb("pad", [128, int(os.environ["MK_PAD"]) * 256])
            hst = sb("hst", [128, 8, NTILE]); dhst = Dep(); dhst2 = Dep()

            def hn_pre(h):
                dn = hst[:, 0, :]
                self.memset("pool", hst[:], 0.0, [dhst])
                self.act(dn, Hn[:, :, 256], AF.Abs, dHn, [dhst])
                self.tt("dve", dn, dn, THR[:, :, h], ALU.max, [dGt], [dhst])
                self.recip(dn, dn, [dhst], [dhst2])

            def hn_stats(h, i):
                nt = tile_nt(i)
                self.act(Hn[0:nt, i, 0:256], Hn[0:nt, i, 0:256], AF.Identity, [dhst2], [dHn[i], dhst], scale=hst[0:nt, 0, i:i + 1],
                         accum=hst[0:nt, 1, i:i + 1])
                self.act(junk[0:nt, :], Hn[0:nt, i, 0:256], AF.Square, [dHn[i]], [dJ, dhst], accum=hst[0:nt, 2, i:i + 1])

            def hn_post(h):
                dn = hst[:, 0, :]; s1 = hst[:, 1, :]; s2 = hst[:, 2, :]; mean = hst[:, 3, :]; msq = hst[:, 4, :]
                rstd = hst[:, 5, :]; nb = hst[:, 6, :]
                self.ts("dve", mean, s1, 1.0 / 256, None, ALU.mult, None, [dhst], [dhst])
                self.tt("dve", msq, mean, mean, ALU.mult, [dhst], [dhst])
                self.ts("dve", rstd, s2, 1.0 / 256, HN_EPS, ALU.mult, ALU.add, [dhst], [dhst])
                self.tt("dve", rstd, rstd, msq, ALU.subtract, [dhst], [dhst])
                self.act(rstd, rstd, AF.Sqrt, [dhst], [dhst])
                self.recip(rstd, rstd, [dhst], [dhst])
                self.tt("dve", nb, mean, rstd, ALU.mult, [dhst], [dhst])
                self.ts("dve", nb, nb, -1.0, None, ALU.mult, None, [dhst], [dhst])
                pts = {}

                def n1(i):
                    nt = tile_nt(i)
                    self.act(Hn[0:nt, i, 0:256], Hn[0:nt, i, 0:256], AF.Identity, [dhst], [dHn[i]], bias=hst[0:nt, 6, i:i + 1],
                             scale=hst[0:nt, 5, i:i + 1])
                    self.tt("dve", Hn[0:nt, i, 0:256], Hn[0:nt, i, 0:256], so_tms[h % 2][0:nt, i, :], ALU.mult, [dSO[h % 2][i]], [dHn[i]])

                def n2(i):
                    nt = tile_nt(i)
                    pt, dpt = self.bank()
                    for vb in range(2):
                        self.tr(pt[:, vb * 128:vb * 128 + nt], Hn[0:nt, i, vb * 128:(vb + 1) * 128], [dHn[i]], [dpt])
                    pts[i] = (pt, dpt)

                def n3(i):
                    nt = tile_nt(i)
                    t0 = 128 * i
                    pt, dpt = pts.pop(i)
                    self.cp("act" if i % 2 == 0 else "dve", hgT[:, 2 * h:2 * h + 2, t0:t0 + nt],
                            pt[:, 0:256].rearrange("p (a b) -> p a b", a=2)[:, :, 0:nt], [dpt], [dHG[min(i // 4, 4)]])

                for step in range(NTILE + 4):
                    if step < NTILE:
                        n1(step)
                    if 0 <= step - 2 < NTILE:
                        n2(step - 2)
                    if 0 <= step - 4 < NTILE:
                        n3(step - 4)

            def head_tail(po, dpo, i, h, nt):
                self.cp("act", Hn[0:nt, i, :], po[0:nt, 0:257], [dpo], [dHn[i]])

            Wm2 = sb("Wm2", [128, 8, 768], BF16); dWm2 = Dep()
            so_tm2 = sb("so_tm2", [128, NTILE, 256], BF16)
            Wms = [Wm, Wm2]; dWms = [dWm, dWm2]; so_tms = [so_tm, so_tm2]
            dSO = [[Dep() for _ in range(NTILE)] for _ in range(2)]

            def load_w(h):
                W_ = Wms[h % 2]; d_ = dWms[h % 2]
                self.ldc(W_[:, :, 0:128], wsrc(MQ0 + h * 128, 128), [d_])
                self.ldc(W_[:, :, 128:256], wsrc(MK0 + h * 128, 128), [d_])
                self.ldc(W_[:, :, 256:512], wsrc(MV0 + h * 256, 256), [d_])
                self.ldc(W_[:, :, 512:768], wsrc(MO0 + h * 256, 256), [d_])

            def c0n_load(h, half):
                for bq in range(8):
                    self.ld(C0n[:, bq, :, :], L["stC"][8 * half + bq, h].rearrange("(vb p) d -> p vb d", p=128), [dC0])

            def proj(h, per_tile=None):
                    for b, (t0, tn) in enumerate(TBLK):
                        pq, dpq = self.bank(); pk, dpk = self.bank()
                        for k in range(8):
                            self.mm(pq[:, 0:tn], Wms[h % 2][:, k, 0:128], hT[:, k, t0:t0 + tn], k == 0, k == 7, [dWms[h % 2], dHT[b]], [dpq])
                        for k in range(8):
                            self.mm(pk[:, 0:tn], Wms[h % 2][:, k, 128:256], hT[:, k, t0:t0 + tn], k == 0, k == 7, [dWms[h % 2], dHT[b]], [dpk])
                        self.cp("act", qTh[:, t0:t0 + tn], pq[:, 0:tn], [dpq], [dQK[b]])
                        self.S.op("act", lambda e, t0=t0, tn=tn, pk=pk: e.mul(out=kTh[:, t0:t0 + tn], in_=pk[:, 0:tn], mul=KS), [dpk], [dQK[b]])
                    for i in range(NTILE):
                        nt = tile_nt(i)
                        pa, dpa = self.bank(); pb_, dpb_ = self.bank()
                        for k in range(8):
                            self.mm(pa[0:nt, 0:384], hT[:, k, 128 * i:128 * i + nt], Wms[h % 2][:, k, 128:512], k == 0, k == 7, [dWms[h % 2], dHT[min(i // 4, 4)]], [dpa])
                        for k in range(8):
                            self.mm(pb_[0:nt, 0:256], hT[:, k, 128 * i:128 * i + nt], Wms[h % 2][:, k, 512:768], k == 0, k == 7, [dWms[h % 2], dHT[min(i // 4, 4)]], [dpb_])
                        self.S.op("act", lambda e, i=i, nt=nt, pa=pa: e.mul(out=k_tm[0:nt, i, :], in_=pa[0:nt, 0:128], mul=KS), [dpa], [dTM[i]])
                        self.cp("dve", v_aug[0:nt, i, 0:256], pa[0:nt, 128:384], [dpa], [dTM[i]])
                        self.act(so_tms[h % 2][0:nt, i, :], pb_[0:nt, 0:256], AF.Sigmoid, [dpb_], [dSO[h % 2][i]])
                        if per_tile is not None:
                            per_tile(i)

            def body_(h):
                    i = 16
                    ps_, dps = self.bank()
                    self.mm(ps_[0:64, 0:64], kTh[:, SEQ:NT], qTh[:, SEQ:NT], True, True, [dQK[4]], [dps])
                    self.act(tmpS[0:64, 0:64], ps_[0:64, 0:64], AF.Identity, [dps, dGt], [dtS], scale=W_tm[0:64, 16, h:h + 1])
                    self.tt("dve", sTm[0:64, 0:64], tmpS[0:64, 0:64], mblkc_f[0:64, 0:64], ALU.mult, [dtS, dMk], [dsT])
                    self.tt("dve", maskE[:], bmk[:], ECrep[:, h, 16:32].unsqueeze(2).to_broadcast([128, 16, 64]), ALU.mult, [dBm, dRep], [dmE])
                    self.tt("dve", Qblk[:], maskE[:], qTh[:, SEQ:NT].unsqueeze(1).to_broadcast([128, 16, 64]), ALU.mult, [dmE, dQK[4]], [dQb])
                    pn0, dpn0 = self.bank()
                    self.tr(pn0[:, 0:64], n0tok[0:64, h * 128:(h + 1) * 128], [dN0], [dpn0])
                    self.cp("dve", n0T[:], pn0[:, 0:64].rearrange("p (b j) -> p b j", j=4)[:, :, 0], [dpn0], [dn0T])
                    self.act(wv[0:64, 0:257], v_aug[0:64, 16, 0:257], AF.Identity, [dTM[16], dGt], [dwv], scale=W_tm[0:64, 16, h:h + 1])
                    def halfproc(half):
                            b0 = 8 * half
                            for bb in range(8):
                                if bb % 2 == 0:
                                    pt, dpt = self.bank()
                                for vb in range(2):
                                    c0 = (bb % 2) * 256 + vb * 128
                                    self.tr(pt[:, c0:c0 + 128], C0n[:, bb, vb, :], [dC0], [dpt])
                                if bb % 2 == 1:
                                    self.cp("act", C0T[:, b0 + bb - 1:b0 + bb + 1, 0:256], pt[:, :].rearrange("p (a b) -> p a b", a=2), [dpt], [dC0T])
                            self.tt("dve", wvblk[0:64, :, :], wv[0:64, 0:256].unsqueeze(1).to_broadcast([64, 8, 256]),
                                    rmask_f[0:64, b0:b0 + 8].unsqueeze(2).to_broadcast([64, 8, 256]), ALU.mult, [dwv, dMk], [dwb])
                            for bb in range(8):
                                b = b0 + bb
                                pC, dpC = self.bank()
                                for vb in range(2):
                                    self.mm(pC[:, vb * 128:(vb + 1) * 128], wvblk[0:64, bb, vb * 128:(vb + 1) * 128], k_tm[0:64, 16, :], True, True,
                                            [dwb, dTM[16]], [dpC])
                                co = b % 2
                                self.stt("dve", Cout[co][:].rearrange("p a b -> p (a b)"), C0n[:, bb, :, :].rearrange("p a b -> p (a b)"),
                                         ECrep[:, h, 16 + b:17 + b], pC[:, 0:256], ALU.mult, ALU.add, [dC0, dRep, dpC], [dCo[co]])
                                self.store(L["Cs"][b, h].rearrange("(vb p) d -> p vb d", p=128), Cout[co][:], [dCo[co]])
                    halfproc(0)
                    c0n_load(h, 1)
                    pcs = {}

                    def pre(c):
                        t0 = 128 * c
                        ps_, dps = self.bank()
                        self.mm(ps_[:, 0:128], kTh[:, t0:t0 + 128], qTh[:, t0:t0 + 128], True, True, [dQK[c // 4]], [dps])
                        a = c % 2; b3 = c % 3
                        self.act(tmpSs[a][:], ps_[:, 0:128], AF.Identity, [dps, dGt], [dtSs[a]], scale=W_tm[:, c, h:h + 1])
                        self.tt("dve", sTms[b3][:], tmpSs[a][:], mle_f, ALU.mult, [dtSs[a], dMk], [dsTs[b3]])
                        self.act(wvs[b3][:, 0:257], v_aug[:, c, 0:257], AF.Identity, [dTM[c], dGt], [dwvs[b3]], scale=W_tm[:, c, h:h + 1])
                        pc, dpc = self.bank()
                        self.mm(pc[:, 0:257], k_tm[:, c, :], wvs[b3][:, 0:257], True, True, [dTM[c], dwvs[b3]], [dpc])
                        pcs[c] = (pc, dpc)

                    def main(c):
                        t0 = 128 * c
                        b3 = c % 3
                        po, dpo = self.bank()
                        self.mm(po[:, 0:257], sTms[b3][:], v_aug[:, c, 0:257], True, c == 0, [dsTs[b3], dTM[c]], [dpo])
                        if c > 0:
                            self.act(Cdec[:, 0:257], ST[:], AF.Identity, [dST, dRep], [dCd], scale=ECrep[:, h, c:c + 1])
                            self.mm(po[:, 0:257], qTh[:, t0:t0 + 128], Cdec[:, 0:257], False, True, [dQK[c // 4], dCd], [dpo])
                        head_tail(po, dpo, c, h, 128)
                        pc, dpc = pcs.pop(c)
                        if c == 0:
                            self.cp("dve", ST[:], pc[:, 0:257], [dpc], [dST])
                        else:
                            self.stt("dve", ST[:], ST[:], ECrep[:, h, c:c + 1], pc[:, 0:257], ALU.mult, ALU.add, [dpc, dRep], [dST])

                    pre(0)
                    for c in range(16):
                        if c + 1 < 16:
                            pre(c + 1)
                        main(c)
                    pt, dpt = self.bank()
                    for vb in range(2):
                        self.tr(pt[:, vb * 128:(vb + 1) * 128], ST[:, vb * 128:(vb + 1) * 128], [dST], [dpt])
                    self.cp("dve", Cout[0][:].rearrange("p a b -> p (a b)"), pt[:, 0:256], [dpt], [dCo[0]])
                    self.store(L["Cp"][h].rearrange("(vb p) d -> p vb d", p=128), Cout[0][:], [dCo[0]])
                    self.store_nc(L["np"][h:h + 1, :].rearrange("o d -> d o"), ST[:, 256:257], [dST])
                    halfproc(1)
                    self.cp("dve", C0T[:, :, 256], n0T[:, :], [dn0T], [dC0T])
                    po, dpo = self.bank()
                    self.mm(po[0:64, 0:257], sTm[0:64, 0:64], v_aug[0:64, 16, 0:257], True, False, [dsT, dTM[16]], [dpo])
                    for b in range(16):
                        self.mm(po[0:64, 0:257], Qblk[:, b, :], C0T[:, b, 0:257], False, b == 15, [dQb, dC0T], [dpo])
                    head_tail(po, dpo, 16, h, 64)
                    self.act(BselW[0:64, :], blk_f[0:64, 0:64], AF.Identity, [dMk, dGt], [dBs], scale=W_tm[0:64, 16, h:h + 1])
                    pN, dpN = self.bank()
                    self.mm(pN[0:64, 0:128], BselW[0:64, :], k_tm[0:64, 16, :], True, True, [dBs, dTM[16]], [dpN])
                    self.stt("dve", nnew[0:64, h * 128:(h + 1) * 128], n0tok[0:64, h * 128:(h + 1) * 128], EcTok[0:64, h:h + 1], pN[0:64, 0:128],
                             ALU.mult, ALU.add, [dN0, dEt, dpN], [dNn])

            load_w(0); c0n_load(0, 0); proj(0); load_w(1)
            for h in range(4):
                body_(h)
                hn_pre(h)
                if h + 1 < 4:
                    c0n_load(h + 1, 0)
                    proj(h + 1, per_tile=lambda i, h=h: hn_stats(h, i))
                    if h + 2 < 4:
                        load_w(h + 2)
                else:
                    for i in range(NTILE):
                        hn_stats(h, i)
                hn_post(h)
            for b in range(16):
                self.store(L["ns"][b:b + 1, :], nnew[4 * b:4 * b + 1, :], [dNn])
            S.emit()

    def ln_tile(self, x, dX, nt, lnG, lnB, dLn, st, dSt, junk, dJ, eps):
        self.memset("pool", st[0:nt, 0:2], 0.0, [dSt])
        self.act(junk[0:nt, :], x, AF.Identity, [dX], [dJ, dSt], accum=st[0:nt, 0:1])
        self.act(junk[0:nt, :], x, AF.Square, [dX], [dJ, dSt], accum=st[0:nt, 1:2])
        self.ts("dve", st[0:nt, 2:3], st[0:nt, 0:1], 1.0 / D, None, ALU.mult, None, [dSt], [dSt])
        self.tt("dve", st[0:nt, 3:4], st[0:nt, 2:3], st[0:nt, 2:3], ALU.mult, [dSt], [dSt])
        self.ts("dve", st[0:nt, 4:5], st[0:nt, 1:2], 1.0 / D, eps, ALU.mult, ALU.add, [dSt], [dSt])
        self.tt("dve", st[0:nt, 4:5], st[0:nt, 4:5], st[0:nt, 3:4], ALU.subtract, [dSt], [dSt])
        self.act(st[0:nt, 4:5], st[0:nt, 4:5], AF.Sqrt, [dSt], [dSt])
        self.recip(st[0:nt, 5:6], st[0:nt, 4:5], [dSt], [dSt])
        self.tt("dve", st[0:nt, 6:7], st[0:nt, 2:3], st[0:nt, 5:6], ALU.mult, [dSt], [dSt])
        self.ts("dve", st[0:nt, 6:7], st[0:nt, 6:7], -1.0, None, ALU.mult, None, [dSt], [dSt])
        self.act(x, x, AF.Identity, [dSt], [dX], bias=st[0:nt, 6:7], scale=st[0:nt, 5:6])
        self.tt("dve", x, x, lnG[0:nt, :], ALU.mult, [dLn], [dX])
        self.tt("dve", x, x, lnB[0:nt, :], ALU.add, [dLn], [dX])

    def merge(self, L, hT, dHT, hgT, dHG, next_ada=None):
        nc, S = self.nc, self.S
        w_in = L["w_in"]
        GM0, GA0 = 4616, 5640
        wsrc = lambda w, c0, n: w[:, c0:c0 + n].rearrange("(kc p) n -> p kc n", p=128)
        with ExitStack() as es:
            sb = lambda name, shape, dt=F32: es.enter_context(nc.sbuf_tensor(self.uname(name), list(shape), dt))
            oaT = sb("oaT2", [128, 8, NT], BF16); dOA = Dep()
            self.ld(oaT[:].rearrange("p c t -> p (c t)"), L["oa_scr"][:, :], [dOA])
            Wa = sb("Wa", [128, 8, D], BF16); dWa = Dep()
            Wg = sb("Wg", [128, 8, D], BF16); dWg = Dep()
            mraw = sb("mraw", [8, 128]); dMr = Dep(); mngT = sb("mngT", [128, 8]); dMn = Dep()
            zt = sb("zt", [128, 8, 512], BF16); dzt = Dep()
            sg = [sb(f"sg{i}", [128, 512]) for i in range(2)]; dsg = [Dep(), Dep()]
            za = [sb(f"za{i}", [128, 512]) for i in range(2)]; dza = [Dep(), Dep()]
            xt = sb("xt", [128, 3, D]); dxt = [Dep(), Dep(), Dep()]
            tG = [sb(f"tG{i}", [128, 512]) for i in range(2)]; dtG = [Dep(), Dep()]
            lnG = sb("lnG", [128, D]); lnB = sb("lnB", [128, D]); dLn = Dep()
            sts = [sb(f"st{i}", [128, 8]) for i in range(2)]; dSts = [Dep(), Dep()]
            junk = sb("junk", [128, D], BF16); dJ = Dep()
            self.ld(lnG[:], L["ln"]["ln2_g"][0:1, :].broadcast_to([128, D]), [dLn])
            self.ld(lnB[:], L["ln"]["ln2_b"][0:1, :].broadcast_to([128, D]), [dLn])
            self.ld(mraw[:], L["mng"][:, :], [dMr])
            pm, dpm = self.bank()
            self.tr(pm[:, 0:8], mraw[0:8, :], [dMr], [dpm])
            self.cp("dve", mngT[:], pm[:, 0:8], [dpm], [dMn])
            self.ldc(Wa[:], wsrc(L["w_bm"], 0, D), [dWa])
            for k in range(8):
                self.act(Wa[:, k, :], Wa[:, k, :], AF.Identity, [dMn], [dWa], scale=mngT[:, k:k + 1])
            self.ldc(Wg[:], wsrc(w_in, GM0, D), [dWg])
            cnt = [0]

            def branch(src, dsrc, stage):
                for b, (t0, tn) in enumerate(TBLK):
                    for oc in range(8):
                        py, dpy = self.bank(); pg, dpg = self.bank()
                        for k in range(8):
                            self.mm(py[:, 0:tn], Wa[:, k, oc * 128:(oc + 1) * 128], src[:, k, t0:t0 + tn], k == 0, k == 7, [dWa, dsrc(b)], [dpy])
                        for k in range(8):
                            self.mm(pg[:, 0:tn], Wg[:, k, oc * 128:(oc + 1) * 128], hT[:, k, t0:t0 + tn], k == 0, k == 7, [dWg, dHT[b]], [dpg])
                        i = cnt[0] % 2; cnt[0] += 1
                        self.act(sg[i][:, 0:tn], pg[:, 0:tn], AF.Sigmoid, [dpg], [dsg[i]])
                        if stage == 1:
                            self.tt("dve", zt[:, oc, 0:tn], sg[i][:, 0:tn], py[:, 0:tn], ALU.mult, [dsg[i], dpy], [dzt])
                        else:
                            self.tt("dve", za[i][:, 0:tn], sg[i][:, 0:tn], py[:, 0:tn], ALU.mult, [dsg[i], dpy], [dza[i]])
                            self.tt("pool", hgT[:, oc, t0:t0 + tn], hgT[:, oc, t0:t0 + tn], za[i][:, 0:tn], ALU.add, [dza[i]], [dHG[b]])
                    if stage == 1:
                        self.cp("pool", hgT[:, :, t0:t0 + tn], zt[:, :, 0:tn], [dzt], [dHG[b]])

            branch(hgT, lambda b: dHG[b], 1)
            self.ldc(Wa[:], wsrc(L["w_ba"], 0, D), [dWa])
            self.ldc(Wg[:], wsrc(w_in, GA0, D), [dWg])
            branch(oaT, lambda b: dOA, 2)
            if "mixT" in self.debug:
                self.dbg("mixT", hgT[:], dHG, [128, 8, NT])
            S.emit()
        with ExitStack() as es:
            sb = lambda name, shape, dt=F32: es.enter_context(nc.sbuf_tensor(self.uname(name), list(shape), dt))
            Wo = sb("Wo", [128, 8, D], BF16); dWo = Dep()
            xres = sb("xres", [128, NTILE, D]); dX = [Dep() for _ in range(NTILE)]
            tG = [sb(f"tG{i}", [128, 512]) for i in range(2)]; dtG = [Dep(), Dep()]
            lnG = sb("lnG", [128, D]); lnB = sb("lnB", [128, D]); dLn = Dep()
            st = sb("lnst", [128, 7, NTILE]); dSt = Dep()
            junk = sb("junk", [128, D], BF16); dJ = Dep()
            self.memset("pool", st[:], 0.0, [dSt])
            self.ldc(Wo[:], wsrc(L["w_out"], 0, D), [dWo])
            if next_ada is not None:
                hflat = hT[:].rearrange("p c t -> p (c t)")
                a_wa = [hflat[:, 0:8 * D].rearrange("p (k n) -> p k n", k=8), hflat[:, 8 * D:16 * D].rearrange("p (k n) -> p k n", k=8)]
                a_dw = [Dep(), Dep()]
                a_bG = sb("a_bG", [128, D]); a_tmpG = sb("a_tmpG", [128, 512]); a_dbG = Dep(); a_dtG = Dep()
                self.ada_load(next_ada[0], 0, next_ada[1], a_wa[0], a_dw[0])
                self.ada_load(next_ada[0], 1, next_ada[1], a_wa[1], a_dw[1])
            self.ld(lnG[:], L["ln"]["ln2_g"][0:1, :].broadcast_to([128, D]), [dLn])
            self.ld(lnB[:], L["ln"]["ln2_b"][0:1, :].broadcast_to([128, D]), [dLn])
            for i in range(NTILE):
                nt = tile_nt(i)
                src = L["x1p"][128 * i:128 * i + nt, :] if i < 16 else L["x1s"][:, :]
                self.ld(xres[0:nt, i, :], src, [dX[i]])
                self.S.op("act", lambda e, i=i, nt=nt: e.mul(out=xres[0:nt, i, :], in_=xres[0:nt, i, :], mul=ALPHA), [dX[i]], [dX[i]])
            for i in range(NTILE):
                nt = tile_nt(i)
                which = 0 if i < 16 else 1
                for cb in range(2):
                    po, dpo = self.bank()
                    for k in range(8):
                        self.mm(po[0:nt, :], hgT[:, k, 128 * i:128 * i + nt], Wo[:, k, cb * 512:(cb + 1) * 512], k == 0, k == 7,
                                [dHG[min(i // 4, 4)], dWo], [dpo])
                    self.tt("dve", tG[cb][0:nt, :], po[0:nt, :], self.G[0:nt, which, cb * 512:(cb + 1) * 512], ALU.mult, [dpo, self.dG], [dtG[cb]])
                    self.tt("pool", xres[0:nt, i, cb * 512:(cb + 1) * 512], xres[0:nt, i, cb * 512:(cb + 1) * 512], tG[cb][0:nt, :], ALU.add,
                            [dtG[cb]], [dX[i]])
            if next_ada is not None:
                self.ada_compute(next_ada[0], 0, a_wa[0], a_dw[0], a_bG, a_dbG, a_tmpG, a_dtG)
                self.ada_load(next_ada[0], 2, next_ada[1], a_wa[0], a_dw[0])
                self.ada_compute(next_ada[0], 1, a_wa[1], a_dw[1], a_bG, a_dbG, a_tmpG, a_dtG)
            self.layer_norm_tiles(xres, dX, lnG, lnB, dLn, st, dSt, junk, dJ, LN_EPS)
            for i in range(NTILE):
                nt = tile_nt(i)
                dst = L["x2p"][128 * i:128 * i + nt, :] if i < 16 else L["x2s"][:, :]
                self.store(dst, xres[0:nt, i, :], [dX[i]])
            if next_ada is not None:
                self.ada_compute(next_ada[0], 2, a_wa[0], a_dw[0], a_bG, a_dbG, a_tmpG, a_dtG)
                self.ada_finish()
            S.emit()
            if "x2" in self.debug:
                o1 = nc.dram_tensor("dbg_x2p", [SEQ, D], F32, kind="ExternalOutput").ap()
                o2 = nc.dram_tensor("dbg_x2s", [NS, D], F32, kind="ExternalOutput").ap()
                self.store(o1[:, :], L["x2p"][:, :], [])
                self.store(o2[:, :], L["x2s"][:, :], [])
                S.emit()


_CACHE = {}


def _consts():
    half = 32
    inv = (10000.0 ** (-np.arange(half, dtype=np.float32) / half)).astype(np.float32)
    pos = np.concatenate([np.arange(SEQ), PAST + (np.arange(NS) % 4)]).astype(np.float32)
    ang = pos[None, :] * inv[:, None]
    cos = np.cos(ang).astype(np.float32); sin = np.sin(ang).astype(np.float32)
    ropec = np.concatenate([cos, cos, cos, cos], 0)
    ropes = np.concatenate([-sin, sin, -sin, sin], 0)
    ident = np.eye(128, dtype=np.float32)
    s = np.arange(128)[:, None]; t = np.arange(128)[None, :]
    m_le = (s <= t).astype(np.float32)
    m_gt = (s > t).astype(np.float32)
    blk = ((s // 4) == (t // 4)).astype(np.float32)
    m_blkc = blk * m_le
    m_cache = np.zeros((128, 128), np.float32)
    for col in range(128):
        tt = col % 4
        m_cache[:, col] = (np.arange(128) > tt)
    last = np.zeros((128, 128), np.float32)
    last[:, 0:16] = ((np.arange(128)[:, None] // 4) == np.arange(16)[None, :])
    cmask = np.concatenate([m_le, m_gt, m_blkc, blk, m_cache, last], 1)
    bm = ((np.arange(64)[None, :] // 4) == np.arange(16)[:, None]).astype(np.float32)
    bmask = np.broadcast_to(bm.reshape(1, 1024), (128, 1024)).copy()
    return dict(ident=ident, ropec=ropec.astype(np.float32), ropes=ropes.astype(np.float32), cmask=cmask.astype(np.float32),
                bmask=bmask)


def kernel(**inputs):
    debug = tuple(os.environ.get("MK_DEBUG", "").split(",")) if os.environ.get("MK_DEBUG") else ()
    key = debug
    if key not in _CACHE:
        kb = KB(debug)
        kb.build()
        _CACHE[key] = kb
    kb = _CACHE[key]
    f = lambda a: np.ascontiguousarray(np.asarray(a, dtype=np.float32))
    I = {k: f(v) for k, v in inputs.items()}
    cst = _consts()
    shared = dict(
        w_ada=I["w_ada"][0], b_ada=I["b_ada"][0].reshape(72, 128), b_ada_row=I["b_ada"][0].reshape(9, D),
        w_up1=I["w_ffn1_up"][0], w_dn1=I["w_ffn1_down"][0], w_up2=I["w_ffn2_up"][0], w_dn2=I["w_ffn2_down"][0],
        ln1_g=I["ln1_g"], ln1_b=I["ln1_b"], ln2_g=I["ln2_g"], ln2_b=I["ln2_b"], ln3_g=I["ln3_g"], ln3_b=I["ln3_b"],
        w_in=I["w_in"][0], b_ig=I["b_igate"], b_fg=I["b_fgate"], mng=I["m_norm_g"][0].reshape(8, 128), sinks=I["sinks"],
        w_bm=I["w_branch_m"][0], w_ba=I["w_branch_a"][0], w_out=I["w_out"][0], **cst)
    in_maps = []
    for c in range(8):
        sl = slice(16 * c, 16 * c + 16)
        m = dict(shared)
        m["xp"] = I["x_prompt"][c]
        m["xs"] = I["x_sample"][sl].reshape(NS, D)
        m["c_all"] = np.concatenate([I["c_prompt"][c:c + 1], I["c_sample"][sl]], 0)
        m["stC"] = I["state_mlstm_C"][0][sl]
        m["stn"] = I["state_mlstm_n"][0][sl].reshape(16, 512)
        m["stm"] = I["state_mlstm_m"][0][sl]
        m["ck"] = I["cache_swa_k"][0][sl].reshape(16, 128, 256)
        m["cv"] = I["cache_swa_v"][0][sl].reshape(16, 128, 256)
        in_maps.append(m)
    res = run_bass_kernel_spmd(kb.nc, in_maps, core_ids=list(range(8)))
    R = res.results
    kernel.last_results = R
    cat = lambda k: np.stack([np.asarray(r[k]) for r in R], 0)
    y_p = cat("yp").reshape(8, SEQ, D)
    y_s = cat("ys").reshape(128, 4, D)
    C_p = cat("Cp").reshape(1, 8, 4, 256, 128)
    n_p = cat("np").reshape(1, 8, 4, 128)
    m_p = cat("mp").reshape(1, 8, 4)
    k_p = cat("kp").reshape(1, 8, 128, 4, 64)
    v_p = cat("vp").reshape(1, 8, 128, 4, 64)
    C_s = cat("Cs").reshape(1, 128, 4, 256, 128)
    n_s = cat("ns").reshape(1, 128, 4, 128)
    m_s = cat("ms").reshape(1, 128, 4)
    k_s = cat("ks").reshape(1, 128, 128, 4, 64)
    v_s = cat("vs").reshape(1, 128, 128, 4, 64)
    return tuple(np.ascontiguousarray(a, dtype=np.float32) for a in (y_p, y_s, C_p, n_p, m_p, k_p, v_p, C_s, n_s, m_s, k_s, v_s))
```

```python
import os
import numpy as np
from contextlib import ExitStack
import concourse.bass as bass
import concourse.mybir as mybir
from concourse.bass_utils import run_bass_kernel_spmd

F32 = mybir.dt.float32
BF16 = mybir.dt.bfloat16
AF = mybir.ActivationFunctionType
ALU = mybir.AluOpType
AX = mybir.AxisListType

D = 1024
SEQ = 2048
NS = 64
NT = SEQ + NS
NTILE = 17
DFF = 2816
NFC = 22
ALPHA = 2.0 ** 0.25
LN_EPS = 1e-5
HN_EPS = 1e-6
PAST = 8192
SEM_LIMIT = 28000
TBLK = [(0, 512), (512, 512), (1024, 512), (1536, 512), (2048, 64)]


def tile_nt(i):
    return 128 if i < 16 else 64


def blk_tiles(b):
    return [4 * b + j for j in range(4)] if b < 4 else [16]


class Dep:
    __slots__ = ("w", "r")

    def __init__(self):
        self.w = {}
        self.r = {}


class Sched:
    ENGS = ("pe", "act", "dve", "pool", "sp")

    def __init__(self, nc, es):
        self.nc = nc
        self.es = es
        self.ops = {e: [] for e in self.ENGS}
        self.cnt = {e: 0 for e in self.ENGS}
        self.csem = {e: None for e in self.ENGS}
        self.seen = {e: {} for e in self.ENGS}
        self.nsem = 0
        self.dstates = []
        self.misc = {e: [[None, 0] for _ in range(16)] for e in ("sp", "pool", "act")}
        for e in self.misc:
            for m in self.misc[e]:
                self.dstates.append(m)
        self.misc_i = {e: 0 for e in self.misc}
        self.final = []

    def new_sem(self, name):
        self.nsem += 1
        return self.es.enter_context(self.nc.semaphore(f"{name}_{self.nsem}"))

    def dstate(self):
        s = [None, 0]
        self.dstates.append(s)
        return s

    def _counter(self, eng):
        if self.csem[eng] is None or self.cnt[eng] >= SEM_LIMIT:
            self.csem[eng] = self.new_sem("c" + eng)
            self.cnt[eng] = 0
        return self.csem[eng]

    def _collect(self, eng, reads, writes, skip_own=True, extra=()):
        need = {}

        def add(tok):
            sem, v = tok
            k = id(sem)
            if k not in need or need[k][1] < v:
                need[k] = (sem, v)
        for d in reads:
            for tok in d.w.values():
                add(tok)
        for d in writes:
            for tok in d.w.values():
                add(tok)
            for tok in d.r.values():
                add(tok)
        for tok in extra:
            add(tok)
        waits = []
        seen = self.seen[eng]
        own = self.csem[eng]
        for k, (sem, v) in need.items():
            if skip_own and own is not None and sem is own:
                continue
            if seen.get(k, 0) >= v:
                continue
            seen[k] = v
            waits.append((sem, v))
        return waits

    def op(self, eng, fn, reads=(), writes=()):
        waits = self._collect(eng, reads, writes, skip_own=(eng == "pe"))
        sem = self._counter(eng)
        self.cnt[eng] += 1
        tok = (sem, self.cnt[eng])
        k = id(sem)
        for d in reads:
            d.r[k] = tok
        for d in writes:
            d.w[k] = tok
        self.ops[eng].append((waits, fn, (sem, 1)))
        return tok

    def dma(self, eng, fn, reads=(), writes=(), st=None, final=False):
        if st is None:
            st = self.misc[eng][self.misc_i[eng] % len(self.misc[eng])]
            self.misc_i[eng] += 1
        extra = []
        if st[0] is not None and st[1] + 16 > SEM_LIMIT:
            st[0] = None
        if st[0] is None:
            st[0] = self.new_sem("d")
            st[1] = 0
        elif st[1] > 0:
            extra.append((st[0], st[1]))
        waits = self._collect(eng, reads, writes, skip_own=False, extra=extra)
        st[1] += 16
        tok = (st[0], st[1])
        k = id(st[0])
        for d in reads:
            d.r[k] = tok
        for d in writes:
            d.w[k] = tok
        self.ops[eng].append((waits, fn, (st[0], 16)))
        if final:
            self.final.append(tok)
        return tok

    def emit(self, last=False):
        nc = self.nc
        bar = []
        for e in self.ENGS:
            if self.csem[e] is not None and self.cnt[e] > 0:
                bar.append((e, self.csem[e], self.cnt[e]))
        dbar = [(s[0], s[1]) for s in self.dstates if s[0] is not None and s[1] > 0]
        ops = self.ops
        seen = self.seen

        def run(engine, name):
            for waits, fn, (sem, inc) in ops[name]:
                for ws, wv in waits:
                    engine.wait_ge(ws, wv)
                fn(engine).then_inc(sem, inc)
            for e, sem, v in bar:
                if e != name and seen[name].get(id(sem), 0) < v:
                    engine.wait_ge(sem, v)
                    seen[name][id(sem)] = v
            for sem, v in dbar:
                if seen[name].get(id(sem), 0) < v:
                    engine.wait_ge(sem, v)
                    seen[name][id(sem)] = v

        with nc.Block() as block:
            @block.sync
            def _(e):
                run(e, "sp")

            @block.tensor
            def _(e):
                run(e, "pe")

            @block.scalar
            def _(e):
                run(e, "act")

            @block.vector
            def _(e):
                run(e, "dve")

            @block.gpsimd
            def _(e):
                run(e, "pool")
        self.ops = {e: [] for e in self.ENGS}


class KB:
    def __init__(self, debug=()):
        self.debug = set(debug)
        self.nc = bass.Bass("TRN2", target_bir_lowering=False)
        self.dram = {}
        self.dbg_out = {}

    def uname(self, name):
        self.ucnt = getattr(self, "ucnt", 0) + 1
        return f"s{self.ucnt}_{name}"

    def din(self, name, shape):
        self.dram[name] = self.nc.dram_tensor(name, list(shape), F32, kind="ExternalInput").ap()
        return self.dram[name]

    def dout(self, name, shape):
        self.dram[name] = self.nc.dram_tensor(name, list(shape), F32, kind="ExternalOutput").ap()
        return self.dram[name]

    def dscr(self, name, shape):
        self.dram[name] = self.nc.dram_tensor(name, list(shape), F32, kind="Internal").ap()
        return self.dram[name]

    def mm(self, out, lhsT, rhs, start, stop, R, W):
        self.S.op("pe", lambda e: e.matmul(out, lhsT=lhsT, rhs=rhs, start=start, stop=stop), R, W)

    def tr(self, out, in_, R, W):
        n = in_.shape[0]
        ident = self.ident[0:n, 0:n]
        self.S.op("pe", lambda e: e.transpose(out=out, in_=in_, identity=ident), list(R) + [self.dIdent], W)

    def act(self, out, in_, func, R, W, bias=None, scale=None, accum=None):
        kw = {}
        if bias is not None:
            kw["bias"] = bias
        if scale is not None:
            kw["scale"] = scale
        if accum is not None:
            kw["accum_out"] = accum
        self.S.op("act", lambda e: e.activation(out=out, in_=in_, func=func, **kw), R, W)

    def tt(self, eng, out, in0, in1, op, R, W):
        self.S.op(eng, lambda e: e.tensor_tensor(out=out, in0=in0, in1=in1, op=op), R, W)

    def ts(self, eng, out, in0, s1, s2, op0, op1, R, W):
        if s2 is None:
            s2 = 0.0
            op1 = ALU.add
        self.S.op(eng, lambda e: e.tensor_scalar(out=out, in0=in0, scalar1=s1, scalar2=s2, op0=op0, op1=op1), R, W)

    def stt(self, eng, out, in0, scalar, in1, op0, op1, R, W):
        self.S.op(eng, lambda e: e.scalar_tensor_tensor(out=out, in0=in0, scalar=scalar, in1=in1, op0=op0, op1=op1), R, W)

    def cp(self, eng, out, in_, R, W):
        if eng == "act":
            self.S.op("act", lambda e: e.copy(out=out, in_=in_), R, W)
        else:
            self.S.op(eng, lambda e: e.tensor_copy(out=out, in_=in_), R, W)

    def memset(self, eng, ap, val, W):
        self.S.op(eng, lambda e: e.memset(ap, val), (), W)

    def recip(self, out, in_, R, W):
        self.S.op("dve", lambda e: e.reciprocal(out=out, in_=in_), R, W)

    def ld(self, out, in_, W, R=(), eng="sp", st=None):
        return self.S.dma(eng, lambda e: e.dma_start(out=out, in_=in_), R, W, st=st)

    def ldc(self, out, in_, W, R=(), st=None):
        return self.S.dma("pool", lambda e: e.dma_start(out=out, in_=in_), R, W, st=st)

    def store(self, out, in_, R, W=(), final=True, eng="sp"):
        return self.S.dma(eng, lambda e: e.dma_start(out=out, in_=in_), R, W, final=final)

    def ld_nc(self, out, in_, W, R=()):
        return self.S.dma("sp", lambda e: e.dma_start(out=out, in_=in_, allow_slow_non_contiguous=True), R, W)

    def store_nc(self, out, in_, R):
        return self.S.dma("sp", lambda e: e.dma_start(out=out, in_=in_, allow_slow_non_contiguous=True), R, (), final=True)

    def dbg(self, name, ap, dep, shape):
        if name not in self.debug:
            return
        o = self.nc.dram_tensor("dbg_" + name, list(shape), ap.dtype, kind="ExternalOutput").ap()
        self.dbg_out[name] = o
        deps = dep if isinstance(dep, (list, tuple)) else [dep]
        self.store(o, ap, list(deps))

    def bank(self, hold=None):
        held = getattr(self, "held", None)
        if held is None:
            held = self.held = {}
        for _ in range(8):
            i = self.bank_i % 8
            self.bank_i += 1
            if i not in held:
                break
        else:
            raise RuntimeError("all PSUM banks held")
        if hold is not None:
            held[i] = hold
        return self.PB[i], self.dPB[i]

    def release(self, key):
        for i in [i for i, k in self.held.items() if k == key]:
            del self.held[i]

    def build(self):
        nc = self.nc
        din, dout = self.din, self.dout
        xp = din("xp", [SEQ, D]); xs = din("xs", [NS, D]); c_all = din("c_all", [17, D])
        stC = din("stC", [16, 4, 256, 128]); stn = din("stn", [16, 512]); stm = din("stm", [16, 4])
        ck = din("ck", [16, 128, 256]); cv = din("cv", [16, 128, 256])
        w_ada = din("w_ada", [D, 9 * D]); b_ada = din("b_ada", [72, 128]); self.b_ada_row = din("b_ada_row", [9, D])
        w_up1 = din("w_up1", [D, 2 * DFF]); w_dn1 = din("w_dn1", [DFF, D])
        w_up2 = din("w_up2", [D, 2 * DFF]); w_dn2 = din("w_dn2", [DFF, D])
        ln = {k: din(k, [1, D]) for k in ("ln1_g", "ln1_b", "ln2_g", "ln2_b", "ln3_g", "ln3_b")}
        w_in = din("w_in", [D, 6664]); b_ig = din("b_ig", [1, 4]); b_fg = din("b_fg", [1, 4])
        mng = din("mng", [8, 128]); sinks = din("sinks", [1, 16])
        w_bm = din("w_bm", [D, D]); w_ba = din("w_ba", [D, D]); w_out = din("w_out", [D, D])
        identd = din("ident", [128, 128]); ropec = din("ropec", [128, NT]); ropes = din("ropes", [128, NT])
        cmask = din("cmask", [128, 128 * 6]); bmask = din("bmask", [128, 1024])
        oa_scr = nc.dram_tensor("oa_scr", [128, 8 * NT], BF16, kind="Internal").ap()
        yp = dout("yp", [SEQ, D]); ys = dout("ys", [NS, D])
        Cp = dout("Cp", [4, 256, 128]); np_ = dout("np", [4, 128]); mp = dout("mp", [4, 1])
        kp = dout("kp", [128, 256]); vp = dout("vp", [128, 256])
        Cs = dout("Cs", [16, 4, 256, 128]); ns_ = dout("ns", [16, 512]); ms = dout("ms", [16, 4])
        ks = dout("ks", [16, 128, 256]); vs = dout("vs", [16, 128, 256])
        x1p = self.dscr("x1p", [SEQ, D]); x1s = self.dscr("x1s", [NS, D])
        x2p = self.dscr("x2p", [SEQ, D]); x2s = self.dscr("x2s", [NS, D])

        with ExitStack() as es:
            self.es = es
            self.S = S = Sched(nc, es)
            sb = lambda name, shape, dt=F32: es.enter_context(nc.sbuf_tensor(self.uname(name), list(shape), dt))
            self.PB = [es.enter_context(nc.psum_tensor(f"pb{i}", [128, 512], F32)) for i in range(8)]
            self.dPB = [Dep() for _ in range(8)]
            self.bank_i = 0
            self.ident = sb("ident", [128, 128]); self.dIdent = Dep()
            self.ld(self.ident[:], identd[:, :], [self.dIdent])
            self.modT = sb("modT", [128, 2, 8, 17]); self.dModT = Dep()
            self.modS = sb("modS", [128, 2, 8, 64]); self.dModS = Dep()
            self.G = sb("G", [128, 2, D]); self.dG = Dep()
            self.scT = sb("scT", [128, 8, 17], BF16); self.dScT = Dep()
            self.lhsP = sb("lhsP", [128, 8, 128], BF16); self.lhsS = sb("lhsS", [128, 8, 64], BF16); self.dLhs = Dep()
            self.badaT = sb("badaT", [128, 72]); self.dBada = Dep()
            self.setup_ada(c_all, b_ada)
            S.emit()
            self.ada_stage(0, w_ada, b_ada)
            self.dbg("modT", self.modT[:], self.dModT, [128, 2, 8, 17])
            self.dbg("modS", self.modS[:], self.dModS, [128, 2, 8, 64])
            self.dbg("G", self.G[:], self.dG, [128, 2, D])
            S.emit()
            self.ffn(xp, xs, x1p, x1s, w_up1, w_dn1, ln["ln1_g"], ln["ln1_b"], "f1", next_ada=(1, w_ada, b_ada))
            if "x1" in self.debug:
                o1 = nc.dram_tensor("dbg_x1p", [SEQ, D], F32, kind="ExternalOutput").ap()
                o2 = nc.dram_tensor("dbg_x1s", [NS, D], F32, kind="ExternalOutput").ap()
                self.store(o1[:, :], x1p[:, :], [])
                self.store(o2[:, :], x1s[:, :], [])
                S.emit()
            if "stop1" in self.debug:
                return nc
            self.mixer(dict(locals(), **self.dram))
            self.ffn(x2p, x2s, yp, ys, w_up2, w_dn2, ln["ln3_g"], ln["ln3_b"], "f2")
        return nc

    def setup_ada(self, c_all, b_ada):
        with ExitStack() as es:
            nc = self.nc
            sb = lambda name, shape, dt=F32: es.enter_context(nc.sbuf_tensor(self.uname(name), list(shape), dt))
            craw = sb("craw", [17, D]); dC = Dep()
            csil = sb("csil", [17, D]); dCs = Dep()
            braw = sb("braw", [72, 128]); dB = Dep()
            scTf = sb("scTf", [128, 8, 17]); dScTf = Dep()
            self.ld(craw[:], c_all[:, :], [dC])
            self.ld(braw[:], b_ada[:, :], [dB])
            self.act(csil[:], craw[:], AF.Silu, [dC], [dCs])
            pb, dpb = self.bank()
            for k in range(8):
                self.tr(pb[:, k * 17:(k + 1) * 17], csil[0:17, k * 128:(k + 1) * 128], [dCs], [dpb])
            self.cp("dve", scTf[:].rearrange("p a b -> p (a b)"), pb[:, 0:136], [dpb], [dScTf])
            self.cp("dve", self.scT[:], scTf[:], [dScTf], [self.dScT])
            self.cp("dve", self.lhsP[:], scTf[:, :, 0:1].to_broadcast([128, 8, 128]), [dScTf], [self.dLhs])
            for k in range(8):
                self.cp("dve", self.lhsS[:, k, :].rearrange("p (b j) -> p b j", j=4),
                        scTf[:, k, 1:17].unsqueeze(2).to_broadcast([128, 16, 4]), [dScTf], [self.dLhs])
            pb2, dpb2 = self.bank()
            self.tr(pb2[:, 0:72], braw[0:72, :], [dB], [dpb2])
            self.cp("dve", self.badaT[:], pb2[:, 0:72], [dpb2], [self.dBada])
            self.S.emit()

    def ada_load(self, st, ci, w_ada, wa, dwa):
        j = 3 * st + ci
        self.ldc(wa, w_ada[:, j * D:(j + 1) * D].rearrange("(kc p) n -> p kc n", p=128), [dwa])

    def ada_compute(self, st, ci, wa, dwa, bG, dbG, tmpG, dtG):
        j = 3 * st + ci
        if ci < 2:
            pb, dpb = self.bank()
            for f in range(8):
                for k in range(8):
                    self.mm(pb[:, f * 17:(f + 1) * 17], wa[:, k, f * 128:(f + 1) * 128], self.scT[:, k, :],
                            k == 0, k == 7, [dwa, self.dScT], [dpb])
            self.stt("dve", self.modT[:, ci, :, :], pb[:, 0:136].rearrange("p (a b) -> p a b", a=8),
                     1.0 if ci == 1 else 0.0,
                     self.badaT[:, j * 8:(j + 1) * 8].unsqueeze(2).to_broadcast([128, 8, 17]),
                     ALU.add, ALU.add, [dpb, self.dBada], [self.dModT])
        else:
            a = 1.0 if st == 1 else 0.5
            self.ld(bG[:], self.b_ada_row[j:j + 1, :].broadcast_to([128, D]), [dbG])
            for which, lhs, n in ((0, self.lhsP, 128), (1, self.lhsS, 64)):
                for cb in range(2):
                    pb, dpb = self.bank()
                    for k in range(8):
                        self.mm(pb[0:n, :], lhs[:, k, :], wa[:, k, cb * 512:(cb + 1) * 512], k == 0, k == 7,
                                [dwa, self.dLhs], [dpb])
                    self.tt("dve", tmpG[0:n, :], pb[0:n, :], bG[0:n, cb * 512:(cb + 1) * 512], ALU.add, [dpb, dbG], [dtG])
                    self.ts("dve", self.G[0:n, which, cb * 512:(cb + 1) * 512], tmpG[0:n, :], a, a, ALU.mult, ALU.add,
                            [dtG], [self.dG])

    def ada_finish(self):
        for w in range(2):
            self.cp("dve", self.modS[:, w, :, :].rearrange("p k (b j) -> p k b j", j=4),
                    self.modT[:, w, :, 1:17].unsqueeze(3).to_broadcast([128, 8, 16, 4]), [self.dModT], [self.dModS])

    def ada_stage(self, st, w_ada, b_ada):
        nc = self.nc
        with ExitStack() as es:
            sb = lambda name, shape, dt=F32: es.enter_context(nc.sbuf_tensor(self.uname(name), list(shape), dt))
            wa = [sb(f"wa{i}", [128, 8, D], BF16)[:] for i in range(2)]
            dwa = [Dep(), Dep()]
            bG = sb("bG", [128, D]); dbG = Dep()
            tmpG = sb("tmpG", [128, 512]); dtG = Dep()
            for ci in range(3):
                self.ada_load(st, ci, w_ada, wa[ci % 2], dwa[ci % 2])
                self.ada_compute(st, ci, wa[ci % 2], dwa[ci % 2], bG, dbG, tmpG, dtG)
            self.ada_finish()
            self.S.emit()

    def make_xT(self, xT, dXT, src_tile_ap, dsrc, i, tmp, dtmp):
        nt = tile_nt(i)
        t0 = 128 * i
        for half in range(2):
            pb, dpb = self.bank()
            for kk in range(4):
                k = half * 4 + kk
                self.tr(pb[:, kk * nt:(kk + 1) * nt], src_tile_ap[0:nt, k * 128:(k + 1) * 128], [dsrc], [dpb])
            if i < 16:
                for kk in range(4):
                    k = half * 4 + kk
                    self.act(xT[:, k, t0:t0 + nt], pb[:, kk * nt:(kk + 1) * nt], AF.Identity, [dpb, self.dModT], [dXT],
                             bias=self.modT[:, 0, k, 0:1], scale=self.modT[:, 1, k, 0:1])
            else:
                k0 = half * 4
                self.tt("dve", tmp[:, 0:256].rearrange("p (a b) -> p a b", a=4), pb[:, 0:256].rearrange("p (a b) -> p a b", a=4),
                        self.modS[:, 1, k0:k0 + 4, :], ALU.mult, [dpb, self.dModS], [dtmp])
                self.tt("dve", xT[:, k0:k0 + 4, t0:t0 + nt], tmp[:, 0:256].rearrange("p (a b) -> p a b", a=4),
                        self.modS[:, 0, k0:k0 + 4, :], ALU.add, [dtmp, self.dModS], [dXT])

    def layer_norm_tiles(self, xres, dX, lnG, lnB, dLn, st, dSt, junk, dJ, eps):
        for i in range(NTILE):
            nt = tile_nt(i)
            self.act(junk[0:nt, :], xres[0:nt, i, :], AF.Identity, [dX[i]], [dJ, dSt], accum=st[0:nt, 0, i:i + 1])
            self.act(junk[0:nt, :], xres[0:nt, i, :], AF.Square, [dX[i]], [dJ, dSt], accum=st[0:nt, 1, i:i + 1])
        m = st[:, 2, :]; msq = st[:, 3, :]; var = st[:, 4, :]; rstd = st[:, 5, :]; nb = st[:, 6, :]
        self.ts("dve", m, st[:, 0, :], 1.0 / D, None, ALU.mult, None, [dSt], [dSt])
        self.tt("dve", msq, m, m, ALU.mult, [dSt], [dSt])
        self.ts("dve", var, st[:, 1, :], 1.0 / D, eps, ALU.mult, ALU.add, [dSt], [dSt])
        self.tt("dve", var, var, msq, ALU.subtract, [dSt], [dSt])
        self.act(var, var, AF.Sqrt, [dSt], [dSt])
        self.recip(rstd, var, [dSt], [dSt])
        self.tt("dve", nb, m, rstd, ALU.mult, [dSt], [dSt])
        self.ts("dve", nb, nb, -1.0, None, ALU.mult, None, [dSt], [dSt])
        for i in range(NTILE):
            nt = tile_nt(i)
            self.act(xres[0:nt, i, :], xres[0:nt, i, :], AF.Identity, [dSt], [dX[i]], bias=nb[0:nt, i:i + 1], scale=rstd[0:nt, i:i + 1])
            self.tt("dve", xres[0:nt, i, :], xres[0:nt, i, :], lnG[0:nt, :], ALU.mult, [dLn], [dX[i]])
            self.tt("dve", xres[0:nt, i, :], xres[0:nt, i, :], lnB[0:nt, :], ALU.add, [dLn], [dX[i]])

    def ffn(self, src_p, src_s, dst_p, dst_s, w_up, w_dn, ln_g, ln_b, tag, next_ada=None):
        nc = self.nc
        S = self.S
        groups = [(0, 4), (4, 4), (8, 4), (12, 4), (16, 3), (19, 3)]
        GC = 4
        with ExitStack() as es:
            sb = lambda name, shape, dt=F32: es.enter_context(nc.sbuf_tensor(self.uname(name), list(shape), dt))
            xres = sb("xres", [128, NTILE, D]); dX = [Dep() for _ in range(NTILE)]
            xT = sb("xT", [128, 8, NT], BF16); dXT = [Dep() for _ in range(5)]
            wu = [sb(f"wu{i}", [128, 8, 2, GC * 128], BF16) for i in range(2)]; dwu = [Dep(), Dep()]
            wd = [sb(f"wd{i}", [128, GC, D], BF16) for i in range(2)]; dwd = [Dep(), Dep()]
            gT = [sb(f"gT{i}", [128, GC, 512], BF16) for i in range(2)]; dgT = [Dep(), Dep()]
            tsil = [sb(f"tsil{i}", [128, 512], BF16) for i in range(2)]; dts = [Dep(), Dep()]
            tG = [sb(f"tG{i}", [128, 512]) for i in range(2)]; dtG = [Dep(), Dep()]
            lnG = sb("lnG", [128, D]); lnB = sb("lnB", [128, D]); dLn = Dep()
            st = sb("lnst", [128, 7, NTILE]); dSt = Dep()
            junk = sb("junk", [128, D], BF16); dJ = Dep()
            tmpm = sb("tmpm", [128, 256]); dtm = Dep()
            if next_ada is not None:
                a_bG = sb("a_bG", [128, D]); a_tmpG = sb("a_tmpG", [128, 512])
                wa_views = [wu[i][:].rearrange("p k h c -> p k (h c)") for i in range(2)]
            if os.environ.get("MK_PADF"):
                sb("padf", [128, int(os.environ["MK_PADF"]) * 256])
            self.memset("pool", st[:], 0.0, [dSt])
            self.ld(lnG[:], ln_g[0:1, :].broadcast_to([128, D]), [dLn])
            self.ld(lnB[:], ln_b[0:1, :].broadcast_to([128, D]), [dLn])

            def load_group(q):
                c0, gc = groups[q]
                s = q % 2
                for half in range(2):
                    col0 = half * DFF + c0 * 128
                    self.ldc(wu[s][:, :, half, 0:gc * 128], w_up[:, col0:col0 + gc * 128].rearrange("(kc p) n -> p kc n", p=128), [dwu[s]])
                self.ldc(wd[s][:, 0:gc, :], w_dn[c0 * 128:(c0 + gc) * 128, :].rearrange("(c p) n -> p c n", p=128), [dwd[s]])

            load_group(0)
            for i in range(NTILE):
                nt = tile_nt(i)
                src = src_p[128 * i:128 * i + nt, :] if i < 16 else src_s[:, :]
                self.ld(xres[0:nt, i, :], src, [dX[i]])
                self.make_xT(xT, dXT[min(i // 4, 4)], xres[:, i, :], dX[i], i, tmpm, dtm)
                self.S.op("act", lambda e, i=i, nt=nt: e.mul(out=xres[0:nt, i, :], in_=xres[0:nt, i, :], mul=ALPHA), [dX[i]], [dX[i]])
            items = [(q, b) for q in range(len(groups)) for b in range(5)]

            def up(n):
                q, b = items[n]
                c0, gc = groups[q]
                s = q % 2
                g = n % 2
                t0, tn = TBLK[b]
                for j in range(gc):
                    pa, dpa = self.bank()
                    pu, dpu = self.bank()
                    for half, (pp, dpp) in enumerate(((pa, dpa), (pu, dpu))):
                        for k in range(8):
                            self.mm(pp[:, 0:tn], wu[s][:, k, half, j * 128:(j + 1) * 128], xT[:, k, t0:t0 + tn], k == 0, k == 7,
                                    [dwu[s], dXT[b]], [dpp])
                    sl = (n * GC + j) % 2
                    self.act(tsil[sl][:, 0:tn], pa[:, 0:tn], AF.Silu, [dpa], [dts[sl]])
                    self.tt("dve", gT[g][:, j, 0:tn], tsil[sl][:, 0:tn], pu[:, 0:tn], ALU.mult, [dts[sl], dpu], [dgT[g]])

            def down(n):
                q, b = items[n]
                c0, gc = groups[q]
                s = q % 2
                g = n % 2
                for ti, i in enumerate(blk_tiles(b)):
                    nt = tile_nt(i)
                    which = 0 if i < 16 else 1
                    for cb in range(2):
                        po, dpo = self.bank()
                        for j in range(gc):
                            self.mm(po[0:nt, :], gT[g][:, j, ti * 128:ti * 128 + nt], wd[s][:, j, cb * 512:(cb + 1) * 512],
                                    j == 0, j == gc - 1, [dgT[g], dwd[s]], [dpo])
                        sl = (i * 2 + cb) % 2
                        self.tt("dve", tG[sl][0:nt, :], po[0:nt, :], self.G[0:nt, which, cb * 512:(cb + 1) * 512], ALU.mult,
                                [dpo, self.dG], [dtG[sl]])
                        self.tt("pool", xres[0:nt, i, cb * 512:(cb + 1) * 512], xres[0:nt, i, cb * 512:(cb + 1) * 512], tG[sl][0:nt, :],
                                ALU.add, [dtG[sl]], [dX[i]])

            load_group(1)
            up(0)
            for n in range(len(items)):
                if n + 1 < len(items):
                    up(n + 1)
                down(n)
                q, b = items[n]
                if b == 4 and q + 2 < len(groups):
                    load_group(q + 2)
                if next_ada is not None and b == 4 and q == len(groups) - 2:
                    sA = q % 2
                    self.ada_load(next_ada[0], 0, next_ada[1], wa_views[sA], dwu[sA])
            if next_ada is not None:
                st_, wad = next_ada[0], next_ada[1]
                sB = 1 - sA
                a_dbG = Dep(); a_dtG = Dep()
                self.ada_load(st_, 1, wad, wa_views[sB], dwu[sB])
                self.ada_compute(st_, 0, wa_views[sA], dwu[sA], a_bG, a_dbG, a_tmpG, a_dtG)
                self.ada_load(st_, 2, wad, wa_views[sA], dwu[sA])
            if tag == "f1":
                self.dbg("pre", xres[:], dX, [128, NTILE, D])
                self.dbg("xT", xT[:], dXT, [128, 8, NT])
            self.layer_norm_tiles(xres, dX, lnG, lnB, dLn, st, dSt, junk, dJ, LN_EPS)
            if tag == "f1":
                self.dbg("lnst", st[:], dSt, [128, 7, NTILE])
            for i in range(NTILE):
                nt = tile_nt(i)
                dst = dst_p[128 * i:128 * i + nt, :] if i < 16 else dst_s[:, :]
                self.store(dst, xres[0:nt, i, :], [dX[i]])
            if next_ada is not None:
                self.ada_compute(st_, 1, wa_views[sB], dwu[sB], a_bG, a_dbG, a_tmpG, a_dtG)
                self.ada_compute(st_, 2, wa_views[sA], dwu[sA], a_bG, a_dbG, a_tmpG, a_dtG)
                self.ada_finish()
            S.emit()

    def mixer(self, L):
        nc, S = self.nc, self.S
        x1p, x1s = L["x1p"], L["x1s"]
        with ExitStack() as es0:
            sb0 = lambda name, shape, dt=F32: es0.enter_context(nc.sbuf_tensor(self.uname(name), list(shape), dt))
            hT = sb0("hT", [128, 8, NT], BF16); dHT = [Dep() for _ in range(5)]
            with ExitStack() as es:
                sb = lambda name, shape, dt=F32: es.enter_context(nc.sbuf_tensor(self.uname(name), list(shape), dt))
                xt = sb("xt", [128, 2, D]); dxt = [Dep(), Dep()]
                tmpm = sb("tmpm", [128, 256]); dtm = Dep()
                for i in range(NTILE):
                    nt = tile_nt(i)
                    src = x1p[128 * i:128 * i + nt, :] if i < 16 else x1s[:, :]
                    self.ld(xt[0:nt, i % 2, :], src, [dxt[i % 2]])
                    self.make_xT(hT, dHT[min(i // 4, 4)], xt[:, i % 2, :], dxt[i % 2], i, tmpm, dtm)
                S.emit()
            with ExitStack() as es:
                oaT = es.enter_context(nc.sbuf_tensor(self.uname("oaT"), [128, 8, NT], BF16)); dOA = [Dep() for _ in range(5)]
                self.swa(L, hT, dHT, oaT, dOA)
                self.store(L["oa_scr"][:, :], oaT[:].rearrange("p c t -> p (c t)"), dOA)
                if "oaT" in self.debug:
                    self.dbg("oaT", oaT[:], dOA, [128, 8, NT])
                S.emit()
            if "stop2" in self.debug:
                return
            hgT = sb0("hgT", [128, 8, NT], BF16); dHG = [Dep() for _ in range(5)]
            self.mlstm(L, hT, dHT, hgT, dHG)
            if "hgT" in self.debug:
                self.dbg("hgT", hgT[:], dHG, [128, 8, NT])
                S.emit()
            if "stop3" in self.debug:
                return
            self.merge(L, hT, dHT, hgT, dHG, next_ada=(2, L["w_ada"]))

    def swa(self, L, hT, dHT, oaT, dOA):
        nc, S = self.nc, self.S
        w_in = L["w_in"]
        AQ0, AK0, AV0 = 3080, 4104, 4360
        with ExitStack() as es:
            sb = lambda name, shape, dt=F32: es.enter_context(nc.sbuf_tensor(self.uname(name), list(shape), dt))
            if os.environ.get("MK_PADS"):
                sb("pads", [128, int(os.environ["MK_PADS"]) * 256])
            cosT = sb("cosT", [128, NT]); sinT = sb("sinT", [128, NT]); dRope = Dep()
            self.ld(cosT[:], L["ropec"][:, :], [dRope]); self.ld(sinT[:], L["ropes"][:, :], [dRope])
            mk = sb("mk", [128, 768]); dMk = Dep()
            self.ld(mk[:], L["cmask"][:, :], [dMk])
            mle = sb("mle", [128, 128], BF16); mgt = sb("mgt", [128, 128], BF16); mblkc = sb("mblkc", [128, 128], BF16)
            mcache = sb("mcache", [128, 4], BF16); dMb = Dep()
            self.cp("dve", mle[:], mk[:, 0:128], [dMk], [dMb]); self.cp("dve", mgt[:], mk[:, 128:256], [dMk], [dMb])
            self.cp("dve", mblkc[:], mk[:, 256:384], [dMk], [dMb]); self.cp("dve", mcache[:], mk[:, 512:516], [dMk], [dMb])
            ones = sb("ones", [128, 128], BF16); dOnes = Dep()
            self.memset("pool", ones[:], 1.0, [dOnes])
            esink = sb("esink", [128, 16]); dEs = Dep()
            self.ld(esink[:], L["sinks"][0:1, :].broadcast_to([128, 16]), [dEs])
            self.act(esink[:], esink[:], AF.Exp, [dEs], [dEs])
            ckf = sb("ckf", [128, 16, 256]); dCk = Dep()
            for b0 in range(0, 16, 4):
                self.ld(ckf[:, b0:b0 + 4, :], L["ck"][b0:b0 + 4].rearrange("b p c -> p b c"), [dCk])
            cvv = L["cv"].rearrange("b p c -> p b c")
            KcTn = sb("KcTn", [128, 2, 16, 128], BF16); dKn = Dep()
            for gp in range(2):
                for b0 in range(0, 16, 4):
                    pb, dpb = self.bank()
                    for bb in range(4):
                        self.tr(pb[:, bb * 128:(bb + 1) * 128], ckf[:, b0 + bb, gp * 128:(gp + 1) * 128], [dCk], [dpb])
                    self.cp("act", KcTn[:, gp, b0:b0 + 4, :].rearrange("p b c -> p (b c)"), pb[:, :], [dpb], [dKn])
            kout_p = sb("kout_p", [128, 256]); vfin = sb("vfin", [128, 2, 256]); knew_s = sb("knew_s", [128, 256])
            dKo = Dep(); dVf = Dep(); dKs = Dep()
            WQ = sb("WQ", [128, 8, 256], BF16); WQs = sb("WQs", [128, 8, 256], BF16)
            WK2 = sb("WK2", [128, 8, 128], BF16); WK2s = sb("WK2s", [128, 8, 128], BF16)
            WV = sb("WV", [128, 8, 64], BF16); dW = Dep(); dWs = Dep()
            qT = sb("qT", [128, 2, NT], BF16); dQ = [Dep() for _ in range(5)]
            kT2 = sb("kT2", [128, NT], BF16); dK = [Dep() for _ in range(5)]
            vtm2 = sb("vtm2", [128, NTILE, 128], BF16); dV = [Dep() for _ in range(NTILE)]
            vlh = sb("vlh", [128, 16, 2, 128], BF16)
            ones_lh = sb("ones_lh", [128, 2, 128], BF16); dOlh = Dep()
            esg = sb("esg", [128, 2]); dEsg = Dep()
            self.memset("pool", vlh[:], 0.0, dV[0:16])
            self.memset("pool", ones_lh[:], 0.0, [dOlh])
            self.memset("pool", ones_lh[:, 0, 0:64], 1.0, [dOlh])
            self.memset("pool", ones_lh[:, 1, 64:128], 1.0, [dOlh])
            kfin = sb("kfin", [128, 192]); dKf = Dep()
            t1 = [sb(f"t1_{i}", [128, 512]) for i in range(2)]; dt1 = [Dep(), Dep()]
            t2 = [sb(f"t2_{i}", [128, 512]) for i in range(2)]; dt2 = [Dep(), Dep()]
            pT = [sb(f"pT{i}", [128, 512], BF16) for i in range(4)]; dpT = [Dep() for _ in range(4)]
            pTm = [sb(f"pTm{i}", [128, 512], BF16) for i in range(4)]; dpTm = [Dep() for _ in range(4)]
            recs = [sb(f"rec{i}", [128, 512]) for i in range(2)]; dRecs = [Dep(), Dep()]
            rec = recs[0]; dRec = dRecs[0]
            KcT2 = sb("KcT2", [128, 16, 128], BF16); dKc2 = Dep()
            Vc2 = sb("Vc2", [128, 16, 128], BF16); dVc2 = Dep()
            pTn = sb("pTn", [128, 256], BF16); dpTn = Dep()
            pTnm = sb("pTnm", [128, 256], BF16); dpTnm = Dep()
            cnt = [0]

            def rope_evac(pa, dpa, pb_, dpb_, t0, tn, out_ap, dOut, fin=None):
                i = cnt[0] % 2; cnt[0] += 1
                self.tt("dve", t1[i][:, 0:tn], pa[:, 0:tn], cosT[:, t0:t0 + tn], ALU.mult, [dpa, dRope], [dt1[i]])
                self.tt("dve", t2[i][:, 0:tn], pb_[:, 0:tn], sinT[:, t0:t0 + tn], ALU.mult, [dpb_, dRope], [dt2[i]])
                self.tt("pool", out_ap, t1[i][:, 0:tn], t2[i][:, 0:tn], ALU.add, [dt1[i], dt2[i]], [dOut])
                if fin is not None:
                    fo, a, n = fin
                    self.tt("pool", kfin[:, fo:fo + n], t1[i][:, a:a + n], t2[i][:, a:a + n], ALU.add, [dt1[i], dt2[i]], [dKf])

            lvl = 9
            att = 9
            for f_ in self.debug:
                if f_.startswith("att"):
                    att = int(f_[3:])
            for f_ in self.debug:
                if f_.startswith("swa"):
                    lvl = int(f_[3:])
            for g in range(4 if lvl >= 5 else 1):
                if lvl < 1:
                    break
                wsrc = lambda c0, n: w_in[:, c0:c0 + n].rearrange("(kc p) n -> p kc n", p=128)
                self.ldc(WQ[:], wsrc(AQ0 + g * 256, 256), [dW])
                self.ldc(WK2[:, :, 0:64], wsrc(AK0 + g * 64, 64), [dW])
                self.ldc(WK2[:, :, 64:128], wsrc(AK0 + g * 64, 64), [dW])
                self.ldc(WV[:], wsrc(AV0 + g * 64, 64), [dW])
                for src_t, dst_t, nh in ((WQ, WQs, 4), (WK2, WK2s, 2)):
                    sv = src_t[:].rearrange("p k (h two d) -> p k h two d", two=2, d=32)
                    dv = dst_t[:].rearrange("p k (h two d) -> p k h two d", two=2, d=32)
                    for k in range(8):
                        self.cp("pool", dv[:, k, :, 0, :], sv[:, k, :, 1, :], [dW], [dWs])
                        self.cp("pool", dv[:, k, :, 1, :], sv[:, k, :, 0, :], [dW], [dWs])
                for b, (t0, tn) in enumerate(TBLK):
                    for c in range(2):
                        pa, dpa = self.bank(); pb_, dpb_ = self.bank()
                        for k in range(8):
                            self.mm(pa[:, 0:tn], WQ[:, k, c * 128:(c + 1) * 128], hT[:, k, t0:t0 + tn], k == 0, k == 7, [dW, dHT[b]], [dpa])
                        for k in range(8):
                            self.mm(pb_[:, 0:tn], WQs[:, k, c * 128:(c + 1) * 128], hT[:, k, t0:t0 + tn], k == 0, k == 7, [dWs, dHT[b]], [dpb_])
                        rope_evac(pa, dpa, pb_, dpb_, t0, tn, qT[:, c, t0:t0 + tn], dQ[b])
                    pa, dpa = self.bank(); pb_, dpb_ = self.bank()
                    for k in range(8):
                        self.mm(pa[:, 0:tn], WK2[:, k, :], hT[:, k, t0:t0 + tn], k == 0, k == 7, [dW, dHT[b]], [dpa])
                    for k in range(8):
                        self.mm(pb_[:, 0:tn], WK2s[:, k, :], hT[:, k, t0:t0 + tn], k == 0, k == 7, [dWs, dHT[b]], [dpb_])
                    fin = (0, 384, 128) if b == 3 else ((128, 0, 64) if b == 4 else None)
                    rope_evac(pa, dpa, pb_, dpb_, t0, tn, kT2[:, t0:t0 + tn], dK[b], fin)
                for i in range(NTILE):
                    nt = tile_nt(i)
                    pv, dpv = self.bank()
                    for k in range(8):
                        self.mm(pv[0:nt, 0:64], hT[:, k, 128 * i:128 * i + nt], WV[:, k, :], k == 0, k == 7, [dW, dHT[min(i // 4, 4)]], [dpv])
                    if i < 16:
                        self.cp("act", vlh[0:nt, i, 0, 0:64], pv[0:nt, 0:64], [dpv], [dV[i]])
                        self.cp("dve", vlh[0:nt, i, 1, 64:128], pv[0:nt, 0:64], [dpv], [dV[i]])
                    else:
                        self.cp("act", vtm2[0:nt, i, 0:64], pv[0:nt, 0:64], [dpv], [dV[i]])
                        self.cp("dve", vtm2[0:nt, i, 64:128], pv[0:nt, 0:64], [dpv], [dV[i]])
                    if i >= 15:
                        self.cp("act", vfin[0:nt, i - 15, g * 64:(g + 1) * 64], pv[0:nt, 0:64], [dpv], [dVf])
                if lvl < 2:
                    continue
                pk, dpk = self.bank()
                self.tr(pk[:, 0:64], kfin[0:64, 0:128], [dKf], [dpk])
                self.tr(pk[0:64, 64:128], kfin[0:64, 128:192], [dKf], [dpk])
                self.cp("act", kout_p[:, g * 64:(g + 1) * 64], pk[:, 0:64], [dpk], [dKo])
                self.cp("act", knew_s[0:64, g * 64:(g + 1) * 64], pk[0:64, 64:128], [dpk], [dKs])
                if lvl < 3:
                    continue
                def attA(n):
                    q0 = 128 * n
                    kbs = ([n - 1] if n > 0 else []) + [n]
                    for ki, kb in enumerate(kbs):
                        psA, dpsA = self.bank(); psB, dpsB = self.bank()
                        for c in range(2):
                            self.mm(psA[:, c * 128:(c + 1) * 128], kT2[0:64, 128 * kb:128 * kb + 128], qT[0:64, c, q0:q0 + 128], True, True,
                                    [dK[kb // 4], dQ[n // 4]], [dpsA])
                            self.mm(psB[:, c * 128:(c + 1) * 128], kT2[64:128, 128 * kb:128 * kb + 128], qT[64:128, c, q0:q0 + 128], True, True,
                                    [dK[kb // 4], dQ[n // 4]], [dpsB])
                        sl = (2 * n + ki) % 4
                        self.act(pT[sl][:, 0:256], psA[:, 0:256], AF.Exp, [dpsA], [dpT[sl]], scale=0.125)
                        self.act(pT[sl][:, 256:512], psB[:, 0:256], AF.Exp, [dpsB], [dpT[sl]], scale=0.125)
                        msk = mle if kb == n else mgt
                        self.tt("dve", pTm[sl][:].rearrange("p (r q) -> p r q", r=4), pT[sl][:].rearrange("p (r q) -> p r q", r=4),
                                msk[:, :].unsqueeze(1).to_broadcast([128, 4, 128]), ALU.mult, [dpT[sl], dMb], [dpTm[sl]])

                esv_ = esink[:, 4 * g:4 * g + 4].rearrange("p (c h) -> p h c", h=2)
                self.cp("dve", esg[0:64, :], esv_[0:64, 0, :], [dEs], [dEsg])
                self.cp("dve", esg[64:128, :], esv_[64:128, 1, :], [dEs], [dEsg])

                def attB(n):
                    q0 = 128 * n
                    kbs = ([n - 1] if n > 0 else []) + [n]
                    pnum, dpnum = self.bank(); pden, dpden = self.bank()
                    nmm = 2 * len(kbs)
                    j = 0
                    for ki, kb in enumerate(kbs):
                        sl = (2 * n + ki) % 4
                        for hh in range(2):
                            self.mm(pnum[:, 0:256], vlh[:, kb, hh, :], pTm[sl][:, hh * 256:(hh + 1) * 256], j == 0, j == nmm - 1,
                                    [dV[kb], dpTm[sl]], [dpnum])
                            self.mm(pden[:, 0:256], ones_lh[:, hh, :], pTm[sl][:, hh * 256:(hh + 1) * 256], j == 0, j == nmm - 1,
                                    [dOlh, dpTm[sl]], [dpden])
                            j += 1
                    rc = recs[n % 2]; drc = dRecs[n % 2]
                    self.tt("dve", rc[:, 0:256].rearrange("p (c q) -> p c q", c=2), pden[:, 0:256].rearrange("p (c q) -> p c q", c=2),
                            esg[:, :].unsqueeze(2).to_broadcast([128, 2, 128]), ALU.add, [dpden, dEsg], [drc])
                    self.act(rc[:, 0:256], rc[:, 0:256], AF.Ln, [drc], [drc])
                    self.act(rc[:, 0:256], rc[:, 0:256], AF.Exp, [drc], [drc], scale=-1.0)
                    self.tt("dve", oaT[:, 2 * g:2 * g + 2, q0:q0 + 128], pnum[:, 0:256].rearrange("p (c q) -> p c q", c=2),
                            rc[:, 0:256].rearrange("p (c q) -> p c q", c=2), ALU.mult, [dpnum, drc], [dOA[n // 4]])

                attA(0)
                for n in range(16):
                    if n + 1 < 16:
                        attA(n + 1)
                    attB(n)
                if lvl < 4:
                    continue
                gp, go = g // 2, (g % 2) * 64
                self.cp("act", KcT2[0:64, :, :], KcTn[go:go + 64, gp, :, :], [dKn], [dKc2])
                self.cp("dve", KcT2[64:128, :, :], KcTn[go:go + 64, gp, :, :], [dKn], [dKc2])
                for b0 in range(0, 16, 4):
                    self.ldc(Vc2[:, b0:b0 + 4, 0:64], cvv[:, b0:b0 + 4, g * 64:(g + 1) * 64], [dVc2])
                    self.ldc(Vc2[:, b0:b0 + 4, 64:128], cvv[:, b0:b0 + 4, g * 64:(g + 1) * 64], [dVc2])
                pscA, dpscA = self.bank(); pscB, dpscB = self.bank()
                psnA, dpsnA = self.bank(); psnB, dpsnB = self.bank()
                for b in range(16):
                    for c in range(2):
                        c0 = b * 8 + c * 4
                        self.mm(pscA[:, c0:c0 + 4], KcT2[0:64, b, :], qT[0:64, c, SEQ + 4 * b:SEQ + 4 * b + 4], True, True, [dKc2, dQ[4]], [dpscA])
                        self.mm(pscB[:, c0:c0 + 4], KcT2[64:128, b, :], qT[64:128, c, SEQ + 4 * b:SEQ + 4 * b + 4], True, True, [dKc2, dQ[4]], [dpscB])
                for c in range(2):
                    for hh, (pp, dpp) in enumerate(((psnA, dpsnA), (psnB, dpsnB))):
                        off = hh * 64
                        self.mm(pp[0:64, 0:128].rearrange("p (b c t) -> p b c t", b=16, c=2)[:, :, c, :], kT2[off:off + 64, SEQ:NT],
                                qT[off:off + 64, c, SEQ:NT].rearrange("p (b t) -> p b t", t=4), True, True, [dK[4], dQ[4]], [dpp])
                self.act(pTn[:, 0:128], pscA[:, 0:128], AF.Exp, [dpscA], [dpTn], scale=0.125)
                self.act(pTn[:, 128:256], pscB[:, 0:128], AF.Exp, [dpscB], [dpTn], scale=0.125)
                self.tt("dve", pTnm[:, :].rearrange("p (a t) -> p a t", t=4), pTn[:, :].rearrange("p (a t) -> p a t", t=4),
                        mcache[:, :].unsqueeze(1).to_broadcast([128, 64, 4]), ALU.mult, [dpTn, dMb], [dpTnm])
                sl = cnt[0] % 2; cnt[0] += 1
                self.act(pT[sl][0:64, 0:128], psnA[0:64, 0:128], AF.Exp, [dpsnA], [dpT[sl]], scale=0.125)
                self.act(pT[sl][0:64, 128:256], psnB[0:64, 0:128], AF.Exp, [dpsnB], [dpT[sl]], scale=0.125)
                for hh in range(2):
                    self.tt("dve", pTm[sl][0:64, hh * 128:(hh + 1) * 128].rearrange("p (b c t) -> p b c t", b=16, c=2),
                            pT[sl][0:64, hh * 128:(hh + 1) * 128].rearrange("p (b c t) -> p b c t", b=16, c=2),
                            mblkc[0:64, 0:64].rearrange("p (b t) -> p b t", t=4).unsqueeze(2).to_broadcast([64, 16, 2, 4]), ALU.mult,
                            [dpT[sl], dMb], [dpTm[sl]])
                pnum, dpnum = self.bank(); pden, dpden = self.bank()
                self.mm(pnum[:, 0:256], vtm2[0:64, 16, :], pTm[sl][0:64, 0:256], True, False, [dV[16], dpTm[sl]], [dpnum])
                self.mm(pden[:, 0:256], ones[0:64, :], pTm[sl][0:64, 0:256], True, False, [dOnes, dpTm[sl]], [dpden])
                for b in range(16):
                    pv_ = pTnm[:, :].rearrange("p (h b x) -> p h b x", h=2, b=16)[:, :, b, :]
                    self.mm(pnum[:, 0:256].rearrange("p (h b x) -> p h b x", h=2, b=16)[:, :, b, :], Vc2[:, b, :], pv_, False, True,
                            [dVc2, dpTnm], [dpnum])
                    self.mm(pden[:, 0:256].rearrange("p (h b x) -> p h b x", h=2, b=16)[:, :, b, :], ones[:, :], pv_, False, True,
                            [dOnes, dpTnm], [dpden])
                esv = esink[:, 4 * g:4 * g + 4].rearrange("p (c h) -> p h c", h=2)
                for hh in range(2):
                    self.tt("dve", rec[:, hh * 128:(hh + 1) * 128].rearrange("p (b c t) -> p b c t", b=16, c=2),
                            pden[:, hh * 128:(hh + 1) * 128].rearrange("p (b c t) -> p b c t", b=16, c=2),
                            esv[:, hh, :].unsqueeze(1).unsqueeze(3).to_broadcast([128, 16, 2, 4]), ALU.add, [dpden, dEs], [dRec])
                self.recip(rec[:, 0:256], rec[:, 0:256], [dRec], [dRec])
                for hh in range(2):
                    off = hh * 64
                    for c in range(2):
                        nv = pnum[off:off + 64, hh * 128:(hh + 1) * 128].rearrange("p (b c t) -> p b c t", b=16, c=2)[:, :, c, :]
                        rv = rec[off:off + 64, hh * 128:(hh + 1) * 128].rearrange("p (b c t) -> p b c t", b=16, c=2)[:, :, c, :]
                        self.tt("dve", oaT[off:off + 64, 2 * g + c, SEQ:NT].rearrange("p (b t) -> p b t", t=4), nv, rv, ALU.mult,
                                [dpnum, dRec], [dOA[4]])
            if lvl < 6:
                S.emit()
                return
            self.store(L["kp"][:, :], kout_p[:], [dKo])
            self.store(L["vp"][:, :], vfin[:, 0, :], [dVf])
            for b in range(16):
                self.store(L["ks"][b, 0:124, :], L["ck"][b, 4:128, :], [])
                self.store(L["vs"][b, 0:124, :], L["cv"][b, 4:128, :], [])
                self.store(L["ks"][b, 124:128, :], knew_s[4 * b:4 * b + 4, :], [dKs])
                self.store(L["vs"][b, 124:128, :], vfin[4 * b:4 * b + 4, 1, :], [dVf])
            S.emit()

    def mlstm(self, L, hT, dHT, hgT, dHG):
        nc, S = self.nc, self.S
        w_in = L["w_in"]
        MQ0, MK0, MV0, MO0, MG0 = 0, 512, 1024, 2048, 3072
        KS = 128.0 ** -0.5
        wsrc = lambda c0, n: w_in[:, c0:c0 + n].rearrange("(kc p) n -> p kc n", p=128)
        with ExitStack() as es:
            sb = lambda name, shape, dt=F32: es.enter_context(nc.sbuf_tensor(self.uname(name), list(shape), dt))
            A_tm = sb("A_tm", [128, NTILE, 4]); B_tm = sb("B_tm", [128, NTILE, 4]); MendTok = sb("MendTok", [128, NTILE, 4])
            W_tm = sb("W_tm", [128, NTILE, 4]); THR = sb("THR", [128, NTILE, 4]); dGt = Dep()
            ECrep = sb("ECrep", [128, 4, 32]); MendRep = sb("MendRep", [128, 4, 16]); dRep = Dep()
            EcTok = sb("EcTok", [128, 4]); dEt = Dep()
            mk = sb("mk", [128, 768]); dMk = Dep()
            self.ld(mk[:], L["cmask"][:, :], [dMk])
            mle_f = mk[:, 0:128]; mblkc_f = mk[:, 256:384]; blk_f = mk[:, 384:512]; rmask_f = mk[:, 640:656]
            n0tok = sb("n0tok", [128, 512]); dN0 = Dep()
            nnew = sb("nnew", [128, 512]); dNn = Dep()
            for b in range(16):
                self.ld(n0tok[4 * b:4 * b + 4, :], L["stn"][b:b + 1, :].broadcast_to([4, 512]), [dN0])
            for t_ in (A_tm, B_tm, MendTok):
                self.memset("pool", t_[:], 0.0, [dGt])
            with ExitStack() as es2:
                sb2 = lambda name, shape, dt=F32: es2.enter_context(nc.sbuf_tensor(self.uname(name), list(shape), dt))
                Wg = sb2("Wg", [128, 8, 8], BF16); dWg = Dep()
                self.ldc(Wg[:], wsrc(MG0, 8), [dWg])
                bigf = sb2("bigf", [128, 8]); dBg = Dep()
                self.ld(bigf[:, 0:4], L["b_ig"][0:1, :].broadcast_to([128, 4]), [dBg])
                self.ld(bigf[:, 4:8], L["b_fg"][0:1, :].broadcast_to([128, 4]), [dBg])
                g_tm = sb2("g_tm", [128, NTILE, 8]); dGtm = Dep()
                GI = sb2("GI", [4, NT]); GF = sb2("GF", [4, NT]); Bf = sb2("Bf", [4, NT]); Mf = sb2("Mf", [4, NT])
                Z = sb2("Z", [4, NT]); T1 = sb2("T1", [4, NT]); T2 = sb2("T2", [4, NT]); dF = Dep()
                ones4 = sb2("ones4", [4, 128]); dO4 = Dep()
                m0T = sb2("m0T", [4, 16]); dM0 = Dep()
                V48 = sb2("V48", [4, 48]); X = sb2("X", [4, 4, 48]); dV = Dep()
                Sx = sb2("Sx", [4, 2, 16, 4]); dSx = Dep()
                mout = sb2("mout", [4, 17]); dMo = Dep()
                self.memset("pool", Z[:], 0.0, [dF]); self.memset("pool", ones4[:], 1.0, [dO4])
                self.memset("pool", g_tm[:], 0.0, [dGtm])
                self.ld_nc(m0T[:], L["stm"].rearrange("b h -> h b"), [dM0])
                for i in range(NTILE):
                    nt = tile_nt(i)
                    pg, dpg = self.bank()
                    for k in range(8):
                        self.mm(pg[0:nt, 0:8], hT[:, k, 128 * i:128 * i + nt], Wg[:, k, :], k == 0, k == 7, [dWg, dHT[min(i // 4, 4)]], [dpg])
                    self.tt("dve", g_tm[0:nt, i, :], pg[0:nt, 0:8], bigf[0:nt, :], ALU.add, [dpg, dBg], [dGtm])
                for b, (t0, tn) in enumerate(TBLK):
                    p1, dp1 = self.bank(); p2, dp2 = self.bank()
                    for j, i in enumerate(blk_tiles(b)):
                        nt = tile_nt(i)
                        self.tr(p1[0:4, j * 128:j * 128 + nt], g_tm[0:nt, i, 0:4], [dGtm], [dp1])
                        self.tr(p2[0:4, j * 128:j * 128 + nt], g_tm[0:nt, i, 4:8], [dGtm], [dp2])
                    self.cp("dve", GI[:, t0:t0 + tn], p1[0:4, 0:tn], [dp1], [dF])
                    self.cp("dve", GF[:, t0:t0 + tn], p2[0:4, 0:tn], [dp2], [dF])
                self.act(T1[:], GF[:], AF.Abs, [dF], [dF])
                self.act(T1[:], T1[:], AF.Exp, [dF], [dF], scale=-1.0)
                self.act(T1[:], T1[:], AF.Ln, [dF], [dF], bias=1.0)
                self.ts("dve", T2[:], GF[:], 0.0, None, ALU.min, None, [dF], [dF])
                self.tt("dve", GF[:], T2[:], T1[:], ALU.subtract, [dF], [dF])
                self.S.op("dve", lambda e: e.tensor_tensor_scan(out=Bf[:, 0:SEQ], data0=GF[:, 0:SEQ], data1=Z[:, 0:SEQ], initial=0.0,
                                                                op0=ALU.add, op1=ALU.add), [dF], [dF])
                gfs = GF[:, SEQ:NT].rearrange("h (b j) -> h b j", j=4); bfs = Bf[:, SEQ:NT].rearrange("h (b j) -> h b j", j=4)
                self.cp("dve", bfs[:, :, 0], gfs[:, :, 0], [dF], [dF])
                for j in range(1, 4):
                    self.tt("dve", bfs[:, :, j], bfs[:, :, j - 1], gfs[:, :, j], ALU.add, [dF], [dF])
                self.tt("dve", GI[:], GI[:], Bf[:], ALU.subtract, [dF], [dF])
                self.S.op("dve", lambda e: e.tensor_tensor_scan(out=Mf[:, 0:SEQ], data0=Z[:, 0:SEQ], data1=GI[:, 0:SEQ], initial=0.0,
                                                                op0=ALU.add, op1=ALU.max), [dF], [dF])
                as_ = GI[:, SEQ:NT].rearrange("h (b j) -> h b j", j=4); ms_ = Mf[:, SEQ:NT].rearrange("h (b j) -> h b j", j=4)
                self.tt("dve", ms_[:, :, 0], as_[:, :, 0], m0T[:, :], ALU.max, [dF, dM0], [dF])
                for j in range(1, 4):
                    self.tt("dve", ms_[:, :, j], ms_[:, :, j - 1], as_[:, :, j], ALU.max, [dF], [dF])
                mend_p = Mf[:, 0:SEQ].rearrange("h (c t) -> h c t", t=128)[:, :, 127]
                self.cp("dve", V48[:, 0:16], mend_p, [dF], [dV])
                self.cp("dve", V48[:, 16:17], Z[:, 0:1], [dF], [dV])
                self.cp("dve", V48[:, 17:32], V48[:, 0:15], [dV], [dV])
                self.tt("dve", V48[:, 16:32], V48[:, 16:32], V48[:, 0:16], ALU.subtract, [dV], [dV])
                self.tt("dve", V48[:, 32:48], m0T[:, :], ms_[:, :, 3], ALU.subtract, [dF, dM0], [dV])
                self.act(V48[:, 16:48], V48[:, 16:48], AF.Exp, [dV], [dV])
                self.tt("dve", X[:], V48[:, :].unsqueeze(1).to_broadcast([4, 4, 48]),
                        self.ident[0:4, 0:4].unsqueeze(2).to_broadcast([4, 4, 48]), ALU.mult, [dV, self.dIdent], [dV])
                pr, dpr = self.bank()
                self.mm(pr[:, 0:192], ones4[:, :], X[:].rearrange("h a c -> h (a c)"), True, True, [dO4, dV], [dpr])
                prv = pr[:, 0:192].rearrange("p (a c) -> p a c", a=4)
                self.cp("dve", MendRep[:], prv[:, :, 0:16], [dpr], [dRep])
                self.cp("dve", ECrep[:], prv[:, :, 16:48], [dpr], [dRep])
                self.cp("dve", Sx[:, 0, :, :], ms_[:, :, 3].unsqueeze(2).to_broadcast([4, 16, 4]), [dF], [dSx])
                self.cp("dve", Sx[:, 1, :, :], V48[:, 32:48].unsqueeze(2).to_broadcast([4, 16, 4]), [dV], [dSx])
                pa_, dpa_ = self.bank(); pb_, dpb_ = self.bank()
                for i in range(NTILE):
                    nt = tile_nt(i)
                    self.tr(pa_[0:nt, i * 4:i * 4 + 4], GI[0:4, 128 * i:128 * i + nt], [dF], [dpa_])
                    self.tr(pb_[0:nt, i * 4:i * 4 + 4], Bf[0:4, 128 * i:128 * i + nt], [dF], [dpb_])
                self.tr(pa_[0:64, 68:72], Sx[0:4, 0, :, :].rearrange("h b j -> h (b j)"), [dSx], [dpa_])
                self.tr(pb_[0:64, 68:72], Sx[0:4, 1, :, :].rearrange("h b j -> h (b j)"), [dSx], [dpb_])
                self.cp("dve", A_tm[:, 0:16, :].rearrange("p c h -> p (c h)"), pa_[:, 0:64], [dpa_], [dGt])
                self.cp("dve", A_tm[0:64, 16, :], pa_[0:64, 64:68], [dpa_], [dGt])
                self.cp("dve", B_tm[:, 0:16, :].rearrange("p c h -> p (c h)"), pb_[:, 0:64], [dpb_], [dGt])
                self.cp("dve", B_tm[0:64, 16, :], pb_[0:64, 64:68], [dpb_], [dGt])
                self.cp("dve", MendTok[0:64, 16, :], pa_[0:64, 68:72], [dpa_], [dGt])
                self.cp("dve", EcTok[0:64, :], pb_[0:64, 68:72], [dpb_], [dEt])
                self.cp("dve", MendTok[:, 0:16, :], MendRep[:].rearrange("p h c -> p c h"), [dRep], [dGt])
                self.tt("dve", W_tm[:], A_tm[:], MendTok[:], ALU.subtract, [dGt], [dGt])
                self.act(W_tm[:], W_tm[:], AF.Exp, [dGt], [dGt])
                self.tt("dve", THR[:], B_tm[:], MendTok[:], ALU.add, [dGt], [dGt])
                self.act(THR[:], THR[:], AF.Exp, [dGt], [dGt], scale=-1.0)
                self.tt("dve", mout[:, 0:1], Mf[:, SEQ - 1:SEQ], Bf[:, SEQ - 1:SEQ], ALU.add, [dF], [dMo])
                self.tt("dve", mout[:, 1:17], ms_[:, :, 3], bfs[:, :, 3], ALU.add, [dF], [dMo])
                self.store(L["mp"][:, :], mout[:, 0:1], [dMo])
                self.store_nc(L["ms"].rearrange("b h -> h b"), mout[:, 1:17], [dMo])
                S.emit()
            Wm = sb("Wm", [128, 8, 768], BF16); dWm = Dep()
            qTh = sb("qTh", [128, NT], BF16); kTh = sb("kTh", [128, NT], BF16); dQK = [Dep() for _ in range(5)]
            k_tm = sb("k_tm", [128, NTILE, 128], BF16); v_aug = sb("v_aug", [128, NTILE, 258], BF16); so_tm = sb("so_tm", [128, NTILE, 256], BF16)
            dTM = [Dep() for _ in range(NTILE)]
            ST = sb("ST", [128, 257]); dST = Dep()
            tmpSs = [sb(f"tmpS{i}", [128, 128]) for i in range(2)]; dtSs = [Dep(), Dep()]
            sTms = [sb(f"sTm{i}", [128, 128], BF16) for i in range(3)]; dsTs = [Dep() for _ in range(3)]
            wvs = [sb(f"wv{i}", [128, 258], BF16) for i in range(3)]; dwvs = [Dep() for _ in range(3)]
            tmpS = tmpSs[0]; dtS = dtSs[0]
            sTm = sb("sTm_s", [128, 64], BF16); dsT = Dep()
            Cdec = sb("Cdec", [128, 258], BF16); dCd = Dep()
            junk = sb("junk", [128, 256], BF16); dJ = Dep()
            wv = sb("wv_s", [128, 258], BF16); dwv = Dep()
            bmk = sb("bmk", [128, 16, 64], BF16); dBm = Dep()
            self.ldc(bmk[:].rearrange("p b t -> p (b t)"), L["bmask"][:, :], [dBm])
            C0n = sb("C0n", [128, 8, 2, 128]); dC0 = Dep()
            C0T = sb("C0T", [128, 16, 258], BF16); dC0T = Dep()
            n0T = sb("n0T", [128, 16]); dn0T = Dep()
            maskE = sb("maskE", [128, 16, 64], BF16); dmE = Dep()
            Qblk = sb("Qblk", [128, 16, 64], BF16); dQb = Dep()
            wvblk = sb("wvblk", [128, 8, 256], BF16); dwb = Dep()
            Cout = [sb(f"Cout{i}", [128, 2, 128]) for i in range(2)]; dCo = [Dep(), Dep()]
            BselW = sb("BselW", [128, 64], BF16); dBs = Dep()
            self.memset("pool", v_aug[:], 1.0, dTM)

            Hn = sb("Hn", [128, NTILE, 257]); dHn = [Dep() for _ in range(NTILE)]
            if os.environ.get("MK_PAD"):
                sb("pad", [128, int(os.environ["MK_PAD"]) * 256])
            hst = sb("hst", [128, 8, NTILE]); dhst = Dep(); dhst2 = Dep()

            def hn_pre(h):
                dn = hst[:, 0, :]
                self.memset("pool", hst[:], 0.0, [dhst])
                self.act(dn, Hn[:, :, 256], AF.Abs, dHn, [dhst])
                self.tt("dve", dn, dn, THR[:, :, h], ALU.max, [dGt], [dhst])
                self.recip(dn, dn, [dhst], [dhst2])

            def hn_stats(h, i):
                nt = tile_nt(i)
                self.act(Hn[0:nt, i, 0:256], Hn[0:nt, i, 0:256], AF.Identity, [dhst2], [dHn[i], dhst], scale=hst[0:nt, 0, i:i + 1],
                         accum=hst[0:nt, 1, i:i + 1])
                self.act(junk[0:nt, :], Hn[0:nt, i, 0:256], AF.Square, [dHn[i]], [dJ, dhst], accum=hst[0:nt, 2, i:i + 1])

            def hn_tiny(h):
                dn = hst[:, 0, :]; s1 = hst[:, 1, :]; s2 = hst[:, 2, :]; mean = hst[:, 3, :]; msq = hst[:, 4, :]
                rstd = hst[:, 5, :]; nb = hst[:, 6, :]
                self.ts("dve", mean, s1, 1.0 / 256, None, ALU.mult, None, [dhst], [dhst])
                self.tt("dve", msq, mean, mean, ALU.mult, [dhst], [dhst])
                self.ts("dve", rstd, s2, 1.0 / 256, HN_EPS, ALU.mult, ALU.add, [dhst], [dhst])
                self.tt("dve", rstd, rstd, msq, ALU.subtract, [dhst], [dhst])
                self.act(rstd, rstd, AF.Sqrt, [dhst], [dhst])
                self.recip(rstd, rstd, [dhst], [dhst])
                self.tt("dve", nb, mean, rstd, ALU.mult, [dhst], [dhst])
                self.ts("dve", nb, nb, -1.0, None, ALU.mult, None, [dhst], [dhst])

            def hn_steps(h):
                pts = {}

                def n1(i):
                    nt = tile_nt(i)
                    self.act(Hn[0:nt, i, 0:256], Hn[0:nt, i, 0:256], AF.Identity, [dhst], [dHn[i]], bias=hst[0:nt, 6, i:i + 1],
                             scale=hst[0:nt, 5, i:i + 1])
                    self.tt("dve", Hn[0:nt, i, 0:256], Hn[0:nt, i, 0:256], so_tms[h % 2][0:nt, i, :], ALU.mult, [dSO[h % 2][i]], [dHn[i]])

                def n2(i):
                    nt = tile_nt(i)
                    pt, dpt = self.bank(hold=("hn", h, i))
                    for vb in range(2):
                        self.tr(pt[:, vb * 128:vb * 128 + nt], Hn[0:nt, i, vb * 128:(vb + 1) * 128], [dHn[i]], [dpt])
                    pts[i] = (pt, dpt)

                def n3(i):
                    nt = tile_nt(i)
                    t0 = 128 * i
                    pt, dpt = pts.pop(i)
                    self.cp("act" if i % 2 == 0 else "dve", hgT[:, 2 * h:2 * h + 2, t0:t0 + nt],
                            pt[:, 0:256].rearrange("p (a b) -> p a b", a=2)[:, :, 0:nt], [dpt], [dHG[min(i // 4, 4)]])
                    self.release(("hn", h, i))

                def step(k):
                    if k == 0:
                        n1(0)
                    if k + 1 < NTILE:
                        n1(k + 1)
                    if 0 <= k < NTILE:
                        n2(k)
                    if 0 <= k - 1 < NTILE:
                        n3(k - 1)
                return step

            def head_tail(po, dpo, i, h, nt):
                self.cp("act", Hn[0:nt, i, :], po[0:nt, 0:257], [dpo], [dHn[i]])

            Wm2 = sb("Wm2", [128, 8, 768], BF16); dWm2 = Dep()
            so_tm2 = sb("so_tm2", [128, NTILE, 256], BF16)
            Wms = [Wm, Wm2]; dWms = [dWm, dWm2]; so_tms = [so_tm, so_tm2]
            dSO = [[Dep() for _ in range(NTILE)] for _ in range(2)]

            def load_w(h):
                W_ = Wms[h % 2]; d_ = dWms[h % 2]
                self.ldc(W_[:, :, 0:128], wsrc(MQ0 + h * 128, 128), [d_])
                self.ldc(W_[:, :, 128:256], wsrc(MK0 + h * 128, 128), [d_])
                self.ldc(W_[:, :, 256:512], wsrc(MV0 + h * 256, 256), [d_])
                self.ldc(W_[:, :, 512:768], wsrc(MO0 + h * 256, 256), [d_])

            def c0n_load(h, half):
                for bq in range(8):
                    self.ld(C0n[:, bq, :, :], L["stC"][8 * half + bq, h].rearrange("(vb p) d -> p vb d", p=128), [dC0])

            def proj(h, per_tile=None):
                    for b, (t0, tn) in enumerate(TBLK):
                        pq, dpq = self.bank(); pk, dpk = self.bank()
                        for k in range(8):
                            self.mm(pq[:, 0:tn], Wms[h % 2][:, k, 0:128], hT[:, k, t0:t0 + tn], k == 0, k == 7, [dWms[h % 2], dHT[b]], [dpq])
                        for k in range(8):
                            self.mm(pk[:, 0:tn], Wms[h % 2][:, k, 128:256], hT[:, k, t0:t0 + tn], k == 0, k == 7, [dWms[h % 2], dHT[b]], [dpk])
                        self.cp("act", qTh[:, t0:t0 + tn], pq[:, 0:tn], [dpq], [dQK[b]])
                        self.S.op("act", lambda e, t0=t0, tn=tn, pk=pk: e.mul(out=kTh[:, t0:t0 + tn], in_=pk[:, 0:tn], mul=KS), [dpk], [dQK[b]])
                    for i in range(NTILE):
                        nt = tile_nt(i)
                        pa, dpa = self.bank(); pb_, dpb_ = self.bank()
                        for k in range(8):
                            self.mm(pa[0:nt, 0:384], hT[:, k, 128 * i:128 * i + nt], Wms[h % 2][:, k, 128:512], k == 0, k == 7, [dWms[h % 2], dHT[min(i // 4, 4)]], [dpa])
                        for k in range(8):
                            self.mm(pb_[0:nt, 0:256], hT[:, k, 128 * i:128 * i + nt], Wms[h % 2][:, k, 512:768], k == 0, k == 7, [dWms[h % 2], dHT[min(i // 4, 4)]], [dpb_])
                        self.S.op("act", lambda e, i=i, nt=nt, pa=pa: e.mul(out=k_tm[0:nt, i, :], in_=pa[0:nt, 0:128], mul=KS), [dpa], [dTM[i]])
                        self.cp("dve", v_aug[0:nt, i, 0:256], pa[0:nt, 128:384], [dpa], [dTM[i]])
                        self.act(so_tms[h % 2][0:nt, i, :], pb_[0:nt, 0:256], AF.Sigmoid, [dpb_], [dSO[h % 2][i]])
                        if per_tile is not None:
                            per_tile(i)

            def body_(h, hook=None):
                    i = 16
                    ps_, dps = self.bank()
                    self.mm(ps_[0:64, 0:64], kTh[:, SEQ:NT], qTh[:, SEQ:NT], True, True, [dQK[4]], [dps])
                    self.act(tmpS[0:64, 0:64], ps_[0:64, 0:64], AF.Identity, [dps, dGt], [dtS], scale=W_tm[0:64, 16, h:h + 1])
                    self.tt("dve", sTm[0:64, 0:64], tmpS[0:64, 0:64], mblkc_f[0:64, 0:64], ALU.mult, [dtS, dMk], [dsT])
                    self.tt("dve", maskE[:], bmk[:], ECrep[:, h, 16:32].unsqueeze(2).to_broadcast([128, 16, 64]), ALU.mult, [dBm, dRep], [dmE])
                    self.tt("dve", Qblk[:], maskE[:], qTh[:, SEQ:NT].unsqueeze(1).to_broadcast([128, 16, 64]), ALU.mult, [dmE, dQK[4]], [dQb])
                    pn0, dpn0 = self.bank()
                    self.tr(pn0[:, 0:64], n0tok[0:64, h * 128:(h + 1) * 128], [dN0], [dpn0])
                    self.cp("dve", n0T[:], pn0[:, 0:64].rearrange("p (b j) -> p b j", j=4)[:, :, 0], [dpn0], [dn0T])
                    self.act(wv[0:64, 0:257], v_aug[0:64, 16, 0:257], AF.Identity, [dTM[16], dGt], [dwv], scale=W_tm[0:64, 16, h:h + 1])
                    def halfproc(half):
                            b0 = 8 * half
                            for bb in range(8):
                                if bb % 2 == 0:
                                    pt, dpt = self.bank()
                                for vb in range(2):
                                    c0 = (bb % 2) * 256 + vb * 128
                                    self.tr(pt[:, c0:c0 + 128], C0n[:, bb, vb, :], [dC0], [dpt])
                                if bb % 2 == 1:
                                    self.cp("act", C0T[:, b0 + bb - 1:b0 + bb + 1, 0:256], pt[:, :].rearrange("p (a b) -> p a b", a=2), [dpt], [dC0T])
                            self.tt("dve", wvblk[0:64, :, :], wv[0:64, 0:256].unsqueeze(1).to_broadcast([64, 8, 256]),
                                    rmask_f[0:64, b0:b0 + 8].unsqueeze(2).to_broadcast([64, 8, 256]), ALU.mult, [dwv, dMk], [dwb])
                            for bb in range(8):
                                b = b0 + bb
                                pC, dpC = self.bank()
                                for vb in range(2):
                                    self.mm(pC[:, vb * 128:(vb + 1) * 128], wvblk[0:64, bb, vb * 128:(vb + 1) * 128], k_tm[0:64, 16, :], True, True,
                                            [dwb, dTM[16]], [dpC])
                                co = b % 2
                                self.stt("dve", Cout[co][:].rearrange("p a b -> p (a b)"), C0n[:, bb, :, :].rearrange("p a b -> p (a b)"),
                                         ECrep[:, h, 16 + b:17 + b], pC[:, 0:256], ALU.mult, ALU.add, [dC0, dRep, dpC], [dCo[co]])
                                self.store(L["Cs"][b, h].rearrange("(vb p) d -> p vb d", p=128), Cout[co][:], [dCo[co]])
                    halfproc(0)
                    c0n_load(h, 1)
                    pcs = {}

                    def pre(c):
                        t0 = 128 * c
                        ps_, dps = self.bank()
                        self.mm(ps_[:, 0:128], kTh[:, t0:t0 + 128], qTh[:, t0:t0 + 128], True, True, [dQK[c // 4]], [dps])
                        a = c % 2; b3 = c % 3
                        self.act(tmpSs[a][:], ps_[:, 0:128], AF.Identity, [dps, dGt], [dtSs[a]], scale=W_tm[:, c, h:h + 1])
                        self.tt("dve", sTms[b3][:], tmpSs[a][:], mle_f, ALU.mult, [dtSs[a], dMk], [dsTs[b3]])
                        self.act(wvs[b3][:, 0:257], v_aug[:, c, 0:257], AF.Identity, [dTM[c], dGt], [dwvs[b3]], scale=W_tm[:, c, h:h + 1])
                        pc, dpc = self.bank(hold=("pc", h, c))
                        self.mm(pc[:, 0:257], k_tm[:, c, :], wvs[b3][:, 0:257], True, True, [dTM[c], dwvs[b3]], [dpc])
                        pcs[c] = (pc, dpc)

                    def main(c):
                        t0 = 128 * c
                        b3 = c % 3
                        po, dpo = self.bank()
                        self.mm(po[:, 0:257], sTms[b3][:], v_aug[:, c, 0:257], True, c == 0, [dsTs[b3], dTM[c]], [dpo])
                        if c > 0:
                            self.act(Cdec[:, 0:257], ST[:], AF.Identity, [dST, dRep], [dCd], scale=ECrep[:, h, c:c + 1])
                            self.mm(po[:, 0:257], qTh[:, t0:t0 + 128], Cdec[:, 0:257], False, True, [dQK[c // 4], dCd], [dpo])
                        head_tail(po, dpo, c, h, 128)
                        pc, dpc = pcs.pop(c)
                        if c == 0:
                            self.cp("dve", ST[:], pc[:, 0:257], [dpc], [dST])
                        else:
                            self.stt("dve", ST[:], ST[:], ECrep[:, h, c:c + 1], pc[:, 0:257], ALU.mult, ALU.add, [dpc, dRep], [dST])
                        self.release(("pc", h, c))

                    pre(0)
                    for c in range(16):
                        if hook is not None:
                            hook(c)
                        if c + 1 < 16:
                            pre(c + 1)
                        main(c)
                    if hook is not None:
                        for k in range(16, NTILE + 2):
                            hook(k)
                    pt, dpt = self.bank()
                    for vb in range(2):
                        self.tr(pt[:, vb * 128:(vb + 1) * 128], ST[:, vb * 128:(vb + 1) * 128], [dST], [dpt])
                    self.cp("dve", Cout[0][:].rearrange("p a b -> p (a b)"), pt[:, 0:256], [dpt], [dCo[0]])
                    self.store(L["Cp"][h].rearrange("(vb p) d -> p vb d", p=128), Cout[0][:], [dCo[0]])
                    self.store_nc(L["np"][h:h + 1, :].rearrange("o d -> d o"), ST[:, 256:257], [dST])
                    halfproc(1)
                    self.cp("dve", C0T[:, :, 256], n0T[:, :], [dn0T], [dC0T])
                    po, dpo = self.bank()
                    self.mm(po[0:64, 0:257], sTm[0:64, 0:64], v_aug[0:64, 16, 0:257], True, False, [dsT, dTM[16]], [dpo])
                    for b in range(16):
                        self.mm(po[0:64, 0:257], Qblk[:, b, :], C0T[:, b, 0:257], False, b == 15, [dQb, dC0T], [dpo])
                    head_tail(po, dpo, 16, h, 64)
                    self.act(BselW[0:64, :], blk_f[0:64, 0:64], AF.Identity, [dMk, dGt], [dBs], scale=W_tm[0:64, 16, h:h + 1])
                    pN, dpN = self.bank()
                    self.mm(pN[0:64, 0:128], BselW[0:64, :], k_tm[0:64, 16, :], True, True, [dBs, dTM[16]], [dpN])
                    self.stt("dve", nnew[0:64, h * 128:(h + 1) * 128], n0tok[0:64, h * 128:(h + 1) * 128], EcTok[0:64, h:h + 1], pN[0:64, 0:128],
                             ALU.mult, ALU.add, [dN0, dEt, dpN], [dNn])

            load_w(0); c0n_load(0, 0); proj(0); load_w(1)
            prev_steps = None
            for h in range(4):
                body_(h, hook=prev_steps)
                hn_pre(h)
                if h + 1 < 4:
                    c0n_load(h + 1, 0)
                    proj(h + 1, per_tile=lambda i, h=h: hn_stats(h, i))
                    if h + 2 < 4:
                        load_w(h + 2)
                else:
                    for i in range(NTILE):
                        hn_stats(h, i)
                hn_tiny(h)
                prev_steps = hn_steps(h)
            for k in range(NTILE + 2):
                prev_steps(k)
            for b in range(16):
                self.store(L["ns"][b:b + 1, :], nnew[4 * b:4 * b + 1, :], [dNn])
            S.emit()

    def ln_tile(self, x, dX, nt, lnG, lnB, dLn, st, dSt, junk, dJ, eps):
        self.memset("pool", st[0:nt, 0:2], 0.0, [dSt])
        self.act(junk[0:nt, :], x, AF.Identity, [dX], [dJ, dSt], accum=st[0:nt, 0:1])
        self.act(junk[0:nt, :], x, AF.Square, [dX], [dJ, dSt], accum=st[0:nt, 1:2])
        self.ts("dve", st[0:nt, 2:3], st[0:nt, 0:1], 1.0 / D, None, ALU.mult, None, [dSt], [dSt])
        self.tt("dve", st[0:nt, 3:4], st[0:nt, 2:3], st[0:nt, 2:3], ALU.mult, [dSt], [dSt])
        self.ts("dve", st[0:nt, 4:5], st[0:nt, 1:2], 1.0 / D, eps, ALU.mult, ALU.add, [dSt], [dSt])
        self.tt("dve", st[0:nt, 4:5], st[0:nt, 4:5], st[0:nt, 3:4], ALU.subtract, [dSt], [dSt])
        self.act(st[0:nt, 4:5], st[0:nt, 4:5], AF.Sqrt, [dSt], [dSt])
        self.recip(st[0:nt, 5:6], st[0:nt, 4:5], [dSt], [dSt])
        self.tt("dve", st[0:nt, 6:7], st[0:nt, 2:3], st[0:nt, 5:6], ALU.mult, [dSt], [dSt])
        self.ts("dve", st[0:nt, 6:7], st[0:nt, 6:7], -1.0, None, ALU.mult, None, [dSt], [dSt])
        self.act(x, x, AF.Identity, [dSt], [dX], bias=st[0:nt, 6:7], scale=st[0:nt, 5:6])
        self.tt("dve", x, x, lnG[0:nt, :], ALU.mult, [dLn], [dX])
        self.tt("dve", x, x, lnB[0:nt, :], ALU.add, [dLn], [dX])

    def merge(self, L, hT, dHT, hgT, dHG, next_ada=None):
        nc, S = self.nc, self.S
        w_in = L["w_in"]
        GM0, GA0 = 4616, 5640
        wsrc = lambda w, c0, n: w[:, c0:c0 + n].rearrange("(kc p) n -> p kc n", p=128)
        with ExitStack() as es:
            sb = lambda name, shape, dt=F32: es.enter_context(nc.sbuf_tensor(self.uname(name), list(shape), dt))
            oaT = sb("oaT2", [128, 8, NT], BF16); dOA = Dep()
            self.ld(oaT[:].rearrange("p c t -> p (c t)"), L["oa_scr"][:, :], [dOA])
            Wa = sb("Wa", [128, 8, D], BF16); dWa = Dep()
            Wg = sb("Wg", [128, 8, D], BF16); dWg = Dep()
            mraw = sb("mraw", [8, 128]); dMr = Dep(); mngT = sb("mngT", [128, 8]); dMn = Dep()
            zt = sb("zt", [128, 8, 512], BF16); dzt = Dep()
            sg = [sb(f"sg{i}", [128, 512]) for i in range(2)]; dsg = [Dep(), Dep()]
            za = [sb(f"za{i}", [128, 512]) for i in range(2)]; dza = [Dep(), Dep()]
            xt = sb("xt", [128, 3, D]); dxt = [Dep(), Dep(), Dep()]
            tG = [sb(f"tG{i}", [128, 512]) for i in range(2)]; dtG = [Dep(), Dep()]
            lnG = sb("lnG", [128, D]); lnB = sb("lnB", [128, D]); dLn = Dep()
            sts = [sb(f"st{i}", [128, 8]) for i in range(2)]; dSts = [Dep(), Dep()]
            junk = sb("junk", [128, D], BF16); dJ = Dep()
            self.ld(lnG[:], L["ln"]["ln2_g"][0:1, :].broadcast_to([128, D]), [dLn])
            self.ld(lnB[:], L["ln"]["ln2_b"][0:1, :].broadcast_to([128, D]), [dLn])
            self.ld(mraw[:], L["mng"][:, :], [dMr])
            pm, dpm = self.bank()
            self.tr(pm[:, 0:8], mraw[0:8, :], [dMr], [dpm])
            self.cp("dve", mngT[:], pm[:, 0:8], [dpm], [dMn])
            self.ldc(Wa[:], wsrc(L["w_bm"], 0, D), [dWa])
            for k in range(8):
                self.act(Wa[:, k, :], Wa[:, k, :], AF.Identity, [dMn], [dWa], scale=mngT[:, k:k + 1])
            self.ldc(Wg[:], wsrc(w_in, GM0, D), [dWg])
            cnt = [0]

            def branch(src, dsrc, stage):
                for b, (t0, tn) in enumerate(TBLK):
                    for oc in range(8):
                        py, dpy = self.bank(); pg, dpg = self.bank()
                        for k in range(8):
                            self.mm(py[:, 0:tn], Wa[:, k, oc * 128:(oc + 1) * 128], src[:, k, t0:t0 + tn], k == 0, k == 7, [dWa, dsrc(b)], [dpy])
                        for k in range(8):
                            self.mm(pg[:, 0:tn], Wg[:, k, oc * 128:(oc + 1) * 128], hT[:, k, t0:t0 + tn], k == 0, k == 7, [dWg, dHT[b]], [dpg])
                        i = cnt[0] % 2; cnt[0] += 1
                        self.act(sg[i][:, 0:tn], pg[:, 0:tn], AF.Sigmoid, [dpg], [dsg[i]])
                        if stage == 1:
                            self.tt("dve", zt[:, oc, 0:tn], sg[i][:, 0:tn], py[:, 0:tn], ALU.mult, [dsg[i], dpy], [dzt])
                        else:
                            self.tt("dve", za[i][:, 0:tn], sg[i][:, 0:tn], py[:, 0:tn], ALU.mult, [dsg[i], dpy], [dza[i]])
                            self.tt("pool", hgT[:, oc, t0:t0 + tn], hgT[:, oc, t0:t0 + tn], za[i][:, 0:tn], ALU.add, [dza[i]], [dHG[b]])
                    if stage == 1:
                        self.cp("pool", hgT[:, :, t0:t0 + tn], zt[:, :, 0:tn], [dzt], [dHG[b]])

            branch(hgT, lambda b: dHG[b], 1)
            self.ldc(Wa[:], wsrc(L["w_ba"], 0, D), [dWa])
            self.ldc(Wg[:], wsrc(w_in, GA0, D), [dWg])
            branch(oaT, lambda b: dOA, 2)
            if "mixT" in self.debug:
                self.dbg("mixT", hgT[:], dHG, [128, 8, NT])
            S.emit()
        with ExitStack() as es:
            sb = lambda name, shape, dt=F32: es.enter_context(nc.sbuf_tensor(self.uname(name), list(shape), dt))
            Wo = sb("Wo", [128, 8, D], BF16); dWo = Dep()
            xres = sb("xres", [128, NTILE, D]); dX = [Dep() for _ in range(NTILE)]
            tG = [sb(f"tG{i}", [128, 512]) for i in range(2)]; dtG = [Dep(), Dep()]
            lnG = sb("lnG", [128, D]); lnB = sb("lnB", [128, D]); dLn = Dep()
            st = sb("lnst", [128, 7, NTILE]); dSt = Dep()
            junk = sb("junk", [128, D], BF16); dJ = Dep()
            self.memset("pool", st[:], 0.0, [dSt])
            self.ldc(Wo[:], wsrc(L["w_out"], 0, D), [dWo])
            if next_ada is not None:
                hflat = hT[:].rearrange("p c t -> p (c t)")
                a_wa = [hflat[:, 0:8 * D].rearrange("p (k n) -> p k n", k=8), hflat[:, 8 * D:16 * D].rearrange("p (k n) -> p k n", k=8)]
                a_dw = [Dep(), Dep()]
                a_bG = sb("a_bG", [128, D]); a_tmpG = sb("a_tmpG", [128, 512]); a_dbG = Dep(); a_dtG = Dep()
                self.ada_load(next_ada[0], 0, next_ada[1], a_wa[0], a_dw[0])
                self.ada_load(next_ada[0], 1, next_ada[1], a_wa[1], a_dw[1])
            self.ld(lnG[:], L["ln"]["ln2_g"][0:1, :].broadcast_to([128, D]), [dLn])
            self.ld(lnB[:], L["ln"]["ln2_b"][0:1, :].broadcast_to([128, D]), [dLn])
            for i in range(NTILE):
                nt = tile_nt(i)
                src = L["x1p"][128 * i:128 * i + nt, :] if i < 16 else L["x1s"][:, :]
                self.ld(xres[0:nt, i, :], src, [dX[i]])
                self.S.op("act", lambda e, i=i, nt=nt: e.mul(out=xres[0:nt, i, :], in_=xres[0:nt, i, :], mul=ALPHA), [dX[i]], [dX[i]])
            for i in range(NTILE):
                nt = tile_nt(i)
                which = 0 if i < 16 else 1
                for cb in range(2):
                    po, dpo = self.bank()
                    for k in range(8):
                        self.mm(po[0:nt, :], hgT[:, k, 128 * i:128 * i + nt], Wo[:, k, cb * 512:(cb + 1) * 512], k == 0, k == 7,
                                [dHG[min(i // 4, 4)], dWo], [dpo])
                    self.tt("dve", tG[cb][0:nt, :], po[0:nt, :], self.G[0:nt, which, cb * 512:(cb + 1) * 512], ALU.mult, [dpo, self.dG], [dtG[cb]])
                    self.tt("pool", xres[0:nt, i, cb * 512:(cb + 1) * 512], xres[0:nt, i, cb * 512:(cb + 1) * 512], tG[cb][0:nt, :], ALU.add,
                            [dtG[cb]], [dX[i]])
            if next_ada is not None:
                self.ada_compute(next_ada[0], 0, a_wa[0], a_dw[0], a_bG, a_dbG, a_tmpG, a_dtG)
                self.ada_load(next_ada[0], 2, next_ada[1], a_wa[0], a_dw[0])
                self.ada_compute(next_ada[0], 1, a_wa[1], a_dw[1], a_bG, a_dbG, a_tmpG, a_dtG)
            self.layer_norm_tiles(xres, dX, lnG, lnB, dLn, st, dSt, junk, dJ, LN_EPS)
            for i in range(NTILE):
                nt = tile_nt(i)
                dst = L["x2p"][128 * i:128 * i + nt, :] if i < 16 else L["x2s"][:, :]
                self.store(dst, xres[0:nt, i, :], [dX[i]])
            if next_ada is not None:
                self.ada_compute(next_ada[0], 2, a_wa[0], a_dw[0], a_bG, a_dbG, a_tmpG, a_dtG)
                self.ada_finish()
            S.emit()
            if "x2" in self.debug:
                o1 = nc.dram_tensor("dbg_x2p", [SEQ, D], F32, kind="ExternalOutput").ap()
                o2 = nc.dram_tensor("dbg_x2s", [NS, D], F32, kind="ExternalOutput").ap()
                self.store(o1[:, :], L["x2p"][:, :], [])
                self.store(o2[:, :], L["x2s"][:, :], [])
                S.emit()


_CACHE = {}


def _consts():
    half = 32
    inv = (10000.0 ** (-np.arange(half, dtype=np.float32) / half)).astype(np.float32)
    pos = np.concatenate([np.arange(SEQ), PAST + (np.arange(NS) % 4)]).astype(np.float32)
    ang = pos[None, :] * inv[:, None]
    cos = np.cos(ang).astype(np.float32); sin = np.sin(ang).astype(np.float32)
    ropec = np.concatenate([cos, cos, cos, cos], 0)
    ropes = np.concatenate([-sin, sin, -sin, sin], 0)
    ident = np.eye(128, dtype=np.float32)
    s = np.arange(128)[:, None]; t = np.arange(128)[None, :]
    m_le = (s <= t).astype(np.float32)
    m_gt = (s > t).astype(np.float32)
    blk = ((s // 4) == (t // 4)).astype(np.float32)
    m_blkc = blk * m_le
    m_cache = np.zeros((128, 128), np.float32)
    for col in range(128):
        tt = col % 4
        m_cache[:, col] = (np.arange(128) > tt)
    last = np.zeros((128, 128), np.float32)
    last[:, 0:16] = ((np.arange(128)[:, None] // 4) == np.arange(16)[None, :])
    cmask = np.concatenate([m_le, m_gt, m_blkc, blk, m_cache, last], 1)
    bm = ((np.arange(64)[None, :] // 4) == np.arange(16)[:, None]).astype(np.float32)
    bmask = np.broadcast_to(bm.reshape(1, 1024), (128, 1024)).copy()
    return dict(ident=ident, ropec=ropec.astype(np.float32), ropes=ropes.astype(np.float32), cmask=cmask.astype(np.float32),
                bmask=bmask)


def kernel(**inputs):
    debug = tuple(os.environ.get("MK_DEBUG", "").split(",")) if os.environ.get("MK_DEBUG") else ()
    key = debug
    if key not in _CACHE:
        kb = KB(debug)
        kb.build()
        _CACHE[key] = kb
    kb = _CACHE[key]
    f = lambda a: np.ascontiguousarray(np.asarray(a, dtype=np.float32))
    I = {k: f(v) for k, v in inputs.items()}
    cst = _consts()
    shared = dict(
        w_ada=I["w_ada"][0], b_ada=I["b_ada"][0].reshape(72, 128), b_ada_row=I["b_ada"][0].reshape(9, D),
        w_up1=I["w_ffn1_up"][0], w_dn1=I["w_ffn1_down"][0], w_up2=I["w_ffn2_up"][0], w_dn2=I["w_ffn2_down"][0],
        ln1_g=I["ln1_g"], ln1_b=I["ln1_b"], ln2_g=I["ln2_g"], ln2_b=I["ln2_b"], ln3_g=I["ln3_g"], ln3_b=I["ln3_b"],
        w_in=I["w_in"][0], b_ig=I["b_igate"], b_fg=I["b_fgate"], mng=I["m_norm_g"][0].reshape(8, 128), sinks=I["sinks"],
        w_bm=I["w_branch_m"][0], w_ba=I["w_branch_a"][0], w_out=I["w_out"][0], **cst)
    in_maps = []
    for c in range(8):
        sl = slice(16 * c, 16 * c + 16)
        m = dict(shared)
        m["xp"] = I["x_prompt"][c]
        m["xs"] = I["x_sample"][sl].reshape(NS, D)
        m["c_all"] = np.concatenate([I["c_prompt"][c:c + 1], I["c_sample"][sl]], 0)
        m["stC"] = I["state_mlstm_C"][0][sl]
        m["stn"] = I["state_mlstm_n"][0][sl].reshape(16, 512)
        m["stm"] = I["state_mlstm_m"][0][sl]
        m["ck"] = I["cache_swa_k"][0][sl].reshape(16, 128, 256)
        m["cv"] = I["cache_swa_v"][0][sl].reshape(16, 128, 256)
        in_maps.append(m)
    res = run_bass_kernel_spmd(kb.nc, in_maps, core_ids=list(range(8)))
    R = res.results
    kernel.last_results = R
    cat = lambda k: np.stack([np.asarray(r[k]) for r in R], 0)
    y_p = cat("yp").reshape(8, SEQ, D)
    y_s = cat("ys").reshape(128, 4, D)
    C_p = cat("Cp").reshape(1, 8, 4, 256, 128)
    n_p = cat("np").reshape(1, 8, 4, 128)
    m_p = cat("mp").reshape(1, 8, 4)
    k_p = cat("kp").reshape(1, 8, 128, 4, 64)
    v_p = cat("vp").reshape(1, 8, 128, 4, 64)
    C_s = cat("Cs").reshape(1, 128, 4, 256, 128)
    n_s = cat("ns").reshape(1, 128, 4, 128)
    m_s = cat("ms").reshape(1, 128, 4)
    k_s = cat("ks").reshape(1, 128, 128, 4, 64)
    v_s = cat("vs").reshape(1, 128, 128, 4, 64)
    return tuple(np.ascontiguousarray(a, dtype=np.float32) for a in (y_p, y_s, C_p, n_p, m_p, k_p, v_p, C_s, n_s, m_s, k_s, v_s))
```
